# Optimizing a Trainium2 kernel written in Bass

```python
import math
import jax, jax.numpy as jnp
from jax import lax
import numpy as np

D_MODEL = 1024
BATCH = 16
SEQ = 256
DEPTH = 4
DEC_BATCH = 8
DEC_SEQ = 4096
PAST_LEN = 512

GRID_W = 64
N_MIXERS = 2
N_RET = (DEPTH + 1) // 2
N_ATTN = DEPTH // 2
RET_HEADS = 4
RET_DK = 256
RET_DV = 512
RET_CHUNK = 128
RET_IN = RET_HEADS * (2 * RET_DK + 2 * RET_DV)
ATT_HQ = 16
ATT_HKV = 4
ATT_G = ATT_HQ // ATT_HKV
ATT_DH = 64
WINDOW = 128
ATT_BLK = 128
ATT_IN = (ATT_HQ + 2 * ATT_HKV) * ATT_DH
ROPE_BASE = 10000.0
PEER_HEADS = 8
PEER_NKEYS = 128
PEER_EXPERTS = PEER_NKEYS * PEER_NKEYS
PEER_DQ = 256
PEER_DHALF = PEER_DQ // 2
PEER_TOPK = 16
PEER_CHUNK = 128
DEEPNORM_ALPHA = (2.0 * DEPTH) ** 0.25
DEEPNORM_BETA = (8.0 * DEPTH) ** -0.25
LN_EPS = 1e-5
GN_EPS = 1e-5
NEG_INF = -1e30

kernel_name = 'hybrid_retention_swa_peer_diffusion_step'


def layer_norm(x, g, b):
    xf = x.astype(jnp.float32)
    mu = jnp.mean(xf, -1, keepdims=True)
    var = jnp.mean(jnp.square(xf - mu), -1, keepdims=True)
    y = (xf - mu) * lax.rsqrt(var + LN_EPS) * g.astype(jnp.float32) + b.astype(jnp.float32)
    return y.astype(x.dtype)


def modulation(cond, w, b):
    m = jnp.einsum('nd,de->ne', jax.nn.silu(cond), w) + b
    return jnp.split(m[:, None, :], 6, axis=-1)


def rope_angles(pos, dim):
    inv = ROPE_BASE ** (-jnp.arange(0, dim, 2, dtype=jnp.float32) / dim)
    ang = pos.astype(jnp.float32)[:, None] * inv[None, :]
    return jnp.cos(ang), jnp.sin(ang)


def apply_rope(x, cos, sin):
    half = x.shape[-1] // 2
    x1, x2 = x[..., :half], x[..., half:]
    cs = cos[None, :, None, :].astype(x.dtype)
    sn = sin[None, :, None, :].astype(x.dtype)
    return jnp.concatenate([x1 * cs - x2 * sn, x1 * sn + x2 * cs], -1)


def axial_rope(x):
    L = x.shape[1]
    rows = L // GRID_W
    t = jnp.arange(rows * GRID_W)
    half = x.shape[-1] // 2
    cr, sr = rope_angles(t // GRID_W, half)
    cc, sc = rope_angles(t % GRID_W, half)
    return jnp.concatenate([apply_rope(x[..., :half], cr, sr), apply_rope(x[..., half:], cc, sc)], -1)


def chunk_retention(q, k, v, log_g, s0, strict):
    B, L, H, _ = q.shape
    dv = v.shape[-1]
    nc = L // RET_CHUNK

    def chunks(a):
        return a.reshape(B, nc, RET_CHUNK, H, a.shape[-1]).transpose(1, 0, 3, 2, 4)

    idx = jnp.arange(RET_CHUNK, dtype=jnp.float32)
    diff = idx[:, None] - idx[None, :]
    mask = (diff > 0) if strict else (diff >= 0)
    dmat = jnp.where(mask[None], jnp.exp(jnp.where(mask, diff, 0.0)[None] * log_g[:, None, None]), 0.0)
    q_dec = jnp.exp((idx + 1.0)[None, :] * log_g[:, None])
    k_dec = jnp.exp((RET_CHUNK - 1.0 - idx)[None, :] * log_g[:, None])
    c_dec = jnp.exp(RET_CHUNK * log_g)

    def step(S, qkv):
        qc, kc, vc = qkv
        att = jnp.einsum('bhid,bhjd->bhij', qc, kc) * dmat
        o = jnp.einsum('bhij,bhjv->bhiv', att, vc) + jnp.einsum('bhid,bhdv->bhiv', qc * q_dec[..., None], S)
        S = S * c_dec[:, None, None] + jnp.einsum('bhjd,bhjv->bhdv', kc * k_dec[..., None], vc)
        return S, o

    s_fin, o = lax.scan(step, s0, (chunks(q), chunks(k), chunks(v)))
    o = o.transpose(1, 0, 3, 2, 4).reshape(B, L, H, dv)
    return o, s_fin


def bidir_retention(q, k, v, log_gf, log_gb, s0f, s0b):
    of, sf = chunk_retention(q, k, v, log_gf, s0f, False)
    ob, sb = chunk_retention(q[:, ::-1], k[:, ::-1], v[:, ::-1], log_gb, s0b, True)
    return of + ob[:, ::-1], sf, sb


def retention_mixer(h, s0f, s0b, w_in, w_out, decay, latent):
    B, L, _ = h.shape
    z = h @ w_in
    q, k, v, g = jnp.split(z, [RET_HEADS * RET_DK, 2 * RET_HEADS * RET_DK,
                               2 * RET_HEADS * RET_DK + RET_HEADS * RET_DV], -1)
    q = q.reshape(B, L, RET_HEADS, RET_DK)
    k = k.reshape(B, L, RET_HEADS, RET_DK) * (RET_DK ** -0.5)
    v = v.reshape(B, L, RET_HEADS, RET_DV)
    if latent:
        cos, sin = rope_angles(jnp.arange(L), RET_DK)
        q = apply_rope(q, cos, sin)
        k = apply_rope(k, cos, sin)
    log_g = jax.nn.log_sigmoid(decay.astype(jnp.float32))
    f32 = jnp.float32
    o, sf, sb = bidir_retention(q.astype(f32), k.astype(f32), v.astype(f32), log_g[0], log_g[1],
                                s0f.astype(f32), s0b.astype(f32))
    mu = jnp.mean(o, -1, keepdims=True)
    var = jnp.mean(jnp.square(o - mu), -1, keepdims=True)
    o = ((o - mu) * lax.rsqrt(var + GN_EPS)).reshape(B, L, RET_HEADS * RET_DV).astype(h.dtype)
    y = (jax.nn.silu(g) * o) @ w_out
    return y, sf, sb


def attn_project(h, w_in):
    B, L, _ = h.shape
    z = h @ w_in
    q, k, v = jnp.split(z, [ATT_HQ * ATT_DH, (ATT_HQ + ATT_HKV) * ATT_DH], -1)
    return (q.reshape(B, L, ATT_HQ, ATT_DH), k.reshape(B, L, ATT_HKV, ATT_DH),
            v.reshape(B, L, ATT_HKV, ATT_DH))


def attn_context(h, w_in, w_out, sink):
    B, L, _ = h.shape
    q, k, v = attn_project(h, w_in)
    qg = q.reshape(B, L, ATT_HKV, ATT_G, ATT_DH)
    s = jnp.einsum('bqhgd,bkhd->bhgqk', qg, k).astype(jnp.float32) * (ATT_DH ** -0.5)
    s_sink = jnp.broadcast_to(sink.reshape(ATT_HKV, ATT_G, 1, 1).astype(jnp.float32), s.shape[:-1] + (1,))
    p = jax.nn.softmax(jnp.concatenate([s, s_sink], -1), -1)[..., :L].astype(v.dtype)
    o = jnp.einsum('bhgqk,bkhd->bqhgd', p, v).reshape(B, L, ATT_HQ * ATT_DH)
    return o @ w_out, k, v


def attn_latent(h, ck, cv, w_in, w_out, sink):
    B, L, _ = h.shape
    Lc = ck.shape[1]
    nb = L // ATT_BLK
    q, k, v = attn_project(h, w_in)
    q = axial_rope(q)
    k = axial_rope(k)
    qg = q.reshape(B, L, ATT_HKV, ATT_G, ATT_DH)
    pad = ((0, 0), (ATT_BLK, ATT_BLK), (0, 0), (0, 0))
    kp = jnp.pad(k, pad)
    vp = jnp.pad(v, pad)
    ck = ck.astype(k.dtype)
    cv = cv.astype(v.dtype)
    i = jnp.arange(ATT_BLK)[:, None]
    j = jnp.arange(3 * ATT_BLK)[None, :]
    rel = j - ATT_BLK - i
    sink_f = sink.reshape(ATT_HKV, ATT_G, 1, 1).astype(jnp.float32)
    scale = ATT_DH ** -0.5

    def block(b):
        start = b * ATT_BLK
        qb = lax.dynamic_slice_in_dim(qg, start, ATT_BLK, axis=1)
        kb = lax.dynamic_slice_in_dim(kp, start, 3 * ATT_BLK, axis=1)
        vb = lax.dynamic_slice_in_dim(vp, start, 3 * ATT_BLK, axis=1)
        kpos = start - ATT_BLK + j
        valid = (jnp.abs(rel) <= WINDOW) & (kpos >= 0) & (kpos < L)
        s_loc = jnp.einsum('bqhgd,bkhd->bhgqk', qb, kb).astype(jnp.float32) * scale
        s_loc = jnp.where(valid, s_loc, NEG_INF)
        s_ctx = jnp.einsum('bqhgd,bchd->bhgqc', qb, ck).astype(jnp.float32) * scale
        s_sink = jnp.broadcast_to(sink_f, s_loc.shape[:-1] + (1,))
        p = jax.nn.softmax(jnp.concatenate([s_loc, s_ctx, s_sink], -1), -1).astype(v.dtype)
        return (jnp.einsum('bhgqk,bkhd->bqhgd', p[..., :3 * ATT_BLK], vb)
                + jnp.einsum('bhgqc,bchd->bqhgd', p[..., 3 * ATT_BLK:3 * ATT_BLK + Lc], cv))

    o = lax.map(block, jnp.arange(nb))
    o = o.transpose(1, 0, 2, 3, 4, 5).reshape(B, L, ATT_HQ * ATT_DH)
    return o @ w_out


def peer_ffn(h, wq, keys, u, v):
    B, L, D = h.shape
    xs = h.reshape((B * L) // PEER_CHUNK, PEER_CHUNK, D)

    def chunk(xc):
        q = (xc @ wq).reshape(PEER_CHUNK, PEER_HEADS, 2, PEER_DHALF)
        s = jnp.einsum('tphd,phnd->tphn', q, keys).astype(jnp.float32)
        sv, si = lax.top_k(s, PEER_TOPK)
        comb = (sv[:, :, 0, :, None] + sv[:, :, 1, None, :]).reshape(PEER_CHUNK, PEER_HEADS, PEER_TOPK * PEER_TOPK)
        cs, ci = lax.top_k(comb, PEER_TOPK)
        i1 = jnp.take_along_axis(si[:, :, 0], ci // PEER_TOPK, -1)
        i2 = jnp.take_along_axis(si[:, :, 1], ci % PEER_TOPK, -1)
        e = i1 * PEER_NKEYS + i2
        w = jax.nn.softmax(cs, -1)
        a = jax.nn.gelu(jnp.einsum('tpkd,td->tpk', u[e], xc))
        coef = (w * a.astype(jnp.float32)).astype(xc.dtype)
        return jnp.einsum('tpk,tpkd->td', coef, v[e])

    return lax.map(chunk, xs).reshape(B, L, D)


def setup_inputs(seed: int = 0) -> dict:
    key = jax.random.key(seed)
    ks = jax.random.split(key, 22)
    f32 = jnp.float32
    D = D_MODEL

    def nrm(k, shape, scale):
        return jax.random.normal(k, shape, f32) * scale

    gamma0 = 1.0 - 2.0 ** (-5.0 - jnp.arange(RET_HEADS, dtype=f32))
    logit0 = jnp.log(gamma0) - jnp.log1p(-gamma0)
    return {
        'x_prompt': nrm(ks[0], (BATCH, SEQ, D), 1.0),
        'x_sample': nrm(ks[1], (DEC_BATCH, DEC_SEQ, D), 1.0),
        'state_ret_fwd': nrm(ks[2], (DEC_BATCH, N_RET, RET_HEADS, RET_DK, RET_DV), 0.5),
        'state_ret_bwd': nrm(ks[3], (DEC_BATCH, N_RET, RET_HEADS, RET_DK, RET_DV), 0.5),
        'cache_k': nrm(ks[4], (DEC_BATCH, N_ATTN, PAST_LEN, ATT_HKV, ATT_DH), 1.0),
        'cache_v': nrm(ks[5], (DEC_BATCH, N_ATTN, PAST_LEN, ATT_HKV, ATT_DH), 1.0),
        'c': nrm(ks[6], (DEC_BATCH, D), 1.0),
        'c_ctx': nrm(ks[7], (D,), 1.0),
        'mod_w': nrm(ks[8], (DEPTH, D, 6 * D), 0.5 * D ** -0.5),
        'mod_b': nrm(ks[9], (DEPTH, 6 * D), 0.02),
        'ln_g': 1.0 + nrm(ks[10], (DEPTH, 2, D), 0.02),
        'ln_b': nrm(ks[11], (DEPTH, 2, D), 0.02),
        'ret_w_in': nrm(ks[12], (N_RET, D, RET_IN), D ** -0.5),
        'ret_w_out': nrm(ks[13], (N_RET, RET_HEADS * RET_DV, D), DEEPNORM_BETA * (RET_HEADS * RET_DV) ** -0.5),
        'ret_decay': logit0 + nrm(ks[14], (N_RET, 2, RET_HEADS), 0.1),
        'attn_w_in': nrm(ks[15], (N_ATTN, D, ATT_IN), D ** -0.5),
        'attn_w_out': nrm(ks[16], (N_ATTN, ATT_HQ * ATT_DH, D), DEEPNORM_BETA * (ATT_HQ * ATT_DH) ** -0.5),
        'attn_sink': nrm(ks[17], (N_ATTN, ATT_HQ), 0.5),
        'peer_wq': nrm(ks[18], (DEPTH, D, PEER_HEADS * PEER_DQ), D ** -0.5),
        'peer_keys': nrm(ks[19], (DEPTH, PEER_HEADS, 2, PEER_NKEYS, PEER_DHALF), PEER_DHALF ** -0.5),
        'peer_u': nrm(ks[20], (DEPTH, PEER_EXPERTS, D), D ** -0.5),
        'peer_v': nrm(ks[21], (DEPTH, PEER_EXPERTS, D), DEEPNORM_BETA * PEER_HEADS ** -0.5),
    }


def reference(x_prompt, x_sample, state_ret_fwd, state_ret_bwd, cache_k, cache_v, c, c_ctx,
              mod_w, mod_b, ln_g, ln_b, ret_w_in, ret_w_out, ret_decay, attn_w_in, attn_w_out, attn_sink,
              peer_wq, peer_keys, peer_u, peer_v):
    xp, xs = x_prompt, x_sample
    B = xp.shape[0]
    new_sf, new_sb, new_k, new_v = [], [], [], []
    for i in range(DEPTH):
        j = i // N_MIXERS
        shp1, scp1, gp1, shp2, scp2, gp2 = modulation(c_ctx[None, :], mod_w[i], mod_b[i])
        shs1, scs1, gs1, shs2, scs2, gs2 = modulation(c, mod_w[i], mod_b[i])
        hp = xp * (1.0 + scp1) + shp1
        hs = xs * (1.0 + scs1) + shs1
        if i % N_MIXERS == 0:
            z0 = jnp.zeros((B, RET_HEADS, RET_DK, RET_DV), jnp.float32)
            op, sf, sb = retention_mixer(hp, z0, z0, ret_w_in[j], ret_w_out[j], ret_decay[j], False)
            os_, _, _ = retention_mixer(hs, state_ret_fwd[:, j], state_ret_bwd[:, j],
                                        ret_w_in[j], ret_w_out[j], ret_decay[j], True)
            new_sf.append(sf)
            new_sb.append(sb)
        else:
            op, kc, vc = attn_context(hp, attn_w_in[j], attn_w_out[j], attn_sink[j])
            os_ = attn_latent(hs, cache_k[:, j], cache_v[:, j], attn_w_in[j], attn_w_out[j], attn_sink[j])
            new_k.append(kc)
            new_v.append(vc)
        xp = layer_norm(DEEPNORM_ALPHA * xp + gp1 * op, ln_g[i, 0], ln_b[i, 0])
        xs = layer_norm(DEEPNORM_ALPHA * xs + gs1 * os_, ln_g[i, 0], ln_b[i, 0])
        hp = xp * (1.0 + scp2) + shp2
        hs = xs * (1.0 + scs2) + shs2
        fp = peer_ffn(hp, peer_wq[i], peer_keys[i], peer_u[i], peer_v[i])
        fs = peer_ffn(hs, peer_wq[i], peer_keys[i], peer_u[i], peer_v[i])
        xp = layer_norm(DEEPNORM_ALPHA * xp + gp2 * fp, ln_g[i, 1], ln_b[i, 1])
        xs = layer_norm(DEEPNORM_ALPHA * xs + gs2 * fs, ln_g[i, 1], ln_b[i, 1])
    return (xp, xs, jnp.stack(new_sf, 1), jnp.stack(new_sb, 1), jnp.stack(new_k, 1), jnp.stack(new_v, 1))
```

```python
import numpy as np
from contextlib import ExitStack
import concourse.bass as bass
import concourse.mybir as mybir
from concourse.bass_utils import run_bass_kernel_spmd

F32 = mybir.dt.float32
BF16 = mybir.dt.bfloat16
I32 = mybir.dt.int32
U32 = mybir.dt.uint32
ALU = mybir.AluOpType
AF = mybir.ActivationFunctionType
AX = mybir.AxisListType


class Res:
    __slots__ = ("name", "lw", "rd", "sem", "t")

    def __init__(self, name, t=None):
        self.name = name
        self.lw = None
        self.rd = {}
        self.sem = None
        self.t = t

    def __getitem__(self, k):
        return self.t[k]


class Op:
    __slots__ = ("eng", "fn", "reads", "writes", "dma", "deps", "hasdep", "ev", "waits", "idx")


class Sched:
    ENGS = ("pe", "dve", "act", "pool", "sp")

    def __init__(self, nc, es, max_dma_sems=94):
        self.nc = nc
        self.es = es
        self.es_root = es
        self.ops = []
        self.nres = 0
        self.dma_sems = []
        self.max_dma_sems = max_dma_sems
        self.rr = 0
        self.sem_load = []

    def sbuf(self, name, shape, dtype):
        self.nres += 1
        name = "sb%d_%s" % (self.nres, name)
        t = self.es.enter_context(self.nc.sbuf_tensor(name, list(shape), dtype))
        return Res(name, t)

    def psum(self, name, shape, dtype):
        self.nres += 1
        name = "ps%d_%s" % (self.nres, name)
        t = self.es.enter_context(self.nc.psum_tensor(name, list(shape), dtype))
        return Res(name, t)

    def dram(self, name):
        return Res(name, None)

    def _dsem(self, res):
        if res.sem is None:
            if len(self.dma_sems) < self.max_dma_sems:
                s = self.es_root.enter_context(self.nc.semaphore("dq%d" % len(self.dma_sems)))
                self.dma_sems.append(s)
                res.sem = len(self.dma_sems) - 1
                self.sem_load.append(0)
            else:
                res.sem = min(range(len(self.dma_sems)), key=lambda k: self.sem_load[k])
                self.sem_load[res.sem] += 64
        return res.sem

    def add(self, eng, fn, reads=(), writes=(), dma=None):
        op = Op()
        op.eng = eng
        op.fn = fn
        op.reads = tuple(reads)
        op.writes = tuple(writes)
        op.dma = None if dma is None else self._dsem(dma)
        if op.dma is not None:
            self.sem_load[op.dma] += 1
        op.idx = len(self.ops)
        self.ops.append(op)
        return op

    def barrier(self):
        op = Op()
        op.eng = None
        op.fn = None
        op.reads = ()
        op.writes = ()
        op.dma = None
        op.idx = len(self.ops)
        self.ops.append(op)

    def pe(self, fn, reads=(), writes=()):
        return self.add("pe", fn, reads, writes)

    def dve(self, fn, reads=(), writes=()):
        return self.add("dve", fn, reads, writes)

    def act(self, fn, reads=(), writes=()):
        return self.add("act", fn, reads, writes)

    def pool(self, fn, reads=(), writes=()):
        return self.add("pool", fn, reads, writes)

    def dma(self, out, in_, reads, writes, semres, eng="sp", **kw):
        return self.add(eng, lambda e: e.dma_start(out=out, in_=in_, **kw), reads, writes, dma=semres)

    def finalize(self):
        nc = self.nc
        ops = self.ops
        latest = {}
        bar = {}
        bar_pending = set()
        for op in ops:
            if op.eng is None:
                bar = dict(latest)
                bar_pending = set(self.ENGS)
                op.deps = []
                op.hasdep = False
                continue
            deps = {}
            if op.eng in bar_pending:
                bar_pending.discard(op.eng)
                for x in bar.values():
                    deps[x.idx] = x
            for r in op.reads:
                if r.lw is not None:
                    deps[r.lw.idx] = r.lw
            for w in op.writes:
                if w.lw is not None:
                    deps[w.lw.idx] = w.lw
                for x in w.rd.values():
                    deps[x.idx] = x
            deps.pop(op.idx, None)
            dl = []
            for d in deps.values():
                if op.eng == "pe" and d.eng == "pe" and op.dma is None and d.dma is None:
                    continue
                dl.append(d)
            op.deps = dl
            op.hasdep = False
            key = op.eng if op.dma is None else ("d", op.dma)
            latest[key] = op
            for r in op.reads:
                r.rd[key] = op
            for w in op.writes:
                w.lw = op
                w.rd = {}
        for op in ops:
            for d in op.deps:
                d.hasdep = True
        EPOCH = 30000
        engsem = {}

        def get_engsem(e, ep):
            if (e, ep) not in engsem:
                engsem[(e, ep)] = self.es_root.enter_context(nc.semaphore("eng_%s_%d" % (e, ep)))
            return engsem[(e, ep)]
        engcnt = {e: 0 for e in self.ENGS}
        dcnt = [0] * len(self.dma_sems)
        waited = {e: {} for e in self.ENGS}
        nwaits = 0
        ops = [o for o in ops if o.eng is not None]
        for op in ops:
            need = {}
            for d in op.deps:
                if d.dma is not None:
                    k = ("d", d.dma)
                    v = dcnt[d.dma]
                else:
                    k = ("e", d.eng, d.ev[0])
                    v = d.ev[1]
                if need.get(k, 0) < v:
                    need[k] = v
            w = []
            wd = waited[op.eng]
            for k, v in need.items():
                if wd.get(k, 0) >= v:
                    continue
                wd[k] = v
                w.append((k, v))
            op.waits = w
            nwaits += len(w)
            if op.dma is not None:
                dcnt[op.dma] += 16
                op.ev = dcnt[op.dma]
            elif op.hasdep:
                engcnt[op.eng] += 1
                ep, c = divmod(engcnt[op.eng] - 1, EPOCH)
                op.ev = (ep, c + 1)
                get_engsem(op.eng, ep)
            else:
                op.ev = None
        self.final_dcnt = dcnt
        self.stats = dict(nops=len(ops), nwaits=nwaits, engcnt=dict(engcnt),
                          per_eng={e: sum(1 for o in ops if o.eng == e) for e in self.ENGS})
        per = {e: [o for o in ops if o.eng == e] for e in self.ENGS}
        dma_sems = self.dma_sems

        def semof(k):
            return dma_sems[k[1]] if k[0] == "d" else engsem[(k[1], k[2])]

        def run(eng_obj, name):
            for op in per[name]:
                for k, v in op.waits:
                    eng_obj.wait_ge(semof(k), v)
                ins = op.fn(eng_obj)
                if op.dma is not None:
                    ins.then_inc(dma_sems[op.dma], 16)
                elif op.ev is not None:
                    ins.then_inc(engsem[(name, op.ev[0])], 1)
            if name == "sp":
                for i, s in enumerate(dma_sems):
                    if dcnt[i] > 0:
                        eng_obj.wait_ge(s, dcnt[i])

        with nc.Block() as block:
            @block.tensor
            def _(e):
                run(e, "pe")

            @block.vector
            def _(e):
                run(e, "dve")

            @block.scalar
            def _(e):
                run(e, "act")

            @block.gpsimd
            def _(e):
                run(e, "pool")

            @block.sync
            def _(e):
                run(e, "sp")

D = 1024
ALPHA = 8.0 ** 0.25
EPS = 1e-5
NEXP = 16384
OPTS = dict(GS=4, uvbf16=True, actgelu=True, nogather=0, nodots=0, novside=0, noroute=0, uvtab=1)


def build_program(DEPTH=4, SQ=4096, peer=True):
    nc = bass.Bass("TRN2", target_bir_lowering=False)
    NRET = (DEPTH + 1) // 2
    NATT = DEPTH // 2
    NATTd = max(NATT, 1)
    NP = 4
    NS = SQ // 128
    NT = NP + NS
    seqs = [(0, 2, False, 0), (2, 2, False, 0), (4, NS, True, 1)]

    def din(name, shape, dt=F32):
        return nc.dram_tensor(name, list(shape), dt, kind="ExternalInput").ap()

    def dout(name, shape, dt=F32):
        return nc.dram_tensor(name, list(shape), dt, kind="ExternalOutput").ap()

    x_d = din("x", [NT * 128, D])
    cond_d = din("cond", [2, D])
    sretf_d = din("sret_f", [NRET, 4, 256, 512])
    sretb_d = din("sret_b", [NRET, 4, 256, 512])
    ck_d = din("ck", [NATTd, 512, 256])
    cv_d = din("cv", [NATTd, 512, 256])
    modw_d = din("mod_w", [DEPTH, D, 6 * D])
    modb_d = din("mod_b", [DEPTH, 6 * D])
    lng_d = din("ln_g", [DEPTH, 2, D])
    lnb_d = din("ln_b", [DEPTH, 2, D])
    rwin_d = din("ret_w_in", [NRET, D, 6144])
    rwout_d = din("ret_w_out", [NRET, 2048, D])
    rdec_d = din("ret_decay", [NRET, 8])
    awin_d = din("attn_w_in", [NATTd, D, 1536])
    awout_d = din("attn_w_out", [NATTd, D, D])
    asink_d = din("attn_sink", [NATTd, 16])
    pwq_d = din("peer_wq", [DEPTH, D, 2048])
    pkeys_d = din("peer_keys", [DEPTH, 16, 128, 128])
    puv_d = din("peer_uv", [DEPTH * NEXP, 2 * D])
    cst_d = din("cst", [128, 6, 128])
    pvec_d = din("pvec", [128, 8])
    iota_d = din("iota16", [128, 16])
    rrc_d = din("rope_ret_cos", [SQ, 128])
    rrs_d = din("rope_ret_sin", [SQ, 128])
    rac_d = din("rope_att_cos", [SQ, 32])
    ras_d = din("rope_att_sin", [SQ, 32])

    y_d = dout("y", [NT * 128, D])
    nsf_d = dout("nsf", [2, NRET, 4, 256, 512])
    nsb_d = dout("nsb", [2, NRET, 4, 256, 512])
    nk_d = dout("nk", [2, NATTd, 256, 256])
    nv_d = dout("nv", [2, NATTd, 256, 256])

    z_d = nc.dram_tensor("z_scr", [NT * 128, 6144], BF16).ap()
    pb_d = nc.dram_tensor("pb_scr", [NT * 128, 2048], F32).ap()
    qs_d = nc.dram_tensor("qs_scr", [NT * 128, 1024], BF16).ap()
    uvb_d = nc.dram_tensor("uvb_scr", [DEPTH * NEXP, 2 * D], BF16).ap() if OPTS["uvtab"] else None
    CVR = 512

    root = ExitStack()
    with root:
        S = Sched(nc, root)
        RIN = S.dram("inputs")
        R_y = [S.dram("y%d" % t) for t in range(NT)]
        R_z = [S.dram("z%d" % t) for t in range(NT)]
        R_pb = [S.dram("pb%d" % t) for t in range(NT)]
        R_qs = [S.dram("qs%d" % t) for t in range(NT)]
        R_o = S.dram("small_outs")
        R_uvb = [[S.dram("uvb%d_%d" % (i_, c_)) for c_ in range(NEXP // CVR)] for i_ in range(DEPTH)]
        cvt = S.dram("cvt")

        def convert_uv(i_):
            if not OPTS["uvtab"]:
                return
            for c_ in range(NEXP // CVR):
                r0 = i_ * NEXP + c_ * CVR
                S.dma(uvb_d[r0:r0 + CVR, :], puv_d[r0:r0 + CVR, :], [RIN], [R_uvb[i_][c_]], cvt, eng="pool")

        def rows(ap, t):
            return ap[t * 128:(t + 1) * 128, :]

        ident_f = S.sbuf("ident_f", [128, 128], F32)
        ident = S.sbuf("ident", [128, 128], BF16)
        ones1 = S.sbuf("ones1", [1, 128], F32)
        condT = S.sbuf("condT", [128, 2, 8], F32)
        condrep = S.sbuf("condrep", [128, 2, 8, 128], F32)
        modbc = S.sbuf("modbc", [128, 2, 3, D], F32)
        lnbc = S.sbuf("lnbc", [128, 2, D], F32)
        cst = S.sbuf("cst", [128, 6, 128], F32)
        pvec = S.sbuf("pvec", [128, 8], F32)
        iota16 = S.sbuf("iota16", [128, 16], F32)
        epsc = S.sbuf("epsc", [128, 1], F32)

        S.pool(lambda e: e.memset(ident_f[:], 0.0), [], [ident_f])
        S.pool(lambda e: e.affine_select(out=ident_f[:], in_=ident_f[:], pattern=[[-1, 128]],
                                         compare_op=ALU.not_equal, fill=1.0, base=0, channel_multiplier=1),
               [ident_f], [ident_f])
        S.dve(lambda e: e.tensor_copy(out=ident[:], in_=ident_f[:]), [ident_f], [ident])
        S.dve(lambda e: e.memset(ones1[:], 1.0), [], [ones1])
        S.dve(lambda e: e.memset(epsc[:], EPS), [], [epsc])
        S.dma(cst[:], cst_d, [RIN], [cst], cst)
        S.dma(pvec[:], pvec_d, [RIN], [pvec], pvec)
        S.dma(iota16[:], iota_d, [RIN], [iota16], iota16)
        S.dma(condT[:], cond_d.rearrange("j (kc p) -> p j kc", p=128), [RIN], [condT], condT,
              allow_slow_non_contiguous=True)
        S.act(lambda e: e.activation(out=condT[:], in_=condT[:], func=AF.Silu), [condT], [condT])
        S.dve(lambda e: e.tensor_copy(out=condrep[:].rearrange("p j k m -> p (j k) m"),
                                      in_=condT[:].rearrange("p j k -> p (j k)").unsqueeze(2).to_broadcast([128, 16, 128])),
              [condT], [condrep])

        def modulation(i, s):
            sc = ExitStack()
            S.es = sc
            with sc:
                wch = [S.sbuf("modw%d" % k, [128, 8, D], F32) for k in range(2)]
                brow = S.sbuf("modbrow", [1, 3 * D], F32)
                mps = [S.psum("modps%d" % k, [128, 512], F32) for k in range(2)]
                S.dma(brow[:], modb_d[i:i + 1, s * 3 * D:(s + 1) * 3 * D], [RIN], [brow], brow)
                S.dma(lnbc[:, 0, :], lng_d[i, s:s + 1, :].to_broadcast([128, D]), [RIN], [lnbc], lnbc)
                S.dma(lnbc[:, 1, :], lnb_d[i, s:s + 1, :].to_broadcast([128, D]), [RIN], [lnbc], lnbc)
                n = 0
                for blk in range(3):
                    w = wch[blk % 2]
                    c0 = (s * 3 + blk) * D
                    S.dma(w[:], modw_d[i, :, c0:c0 + D].rearrange("(kc p) n -> p kc n", p=128), [RIN], [w], w)
                    for j in range(2):
                        for nh in range(2):
                            ps = mps[n % 2]
                            n += 1
                            for kc in range(8):
                                S.pe(lambda e, ps=ps, j=j, kc=kc, w=w, nh=nh: e.matmul(
                                    ps[:], lhsT=condrep[:, j, kc, :], rhs=w[:, kc, nh * 512:(nh + 1) * 512],
                                    start=(kc == 0), stop=False), [condrep, w], [ps])
                            S.pe(lambda e, ps=ps, blk=blk, nh=nh: e.matmul(
                                ps[:], lhsT=ones1[:], rhs=brow[:, blk * D + nh * 512: blk * D + (nh + 1) * 512],
                                start=False, stop=True), [ones1, brow], [ps])
                            add = 1.0 if blk == 1 else 0.0
                            S.act(lambda e, ps=ps, j=j, blk=blk, nh=nh, add=add: e.activation(
                                out=modbc[:, j, blk, nh * 512:(nh + 1) * 512], in_=ps[:], func=AF.Identity, bias=add, scale=1.0)
                                if add else e.copy(out=modbc[:, j, blk, nh * 512:(nh + 1) * 512], in_=ps[:]),
                                [ps], [modbc])
                S.barrier()
            S.es = root

        def load_w(dst, src2d, N, tag):
            v = src2d.rearrange("(kc p) n -> p kc n", p=128)
            for n0 in range(0, N, 2048):
                n1 = min(N, n0 + 2048)
                S.dma(dst[:, :, n0:n1], v[:, :, n0:n1], [RIN], [dst], dst, eng="pool")

        def prologue(xt, j, hb, hT, tp, h32=None):
            tgt = h32 if h32 is not None else hb
            S.dve(lambda e: e.tensor_tensor(out=tgt[:], in0=xt[:], in1=modbc[:, j, 1, :], op=ALU.mult), [xt, modbc], [tgt])
            if h32 is not None:
                S.pool(lambda e: e.tensor_tensor(out=h32[:], in0=h32[:], in1=modbc[:, j, 0, :], op=ALU.add), [h32, modbc], [h32])
                S.act(lambda e: e.copy(out=hb[:], in_=h32[:]), [h32], [hb])
            else:
                S.pool(lambda e: e.tensor_tensor(out=hb[:], in0=hb[:], in1=modbc[:, j, 0, :], op=ALU.add), [hb, modbc], [hb])
            for kc in range(8):
                S.pe(lambda e, kc=kc: e.transpose(out=tp[:, kc * 128:(kc + 1) * 128], in_=hb[:, kc * 128:(kc + 1) * 128],
                                                  identity=ident[:]), [hb, ident], [tp])
            S.act(lambda e: e.copy(out=hT[:].rearrange("p a b -> p (a b)"), in_=tp[:]), [tp], [hT])

        def epilogue(Y, yreads, xt, j, t, tmp, r, st, mv, rstd):
            S.dve(lambda e: e.tensor_tensor(out=tmp[:], in0=Y[:], in1=modbc[:, j, 2, :], op=ALU.mult), [modbc] + yreads, [tmp])
            S.dve(lambda e: e.scalar_tensor_tensor(out=r[:], in0=xt[:], scalar=ALPHA, in1=tmp[:], op0=ALU.mult, op1=ALU.add),
                  [xt, tmp], [r])
            for c in range(2):
                S.dve(lambda e, c=c: e.bn_stats(out=st[:, c, :], in_=r[:, c * 512:(c + 1) * 512]), [r], [st])
            S.dve(lambda e: e.bn_aggr(out=mv[:], in_=st[:].rearrange("p a b -> p (a b)")), [st], [mv])
            S.act(lambda e: e.activation(out=rstd[:], in_=mv[:, 1:2], func=AF.Sqrt, bias=epsc[:], scale=1.0), [mv, epsc], [rstd])
            S.dve(lambda e: e.reciprocal(out=rstd[:], in_=rstd[:]), [rstd], [rstd])
            S.dve(lambda e: e.tensor_scalar(out=r[:], in0=r[:], scalar1=mv[:, 0:1], scalar2=rstd[:, 0:1],
                                            op0=ALU.subtract, op1=ALU.mult), [r, mv, rstd], [r])
            S.pool(lambda e: e.tensor_tensor(out=r[:], in0=r[:], in1=lnbc[:, 0, :], op=ALU.mult), [r, lnbc], [r])
            S.pool(lambda e: e.tensor_tensor(out=tmp[:], in0=r[:], in1=lnbc[:, 1, :], op=ALU.add), [r, lnbc], [tmp])
            S.dma(rows(y_d, t), tmp[:], [tmp], [R_y[t]], tmp)

        def xsrc(first):
            return x_d if first else y_d

        def xres(first, t):
            return RIN if first else R_y[t]

        def retention_layer(i, first):
            jr = i // 2
            sc0 = ExitStack()
            S.es = sc0
            with sc0:
                lg = S.sbuf("lg", [128, 8], F32)
                qdec = S.sbuf("qdec", [128, 2, 4], F32)
                kdec = S.sbuf("kdec", [128, 2, 4], F32)
                cdec = S.sbuf("cdec", [128, 2, 4], F32)
                dmask = S.sbuf("dmask", [128, 4, 128], F32)
                dtmp = S.sbuf("dtmp", [128, 128], F32)
                c128 = S.sbuf("c128", [128, 1], F32)
                S.dve(lambda e: e.memset(c128[:], 128.0), [], [c128])
                S.dma(lg[:], rdec_d[jr:jr + 1, :].to_broadcast([128, 8]), [RIN], [lg], lg)
                S.act(lambda e: e.activation(out=lg[:], in_=lg[:], func=AF.Exp, scale=-1.0), [lg], [lg])
                S.act(lambda e: e.activation(out=lg[:], in_=lg[:], func=AF.Ln, bias=1.0, scale=1.0), [lg], [lg])
                S.dve(lambda e: e.tensor_scalar(out=lg[:], in0=lg[:], scalar1=-1.0, scalar2=None, op0=ALU.mult), [lg], [lg])
                for h in range(4):
                    for dr in range(2):
                        col = dr * 4 + h
                        pq = 0 if dr == 0 else 1
                        pk = 2 if dr == 0 else 3
                        S.act(lambda e, col=col, pq=pq, dr=dr, h=h: e.activation(
                            out=qdec[:, dr, h:h + 1], in_=lg[:, col:col + 1], func=AF.Exp, scale=pvec[:, pq:pq + 1]),
                            [lg, pvec], [qdec])
                        S.act(lambda e, col=col, pk=pk, dr=dr, h=h: e.activation(
                            out=kdec[:, dr, h:h + 1], in_=lg[:, col:col + 1], func=AF.Exp, scale=pvec[:, pk:pk + 1]),
                            [lg, pvec], [kdec])
                        S.act(lambda e, col=col, dr=dr, h=h: e.activation(
                            out=cdec[:, dr, h:h + 1], in_=lg[:, col:col + 1], func=AF.Exp, scale=c128[:, 0:1]),
                            [lg, c128], [cdec])
                    S.act(lambda e, h=h: e.activation(out=dmask[:, h, :], in_=cst[:, 0, :], func=AF.Exp, scale=lg[:, h:h + 1]),
                          [cst, lg], [dmask])
                    S.dve(lambda e, h=h: e.tensor_tensor(out=dmask[:, h, :], in0=dmask[:, h, :], in1=cst[:, 1, :], op=ALU.mult),
                          [dmask, cst], [dmask])
                    S.act(lambda e, h=h: e.activation(out=dtmp[:], in_=cst[:, 2, :], func=AF.Exp, scale=lg[:, 4 + h:5 + h]),
                          [cst, lg], [dtmp])
                    S.dve(lambda e: e.tensor_tensor(out=dtmp[:], in0=dtmp[:], in1=cst[:, 3, :], op=ALU.mult), [dtmp, cst], [dtmp])
                    S.dve(lambda e, h=h: e.tensor_tensor(out=dmask[:, h, :], in0=dmask[:, h, :], in1=dtmp[:], op=ALU.add),
                          [dmask, dtmp], [dmask])

                scz = ExitStack()
                S.es = scz
                with scz:
                    win = S.sbuf("rwin", [128, 8, 6144], BF16)
                    load_w(win, rwin_d[jr], 6144, "rwin")
                    convert_uv(i)
                    xts = [S.sbuf("zx%d" % k, [128, D], F32) for k in range(2)]
                    hb = S.sbuf("zhb", [128, D], BF16)
                    hT = S.sbuf("zhT", [128, 8, 128], BF16)
                    qk32 = S.sbuf("zqk32", [128, 2048], F32)
                    ra = S.sbuf("zra", [128, 1024], F32)
                    rb = S.sbuf("zrb", [128, 1024], F32)
                    zrow = [S.sbuf("zrow%d" % k, [128, 6144], BF16) for k in range(2)]
                    rc = [S.sbuf("zrc%d" % k, [128, 2, 128], F32) for k in range(2)]
                    tp = S.psum("ztp", [128, 1024], BF16)
                    zps = [S.psum("zps%d" % k, [128, 512], F32) for k in range(4)]
                    for (t0, ntl, latent, j) in seqs:
                        for tt in range(ntl):
                            t = t0 + tt
                            xt = xts[t % 2]
                            zr = zrow[t % 2]
                            S.dma(xt[:], rows(xsrc(first), t), [xres(first, t)], [xt], xt)
                            prologue(xt, j, hb, hT, tp)
                            if latent:
                                rct = rc[t % 2]
                                S.dma(rct[:, 0, :], rrc_d[tt * 128:(tt + 1) * 128, :], [RIN], [rct], rct)
                                S.dma(rct[:, 1, :], rrs_d[tt * 128:(tt + 1) * 128, :], [RIN], [rct], rct)
                            for nb in range(12):
                                ps = zps[nb % 4]
                                for kc in range(8):
                                    S.pe(lambda e, ps=ps, kc=kc, nb=nb: e.matmul(
                                        ps[:], lhsT=hT[:, kc, :], rhs=win[:, kc, nb * 512:(nb + 1) * 512],
                                        start=(kc == 0), stop=(kc == 7)), [hT, win], [ps])
                                sl = slice(nb * 512, (nb + 1) * 512)
                                if nb < 4:
                                    scl = 1.0 if nb < 2 else 0.0625
                                    if latent:
                                        S.act(lambda e, ps=ps, sl=sl, scl=scl: e.mul(out=qk32[:, sl], in_=ps[:], mul=scl), [ps], [qk32])
                                    else:
                                        S.act(lambda e, ps=ps, sl=sl, scl=scl, zr=zr: e.mul(out=zr[:, sl], in_=ps[:], mul=scl), [ps], [zr])
                                elif nb < 8:
                                    S.act(lambda e, ps=ps, sl=sl, zr=zr: e.copy(out=zr[:, sl], in_=ps[:]), [ps], [zr])
                                else:
                                    S.act(lambda e, ps=ps, sl=sl, zr=zr: e.activation(out=zr[:, sl], in_=ps[:], func=AF.Silu), [ps], [zr])
                            if latent:
                                v4 = qk32[:].rearrange("p (a b c) -> p a b c", a=8, b=2)
                                x1 = v4[:, :, 0, :]
                                x2 = v4[:, :, 1, :]
                                cosb = rct[:, 0, :].unsqueeze(1).to_broadcast([128, 8, 128])
                                sinb = rct[:, 1, :].unsqueeze(1).to_broadcast([128, 8, 128])
                                o4 = zr[:, 0:2048].rearrange("p (a b c) -> p a b c", a=8, b=2)
                                ra3 = ra[:].rearrange("p (a c) -> p a c", a=8)
                                rb3 = rb[:].rearrange("p (a c) -> p a c", a=8)
                                S.dve(lambda e, ra3=ra3, x1=x1, cosb=cosb: e.tensor_tensor(out=ra3, in0=x1, in1=cosb, op=ALU.mult), [qk32, rct], [ra])
                                S.pool(lambda e, rb3=rb3, x2=x2, sinb=sinb: e.tensor_tensor(out=rb3, in0=x2, in1=sinb, op=ALU.mult), [qk32, rct], [rb])
                                S.dve(lambda e, o4=o4, ra3=ra3, rb3=rb3: e.tensor_tensor(out=o4[:, :, 0, :], in0=ra3, in1=rb3, op=ALU.subtract), [ra, rb], [zr])
                                S.dve(lambda e, ra3=ra3, x1=x1, sinb=sinb: e.tensor_tensor(out=ra3, in0=x1, in1=sinb, op=ALU.mult), [qk32, rct, zr], [ra])
                                S.pool(lambda e, rb3=rb3, x2=x2, cosb=cosb: e.tensor_tensor(out=rb3, in0=x2, in1=cosb, op=ALU.mult), [qk32, rct, zr], [rb])
                                S.dve(lambda e, o4=o4, ra3=ra3, rb3=rb3: e.tensor_tensor(out=o4[:, :, 1, :], in0=ra3, in1=rb3, op=ALU.add), [ra, rb], [zr])
                            S.dma(rows(z_d, t), zr[:], [zr], [R_z[t]], zr)
                    S.barrier()
                S.es = sc0

                sca = ExitStack()
                S.es = sca
                with sca:
                    qkv = [S.sbuf("aqkv%d" % k, [128, 4096], BF16) for k in range(2)]
                    qdb = S.sbuf("aqdb", [128, 1024], BF16)
                    kdb = S.sbuf("akdb", [128, 1024], BF16)
                    qdbT = S.sbuf("aqdbT", [128, 8, 128], BF16)
                    Sb32 = S.sbuf("aSb32", [128, 4, 2, 512], F32)
                    Sbb = S.sbuf("aSbb", [128, 4, 2, 512], BF16)
                    pbt = [S.sbuf("apbt%d" % k, [128, 2048], F32) for k in range(2)]
                    tp = S.psum("atp", [128, 1024], BF16)
                    aps = [S.psum("aps%d" % k, [128, 512], F32) for k in range(6)]
                    for si, (t0, ntl, latent, j) in enumerate(seqs):
                        if latent:
                            S.dma(Sb32[:].rearrange("p h c v -> p (h c) v"),
                                  sretb_d[jr].rearrange("h (c p) v -> p (h c) v", p=128), [RIN], [Sb32], Sb32)
                        else:
                            S.dve(lambda e: e.memset(Sb32[:].rearrange("p h c v -> p (h c v)"), 0.0), [], [Sb32])
                        S.act(lambda e: e.copy(out=Sbb[:].rearrange("p h c v -> p (h c v)"),
                                               in_=Sb32[:].rearrange("p h c v -> p (h c v)")), [Sb32], [Sbb])
                        for tt in reversed(range(ntl)):
                            t = t0 + tt
                            qv = qkv[t % 2]
                            S.dma(qv[:], z_d[t * 128:(t + 1) * 128, 0:4096], [R_z[t]], [qv], qv)
                            for h in range(4):
                                S.dve(lambda e, h=h, qv=qv: e.tensor_scalar(out=qdb[:, h * 256:(h + 1) * 256], in0=qv[:, h * 256:(h + 1) * 256],
                                                                      scalar1=qdec[:, 1, h:h + 1], scalar2=None, op0=ALU.mult),
                                      [qv, qdec], [qdb])
                                S.pool(lambda e, h=h, qv=qv: e.tensor_scalar(out=kdb[:, h * 256:(h + 1) * 256],
                                                                       in0=qv[:, 1024 + h * 256:1024 + (h + 1) * 256],
                                                                       scalar1=kdec[:, 1, h:h + 1], scalar2=None, op0=ALU.mult),
                                       [qv, kdec], [kdb])
                            for c in range(8):
                                S.pe(lambda e, c=c: e.transpose(out=tp[:, c * 128:(c + 1) * 128], in_=qdb[:, c * 128:(c + 1) * 128],
                                                                identity=ident[:]), [qdb, ident], [tp])
                            S.act(lambda e: e.copy(out=qdbT[:].rearrange("p a b -> p (a b)"), in_=tp[:]), [tp], [qdbT])
                            pt = pbt[t % 2]
                            for h in range(4):
                                ps = aps[h % 2]
                                for dc in range(2):
                                    S.pe(lambda e, ps=ps, h=h, dc=dc: e.matmul(ps[:], lhsT=qdbT[:, h * 2 + dc, :], rhs=Sbb[:, h, dc, :],
                                                                          start=(dc == 0), stop=(dc == 1)), [qdbT, Sbb], [ps])
                                S.act(lambda e, ps=ps, h=h, pt=pt: e.copy(out=pt[:, h * 512:(h + 1) * 512], in_=ps[:]), [ps], [pt])
                            S.dma(rows(pb_d, t), pt[:], [pt], [R_pb[t]], pt)
                            n = 0
                            for h in range(4):
                                for dc in range(2):
                                    ps = aps[2 + n % 4]
                                    n += 1
                                    S.pe(lambda e, ps=ps, h=h, dc=dc, qv=qv: e.matmul(
                                        ps[:], lhsT=kdb[:, h * 256 + dc * 128: h * 256 + (dc + 1) * 128],
                                        rhs=qv[:, 2048 + h * 512: 2048 + (h + 1) * 512], start=True, stop=True), [kdb, qv], [ps])
                                    S.dve(lambda e, ps=ps, h=h, dc=dc: e.scalar_tensor_tensor(
                                        out=Sb32[:, h, dc, :], in0=Sb32[:, h, dc, :], scalar=cdec[:, 1, h:h + 1], in1=ps[:],
                                        op0=ALU.mult, op1=ALU.add), [Sb32, cdec, ps], [Sb32])
                            S.act(lambda e: e.copy(out=Sbb[:].rearrange("p h c v -> p (h c v)"),
                                                   in_=Sb32[:].rearrange("p h c v -> p (h c v)")), [Sb32], [Sbb])
                        if not latent:
                            S.dma(nsb_d[si, jr].rearrange("h (c p) v -> p (h c) v", p=128),
                                  Sb32[:].rearrange("p h c v -> p (h c) v"), [Sb32], [R_o], Sb32)
                    S.barrier()
                S.es = sc0

                scb = ExitStack()
                S.es = scb
                with scb:
                    wout = S.sbuf("rwout", [128, 16, D], BF16)
                    load_w(wout, rwout_d[jr], D, "rwout")
                    zt = [S.sbuf("bz%d" % k, [128, 6144], BF16) for k in range(2)]
                    pbt = S.sbuf("bpbt", [128, 2048], F32)
                    xts = [S.sbuf("bx%d" % k, [128, D], F32) for k in range(2)]
                    qdf = S.sbuf("bqdf", [128, 1024], BF16)
                    kdf = S.sbuf("bkdf", [128, 1024], BF16)
                    QT = S.sbuf("bQT", [128, 24, 128], BF16)
                    attm = S.sbuf("battm", [128, 512], BF16)
                    Sf32 = S.sbuf("bSf32", [128, 4, 2, 512], F32)
                    Sfb = S.sbuf("bSfb", [128, 4, 2, 512], BF16)
                    o32 = S.sbuf("bo32", [128, 2048], F32)
                    go = S.sbuf("bgo", [128, 2048], BF16)
                    goT = S.sbuf("bgoT", [128, 16, 128], BF16)
                    gst = S.sbuf("bgst", [128, 4, 6], F32)
                    gmv = S.sbuf("bgmv", [128, 4, 2], F32)
                    grs = S.sbuf("bgrs", [128, 4], F32)
                    tmp = S.sbuf("btmp", [128, D], F32)
                    r = S.sbuf("br", [128, D], F32)
                    st = S.sbuf("bst", [128, 2, 6], F32)
                    mv = S.sbuf("bmv", [128, 2], F32)
                    rstd = S.sbuf("brstd", [128, 1], F32)
                    tp = S.psum("btp", [128, 1024], BF16)
                    bps = [S.psum("bps%d" % k, [128, 512], F32) for k in range(5)]
                    Y = S.psum("bY", [128, 1024], F32)
                    for si, (t0, ntl, latent, j) in enumerate(seqs):
                        if latent:
                            S.dma(Sf32[:].rearrange("p h c v -> p (h c) v"),
                                  sretf_d[jr].rearrange("h (c p) v -> p (h c) v", p=128), [RIN], [Sf32], Sf32)
                        else:
                            S.dve(lambda e: e.memset(Sf32[:].rearrange("p h c v -> p (h c v)"), 0.0), [], [Sf32])
                        S.act(lambda e: e.copy(out=Sfb[:].rearrange("p h c v -> p (h c v)"),
                                               in_=Sf32[:].rearrange("p h c v -> p (h c v)")), [Sf32], [Sfb])
                        for tt in range(ntl):
                            t = t0 + tt
                            z = zt[t % 2]
                            xt = xts[t % 2]
                            S.dma(z[:], rows(z_d, t), [R_z[t]], [z], z)
                            S.dma(pbt[:], rows(pb_d, t), [R_pb[t]], [pbt], pbt)
                            S.dma(xt[:], rows(xsrc(first), t), [xres(first, t)], [xt], xt)
                            for h in range(4):
                                S.dve(lambda e, h=h, z=z: e.tensor_scalar(out=qdf[:, h * 256:(h + 1) * 256], in0=z[:, h * 256:(h + 1) * 256],
                                                                     scalar1=qdec[:, 0, h:h + 1], scalar2=None, op0=ALU.mult),
                                      [z, qdec], [qdf])
                                S.pool(lambda e, h=h, z=z: e.tensor_scalar(out=kdf[:, h * 256:(h + 1) * 256],
                                                                      in0=z[:, 1024 + h * 256:1024 + (h + 1) * 256],
                                                                      scalar1=kdec[:, 0, h:h + 1], scalar2=None, op0=ALU.mult),
                                       [z, kdec], [kdf])
                            for grp, (src, off, rr) in enumerate([(z, 0, [z]), (qdf, 0, [qdf]), (z, 1024, [z])]):
                                for c in range(8):
                                    S.pe(lambda e, c=c, src=src, off=off: e.transpose(
                                        out=tp[:, c * 128:(c + 1) * 128], in_=src[:, off + c * 128: off + (c + 1) * 128],
                                        identity=ident[:]), rr + [ident], [tp])
                                S.act(lambda e, grp=grp: e.copy(out=QT[:, grp * 8:(grp + 1) * 8, :].rearrange("p a b -> p (a b)"), in_=tp[:]),
                                      [tp], [QT])
                            pa = bps[4]
                            for h in range(4):
                                for dc in range(2):
                                    S.pe(lambda e, h=h, dc=dc: e.matmul(pa[:, h * 128:(h + 1) * 128], lhsT=QT[:, 16 + h * 2 + dc, :],
                                                                        rhs=QT[:, h * 2 + dc, :], start=(dc == 0), stop=(dc == 1)),
                                         [QT], [pa])
                            S.dve(lambda e: e.tensor_tensor(out=attm[:], in0=pa[:], in1=dmask[:].rearrange("p h i -> p (h i)"), op=ALU.mult),
                                  [pa, dmask], [attm])
                            for h in range(4):
                                ps = bps[h]
                                S.pe(lambda e, ps=ps, h=h, z=z: e.matmul(ps[:], lhsT=attm[:, h * 128:(h + 1) * 128],
                                                                    rhs=z[:, 2048 + h * 512:2048 + (h + 1) * 512], start=True, stop=False),
                                     [attm, z], [ps])
                                for dc in range(2):
                                    S.pe(lambda e, ps=ps, h=h, dc=dc: e.matmul(ps[:], lhsT=QT[:, 8 + h * 2 + dc, :], rhs=Sfb[:, h, dc, :],
                                                                          start=False, stop=(dc == 1)), [QT, Sfb], [ps])
                                S.dve(lambda e, ps=ps, h=h: e.tensor_tensor(out=o32[:, h * 512:(h + 1) * 512], in0=ps[:],
                                                                       in1=pbt[:, h * 512:(h + 1) * 512], op=ALU.add), [ps, pbt], [o32])
                                S.dve(lambda e, h=h: e.bn_stats(out=gst[:, h, :], in_=o32[:, h * 512:(h + 1) * 512]), [o32], [gst])
                                S.dve(lambda e, h=h: e.bn_aggr(out=gmv[:, h, :], in_=gst[:, h, :]), [gst], [gmv])
                            S.act(lambda e: e.activation(out=grs[:], in_=gmv[:, :, 1], func=AF.Sqrt, bias=epsc[:], scale=1.0), [gmv, epsc], [grs])
                            S.dve(lambda e: e.reciprocal(out=grs[:], in_=grs[:]), [grs], [grs])
                            for h in range(4):
                                S.dve(lambda e, h=h: e.tensor_scalar(out=o32[:, h * 512:(h + 1) * 512], in0=o32[:, h * 512:(h + 1) * 512],
                                                                     scalar1=gmv[:, h, 0:1], scalar2=grs[:, h:h + 1],
                                                                     op0=ALU.subtract, op1=ALU.mult), [o32, gmv, grs], [o32])
                            S.pool(lambda e, z=z: e.tensor_tensor(out=go[:], in0=o32[:], in1=z[:, 4096:6144], op=ALU.mult), [o32, z], [go])
                            for half in range(2):
                                for c in range(8):
                                    cc = half * 8 + c
                                    S.pe(lambda e, c=c, cc=cc: e.transpose(out=tp[:, c * 128:(c + 1) * 128], in_=go[:, cc * 128:(cc + 1) * 128],
                                                                           identity=ident[:]), [go, ident], [tp])
                                S.act(lambda e, half=half: e.copy(out=goT[:, half * 8:(half + 1) * 8, :].rearrange("p a b -> p (a b)"), in_=tp[:]),
                                      [tp], [goT])
                            for nh in range(2):
                                for kc in range(16):
                                    S.pe(lambda e, nh=nh, kc=kc: e.matmul(Y[:, nh * 512:(nh + 1) * 512], lhsT=goT[:, kc, :],
                                                                          rhs=wout[:, kc, nh * 512:(nh + 1) * 512],
                                                                          start=(kc == 0), stop=(kc == 15)), [goT, wout], [Y])
                            epilogue(Y, [Y], xt, j, t, tmp, r, st, mv, rstd)
                            n = 0
                            for h in range(4):
                                for dc in range(2):
                                    ps = bps[n % 4]
                                    n += 1
                                    S.pe(lambda e, ps=ps, h=h, dc=dc, z=z: e.matmul(
                                        ps[:], lhsT=kdf[:, h * 256 + dc * 128: h * 256 + (dc + 1) * 128],
                                        rhs=z[:, 2048 + h * 512: 2048 + (h + 1) * 512], start=True, stop=True), [kdf, z], [ps])
                                    S.dve(lambda e, ps=ps, h=h, dc=dc: e.scalar_tensor_tensor(
                                        out=Sf32[:, h, dc, :], in0=Sf32[:, h, dc, :], scalar=cdec[:, 0, h:h + 1], in1=ps[:],
                                        op0=ALU.mult, op1=ALU.add), [Sf32, cdec, ps], [Sf32])
                            S.act(lambda e: e.copy(out=Sfb[:].rearrange("p h c v -> p (h c v)"),
                                                   in_=Sf32[:].rearrange("p h c v -> p (h c v)")), [Sf32], [Sfb])
                        if not latent:
                            S.dma(nsf_d[si, jr].rearrange("h (c p) v -> p (h c) v", p=128),
                                  Sf32[:].rearrange("p h c v -> p (h c) v"), [Sf32], [R_o], Sf32)
                    S.barrier()
                S.es = sc0
            S.es = root

        def attention_layer(i, first):
            ja = i // 2
            sc0 = ExitStack()
            S.es = sc0
            with sc0:
                win = S.sbuf("awin", [128, 8, 1536], BF16)
                wout = S.sbuf("awout", [128, 8, D], BF16)
                load_w(win, awin_d[ja], 1536, "awin")
                load_w(wout, awout_d[ja], D, "awout")
                convert_uv(i)
                esink = S.sbuf("esink", [128, 16], F32)
                S.dma(esink[:], asink_d[ja:ja + 1, :].to_broadcast([128, 16]), [RIN], [esink], esink)
                S.act(lambda e: e.activation(out=esink[:], in_=esink[:], func=AF.Exp), [esink], [esink])
                mprev = S.sbuf("mprev", [128, 4, 128], BF16)
                mnext = S.sbuf("mnext", [128, 4, 128], BF16)
                S.dve(lambda e: e.tensor_copy(out=mprev[:], in_=cst[:, 4, :].unsqueeze(1).to_broadcast([128, 4, 128])), [cst], [mprev])
                S.dve(lambda e: e.tensor_copy(out=mnext[:], in_=cst[:, 5, :].unsqueeze(1).to_broadcast([128, 4, 128])), [cst], [mnext])
                NSm = max(NS, 2)
                KT = S.sbuf("aKT", [64, 4, NSm * 128], BF16)
                VL = S.sbuf("aVL", [128, NSm, 4, 65], BF16)
                CKT = S.sbuf("aCKT", [64, 4, 512], BF16)
                CV = S.sbuf("aCV", [128, 4, 4, 65], BF16)
                c32 = S.sbuf("ac32", [128, 4, 256], F32)
                cb = S.sbuf("acb", [128, 4, 256], BF16)
                xts = [S.sbuf("ax%d" % k, [128, D], F32) for k in range(2)]
                hb = S.sbuf("ahb", [128, D], BF16)
                hT = S.sbuf("ahT", [128, 8, 128], BF16)
                q32 = S.sbuf("aq32", [128, 1536], F32)
                ra = S.sbuf("ara", [128, 512], F32)
                rb = S.sbuf("arb", [128, 512], F32)
                qb = [S.sbuf("aqb%d" % k, [128, 1024], BF16) for k in range(2)]
                kb = S.sbuf("akb", [128, 256], BF16)
                rc = [S.sbuf("arc%d" % k, [128, 2, 32], F32) for k in range(2)]
                qT = S.sbuf("aqT", [64, 16, 128], BF16)
                PT = [S.sbuf("aPT%d" % k, [128, 512], BF16) for k in range(7)]
                rden = S.sbuf("arden", [128, 16], F32)
                on = S.sbuf("aon", [128, 1024], BF16)
                onT = S.sbuf("aonT", [128, 8, 128], BF16)
                tmp = S.sbuf("atmp", [128, D], F32)
                r = S.sbuf("ar", [128, D], F32)
                st = S.sbuf("ast", [128, 2, 6], F32)
                mv = S.sbuf("amv", [128, 2], F32)
                rstd = S.sbuf("arstd", [128, 1], F32)
                tp = S.psum("atp", [128, 1024], BF16)
                tq = S.psum("atq", [128, 2048], BF16)
                sps = [S.psum("asps%d" % k, [128, 512], F32) for k in range(2)]
                ops_ = [S.psum("aops%d" % k, [128, 4, 65], F32) for k in range(1)]
                Y = S.psum("aY", [128, 1024], F32)

                S.dve(lambda e: e.memset(VL[:].rearrange("p a b c -> p (a b c)"), 1.0), [], [VL])
                S.dve(lambda e: e.memset(CV[:].rearrange("p a b c -> p (a b c)"), 1.0), [], [CV])
                S.dma(c32[:], ck_d[ja].rearrange("(b p) f -> p b f", p=128), [RIN], [c32], c32)
                S.dve(lambda e: e.tensor_copy(out=cb[:], in_=c32[:]), [c32], [cb])
                for b in range(4):
                    for g in range(4):
                        S.pe(lambda e, b=b, g=g: e.transpose(out=tp[0:64, g * 128:(g + 1) * 128], in_=cb[:, b, g * 64:(g + 1) * 64],
                                                             identity=ident[:]), [cb, ident], [tp])
                    S.act(lambda e, b=b: e.copy(out=CKT[:, :, b * 128:(b + 1) * 128],
                                                in_=tp[0:64, 0:512].rearrange("p (g k) -> p g k", g=4)), [tp], [CKT])
                S.dma(c32[:], cv_d[ja].rearrange("(b p) f -> p b f", p=128), [RIN], [c32], c32)
                S.dve(lambda e: e.tensor_copy(out=CV[:, :, :, 0:64], in_=c32[:].rearrange("p b (g d) -> p b g d", g=4)), [c32], [CV])

                for si, (t0, ntl, latent, j) in enumerate(seqs):
                    for tt in range(ntl):
                        t = t0 + tt
                        xt = xts[t % 2]
                        S.dma(xt[:], rows(xsrc(first), t), [xres(first, t)], [xt], xt)
                        prologue(xt, j, hb, hT, tp)
                        if latent:
                            rct = rc[t % 2]
                            S.dma(rct[:, 0, :], rac_d[tt * 128:(tt + 1) * 128, :], [RIN], [rct], rct)
                            S.dma(rct[:, 1, :], ras_d[tt * 128:(tt + 1) * 128, :], [RIN], [rct], rct)
                        for nb in range(3):
                            ps = sps[nb % 2]
                            for kc in range(8):
                                S.pe(lambda e, ps=ps, kc=kc, nb=nb: e.matmul(ps[:], lhsT=hT[:, kc, :], rhs=win[:, kc, nb * 512:(nb + 1) * 512],
                                                                        start=(kc == 0), stop=(kc == 7)), [hT, win], [ps])
                            S.act(lambda e, ps=ps, nb=nb: e.copy(out=q32[:, nb * 512:(nb + 1) * 512], in_=ps[:]), [ps], [q32])
                        if not latent:
                            S.dma(nk_d[si, ja, tt * 128:(tt + 1) * 128, :], q32[:, 1024:1280], [q32], [R_o], q32)
                            S.dma(nv_d[si, ja, tt * 128:(tt + 1) * 128, :], q32[:, 1280:1536], [q32], [R_o], q32)
                        q_out = qb[t % 2]
                        if latent:
                            v5 = q32[:, 0:1280].rearrange("p (h a b c) -> p h a b c", h=20, a=2, b=2)
                            for a in range(2):
                                x1 = v5[:, :, a, 0, :]
                                x2 = v5[:, :, a, 1, :]
                                cosb = rct[:, 0, a * 16:(a + 1) * 16].unsqueeze(1).to_broadcast([128, 20, 16])
                                sinb = rct[:, 1, a * 16:(a + 1) * 16].unsqueeze(1).to_broadcast([128, 20, 16])
                                ra3 = ra[:, 0:320].rearrange("p (h c) -> p h c", h=20)
                                rb3 = rb[:, 0:320].rearrange("p (h c) -> p h c", h=20)
                                ra3b = ra[:, 320:640].rearrange("p (h c) -> p h c", h=20) if False else None
                                S.dve(lambda e, x1=x1, cosb=cosb, ra3=ra3: e.tensor_tensor(out=ra3, in0=x1, in1=cosb, op=ALU.mult), [q32, rct], [ra])
                                S.pool(lambda e, x2=x2, sinb=sinb, rb3=rb3: e.tensor_tensor(out=rb3, in0=x2, in1=sinb, op=ALU.mult), [q32, rct], [rb])
                                S.dve(lambda e, ra3=ra3, rb3=rb3: e.tensor_tensor(out=ra3, in0=ra3, in1=rb3, op=ALU.subtract), [ra, rb], [ra])
                                S.pool(lambda e, x1=x1, sinb=sinb, rb3=rb3: e.tensor_tensor(out=rb3, in0=x1, in1=sinb, op=ALU.mult), [q32, rct, ra], [rb])
                                S.dve(lambda e, x1=x1, ra3=ra3: e.tensor_copy(out=x1, in_=ra3), [ra, rb], [q32])
                                S.dve(lambda e, x2=x2, cosb=cosb, ra3=ra3: e.tensor_tensor(out=ra3, in0=x2, in1=cosb, op=ALU.mult), [q32, rct], [ra])
                                S.dve(lambda e, x2=x2, ra3=ra3, rb3=rb3: e.tensor_tensor(out=x2, in0=ra3, in1=rb3, op=ALU.add), [ra, rb], [q32])
                        S.act(lambda e, q_out=q_out: e.mul(out=q_out[:], in_=q32[:, 0:1024], mul=0.125), [q32], [q_out])
                        S.dma(rows(qs_d, t), q_out[:], [q_out], [R_qs[t]], q_out)
                        S.dve(lambda e: e.tensor_copy(out=kb[:], in_=q32[:, 1024:1280]), [q32], [kb])
                        S.dve(lambda e, tt=tt: e.tensor_copy(out=VL[:, tt, :, 0:64], in_=q32[:, 1280:1536].rearrange("p (g d) -> p g d", g=4)),
                              [q32], [VL])
                        for g in range(4):
                            S.pe(lambda e, g=g: e.transpose(out=tp[0:64, g * 128:(g + 1) * 128], in_=kb[:, g * 64:(g + 1) * 64],
                                                            identity=ident[:]), [kb, ident], [tp])
                        S.act(lambda e, tt=tt: e.copy(out=KT[:, :, tt * 128:(tt + 1) * 128],
                                                      in_=tp[0:64, 0:512].rearrange("p (g k) -> p g k", g=4)), [tp], [KT])
                    for tt in range(ntl):
                        t = t0 + tt
                        xt = xts[t % 2]
                        qin = qb[t % 2]
                        S.dma(xt[:], rows(xsrc(first), t), [xres(first, t)], [xt], xt)
                        S.dma(qin[:], rows(qs_d, t), [R_qs[t]], [qin], qin)
                        for h in range(16):
                            S.pe(lambda e, h=h, qin=qin: e.transpose(out=tq[0:64, h * 128:(h + 1) * 128], in_=qin[:, h * 64:(h + 1) * 64],
                                                                     identity=ident[:]), [qin, ident], [tq])
                        S.act(lambda e: e.copy(out=qT[:].rearrange("p a b -> p (a b)"), in_=tq[0:64, :]), [tq], [qT])
                        if latent:
                            blocks = []
                            if tt > 0:
                                blocks.append(("loc", tt - 1, mprev))
                            blocks.append(("loc", tt, None))
                            if tt < ntl - 1:
                                blocks.append(("loc", tt + 1, mnext))
                            for b in range(4):
                                blocks.append(("ctx", b, None))
                        else:
                            blocks = [("loc", b, None) for b in range(ntl)]
                        for g in range(4):
                            for bi, (kind, b, msk) in enumerate(blocks):
                                ps = sps[bi % 2]
                                kT_ap = (KT[:, g, b * 128:(b + 1) * 128] if kind == "loc" else CKT[:, g, b * 128:(b + 1) * 128])
                                kres = KT if kind == "loc" else CKT
                                S.pe(lambda e, ps=ps, kT_ap=kT_ap, g=g, msk=msk: e.matmul(
                                    ps[:], lhsT=kT_ap, rhs=qT[:, 4 * g:4 * g + 4, :], start=True, stop=(msk is None)), [kres, qT], [ps])
                                if msk is not None:
                                    S.pe(lambda e, ps=ps, msk=msk: e.matmul(ps[:], lhsT=ident[:], rhs=msk[:].rearrange("p a b -> p (a b)"),
                                                                           start=False, stop=True), [ident, msk], [ps])
                                S.act(lambda e, ps=ps, bi=bi: e.activation(out=PT[bi][:], in_=ps[:], func=AF.Exp), [ps], [PT[bi]])
                            og = ops_[0]
                            for hh in range(4):
                                for bi, (kind, b, msk) in enumerate(blocks):
                                    v_ap = (VL[:, b, g, :] if kind == "loc" else CV[:, b, g, :])
                                    vres = VL if kind == "loc" else CV
                                    S.pe(lambda e, og=og, hh=hh, bi=bi, v_ap=v_ap, nb=len(blocks): e.matmul(
                                        og[:, hh, :], lhsT=PT[bi][:, hh * 128:(hh + 1) * 128], rhs=v_ap,
                                        start=(bi == 0), stop=(bi == nb - 1)), [PT[bi], vres], [og])
                            S.dve(lambda e, og=og, g=g: e.tensor_tensor(out=rden[:, 4 * g:4 * g + 4], in0=og[:, :, 64],
                                                                   in1=esink[:, 4 * g:4 * g + 4], op=ALU.add), [og, esink], [rden])
                            S.dve(lambda e, g=g: e.reciprocal(out=rden[:, 4 * g:4 * g + 4], in_=rden[:, 4 * g:4 * g + 4]), [rden], [rden])
                            S.dve(lambda e, og=og, g=g: e.tensor_tensor(
                                out=on[:, g * 256:(g + 1) * 256].rearrange("p (h d) -> p h d", h=4), in0=og[:, :, 0:64],
                                in1=rden[:, 4 * g:4 * g + 4].unsqueeze(2).to_broadcast([128, 4, 64]), op=ALU.mult), [og, rden], [on])
                        for c in range(8):
                            S.pe(lambda e, c=c: e.transpose(out=tp[:, c * 128:(c + 1) * 128], in_=on[:, c * 128:(c + 1) * 128],
                                                            identity=ident[:]), [on, ident], [tp])
                        S.act(lambda e: e.copy(out=onT[:].rearrange("p a b -> p (a b)"), in_=tp[:]), [tp], [onT])
                        for nh in range(2):
                            for kc in range(8):
                                S.pe(lambda e, nh=nh, kc=kc: e.matmul(Y[:, nh * 512:(nh + 1) * 512], lhsT=onT[:, kc, :],
                                                                      rhs=wout[:, kc, nh * 512:(nh + 1) * 512],
                                                                      start=(kc == 0), stop=(kc == 7)), [onT, wout], [Y])
                        epilogue(Y, [Y], xt, j, t, tmp, r, st, mv, rstd)
                S.barrier()
            S.es = root

        def peer_layer(i):
            GS = OPTS["GS"]
            UVDT = BF16 if OPTS["uvbf16"] else F32
            NG = 128 // GS
            sc0 = ExitStack()
            S.es = sc0
            with sc0:
                wq = S.sbuf("pwq", [128, 8, 2048], BF16)
                load_w(wq, pwq_d[i], 2048, "pwq")
                keysT = S.sbuf("pkeysT", [128, 16, 128], BF16)
                xts = [S.sbuf("px%d" % k, [128, D], F32) for k in range(2)]
                h32s = [S.sbuf("ph32%d" % k, [128, D], F32) for k in range(2)]
                eis = [S.sbuf("pei%d" % k, [128, 128], I32) for k in range(2)]
                wsms = [S.sbuf("pwsm%d" % k, [128, 8, 16], F32) for k in range(2)]
                hb = S.sbuf("phb", [128, D], BF16)
                hT = S.sbuf("phT", [128, 8, 128], BF16)
                qb = S.sbuf("pqb", [128, 2048], BF16)
                qT = S.sbuf("pqT", [128, 16, 128], BF16)
                s32 = S.sbuf("ps32", [128, 16, 128], F32)
                wk = S.sbuf("pwk", [128, 256], F32)
                sv = S.sbuf("psv", [128, 16, 16], F32)
                siu = S.sbuf("psiu", [128, 16, 16], U32)
                sif = S.sbuf("psif", [128, 16, 16], F32)
                comb = S.sbuf("pcomb", [128, 256], F32)
                cs = S.sbuf("pcs", [128, 8, 16], F32)
                ciu = S.sbuf("pciu", [128, 8, 16], U32)
                cia = S.sbuf("pcia", [128, 8, 16], U32)
                cib = S.sbuf("pcib", [128, 8, 16], U32)
                caf = S.sbuf("pcaf", [128, 8, 16], F32)
                cbf = S.sbuf("pcbf", [128, 8, 16], F32)
                oh = S.sbuf("poh", [128, 16, 16], F32)
                i1f = S.sbuf("pi1f", [128, 8, 16], F32)
                i2f = S.sbuf("pi2f", [128, 8, 16], F32)
                wsum = S.sbuf("pwsum", [128, 8], F32)
                av = S.sbuf("pav", [128, 128], F32)
                ga = S.sbuf("pga", [128, 128], F32)
                gb_ = S.sbuf("pgb", [128, 128], F32)
                coef = S.sbuf("pcoef", [128, 128], F32)
                uvg = [[S.sbuf("puv%d_%d" % (k, s_), [128, 2 * D], UVDT) for s_ in range(GS)] for k in range(2)]
                tv = [S.sbuf("ptv%d" % k, [128, D], BF16) for k in range(3)]
                tmp = S.sbuf("ptmp", [128, D], F32)
                r = S.sbuf("pr", [128, D], F32)
                st = S.sbuf("pst", [128, 2, 6], F32)
                mv = S.sbuf("pmv", [128, 2], F32)
                rstd = S.sbuf("prstd", [128, 1], F32)
                tp = S.psum("ptp", [128, 1024], BF16)
                tq = S.psum("ptq", [128, 2048], BF16)
                qps = [S.psum("pqps%d" % k, [128, 512], F32) for k in range(2)]
                Y = S.psum("pY", [128, 1024], F32)

                sck = ExitStack()
                S.es = sck
                with sck:
                    k32 = S.sbuf("pk32", [128, 16, 128], F32)
                    kbf = S.sbuf("pkbf", [128, 16, 128], BF16)
                    S.dma(k32[:], pkeys_d[i].rearrange("c n d -> n c d"), [RIN], [k32], k32)
                    S.dve(lambda e: e.tensor_copy(out=kbf[:], in_=k32[:]), [k32], [kbf])
                    for half in range(2):
                        for c in range(8):
                            cc = half * 8 + c
                            S.pe(lambda e, c=c, cc=cc: e.transpose(out=tp[:, c * 128:(c + 1) * 128], in_=kbf[:, cc, :], identity=ident[:]),
                                 [kbf, ident], [tp])
                        S.act(lambda e, half=half: e.copy(out=keysT[:, half * 8:(half + 1) * 8, :].rearrange("p a b -> p (a b)"), in_=tp[:]),
                              [tp], [keysT])
                    S.barrier()
                S.es = sc0

                def stage_a(t, j, par):
                    xt = xts[par]
                    h32 = h32s[par]
                    ei = eis[par]
                    wsm = wsms[par]
                    S.dma(xt[:], rows(y_d, t), [R_y[t]], [xt], xt)
                    prologue(xt, j, hb, hT, tp, h32=h32)
                    yield
                    for nb in range(4):
                        ps = qps[nb % 2]
                        for kc in range(8):
                            S.pe(lambda e, ps=ps, kc=kc, nb=nb: e.matmul(ps[:], lhsT=hT[:, kc, :], rhs=wq[:, kc, nb * 512:(nb + 1) * 512],
                                                                    start=(kc == 0), stop=(kc == 7)), [hT, wq], [ps])
                        S.act(lambda e, ps=ps, nb=nb: e.copy(out=qb[:, nb * 512:(nb + 1) * 512], in_=ps[:]), [ps], [qb])
                    for c in range(16):
                        S.pe(lambda e, c=c: e.transpose(out=tq[:, c * 128:(c + 1) * 128], in_=qb[:, c * 128:(c + 1) * 128], identity=ident[:]),
                             [qb, ident], [tq])
                    S.act(lambda e: e.copy(out=qT[:].rearrange("p a b -> p (a b)"), in_=tq[:]), [tq], [qT])
                    for b4 in range(4):
                        ps = qps[b4 % 2]
                        for c4 in range(4):
                            c = b4 * 4 + c4
                            S.pe(lambda e, ps=ps, c=c, c4=c4: e.matmul(ps[:, c4 * 128:(c4 + 1) * 128], lhsT=qT[:, c, :], rhs=keysT[:, c, :],
                                                                  start=True, stop=True), [qT, keysT], [ps])
                        S.act(lambda e, ps=ps, b4=b4: e.copy(out=s32[:, b4 * 4:(b4 + 1) * 4, :].rearrange("p a b -> p (a b)"), in_=ps[:]),
                              [ps], [s32])
                    yield
                    for c in range(16):
                        S.dve(lambda e, c=c: e.max(out=sv[:, c, 0:8], in_=s32[:, c, :]), [s32], [sv])
                        S.dve(lambda e, c=c: e.max_index(out=siu[:, c, 0:8], in_max=sv[:, c, 0:8], in_values=s32[:, c, :]), [s32, sv], [siu])
                        S.dve(lambda e, c=c: e.match_replace(out=wk[:, 0:128], in_to_replace=sv[:, c, 0:8], in_values=s32[:, c, :],
                                                             imm_value=-1e30), [s32, sv], [wk])
                        S.dve(lambda e, c=c: e.max(out=sv[:, c, 8:16], in_=wk[:, 0:128]), [wk], [sv])
                        S.dve(lambda e, c=c: e.max_index(out=siu[:, c, 8:16], in_max=sv[:, c, 8:16], in_values=wk[:, 0:128]), [wk, sv], [siu])
                        yield
                    S.dve(lambda e: e.tensor_copy(out=sif[:], in_=siu[:]), [siu], [sif])
                    for p in range(8):
                        S.dve(lambda e, p=p: e.tensor_tensor(
                            out=comb[:].rearrange("p (a b) -> p a b", a=16),
                            in0=sv[:, 2 * p, :].unsqueeze(2).to_broadcast([128, 16, 16]),
                            in1=sv[:, 2 * p + 1, :].unsqueeze(1).to_broadcast([128, 16, 16]), op=ALU.add), [sv], [comb])
                        S.dve(lambda e, p=p: e.max(out=cs[:, p, 0:8], in_=comb[:]), [comb], [cs])
                        S.dve(lambda e, p=p: e.max_index(out=ciu[:, p, 0:8], in_max=cs[:, p, 0:8], in_values=comb[:]), [comb, cs], [ciu])
                        S.dve(lambda e, p=p: e.match_replace(out=wk[:], in_to_replace=cs[:, p, 0:8], in_values=comb[:],
                                                             imm_value=-1e30), [comb, cs], [wk])
                        S.dve(lambda e, p=p: e.max(out=cs[:, p, 8:16], in_=wk[:]), [wk], [cs])
                        S.dve(lambda e, p=p: e.max_index(out=ciu[:, p, 8:16], in_max=cs[:, p, 8:16], in_values=wk[:]), [wk, cs], [ciu])
                        yield
                    S.dve(lambda e: e.tensor_tensor(out=wsm[:], in0=cs[:], in1=cs[:, :, 0:1].to_broadcast([128, 8, 16]), op=ALU.subtract),
                          [cs], [wsm])
                    S.act(lambda e: e.activation(out=wsm[:], in_=wsm[:], func=AF.Exp), [wsm], [wsm])
                    S.dve(lambda e: e.reduce_sum(out=wsum[:], in_=wsm[:], axis=AX.X), [wsm], [wsum])
                    S.dve(lambda e: e.reciprocal(out=wsum[:], in_=wsum[:]), [wsum], [wsum])
                    S.dve(lambda e: e.tensor_tensor(out=wsm[:], in0=wsm[:], in1=wsum[:].unsqueeze(2).to_broadcast([128, 8, 16]), op=ALU.mult),
                          [wsm, wsum], [wsm])
                    yield
                    S.dve(lambda e: e.tensor_single_scalar(out=cia[:], in_=ciu[:], scalar=4, op=ALU.logical_shift_right), [ciu], [cia])
                    S.dve(lambda e: e.tensor_single_scalar(out=cib[:], in_=ciu[:], scalar=15, op=ALU.bitwise_and), [ciu], [cib])
                    S.dve(lambda e: e.tensor_copy(out=caf[:], in_=cia[:]), [cia], [caf])
                    S.dve(lambda e: e.tensor_copy(out=cbf[:], in_=cib[:]), [cib], [cbf])
                    yield
                    for (cf, which, dst) in ((caf, 0, i1f), (cbf, 1, i2f)):
                        for p in range(8):
                            S.dve(lambda e, cf=cf, p=p: e.tensor_tensor(
                                out=oh[:], in0=cf[:, p, :].unsqueeze(2).to_broadcast([128, 16, 16]),
                                in1=iota16[:].unsqueeze(1).to_broadcast([128, 16, 16]), op=ALU.is_equal), [cf, iota16], [oh])
                            S.dve(lambda e, which=which, p=p: e.tensor_tensor(
                                out=oh[:], in0=oh[:],
                                in1=sif[:, 2 * p + which, :].unsqueeze(1).to_broadcast([128, 16, 16]), op=ALU.mult), [oh, sif], [oh])
                            S.dve(lambda e, dst=dst, p=p: e.reduce_sum(out=dst[:, p, :], in_=oh[:], axis=AX.X), [oh], [dst])
                            yield
                    S.dve(lambda e: e.scalar_tensor_tensor(out=i1f[:], in0=i1f[:], scalar=128.0, in1=i2f[:], op0=ALU.mult, op1=ALU.add),
                          [i1f, i2f], [i1f])
                    S.dve(lambda e: e.tensor_scalar(out=i1f[:], in0=i1f[:], scalar1=float(i * NEXP), scalar2=None, op0=ALU.add), [i1f], [i1f])
                    S.dve(lambda e: e.tensor_copy(out=ei[:], in_=i1f[:].rearrange("p a b -> p (a b)")), [i1f], [ei])
                    yield

                def stage_b(t, j, par, nxt):
                    xt = xts[par]
                    h32 = h32s[par]
                    ei = eis[par]
                    wsm = wsms[par]
                    wflat = wsm[:].rearrange("p a b -> p (a b)")
                    nv = 0
                    for g in range(NG):
                        bufs = uvg[g % 2]
                        sl0 = g * GS
                        for s_ in range(GS):
                            sl = sl0 + s_
                            b_ = bufs[s_]
                            if OPTS["nogather"]:
                                continue
                            S.add("pool", lambda e, b_=b_, sl=sl: e.indirect_dma_start(
                                out=b_[:], out_offset=None, in_=(uvb_d if OPTS["uvtab"] else puv_d),
                                in_offset=bass.IndirectOffsetOnAxis(ap=ei[:, sl:sl + 1], axis=0)),
                                [ei, RIN] + (R_uvb[i] if OPTS["uvtab"] else []), [b_], dma=b_)
                        for s_ in range(GS):
                            sl = sl0 + s_
                            b_ = bufs[s_]
                            if OPTS["nodots"]:
                                continue
                            S.dve(lambda e, b_=b_, sl=sl: e.scalar_tensor_tensor(out=tmp[:], in0=b_[:, 0:D], scalar=1.0, in1=h32[:], op0=ALU.mult,
                                                                           op1=ALU.mult, accum_out=av[:, sl:sl + 1]), [b_, h32], [tmp, av])
                        gsl = slice(sl0, sl0 + GS)
                        if OPTS["actgelu"]:
                            S.act(lambda e, gsl=gsl: e.activation(out=gb_[:, gsl], in_=av[:, gsl], func=AF.Gelu_apprx_tanh), [av], [gb_])
                        else:
                            S.dve(lambda e, gsl=gsl: e.tensor_tensor(out=ga[:, gsl], in0=av[:, gsl], in1=av[:, gsl], op=ALU.mult), [av], [ga])
                            S.dve(lambda e, gsl=gsl: e.tensor_scalar(out=ga[:, gsl], in0=ga[:, gsl], scalar1=0.044715, scalar2=1.0,
                                                                    op0=ALU.mult, op1=ALU.add), [ga], [ga])
                            S.dve(lambda e, gsl=gsl: e.tensor_tensor(out=ga[:, gsl], in0=ga[:, gsl], in1=av[:, gsl], op=ALU.mult), [ga, av], [ga])
                            S.act(lambda e, gsl=gsl: e.activation(out=gb_[:, gsl], in_=ga[:, gsl], func=AF.Tanh, scale=0.7978845608028654),
                                  [ga], [gb_])
                            S.dve(lambda e, gsl=gsl: e.tensor_scalar(out=gb_[:, gsl], in0=gb_[:, gsl], scalar1=1.0, scalar2=0.5,
                                                                    op0=ALU.add, op1=ALU.mult), [gb_], [gb_])
                            S.dve(lambda e, gsl=gsl: e.tensor_tensor(out=gb_[:, gsl], in0=gb_[:, gsl], in1=av[:, gsl], op=ALU.mult), [gb_, av], [gb_])
                        S.dve(lambda e, gsl=gsl: e.tensor_tensor(out=coef[:, gsl], in0=gb_[:, gsl], in1=wflat[:, gsl], op=ALU.mult),
                              [gb_, wsm], [coef])
                        for s_ in range(GS):
                            sl = sl0 + s_
                            b_ = bufs[s_]
                            tv_ = tv[nv % 3]
                            nv += 1
                            if OPTS["novside"] and sl not in (0, 127):
                                continue
                            S.act(lambda e, b_=b_, sl=sl, tv_=tv_: e.activation(out=tv_[:], in_=b_[:, D:2 * D], func=AF.Copy,
                                                                         scale=coef[:, sl:sl + 1]), [b_, coef], [tv_])
                            for nh in range(2):
                                S.pe(lambda e, tv_=tv_, nh=nh, sl=sl: e.matmul(Y[:, nh * 512:(nh + 1) * 512], lhsT=ident[:],
                                                                         rhs=tv_[:, nh * 512:(nh + 1) * 512],
                                                                         start=(sl == 0), stop=(sl == 127)), [ident, tv_], [Y])
                        if nxt is not None and not OPTS["noroute"]:
                            next(nxt, None)
                    if nxt is not None:
                        for _ in nxt:
                            pass
                    epilogue(Y, [Y], xt, j, t, tmp, r, st, mv, rstd)

                tiles = []
                for (t0, ntl, latent, j) in seqs:
                    for tt in range(ntl):
                        tiles.append((t0 + tt, j))
                for _ in stage_a(tiles[0][0], tiles[0][1], 0):
                    pass
                for n_, (t, j) in enumerate(tiles):
                    nxt = None
                    if n_ + 1 < len(tiles):
                        nxt = stage_a(tiles[n_ + 1][0], tiles[n_ + 1][1], (n_ + 1) % 2)
                    stage_b(t, j, n_ % 2, nxt)
                S.barrier()
            S.es = root

        for i in range(DEPTH):
            first = (i == 0)
            modulation(i, 0)
            if i % 2 == 0:
                retention_layer(i, first)
            else:
                attention_layer(i, first)
            modulation(i, 1)
            if peer:
                peer_layer(i)
        S.finalize()
        stats = S.stats
    return nc, stats


def _consts(SQ):
    j = np.arange(128, dtype=np.float32)[:, None]
    i = np.arange(128, dtype=np.float32)[None, :]
    cst = np.zeros((128, 6, 128), np.float32)
    cst[:, 0] = np.maximum(i - j, 0.0)
    cst[:, 1] = (i >= j)
    cst[:, 2] = np.maximum(j - i, 0.0)
    cst[:, 3] = (j > i)
    cst[:, 4] = np.where(j >= i, 0.0, -30000.0)
    cst[:, 5] = np.where(j <= i, 0.0, -30000.0)
    p = np.arange(128, dtype=np.float32)
    pvec = np.zeros((128, 8), np.float32)
    pvec[:, 0] = p + 1
    pvec[:, 1] = 128 - p
    pvec[:, 2] = 127 - p
    pvec[:, 3] = p
    iota16 = np.tile(np.arange(16, dtype=np.float32)[None, :], (128, 1))
    pos = np.arange(SQ, dtype=np.float32)
    inv = (10000.0 ** (-np.arange(0, 256, 2, dtype=np.float32) / 256.0)).astype(np.float32)
    ang = (pos[:, None] * inv[None, :]).astype(np.float32)
    rrc, rrs = np.cos(ang).astype(np.float32), np.sin(ang).astype(np.float32)
    t = np.arange(SQ)
    inv2 = (10000.0 ** (-np.arange(0, 32, 2, dtype=np.float32) / 32.0)).astype(np.float32)
    ar = ((t // 64).astype(np.float32)[:, None] * inv2[None, :]).astype(np.float32)
    ac = ((t % 64).astype(np.float32)[:, None] * inv2[None, :]).astype(np.float32)
    rac = np.concatenate([np.cos(ar), np.cos(ac)], 1).astype(np.float32)
    ras = np.concatenate([np.sin(ar), np.sin(ac)], 1).astype(np.float32)
    return dict(cst=cst, pvec=pvec, iota16=iota16, rope_ret_cos=rrc, rope_ret_sin=rrs, rope_att_cos=rac, rope_att_sin=ras)


_PROG = {}
RUNKW = {}
LAST = {}


def run_step(inp, DEPTH, SQ, ncores, peer=True):
    key = (DEPTH, SQ, peer)
    if key not in _PROG:
        _PROG[key] = build_program(DEPTH=DEPTH, SQ=SQ, peer=peer)
    nc, stats = _PROG[key]
    NRET = (DEPTH + 1) // 2
    NATT = max(DEPTH // 2, 1)
    f = lambda a: np.ascontiguousarray(np.asarray(a, dtype=np.float32))
    cs = _consts(SQ)
    shared = dict(
        mod_w=f(inp["mod_w"]), mod_b=f(inp["mod_b"]), ln_g=f(inp["ln_g"]), ln_b=f(inp["ln_b"]),
        ret_w_in=f(inp["ret_w_in"]), ret_w_out=f(inp["ret_w_out"]), ret_decay=f(inp["ret_decay"]).reshape(NRET, 8),
        attn_w_in=f(inp["attn_w_in"])[:NATT], attn_w_out=f(inp["attn_w_out"])[:NATT], attn_sink=f(inp["attn_sink"])[:NATT],
        peer_wq=f(inp["peer_wq"]), peer_keys=f(inp["peer_keys"]).reshape(DEPTH, 16, 128, 128),
        peer_uv=np.ascontiguousarray(np.concatenate([f(inp["peer_u"]).reshape(DEPTH * 16384, 1024),
                                                     f(inp["peer_v"]).reshape(DEPTH * 16384, 1024)], axis=1)), **cs)
    xp, xs = f(inp["x_prompt"]), f(inp["x_sample"])
    in_maps = []
    for c in range(ncores):
        m = dict(shared)
        m["x"] = np.ascontiguousarray(np.concatenate([xp[2 * c], xp[2 * c + 1], xs[c]], 0))
        m["cond"] = np.ascontiguousarray(np.stack([f(inp["c_ctx"]), f(inp["c"])[c]], 0))
        m["sret_f"] = f(inp["state_ret_fwd"])[c]
        m["sret_b"] = f(inp["state_ret_bwd"])[c]
        m["ck"] = np.ascontiguousarray(f(inp["cache_k"])[c][:NATT].reshape(NATT, 512, 256))
        m["cv"] = np.ascontiguousarray(f(inp["cache_v"])[c][:NATT].reshape(NATT, 512, 256))
        in_maps.append(m)
    res = run_bass_kernel_spmd(nc, in_maps, core_ids=list(range(ncores)), **RUNKW)
    LAST['res'] = res
    rs = res.results
    B = 2 * ncores
    y = np.stack([r["y"] for r in rs], 0)
    yp = y[:, :512].reshape(B, 256, 1024)
    ys = y[:, 512:]
    nsf = np.concatenate([r["nsf"] for r in rs], 0)
    nsb = np.concatenate([r["nsb"] for r in rs], 0)
    nk = np.concatenate([r["nk"] for r in rs], 0).reshape(B, NATT, 256, 4, 64)
    nv = np.concatenate([r["nv"] for r in rs], 0).reshape(B, NATT, 256, 4, 64)
    return tuple(np.ascontiguousarray(a.astype(np.float32)) for a in (yp, ys, nsf, nsb, nk, nv))


def kernel(**inputs):
    return run_step(inputs, DEPTH=4, SQ=4096, ncores=8)
```

```python
import numpy as np
from contextlib import ExitStack
import concourse.bass as bass
import concourse.mybir as mybir
from concourse.bass_utils import run_bass_kernel_spmd

F32 = mybir.dt.float32
BF16 = mybir.dt.bfloat16
I32 = mybir.dt.int32
U32 = mybir.dt.uint32
ALU = mybir.AluOpType
AF = mybir.ActivationFunctionType
AX = mybir.AxisListType


class Res:
    __slots__ = ("name", "lw", "rd", "sem", "t")

    def __init__(self, name, t=None):
        self.name = name
        self.lw = None
        self.rd = {}
        self.sem = None
        self.t = t

    def __getitem__(self, k):
        return self.t[k]


class Op:
    __slots__ = ("eng", "fn", "reads", "writes", "dma", "deps", "hasdep", "ev", "waits", "idx")


class Sched:
    ENGS = ("pe", "dve", "act", "pool", "sp")

    def __init__(self, nc, es, max_dma_sems=94):
        self.nc = nc
        self.es = es
        self.es_root = es
        self.ops = []
        self.nres = 0
        self.dma_sems = []
        self.max_dma_sems = max_dma_sems
        self.rr = 0
        self.sem_load = []

    def sbuf(self, name, shape, dtype):
        self.nres += 1
        name = "sb%d_%s" % (self.nres, name)
        t = self.es.enter_context(self.nc.sbuf_tensor(name, list(shape), dtype))
        return Res(name, t)

    def psum(self, name, shape, dtype):
        self.nres += 1
        name = "ps%d_%s" % (self.nres, name)
        t = self.es.enter_context(self.nc.psum_tensor(name, list(shape), dtype))
        return Res(name, t)

    def dram(self, name):
        return Res(name, None)

    def _dsem(self, res):
        if res.sem is None:
            if len(self.dma_sems) < self.max_dma_sems:
                s = self.es_root.enter_context(self.nc.semaphore("dq%d" % len(self.dma_sems)))
                self.dma_sems.append(s)
                res.sem = len(self.dma_sems) - 1
                self.sem_load.append(0)
            else:
                res.sem = min(range(len(self.dma_sems)), key=lambda k: self.sem_load[k])
                self.sem_load[res.sem] += 64
        return res.sem

    def add(self, eng, fn, reads=(), writes=(), dma=None):
        op = Op()
        op.eng = eng
        op.fn = fn
        op.reads = tuple(reads)
        op.writes = tuple(writes)
        op.dma = None if dma is None else self._dsem(dma)
        if op.dma is not None:
            self.sem_load[op.dma] += 1
        op.idx = len(self.ops)
        self.ops.append(op)
        return op

    def barrier(self):
        op = Op()
        op.eng = None
        op.fn = None
        op.reads = ()
        op.writes = ()
        op.dma = None
        op.idx = len(self.ops)
        self.ops.append(op)

    def pe(self, fn, reads=(), writes=()):
        return self.add("pe", fn, reads, writes)

    def dve(self, fn, reads=(), writes=()):
        return self.add("dve", fn, reads, writes)

    def act(self, fn, reads=(), writes=()):
        return self.add("act", fn, reads, writes)

    def pool(self, fn, reads=(), writes=()):
        return self.add("pool", fn, reads, writes)

    def dma(self, out, in_, reads, writes, semres, eng="sp", **kw):
        return self.add(eng, lambda e: e.dma_start(out=out, in_=in_, **kw), reads, writes, dma=semres)

    def finalize(self):
        nc = self.nc
        ops = self.ops
        latest = {}
        bar = {}
        bar_pending = set()
        for op in ops:
            if op.eng is None:
                bar = dict(latest)
                bar_pending = set(self.ENGS)
                op.deps = []
                op.hasdep = False
                continue
            deps = {}
            if op.eng in bar_pending:
                bar_pending.discard(op.eng)
                for x in bar.values():
                    deps[x.idx] = x
            for r in op.reads:
                if r.lw is not None:
                    deps[r.lw.idx] = r.lw
            for w in op.writes:
                if w.lw is not None:
                    deps[w.lw.idx] = w.lw
                for x in w.rd.values():
                    deps[x.idx] = x
            deps.pop(op.idx, None)
            dl = []
            for d in deps.values():
                if op.eng == "pe" and d.eng == "pe" and op.dma is None and d.dma is None:
                    continue
                dl.append(d)
            op.deps = dl
            op.hasdep = False
            key = op.eng if op.dma is None else ("d", op.dma)
            latest[key] = op
            for r in op.reads:
                r.rd[key] = op
            for w in op.writes:
                w.lw = op
                w.rd = {}
        for op in ops:
            for d in op.deps:
                d.hasdep = True
        EPOCH = 30000
        engsem = {}

        def get_engsem(e, ep):
            if (e, ep) not in engsem:
                engsem[(e, ep)] = self.es_root.enter_context(nc.semaphore("eng_%s_%d" % (e, ep)))
            return engsem[(e, ep)]
        engcnt = {e: 0 for e in self.ENGS}
        dcnt = [0] * len(self.dma_sems)
        waited = {e: {} for e in self.ENGS}
        nwaits = 0
        ops = [o for o in ops if o.eng is not None]
        for op in ops:
            need = {}
            for d in op.deps:
                if d.dma is not None:
                    k = ("d", d.dma)
                    v = dcnt[d.dma]
                else:
                    k = ("e", d.eng, d.ev[0])
                    v = d.ev[1]
                if need.get(k, 0) < v:
                    need[k] = v
            w = []
            wd = waited[op.eng]
            for k, v in need.items():
                if wd.get(k, 0) >= v:
                    continue
                wd[k] = v
                w.append((k, v))
            op.waits = w
            nwaits += len(w)
            if op.dma is not None:
                dcnt[op.dma] += 16
                op.ev = dcnt[op.dma]
            elif op.hasdep:
                engcnt[op.eng] += 1
                ep, c = divmod(engcnt[op.eng] - 1, EPOCH)
                op.ev = (ep, c + 1)
                get_engsem(op.eng, ep)
            else:
                op.ev = None
        self.final_dcnt = dcnt
        self.stats = dict(nops=len(ops), nwaits=nwaits, engcnt=dict(engcnt),
                          per_eng={e: sum(1 for o in ops if o.eng == e) for e in self.ENGS})
        per = {e: [o for o in ops if o.eng == e] for e in self.ENGS}
        dma_sems = self.dma_sems

        def semof(k):
            return dma_sems[k[1]] if k[0] == "d" else engsem[(k[1], k[2])]

        def run(eng_obj, name):
            for op in per[name]:
                for k, v in op.waits:
                    eng_obj.wait_ge(semof(k), v)
                ins = op.fn(eng_obj)
                if op.dma is not None:
                    ins.then_inc(dma_sems[op.dma], 16)
                elif op.ev is not None:
                    ins.then_inc(engsem[(name, op.ev[0])], 1)
            if name == "sp":
                for i, s in enumerate(dma_sems):
                    if dcnt[i] > 0:
                        eng_obj.wait_ge(s, dcnt[i])

        with nc.Block() as block:
            @block.tensor
            def _(e):
                run(e, "pe")

            @block.vector
            def _(e):
                run(e, "dve")

            @block.scalar
            def _(e):
                run(e, "act")

            @block.gpsimd
            def _(e):
                run(e, "pool")

            @block.sync
            def _(e):
                run(e, "sp")

D = 1024
ALPHA = 8.0 ** 0.25
EPS = 1e-5
NEXP = 16384
OPTS = dict(GS=4, uvbf16=True, actgelu=True, nogather=0, nodots=0, novside=0, noroute=0, uvtab=1, NSETS=3, NTV=3)


def build_program(DEPTH=4, SQ=4096, peer=True):
    nc = bass.Bass("TRN2", target_bir_lowering=False)
    NRET = (DEPTH + 1) // 2
    NATT = DEPTH // 2
    NATTd = max(NATT, 1)
    NP = 4
    NS = SQ // 128
    NT = NP + NS
    seqs = [(0, 2, False, 0), (2, 2, False, 0), (4, NS, True, 1)]

    def din(name, shape, dt=F32):
        return nc.dram_tensor(name, list(shape), dt, kind="ExternalInput").ap()

    def dout(name, shape, dt=F32):
        return nc.dram_tensor(name, list(shape), dt, kind="ExternalOutput").ap()

    x_d = din("x", [NT * 128, D])
    cond_d = din("cond", [2, D])
    sretf_d = din("sret_f", [NRET, 4, 256, 512])
    sretb_d = din("sret_b", [NRET, 4, 256, 512])
    ck_d = din("ck", [NATTd, 512, 256])
    cv_d = din("cv", [NATTd, 512, 256])
    modw_d = din("mod_w", [DEPTH, D, 6 * D])
    modb_d = din("mod_b", [DEPTH, 6 * D])
    lng_d = din("ln_g", [DEPTH, 2, D])
    lnb_d = din("ln_b", [DEPTH, 2, D])
    rwin_d = din("ret_w_in", [NRET, D, 6144])
    rwout_d = din("ret_w_out", [NRET, 2048, D])
    rdec_d = din("ret_decay", [NRET, 8])
    awin_d = din("attn_w_in", [NATTd, D, 1536])
    awout_d = din("attn_w_out", [NATTd, D, D])
    asink_d = din("attn_sink", [NATTd, 16])
    pwq_d = din("peer_wq", [DEPTH, D, 2048])
    pkeys_d = din("peer_keys", [DEPTH, 16, 128, 128])
    puv_d = din("peer_uv", [DEPTH * NEXP, 2 * D])
    cst_d = din("cst", [128, 6, 128])
    pvec_d = din("pvec", [128, 8])
    iota_d = din("iota16", [128, 16])
    rrc_d = din("rope_ret_cos", [SQ, 128])
    rrs_d = din("rope_ret_sin", [SQ, 128])
    rac_d = din("rope_att_cos", [SQ, 32])
    ras_d = din("rope_att_sin", [SQ, 32])

    y_d = dout("y", [NT * 128, D])
    nsf_d = dout("nsf", [2, NRET, 4, 256, 512])
    nsb_d = dout("nsb", [2, NRET, 4, 256, 512])
    nk_d = dout("nk", [2, NATTd, 256, 256])
    nv_d = dout("nv", [2, NATTd, 256, 256])

    z_d = nc.dram_tensor("z_scr", [NT * 128, 6144], BF16).ap()
    pb_d = nc.dram_tensor("pb_scr", [NT * 128, 2048], F32).ap()
    qs_d = nc.dram_tensor("qs_scr", [NT * 128, 1024], BF16).ap()
    uvb_d = nc.dram_tensor("uvb_scr", [DEPTH * NEXP, 2 * D], BF16).ap() if OPTS["uvtab"] else None
    CVR = 512

    root = ExitStack()
    with root:
        S = Sched(nc, root)
        RIN = S.dram("inputs")
        R_y = [S.dram("y%d" % t) for t in range(NT)]
        R_z = [S.dram("z%d" % t) for t in range(NT)]
        R_pb = [S.dram("pb%d" % t) for t in range(NT)]
        R_qs = [S.dram("qs%d" % t) for t in range(NT)]
        R_o = S.dram("small_outs")
        R_uvb = [[S.dram("uvb%d_%d" % (i_, c_)) for c_ in range(NEXP // CVR)] for i_ in range(DEPTH)]
        cvt = S.dram("cvt")

        def convert_uv(i_):
            if not OPTS["uvtab"]:
                return
            for c_ in range(NEXP // CVR):
                r0 = i_ * NEXP + c_ * CVR
                S.dma(uvb_d[r0:r0 + CVR, :], puv_d[r0:r0 + CVR, :], [RIN], [R_uvb[i_][c_]], cvt, eng="pool")

        def rows(ap, t):
            return ap[t * 128:(t + 1) * 128, :]

        ident_f = S.sbuf("ident_f", [128, 128], F32)
        ident = S.sbuf("ident", [128, 128], BF16)
        ones1 = S.sbuf("ones1", [1, 128], F32)
        condT = S.sbuf("condT", [128, 2, 8], F32)
        modbc = S.sbuf("modbc", [128, 2, 3, D], F32)
        lnbc = S.sbuf("lnbc", [128, 2, D], F32)
        cst = S.sbuf("cst", [128, 6, 128], F32)
        pvec = S.sbuf("pvec", [128, 8], F32)
        iota16 = S.sbuf("iota16", [128, 16], F32)
        epsc = S.sbuf("epsc", [128, 1], F32)

        S.pool(lambda e: e.memset(ident_f[:], 0.0), [], [ident_f])
        S.pool(lambda e: e.affine_select(out=ident_f[:], in_=ident_f[:], pattern=[[-1, 128]],
                                         compare_op=ALU.not_equal, fill=1.0, base=0, channel_multiplier=1),
               [ident_f], [ident_f])
        S.dve(lambda e: e.tensor_copy(out=ident[:], in_=ident_f[:]), [ident_f], [ident])
        S.dve(lambda e: e.memset(ones1[:], 1.0), [], [ones1])
        S.dve(lambda e: e.memset(epsc[:], EPS), [], [epsc])
        S.dma(cst[:], cst_d, [RIN], [cst], cst)
        S.dma(pvec[:], pvec_d, [RIN], [pvec], pvec)
        S.dma(iota16[:], iota_d, [RIN], [iota16], iota16)
        S.dma(condT[:], cond_d.rearrange("j (kc p) -> p j kc", p=128), [RIN], [condT], condT,
              allow_slow_non_contiguous=True)
        S.act(lambda e: e.activation(out=condT[:], in_=condT[:], func=AF.Silu), [condT], [condT])

        def modulation(i, s):
            sc = ExitStack()
            S.es = sc
            with sc:
                wch = [S.sbuf("modw%d" % k, [128, 8, D], F32) for k in range(2)]
                condrep = S.sbuf("condrep", [128, 2, 8, 128], F32)
                S.dve(lambda e: e.tensor_copy(out=condrep[:].rearrange("p j k m -> p (j k) m"),
                                              in_=condT[:].rearrange("p j k -> p (j k)").unsqueeze(2).to_broadcast([128, 16, 128])),
                      [condT], [condrep])
                brow = S.sbuf("modbrow", [1, 3 * D], F32)
                mps = [S.psum("modps%d" % k, [128, 512], F32) for k in range(2)]
                S.dma(brow[:], modb_d[i:i + 1, s * 3 * D:(s + 1) * 3 * D], [RIN], [brow], brow)
                S.dma(lnbc[:, 0, :], lng_d[i, s:s + 1, :].to_broadcast([128, D]), [RIN], [lnbc], lnbc)
                S.dma(lnbc[:, 1, :], lnb_d[i, s:s + 1, :].to_broadcast([128, D]), [RIN], [lnbc], lnbc)
                n = 0
                for blk in range(3):
                    w = wch[blk % 2]
                    c0 = (s * 3 + blk) * D
                    S.dma(w[:], modw_d[i, :, c0:c0 + D].rearrange("(kc p) n -> p kc n", p=128), [RIN], [w], w)
                    for j in range(2):
                        for nh in range(2):
                            ps = mps[n % 2]
                            n += 1
                            for kc in range(8):
                                S.pe(lambda e, ps=ps, j=j, kc=kc, w=w, nh=nh: e.matmul(
                                    ps[:], lhsT=condrep[:, j, kc, :], rhs=w[:, kc, nh * 512:(nh + 1) * 512],
                                    start=(kc == 0), stop=False), [condrep, w], [ps])
                            S.pe(lambda e, ps=ps, blk=blk, nh=nh: e.matmul(
                                ps[:], lhsT=ones1[:], rhs=brow[:, blk * D + nh * 512: blk * D + (nh + 1) * 512],
                                start=False, stop=True), [ones1, brow], [ps])
                            add = 1.0 if blk == 1 else 0.0
                            S.act(lambda e, ps=ps, j=j, blk=blk, nh=nh, add=add: e.activation(
                                out=modbc[:, j, blk, nh * 512:(nh + 1) * 512], in_=ps[:], func=AF.Identity, bias=add, scale=1.0)
                                if add else e.copy(out=modbc[:, j, blk, nh * 512:(nh + 1) * 512], in_=ps[:]),
                                [ps], [modbc])
                S.barrier()
            S.es = root

        def load_w(dst, src2d, N, tag):
            v = src2d.rearrange("(kc p) n -> p kc n", p=128)
            for n0 in range(0, N, 2048):
                n1 = min(N, n0 + 2048)
                S.dma(dst[:, :, n0:n1], v[:, :, n0:n1], [RIN], [dst], dst, eng="pool")

        def prologue(xt, j, hb, hT, tp, h32=None):
            tgt = h32 if h32 is not None else hb
            S.dve(lambda e: e.tensor_tensor(out=tgt[:], in0=xt[:], in1=modbc[:, j, 1, :], op=ALU.mult), [xt, modbc], [tgt])
            if h32 is not None:
                S.pool(lambda e: e.tensor_tensor(out=h32[:], in0=h32[:], in1=modbc[:, j, 0, :], op=ALU.add), [h32, modbc], [h32])
                S.act(lambda e: e.copy(out=hb[:], in_=h32[:]), [h32], [hb])
            else:
                S.pool(lambda e: e.tensor_tensor(out=hb[:], in0=hb[:], in1=modbc[:, j, 0, :], op=ALU.add), [hb, modbc], [hb])
            for kc in range(8):
                S.pe(lambda e, kc=kc: e.transpose(out=tp[:, kc * 128:(kc + 1) * 128], in_=hb[:, kc * 128:(kc + 1) * 128],
                                                  identity=ident[:]), [hb, ident], [tp])
            S.act(lambda e: e.copy(out=hT[:].rearrange("p a b -> p (a b)"), in_=tp[:]), [tp], [hT])

        def epilogue(Y, yreads, xt, j, t, tmp, r, st, mv, rstd):
            S.dve(lambda e: e.tensor_tensor(out=tmp[:], in0=Y[:], in1=modbc[:, j, 2, :], op=ALU.mult), [modbc] + yreads, [tmp])
            S.dve(lambda e: e.scalar_tensor_tensor(out=r[:], in0=xt[:], scalar=ALPHA, in1=tmp[:], op0=ALU.mult, op1=ALU.add),
                  [xt, tmp], [r])
            for c in range(2):
                S.dve(lambda e, c=c: e.bn_stats(out=st[:, c, :], in_=r[:, c * 512:(c + 1) * 512]), [r], [st])
            S.dve(lambda e: e.bn_aggr(out=mv[:], in_=st[:].rearrange("p a b -> p (a b)")), [st], [mv])
            S.act(lambda e: e.activation(out=rstd[:], in_=mv[:, 1:2], func=AF.Sqrt, bias=epsc[:], scale=1.0), [mv, epsc], [rstd])
            S.dve(lambda e: e.reciprocal(out=rstd[:], in_=rstd[:]), [rstd], [rstd])
            S.dve(lambda e: e.tensor_scalar(out=r[:], in0=r[:], scalar1=mv[:, 0:1], scalar2=rstd[:, 0:1],
                                            op0=ALU.subtract, op1=ALU.mult), [r, mv, rstd], [r])
            S.pool(lambda e: e.tensor_tensor(out=r[:], in0=r[:], in1=lnbc[:, 0, :], op=ALU.mult), [r, lnbc], [r])
            S.pool(lambda e: e.tensor_tensor(out=tmp[:], in0=r[:], in1=lnbc[:, 1, :], op=ALU.add), [r, lnbc], [tmp])
            S.dma(rows(y_d, t), tmp[:], [tmp], [R_y[t]], tmp)

        def xsrc(first):
            return x_d if first else y_d

        def xres(first, t):
            return RIN if first else R_y[t]

        def retention_layer(i, first):
            jr = i // 2
            sc0 = ExitStack()
            S.es = sc0
            with sc0:
                lg = S.sbuf("lg", [128, 8], F32)
                qdec = S.sbuf("qdec", [128, 2, 4], F32)
                kdec = S.sbuf("kdec", [128, 2, 4], F32)
                cdec = S.sbuf("cdec", [128, 2, 4], F32)
                dmask = S.sbuf("dmask", [128, 4, 128], F32)
                dtmp = S.sbuf("dtmp", [128, 128], F32)
                c128 = S.sbuf("c128", [128, 1], F32)
                S.dve(lambda e: e.memset(c128[:], 128.0), [], [c128])
                S.dma(lg[:], rdec_d[jr:jr + 1, :].to_broadcast([128, 8]), [RIN], [lg], lg)
                S.act(lambda e: e.activation(out=lg[:], in_=lg[:], func=AF.Exp, scale=-1.0), [lg], [lg])
                S.act(lambda e: e.activation(out=lg[:], in_=lg[:], func=AF.Ln, bias=1.0, scale=1.0), [lg], [lg])
                S.dve(lambda e: e.tensor_scalar(out=lg[:], in0=lg[:], scalar1=-1.0, scalar2=None, op0=ALU.mult), [lg], [lg])
                for h in range(4):
                    for dr in range(2):
                        col = dr * 4 + h
                        pq = 0 if dr == 0 else 1
                        pk = 2 if dr == 0 else 3
                        S.act(lambda e, col=col, pq=pq, dr=dr, h=h: e.activation(
                            out=qdec[:, dr, h:h + 1], in_=lg[:, col:col + 1], func=AF.Exp, scale=pvec[:, pq:pq + 1]),
                            [lg, pvec], [qdec])
                        S.act(lambda e, col=col, pk=pk, dr=dr, h=h: e.activation(
                            out=kdec[:, dr, h:h + 1], in_=lg[:, col:col + 1], func=AF.Exp, scale=pvec[:, pk:pk + 1]),
                            [lg, pvec], [kdec])
                        S.act(lambda e, col=col, dr=dr, h=h: e.activation(
                            out=cdec[:, dr, h:h + 1], in_=lg[:, col:col + 1], func=AF.Exp, scale=c128[:, 0:1]),
                            [lg, c128], [cdec])
                    S.act(lambda e, h=h: e.activation(out=dmask[:, h, :], in_=cst[:, 0, :], func=AF.Exp, scale=lg[:, h:h + 1]),
                          [cst, lg], [dmask])
                    S.dve(lambda e, h=h: e.tensor_tensor(out=dmask[:, h, :], in0=dmask[:, h, :], in1=cst[:, 1, :], op=ALU.mult),
                          [dmask, cst], [dmask])
                    S.act(lambda e, h=h: e.activation(out=dtmp[:], in_=cst[:, 2, :], func=AF.Exp, scale=lg[:, 4 + h:5 + h]),
                          [cst, lg], [dtmp])
                    S.dve(lambda e: e.tensor_tensor(out=dtmp[:], in0=dtmp[:], in1=cst[:, 3, :], op=ALU.mult), [dtmp, cst], [dtmp])
                    S.dve(lambda e, h=h: e.tensor_tensor(out=dmask[:, h, :], in0=dmask[:, h, :], in1=dtmp[:], op=ALU.add),
                          [dmask, dtmp], [dmask])

                scz = ExitStack()
                S.es = scz
                with scz:
                    win = S.sbuf("rwin", [128, 8, 6144], BF16)
                    load_w(win, rwin_d[jr], 6144, "rwin")
                    convert_uv(i)
                    xts = [S.sbuf("zx%d" % k, [128, D], F32) for k in range(2)]
                    hb = S.sbuf("zhb", [128, D], BF16)
                    hT = S.sbuf("zhT", [128, 8, 128], BF16)
                    qk32 = S.sbuf("zqk32", [128, 2048], F32)
                    ra = S.sbuf("zra", [128, 1024], F32)
                    rb = S.sbuf("zrb", [128, 1024], F32)
                    zrow = [S.sbuf("zrow%d" % k, [128, 6144], BF16) for k in range(2)]
                    rc = [S.sbuf("zrc%d" % k, [128, 2, 128], F32) for k in range(2)]
                    tp = S.psum("ztp", [128, 1024], BF16)
                    zps = [S.psum("zps%d" % k, [128, 512], F32) for k in range(4)]
                    for (t0, ntl, latent, j) in seqs:
                        for tt in range(ntl):
                            t = t0 + tt
                            xt = xts[t % 2]
                            zr = zrow[t % 2]
                            S.dma(xt[:], rows(xsrc(first), t), [xres(first, t)], [xt], xt)
                            prologue(xt, j, hb, hT, tp)
                            if latent:
                                rct = rc[t % 2]
                                S.dma(rct[:, 0, :], rrc_d[tt * 128:(tt + 1) * 128, :], [RIN], [rct], rct)
                                S.dma(rct[:, 1, :], rrs_d[tt * 128:(tt + 1) * 128, :], [RIN], [rct], rct)
                            for nb in range(12):
                                ps = zps[nb % 4]
                                for kc in range(8):
                                    S.pe(lambda e, ps=ps, kc=kc, nb=nb: e.matmul(
                                        ps[:], lhsT=hT[:, kc, :], rhs=win[:, kc, nb * 512:(nb + 1) * 512],
                                        start=(kc == 0), stop=(kc == 7)), [hT, win], [ps])
                                sl = slice(nb * 512, (nb + 1) * 512)
                                if nb < 4:
                                    scl = 1.0 if nb < 2 else 0.0625
                                    if latent:
                                        S.act(lambda e, ps=ps, sl=sl, scl=scl: e.mul(out=qk32[:, sl], in_=ps[:], mul=scl), [ps], [qk32])
                                    else:
                                        S.act(lambda e, ps=ps, sl=sl, scl=scl, zr=zr: e.mul(out=zr[:, sl], in_=ps[:], mul=scl), [ps], [zr])
                                elif nb < 8:
                                    S.act(lambda e, ps=ps, sl=sl, zr=zr: e.copy(out=zr[:, sl], in_=ps[:]), [ps], [zr])
                                else:
                                    S.act(lambda e, ps=ps, sl=sl, zr=zr: e.activation(out=zr[:, sl], in_=ps[:], func=AF.Silu), [ps], [zr])
                            if latent:
                                v4 = qk32[:].rearrange("p (a b c) -> p a b c", a=8, b=2)
                                x1 = v4[:, :, 0, :]
                                x2 = v4[:, :, 1, :]
                                cosb = rct[:, 0, :].unsqueeze(1).to_broadcast([128, 8, 128])
                                sinb = rct[:, 1, :].unsqueeze(1).to_broadcast([128, 8, 128])
                                o4 = zr[:, 0:2048].rearrange("p (a b c) -> p a b c", a=8, b=2)
                                ra3 = ra[:].rearrange("p (a c) -> p a c", a=8)
                                rb3 = rb[:].rearrange("p (a c) -> p a c", a=8)
                                S.dve(lambda e, ra3=ra3, x1=x1, cosb=cosb: e.tensor_tensor(out=ra3, in0=x1, in1=cosb, op=ALU.mult), [qk32, rct], [ra])
                                S.pool(lambda e, rb3=rb3, x2=x2, sinb=sinb: e.tensor_tensor(out=rb3, in0=x2, in1=sinb, op=ALU.mult), [qk32, rct], [rb])
                                S.dve(lambda e, o4=o4, ra3=ra3, rb3=rb3: e.tensor_tensor(out=o4[:, :, 0, :], in0=ra3, in1=rb3, op=ALU.subtract), [ra, rb], [zr])
                                S.dve(lambda e, ra3=ra3, x1=x1, sinb=sinb: e.tensor_tensor(out=ra3, in0=x1, in1=sinb, op=ALU.mult), [qk32, rct, zr], [ra])
                                S.pool(lambda e, rb3=rb3, x2=x2, cosb=cosb: e.tensor_tensor(out=rb3, in0=x2, in1=cosb, op=ALU.mult), [qk32, rct, zr], [rb])
                                S.dve(lambda e, o4=o4, ra3=ra3, rb3=rb3: e.tensor_tensor(out=o4[:, :, 1, :], in0=ra3, in1=rb3, op=ALU.add), [ra, rb], [zr])
                            S.dma(rows(z_d, t), zr[:], [zr], [R_z[t]], zr)
                    S.barrier()
                S.es = sc0

                sca = ExitStack()
                S.es = sca
                with sca:
                    qkv = [S.sbuf("aqkv%d" % k, [128, 4096], BF16) for k in range(2)]
                    qdb = S.sbuf("aqdb", [128, 1024], BF16)
                    kdb = S.sbuf("akdb", [128, 1024], BF16)
                    qdbT = S.sbuf("aqdbT", [128, 8, 128], BF16)
                    Sb32 = S.sbuf("aSb32", [128, 4, 2, 512], F32)
                    Sbb = S.sbuf("aSbb", [128, 4, 2, 512], BF16)
                    pbt = [S.sbuf("apbt%d" % k, [128, 2048], F32) for k in range(2)]
                    tp = S.psum("atp", [128, 1024], BF16)
                    aps = [S.psum("aps%d" % k, [128, 512], F32) for k in range(6)]
                    for si, (t0, ntl, latent, j) in enumerate(seqs):
                        if latent:
                            S.dma(Sb32[:].rearrange("p h c v -> p (h c) v"),
                                  sretb_d[jr].rearrange("h (c p) v -> p (h c) v", p=128), [RIN], [Sb32], Sb32)
                        else:
                            S.dve(lambda e: e.memset(Sb32[:].rearrange("p h c v -> p (h c v)"), 0.0), [], [Sb32])
                        S.act(lambda e: e.copy(out=Sbb[:].rearrange("p h c v -> p (h c v)"),
                                               in_=Sb32[:].rearrange("p h c v -> p (h c v)")), [Sb32], [Sbb])
                        for tt in reversed(range(ntl)):
                            t = t0 + tt
                            qv = qkv[t % 2]
                            S.dma(qv[:], z_d[t * 128:(t + 1) * 128, 0:4096], [R_z[t]], [qv], qv)
                            for h in range(4):
                                S.dve(lambda e, h=h, qv=qv: e.tensor_scalar(out=qdb[:, h * 256:(h + 1) * 256], in0=qv[:, h * 256:(h + 1) * 256],
                                                                      scalar1=qdec[:, 1, h:h + 1], scalar2=None, op0=ALU.mult),
                                      [qv, qdec], [qdb])
                                S.pool(lambda e, h=h, qv=qv: e.tensor_scalar(out=kdb[:, h * 256:(h + 1) * 256],
                                                                       in0=qv[:, 1024 + h * 256:1024 + (h + 1) * 256],
                                                                       scalar1=kdec[:, 1, h:h + 1], scalar2=None, op0=ALU.mult),
                                       [qv, kdec], [kdb])
                            for c in range(8):
                                S.pe(lambda e, c=c: e.transpose(out=tp[:, c * 128:(c + 1) * 128], in_=qdb[:, c * 128:(c + 1) * 128],
                                                                identity=ident[:]), [qdb, ident], [tp])
                            S.act(lambda e: e.copy(out=qdbT[:].rearrange("p a b -> p (a b)"), in_=tp[:]), [tp], [qdbT])
                            pt = pbt[t % 2]
                            for h in range(4):
                                ps = aps[h % 2]
                                for dc in range(2):
                                    S.pe(lambda e, ps=ps, h=h, dc=dc: e.matmul(ps[:], lhsT=qdbT[:, h * 2 + dc, :], rhs=Sbb[:, h, dc, :],
                                                                          start=(dc == 0), stop=(dc == 1)), [qdbT, Sbb], [ps])
                                S.act(lambda e, ps=ps, h=h, pt=pt: e.copy(out=pt[:, h * 512:(h + 1) * 512], in_=ps[:]), [ps], [pt])
                            S.dma(rows(pb_d, t), pt[:], [pt], [R_pb[t]], pt)
                            n = 0
                            for h in range(4):
                                for dc in range(2):
                                    ps = aps[2 + n % 4]
                                    n += 1
                                    S.pe(lambda e, ps=ps, h=h, dc=dc, qv=qv: e.matmul(
                                        ps[:], lhsT=kdb[:, h * 256 + dc * 128: h * 256 + (dc + 1) * 128],
                                        rhs=qv[:, 2048 + h * 512: 2048 + (h + 1) * 512], start=True, stop=True), [kdb, qv], [ps])
                                    S.dve(lambda e, ps=ps, h=h, dc=dc: e.scalar_tensor_tensor(
                                        out=Sb32[:, h, dc, :], in0=Sb32[:, h, dc, :], scalar=cdec[:, 1, h:h + 1], in1=ps[:],
                                        op0=ALU.mult, op1=ALU.add), [Sb32, cdec, ps], [Sb32])
                            S.act(lambda e: e.copy(out=Sbb[:].rearrange("p h c v -> p (h c v)"),
                                                   in_=Sb32[:].rearrange("p h c v -> p (h c v)")), [Sb32], [Sbb])
                        if not latent:
                            S.dma(nsb_d[si, jr].rearrange("h (c p) v -> p (h c) v", p=128),
                                  Sb32[:].rearrange("p h c v -> p (h c) v"), [Sb32], [R_o], Sb32)
                    S.barrier()
                S.es = sc0

                scb = ExitStack()
                S.es = scb
                with scb:
                    wout = S.sbuf("rwout", [128, 16, D], BF16)
                    load_w(wout, rwout_d[jr], D, "rwout")
                    zt = [S.sbuf("bz%d" % k, [128, 6144], BF16) for k in range(2)]
                    pbt = S.sbuf("bpbt", [128, 2048], F32)
                    xts = [S.sbuf("bx%d" % k, [128, D], F32) for k in range(2)]
                    qdf = S.sbuf("bqdf", [128, 1024], BF16)
                    kdf = S.sbuf("bkdf", [128, 1024], BF16)
                    QT = S.sbuf("bQT", [128, 24, 128], BF16)
                    attm = S.sbuf("battm", [128, 512], BF16)
                    Sf32 = S.sbuf("bSf32", [128, 4, 2, 512], F32)
                    Sfb = S.sbuf("bSfb", [128, 4, 2, 512], BF16)
                    o32 = S.sbuf("bo32", [128, 2048], F32)
                    go = S.sbuf("bgo", [128, 2048], BF16)
                    goT = S.sbuf("bgoT", [128, 16, 128], BF16)
                    gst = S.sbuf("bgst", [128, 4, 6], F32)
                    gmv = S.sbuf("bgmv", [128, 4, 2], F32)
                    grs = S.sbuf("bgrs", [128, 4], F32)
                    tmp = S.sbuf("btmp", [128, D], F32)
                    r = S.sbuf("br", [128, D], F32)
                    st = S.sbuf("bst", [128, 2, 6], F32)
                    mv = S.sbuf("bmv", [128, 2], F32)
                    rstd = S.sbuf("brstd", [128, 1], F32)
                    tp = S.psum("btp", [128, 1024], BF16)
                    bps = [S.psum("bps%d" % k, [128, 512], F32) for k in range(5)]
                    Y = S.psum("bY", [128, 1024], F32)
                    for si, (t0, ntl, latent, j) in enumerate(seqs):
                        if latent:
                            S.dma(Sf32[:].rearrange("p h c v -> p (h c) v"),
                                  sretf_d[jr].rearrange("h (c p) v -> p (h c) v", p=128), [RIN], [Sf32], Sf32)
                        else:
                            S.dve(lambda e: e.memset(Sf32[:].rearrange("p h c v -> p (h c v)"), 0.0), [], [Sf32])
                        S.act(lambda e: e.copy(out=Sfb[:].rearrange("p h c v -> p (h c v)"),
                                               in_=Sf32[:].rearrange("p h c v -> p (h c v)")), [Sf32], [Sfb])
                        for tt in range(ntl):
                            t = t0 + tt
                            z = zt[t % 2]
                            xt = xts[t % 2]
                            S.dma(z[:], rows(z_d, t), [R_z[t]], [z], z)
                            S.dma(pbt[:], rows(pb_d, t), [R_pb[t]], [pbt], pbt)
                            S.dma(xt[:], rows(xsrc(first), t), [xres(first, t)], [xt], xt)
                            for h in range(4):
                                S.dve(lambda e, h=h, z=z: e.tensor_scalar(out=qdf[:, h * 256:(h + 1) * 256], in0=z[:, h * 256:(h + 1) * 256],
                                                                     scalar1=qdec[:, 0, h:h + 1], scalar2=None, op0=ALU.mult),
                                      [z, qdec], [qdf])
                                S.pool(lambda e, h=h, z=z: e.tensor_scalar(out=kdf[:, h * 256:(h + 1) * 256],
                                                                      in0=z[:, 1024 + h * 256:1024 + (h + 1) * 256],
                                                                      scalar1=kdec[:, 0, h:h + 1], scalar2=None, op0=ALU.mult),
                                       [z, kdec], [kdf])
                            for grp, (src, off, rr) in enumerate([(z, 0, [z]), (qdf, 0, [qdf]), (z, 1024, [z])]):
                                for c in range(8):
                                    S.pe(lambda e, c=c, src=src, off=off: e.transpose(
                                        out=tp[:, c * 128:(c + 1) * 128], in_=src[:, off + c * 128: off + (c + 1) * 128],
                                        identity=ident[:]), rr + [ident], [tp])
                                S.act(lambda e, grp=grp: e.copy(out=QT[:, grp * 8:(grp + 1) * 8, :].rearrange("p a b -> p (a b)"), in_=tp[:]),
                                      [tp], [QT])
                            pa = bps[4]
                            for h in range(4):
                                for dc in range(2):
                                    S.pe(lambda e, h=h, dc=dc: e.matmul(pa[:, h * 128:(h + 1) * 128], lhsT=QT[:, 16 + h * 2 + dc, :],
                                                                        rhs=QT[:, h * 2 + dc, :], start=(dc == 0), stop=(dc == 1)),
                                         [QT], [pa])
                            S.dve(lambda e: e.tensor_tensor(out=attm[:], in0=pa[:], in1=dmask[:].rearrange("p h i -> p (h i)"), op=ALU.mult),
                                  [pa, dmask], [attm])
                            for h in range(4):
                                ps = bps[h]
                                S.pe(lambda e, ps=ps, h=h, z=z: e.matmul(ps[:], lhsT=attm[:, h * 128:(h + 1) * 128],
                                                                    rhs=z[:, 2048 + h * 512:2048 + (h + 1) * 512], start=True, stop=False),
                                     [attm, z], [ps])
                                for dc in range(2):
                                    S.pe(lambda e, ps=ps, h=h, dc=dc: e.matmul(ps[:], lhsT=QT[:, 8 + h * 2 + dc, :], rhs=Sfb[:, h, dc, :],
                                                                          start=False, stop=(dc == 1)), [QT, Sfb], [ps])
                                S.dve(lambda e, ps=ps, h=h: e.tensor_tensor(out=o32[:, h * 512:(h + 1) * 512], in0=ps[:],
                                                                       in1=pbt[:, h * 512:(h + 1) * 512], op=ALU.add), [ps, pbt], [o32])
                                S.dve(lambda e, h=h: e.bn_stats(out=gst[:, h, :], in_=o32[:, h * 512:(h + 1) * 512]), [o32], [gst])
                                S.dve(lambda e, h=h: e.bn_aggr(out=gmv[:, h, :], in_=gst[:, h, :]), [gst], [gmv])
                            S.act(lambda e: e.activation(out=grs[:], in_=gmv[:, :, 1], func=AF.Sqrt, bias=epsc[:], scale=1.0), [gmv, epsc], [grs])
                            S.dve(lambda e: e.reciprocal(out=grs[:], in_=grs[:]), [grs], [grs])
                            for h in range(4):
                                S.dve(lambda e, h=h: e.tensor_scalar(out=o32[:, h * 512:(h + 1) * 512], in0=o32[:, h * 512:(h + 1) * 512],
                                                                     scalar1=gmv[:, h, 0:1], scalar2=grs[:, h:h + 1],
                                                                     op0=ALU.subtract, op1=ALU.mult), [o32, gmv, grs], [o32])
                            S.pool(lambda e, z=z: e.tensor_tensor(out=go[:], in0=o32[:], in1=z[:, 4096:6144], op=ALU.mult), [o32, z], [go])
                            for half in range(2):
                                for c in range(8):
                                    cc = half * 8 + c
                                    S.pe(lambda e, c=c, cc=cc: e.transpose(out=tp[:, c * 128:(c + 1) * 128], in_=go[:, cc * 128:(cc + 1) * 128],
                                                                           identity=ident[:]), [go, ident], [tp])
                                S.act(lambda e, half=half: e.copy(out=goT[:, half * 8:(half + 1) * 8, :].rearrange("p a b -> p (a b)"), in_=tp[:]),
                                      [tp], [goT])
                            for nh in range(2):
                                for kc in range(16):
                                    S.pe(lambda e, nh=nh, kc=kc: e.matmul(Y[:, nh * 512:(nh + 1) * 512], lhsT=goT[:, kc, :],
                                                                          rhs=wout[:, kc, nh * 512:(nh + 1) * 512],
                                                                          start=(kc == 0), stop=(kc == 15)), [goT, wout], [Y])
                            epilogue(Y, [Y], xt, j, t, tmp, r, st, mv, rstd)
                            n = 0
                            for h in range(4):
                                for dc in range(2):
                                    ps = bps[n % 4]
                                    n += 1
                                    S.pe(lambda e, ps=ps, h=h, dc=dc, z=z: e.matmul(
                                        ps[:], lhsT=kdf[:, h * 256 + dc * 128: h * 256 + (dc + 1) * 128],
                                        rhs=z[:, 2048 + h * 512: 2048 + (h + 1) * 512], start=True, stop=True), [kdf, z], [ps])
                                    S.dve(lambda e, ps=ps, h=h, dc=dc: e.scalar_tensor_tensor(
                                        out=Sf32[:, h, dc, :], in0=Sf32[:, h, dc, :], scalar=cdec[:, 0, h:h + 1], in1=ps[:],
                                        op0=ALU.mult, op1=ALU.add), [Sf32, cdec, ps], [Sf32])
                            S.act(lambda e: e.copy(out=Sfb[:].rearrange("p h c v -> p (h c v)"),
                                                   in_=Sf32[:].rearrange("p h c v -> p (h c v)")), [Sf32], [Sfb])
                        if not latent:
                            S.dma(nsf_d[si, jr].rearrange("h (c p) v -> p (h c) v", p=128),
                                  Sf32[:].rearrange("p h c v -> p (h c) v"), [Sf32], [R_o], Sf32)
                    S.barrier()
                S.es = sc0
            S.es = root

        def attention_layer(i, first):
            ja = i // 2
            sc0 = ExitStack()
            S.es = sc0
            with sc0:
                win = S.sbuf("awin", [128, 8, 1536], BF16)
                wout = S.sbuf("awout", [128, 8, D], BF16)
                load_w(win, awin_d[ja], 1536, "awin")
                load_w(wout, awout_d[ja], D, "awout")
                convert_uv(i)
                esink = S.sbuf("esink", [128, 16], F32)
                S.dma(esink[:], asink_d[ja:ja + 1, :].to_broadcast([128, 16]), [RIN], [esink], esink)
                S.act(lambda e: e.activation(out=esink[:], in_=esink[:], func=AF.Exp), [esink], [esink])
                mprev = S.sbuf("mprev", [128, 4, 128], BF16)
                mnext = S.sbuf("mnext", [128, 4, 128], BF16)
                S.dve(lambda e: e.tensor_copy(out=mprev[:], in_=cst[:, 4, :].unsqueeze(1).to_broadcast([128, 4, 128])), [cst], [mprev])
                S.dve(lambda e: e.tensor_copy(out=mnext[:], in_=cst[:, 5, :].unsqueeze(1).to_broadcast([128, 4, 128])), [cst], [mnext])
                NSm = max(NS, 2)
                KT = S.sbuf("aKT", [64, 4, NSm * 128], BF16)
                VL = S.sbuf("aVL", [128, NSm, 4, 65], BF16)
                CKT = S.sbuf("aCKT", [64, 4, 512], BF16)
                CV = S.sbuf("aCV", [128, 4, 4, 65], BF16)
                c32 = S.sbuf("ac32", [128, 4, 256], F32)
                cb = S.sbuf("acb", [128, 4, 256], BF16)
                xts = [S.sbuf("ax%d" % k, [128, D], F32) for k in range(2)]
                hb = S.sbuf("ahb", [128, D], BF16)
                hT = S.sbuf("ahT", [128, 8, 128], BF16)
                q32 = S.sbuf("aq32", [128, 1536], F32)
                ra = S.sbuf("ara", [128, 512], F32)
                rb = S.sbuf("arb", [128, 512], F32)
                qb = [S.sbuf("aqb%d" % k, [128, 1024], BF16) for k in range(2)]
                kb = S.sbuf("akb", [128, 256], BF16)
                rc = [S.sbuf("arc%d" % k, [128, 2, 32], F32) for k in range(2)]
                qT = S.sbuf("aqT", [64, 16, 128], BF16)
                PT = [S.sbuf("aPT%d" % k, [128, 512], BF16) for k in range(7)]
                rden = S.sbuf("arden", [128, 16], F32)
                on = S.sbuf("aon", [128, 1024], BF16)
                onT = S.sbuf("aonT", [128, 8, 128], BF16)
                tmp = S.sbuf("atmp", [128, D], F32)
                r = S.sbuf("ar", [128, D], F32)
                st = S.sbuf("ast", [128, 2, 6], F32)
                mv = S.sbuf("amv", [128, 2], F32)
                rstd = S.sbuf("arstd", [128, 1], F32)
                tp = S.psum("atp", [128, 1024], BF16)
                tq = S.psum("atq", [128, 2048], BF16)
                sps = [S.psum("asps%d" % k, [128, 512], F32) for k in range(2)]
                ops_ = [S.psum("aops%d" % k, [128, 4, 65], F32) for k in range(1)]
                Y = S.psum("aY", [128, 1024], F32)

                S.dve(lambda e: e.memset(VL[:].rearrange("p a b c -> p (a b c)"), 1.0), [], [VL])
                S.dve(lambda e: e.memset(CV[:].rearrange("p a b c -> p (a b c)"), 1.0), [], [CV])
                S.dma(c32[:], ck_d[ja].rearrange("(b p) f -> p b f", p=128), [RIN], [c32], c32)
                S.dve(lambda e: e.tensor_copy(out=cb[:], in_=c32[:]), [c32], [cb])
                for b in range(4):
                    for g in range(4):
                        S.pe(lambda e, b=b, g=g: e.transpose(out=tp[0:64, g * 128:(g + 1) * 128], in_=cb[:, b, g * 64:(g + 1) * 64],
                                                             identity=ident[:]), [cb, ident], [tp])
                    S.act(lambda e, b=b: e.copy(out=CKT[:, :, b * 128:(b + 1) * 128],
                                                in_=tp[0:64, 0:512].rearrange("p (g k) -> p g k", g=4)), [tp], [CKT])
                S.dma(c32[:], cv_d[ja].rearrange("(b p) f -> p b f", p=128), [RIN], [c32], c32)
                S.dve(lambda e: e.tensor_copy(out=CV[:, :, :, 0:64], in_=c32[:].rearrange("p b (g d) -> p b g d", g=4)), [c32], [CV])

                for si, (t0, ntl, latent, j) in enumerate(seqs):
                    for tt in range(ntl):
                        t = t0 + tt
                        xt = xts[t % 2]
                        S.dma(xt[:], rows(xsrc(first), t), [xres(first, t)], [xt], xt)
                        prologue(xt, j, hb, hT, tp)
                        if latent:
                            rct = rc[t % 2]
                            S.dma(rct[:, 0, :], rac_d[tt * 128:(tt + 1) * 128, :], [RIN], [rct], rct)
                            S.dma(rct[:, 1, :], ras_d[tt * 128:(tt + 1) * 128, :], [RIN], [rct], rct)
                        for nb in range(3):
                            ps = sps[nb % 2]
                            for kc in range(8):
                                S.pe(lambda e, ps=ps, kc=kc, nb=nb: e.matmul(ps[:], lhsT=hT[:, kc, :], rhs=win[:, kc, nb * 512:(nb + 1) * 512],
                                                                        start=(kc == 0), stop=(kc == 7)), [hT, win], [ps])
                            S.act(lambda e, ps=ps, nb=nb: e.copy(out=q32[:, nb * 512:(nb + 1) * 512], in_=ps[:]), [ps], [q32])
                        if not latent:
                            S.dma(nk_d[si, ja, tt * 128:(tt + 1) * 128, :], q32[:, 1024:1280], [q32], [R_o], q32)
                            S.dma(nv_d[si, ja, tt * 128:(tt + 1) * 128, :], q32[:, 1280:1536], [q32], [R_o], q32)
                        q_out = qb[t % 2]
                        if latent:
                            v5 = q32[:, 0:1280].rearrange("p (h a b c) -> p h a b c", h=20, a=2, b=2)
                            for a in range(2):
                                x1 = v5[:, :, a, 0, :]
                                x2 = v5[:, :, a, 1, :]
                                cosb = rct[:, 0, a * 16:(a + 1) * 16].unsqueeze(1).to_broadcast([128, 20, 16])
                                sinb = rct[:, 1, a * 16:(a + 1) * 16].unsqueeze(1).to_broadcast([128, 20, 16])
                                ra3 = ra[:, 0:320].rearrange("p (h c) -> p h c", h=20)
                                rb3 = rb[:, 0:320].rearrange("p (h c) -> p h c", h=20)
                                ra3b = ra[:, 320:640].rearrange("p (h c) -> p h c", h=20) if False else None
                                S.dve(lambda e, x1=x1, cosb=cosb, ra3=ra3: e.tensor_tensor(out=ra3, in0=x1, in1=cosb, op=ALU.mult), [q32, rct], [ra])
                                S.pool(lambda e, x2=x2, sinb=sinb, rb3=rb3: e.tensor_tensor(out=rb3, in0=x2, in1=sinb, op=ALU.mult), [q32, rct], [rb])
                                S.dve(lambda e, ra3=ra3, rb3=rb3: e.tensor_tensor(out=ra3, in0=ra3, in1=rb3, op=ALU.subtract), [ra, rb], [ra])
                                S.pool(lambda e, x1=x1, sinb=sinb, rb3=rb3: e.tensor_tensor(out=rb3, in0=x1, in1=sinb, op=ALU.mult), [q32, rct, ra], [rb])
                                S.dve(lambda e, x1=x1, ra3=ra3: e.tensor_copy(out=x1, in_=ra3), [ra, rb], [q32])
                                S.dve(lambda e, x2=x2, cosb=cosb, ra3=ra3: e.tensor_tensor(out=ra3, in0=x2, in1=cosb, op=ALU.mult), [q32, rct], [ra])
                                S.dve(lambda e, x2=x2, ra3=ra3, rb3=rb3: e.tensor_tensor(out=x2, in0=ra3, in1=rb3, op=ALU.add), [ra, rb], [q32])
                        S.act(lambda e, q_out=q_out: e.mul(out=q_out[:], in_=q32[:, 0:1024], mul=0.125), [q32], [q_out])
                        S.dma(rows(qs_d, t), q_out[:], [q_out], [R_qs[t]], q_out)
                        S.dve(lambda e: e.tensor_copy(out=kb[:], in_=q32[:, 1024:1280]), [q32], [kb])
                        S.dve(lambda e, tt=tt: e.tensor_copy(out=VL[:, tt, :, 0:64], in_=q32[:, 1280:1536].rearrange("p (g d) -> p g d", g=4)),
                              [q32], [VL])
                        for g in range(4):
                            S.pe(lambda e, g=g: e.transpose(out=tp[0:64, g * 128:(g + 1) * 128], in_=kb[:, g * 64:(g + 1) * 64],
                                                            identity=ident[:]), [kb, ident], [tp])
                        S.act(lambda e, tt=tt: e.copy(out=KT[:, :, tt * 128:(tt + 1) * 128],
                                                      in_=tp[0:64, 0:512].rearrange("p (g k) -> p g k", g=4)), [tp], [KT])
                    for tt in range(ntl):
                        t = t0 + tt
                        xt = xts[t % 2]
                        qin = qb[t % 2]
                        S.dma(xt[:], rows(xsrc(first), t), [xres(first, t)], [xt], xt)
                        S.dma(qin[:], rows(qs_d, t), [R_qs[t]], [qin], qin)
                        for h in range(16):
                            S.pe(lambda e, h=h, qin=qin: e.transpose(out=tq[0:64, h * 128:(h + 1) * 128], in_=qin[:, h * 64:(h + 1) * 64],
                                                                     identity=ident[:]), [qin, ident], [tq])
                        S.act(lambda e: e.copy(out=qT[:].rearrange("p a b -> p (a b)"), in_=tq[0:64, :]), [tq], [qT])
                        if latent:
                            blocks = []
                            if tt > 0:
                                blocks.append(("loc", tt - 1, mprev))
                            blocks.append(("loc", tt, None))
                            if tt < ntl - 1:
                                blocks.append(("loc", tt + 1, mnext))
                            for b in range(4):
                                blocks.append(("ctx", b, None))
                        else:
                            blocks = [("loc", b, None) for b in range(ntl)]
                        for g in range(4):
                            for bi, (kind, b, msk) in enumerate(blocks):
                                ps = sps[bi % 2]
                                kT_ap = (KT[:, g, b * 128:(b + 1) * 128] if kind == "loc" else CKT[:, g, b * 128:(b + 1) * 128])
                                kres = KT if kind == "loc" else CKT
                                S.pe(lambda e, ps=ps, kT_ap=kT_ap, g=g, msk=msk: e.matmul(
                                    ps[:], lhsT=kT_ap, rhs=qT[:, 4 * g:4 * g + 4, :], start=True, stop=(msk is None)), [kres, qT], [ps])
                                if msk is not None:
                                    S.pe(lambda e, ps=ps, msk=msk: e.matmul(ps[:], lhsT=ident[:], rhs=msk[:].rearrange("p a b -> p (a b)"),
                                                                           start=False, stop=True), [ident, msk], [ps])
                                S.act(lambda e, ps=ps, bi=bi: e.activation(out=PT[bi][:], in_=ps[:], func=AF.Exp), [ps], [PT[bi]])
                            og = ops_[0]
                            for hh in range(4):
                                for bi, (kind, b, msk) in enumerate(blocks):
                                    v_ap = (VL[:, b, g, :] if kind == "loc" else CV[:, b, g, :])
                                    vres = VL if kind == "loc" else CV
                                    S.pe(lambda e, og=og, hh=hh, bi=bi, v_ap=v_ap, nb=len(blocks): e.matmul(
                                        og[:, hh, :], lhsT=PT[bi][:, hh * 128:(hh + 1) * 128], rhs=v_ap,
                                        start=(bi == 0), stop=(bi == nb - 1)), [PT[bi], vres], [og])
                            S.dve(lambda e, og=og, g=g: e.tensor_tensor(out=rden[:, 4 * g:4 * g + 4], in0=og[:, :, 64],
                                                                   in1=esink[:, 4 * g:4 * g + 4], op=ALU.add), [og, esink], [rden])
                            S.dve(lambda e, g=g: e.reciprocal(out=rden[:, 4 * g:4 * g + 4], in_=rden[:, 4 * g:4 * g + 4]), [rden], [rden])
                            S.dve(lambda e, og=og, g=g: e.tensor_tensor(
                                out=on[:, g * 256:(g + 1) * 256].rearrange("p (h d) -> p h d", h=4), in0=og[:, :, 0:64],
                                in1=rden[:, 4 * g:4 * g + 4].unsqueeze(2).to_broadcast([128, 4, 64]), op=ALU.mult), [og, rden], [on])
                        for c in range(8):
                            S.pe(lambda e, c=c: e.transpose(out=tp[:, c * 128:(c + 1) * 128], in_=on[:, c * 128:(c + 1) * 128],
                                                            identity=ident[:]), [on, ident], [tp])
                        S.act(lambda e: e.copy(out=onT[:].rearrange("p a b -> p (a b)"), in_=tp[:]), [tp], [onT])
                        for nh in range(2):
                            for kc in range(8):
                                S.pe(lambda e, nh=nh, kc=kc: e.matmul(Y[:, nh * 512:(nh + 1) * 512], lhsT=onT[:, kc, :],
                                                                      rhs=wout[:, kc, nh * 512:(nh + 1) * 512],
                                                                      start=(kc == 0), stop=(kc == 7)), [onT, wout], [Y])
                        epilogue(Y, [Y], xt, j, t, tmp, r, st, mv, rstd)
                S.barrier()
            S.es = root

        def peer_layer(i):
            GS = OPTS["GS"]
            UVDT = BF16 if OPTS["uvbf16"] else F32
            NG = 128 // GS
            sc0 = ExitStack()
            S.es = sc0
            with sc0:
                wq = S.sbuf("pwq", [128, 8, 2048], BF16)
                load_w(wq, pwq_d[i], 2048, "pwq")
                keysT = S.sbuf("pkeysT", [128, 16, 128], BF16)
                tp = S.psum("ptp", [128, 1024], BF16)
                sck = ExitStack()
                S.es = sck
                with sck:
                    k32 = S.sbuf("pk32", [128, 16, 128], F32)
                    kbf = S.sbuf("pkbf", [128, 16, 128], BF16)
                    S.dma(k32[:], pkeys_d[i].rearrange("c n d -> n c d"), [RIN], [k32], k32)
                    S.dve(lambda e: e.tensor_copy(out=kbf[:], in_=k32[:]), [k32], [kbf])
                    for half in range(2):
                        for c in range(8):
                            cc = half * 8 + c
                            S.pe(lambda e, c=c, cc=cc: e.transpose(out=tp[:, c * 128:(c + 1) * 128], in_=kbf[:, cc, :], identity=ident[:]),
                                 [kbf, ident], [tp])
                        S.act(lambda e, half=half: e.copy(out=keysT[:, half * 8:(half + 1) * 8, :].rearrange("p a b -> p (a b)"), in_=tp[:]),
                              [tp], [keysT])
                    S.barrier()
                S.es = sc0

                xts = [S.sbuf("px%d" % k, [128, D], F32) for k in range(2)]
                h32s = [S.sbuf("ph32%d" % k, [128, D], F32) for k in range(2)]
                eis = [S.sbuf("pei%d" % k, [128, 128], I32) for k in range(2)]
                wsms = [S.sbuf("pwsm%d" % k, [128, 8, 16], F32) for k in range(2)]
                hb = S.sbuf("phb", [128, D], BF16)
                hT = S.sbuf("phT", [128, 8, 128], BF16)
                qb = S.sbuf("pqb", [128, 2048], BF16)
                qT = S.sbuf("pqT", [128, 16, 128], BF16)
                s32 = S.sbuf("ps32", [128, 16, 128], F32)
                wk = S.sbuf("pwk", [128, 4, 256], F32)
                svR = [S.dram("svR%d" % k) for k in range(16)]
                siuR = [S.dram("siuR%d" % k) for k in range(16)]
                wkR = [S.dram("wkR%d" % k) for k in range(4)]
                combR = [S.dram("combR%d" % k) for k in range(8)]
                csR = [S.dram("csR%d" % k) for k in range(8)]
                ciuR = [S.dram("ciuR%d" % k) for k in range(8)]
                sv = S.sbuf("psv", [128, 16, 16], F32)
                siu = S.sbuf("psiu", [128, 16, 16], U32)
                sif = S.sbuf("psif", [128, 16, 16], F32)
                comb = S.sbuf("pcomb", [128, 8, 256], F32)
                cs = S.sbuf("pcs", [128, 8, 16], F32)
                ciu = S.sbuf("pciu", [128, 8, 16], U32)
                cia = S.sbuf("pcia", [128, 8, 16], U32)
                cib = S.sbuf("pcib", [128, 8, 16], U32)
                caf = S.sbuf("pcaf", [128, 8, 16], F32)
                cbf = S.sbuf("pcbf", [128, 8, 16], F32)
                oh = [comb] * 2
                i1f = S.sbuf("pi1f", [128, 8, 16], F32)
                i2f = S.sbuf("pi2f", [128, 8, 16], F32)
                wsum = S.sbuf("pwsum", [128, 8], F32)
                av = S.sbuf("pav", [128, 128], F32)
                ga = S.sbuf("pga", [128, 128], F32)
                gb_ = S.sbuf("pgb", [128, 128], F32)
                coef = S.sbuf("pcoef", [128, 128], F32)
                uvg = [[S.sbuf("puv%d_%d" % (k, s_), [128, 2 * D], UVDT) for s_ in range(GS)] for k in range(OPTS["NSETS"])]
                tv = [S.sbuf("ptv%d" % k, [128, D], BF16) for k in range(OPTS["NTV"])]
                tmp = S.sbuf("ptmp", [128, D], F32)
                r = S.sbuf("pr", [128, D], F32)
                st = S.sbuf("pst", [128, 2, 6], F32)
                mv = S.sbuf("pmv", [128, 2], F32)
                rstd = S.sbuf("prstd", [128, 1], F32)
                tq = S.psum("ptq", [128, 2048], BF16)
                qps = [S.psum("pqps%d" % k, [128, 512], F32) for k in range(2)]
                Y = S.psum("pY", [128, 1024], F32)

                def stage_a(t, j, par):
                    xt = xts[par]
                    h32 = h32s[par]
                    ei = eis[par]
                    wsm = wsms[par]
                    S.dma(xt[:], rows(y_d, t), [R_y[t]], [xt], xt)
                    prologue(xt, j, hb, hT, tp, h32=h32)
                    yield
                    for nb in range(4):
                        ps = qps[nb % 2]
                        for kc in range(8):
                            S.pe(lambda e, ps=ps, kc=kc, nb=nb: e.matmul(ps[:], lhsT=hT[:, kc, :], rhs=wq[:, kc, nb * 512:(nb + 1) * 512],
                                                                    start=(kc == 0), stop=(kc == 7)), [hT, wq], [ps])
                        S.act(lambda e, ps=ps, nb=nb: e.copy(out=qb[:, nb * 512:(nb + 1) * 512], in_=ps[:]), [ps], [qb])
                    for c in range(16):
                        S.pe(lambda e, c=c: e.transpose(out=tq[:, c * 128:(c + 1) * 128], in_=qb[:, c * 128:(c + 1) * 128], identity=ident[:]),
                             [qb, ident], [tq])
                    S.act(lambda e: e.copy(out=qT[:].rearrange("p a b -> p (a b)"), in_=tq[:]), [tq], [qT])
                    for b4 in range(4):
                        ps = qps[b4 % 2]
                        for c4 in range(4):
                            c = b4 * 4 + c4
                            S.pe(lambda e, ps=ps, c=c, c4=c4: e.matmul(ps[:, c4 * 128:(c4 + 1) * 128], lhsT=qT[:, c, :], rhs=keysT[:, c, :],
                                                                  start=True, stop=True), [qT, keysT], [ps])
                        S.act(lambda e, ps=ps, b4=b4: e.copy(out=s32[:, b4 * 4:(b4 + 1) * 4, :].rearrange("p a b -> p (a b)"), in_=ps[:]),
                              [ps], [s32])
                    yield
                    for c0 in range(0, 16, 4):
                        grp = list(range(c0, c0 + 4))
                        for c in grp:
                            S.dve(lambda e, c=c: e.max(out=sv[:, c, 0:8], in_=s32[:, c, :]), [s32], [svR[c]])
                        for c in grp:
                            S.dve(lambda e, c=c: e.max_index(out=siu[:, c, 0:8], in_max=sv[:, c, 0:8], in_values=s32[:, c, :]),
                                  [s32, svR[c]], [siuR[c]])
                        for k_, c in enumerate(grp):
                            S.dve(lambda e, c=c, k_=k_: e.match_replace(out=wk[:, k_, 0:128], in_to_replace=sv[:, c, 0:8], in_values=s32[:, c, :],
                                                                      imm_value=-1e30), [s32, svR[c]], [wkR[k_]])
                        for k_, c in enumerate(grp):
                            S.dve(lambda e, c=c, k_=k_: e.max(out=sv[:, c, 8:16], in_=wk[:, k_, 0:128]), [wkR[k_]], [svR[c]])
                        for k_, c in enumerate(grp):
                            S.dve(lambda e, c=c, k_=k_: e.max_index(out=siu[:, c, 8:16], in_max=sv[:, c, 8:16], in_values=wk[:, k_, 0:128]),
                                  [wkR[k_], svR[c]], [siuR[c]])
                        yield
                    S.dve(lambda e: e.tensor_copy(out=sif[:], in_=siu[:]), siuR, [sif])
                    sv4 = sv[:].rearrange("p (h two) m -> p h two m", two=2)
                    S.dve(lambda e: e.tensor_tensor(
                        out=comb[:].rearrange("p h (a b) -> p h a b", a=16),
                        in0=sv4[:, :, 0, :].unsqueeze(3).to_broadcast([128, 8, 16, 16]),
                        in1=sv4[:, :, 1, :].unsqueeze(2).to_broadcast([128, 8, 16, 16]), op=ALU.add), svR, combR)
                    yield
                    for p0 in range(0, 8, 4):
                        grp = list(range(p0, p0 + 4))
                        for p in grp:
                            S.dve(lambda e, p=p: e.max(out=cs[:, p, 0:8], in_=comb[:, p, :]), [combR[p]], [csR[p]])
                        for p in grp:
                            S.dve(lambda e, p=p: e.max_index(out=ciu[:, p, 0:8], in_max=cs[:, p, 0:8], in_values=comb[:, p, :]),
                                  [combR[p], csR[p]], [ciuR[p]])
                        for k_, p in enumerate(grp):
                            S.dve(lambda e, p=p, k_=k_: e.match_replace(out=wk[:, k_, :], in_to_replace=cs[:, p, 0:8], in_values=comb[:, p, :],
                                                                      imm_value=-1e30), [combR[p], csR[p]], [wkR[k_]])
                        for k_, p in enumerate(grp):
                            S.dve(lambda e, p=p, k_=k_: e.max(out=cs[:, p, 8:16], in_=wk[:, k_, :]), [wkR[k_]], [csR[p]])
                        for k_, p in enumerate(grp):
                            S.dve(lambda e, p=p, k_=k_: e.max_index(out=ciu[:, p, 8:16], in_max=cs[:, p, 8:16], in_values=wk[:, k_, :]),
                                  [wkR[k_], csR[p]], [ciuR[p]])
                        yield
                    S.dve(lambda e: e.tensor_tensor(out=wsm[:], in0=cs[:], in1=cs[:, :, 0:1].to_broadcast([128, 8, 16]), op=ALU.subtract),
                          csR, [wsm])
                    S.act(lambda e: e.activation(out=wsm[:], in_=wsm[:], func=AF.Exp), [wsm], [wsm])
                    S.dve(lambda e: e.reduce_sum(out=wsum[:], in_=wsm[:], axis=AX.X), [wsm], [wsum])
                    S.dve(lambda e: e.reciprocal(out=wsum[:], in_=wsum[:]), [wsum], [wsum])
                    S.dve(lambda e: e.tensor_tensor(out=wsm[:], in0=wsm[:], in1=wsum[:].unsqueeze(2).to_broadcast([128, 8, 16]), op=ALU.mult),
                          [wsm, wsum], [wsm])
                    yield
                    S.dve(lambda e: e.tensor_single_scalar(out=cia[:], in_=ciu[:], scalar=4, op=ALU.logical_shift_right), ciuR, [cia])
                    S.dve(lambda e: e.tensor_single_scalar(out=cib[:], in_=ciu[:], scalar=15, op=ALU.bitwise_and), ciuR, [cib])
                    S.dve(lambda e: e.tensor_copy(out=caf[:], in_=cia[:]), [cia], [caf])
                    S.dve(lambda e: e.tensor_copy(out=cbf[:], in_=cib[:]), [cib], [cbf])
                    yield
                    sif4 = sif[:].rearrange("p (h two) m -> p h two m", two=2)
                    ohv = comb[:].rearrange("p h (a b) -> p h a b", a=16)
                    for (cf, which, dst, ohx) in ((caf, 0, i1f, oh[0]), (cbf, 1, i2f, oh[1])):
                        S.dve(lambda e, cf=cf, ohx=ohx: e.tensor_tensor(
                            out=ohv, in0=cf[:].unsqueeze(3).to_broadcast([128, 8, 16, 16]),
                            in1=iota16[:].unsqueeze(1).unsqueeze(1).to_broadcast([128, 8, 16, 16]), op=ALU.is_equal), [cf, iota16] + combR, [ohx] + combR)
                        S.dve(lambda e, which=which, ohx=ohx: e.tensor_tensor(
                            out=ohv, in0=ohv,
                            in1=sif4[:, :, which, :].unsqueeze(2).to_broadcast([128, 8, 16, 16]), op=ALU.mult), [ohx, sif] + combR, [ohx] + combR)
                        S.dve(lambda e, dst=dst, ohx=ohx: e.reduce_sum(out=dst[:].rearrange("p a b -> p (a b)"),
                                                              in_=ohv.rearrange("p a b c -> p (a b) c"), axis=AX.X), [ohx] + combR, [dst])
                        yield
                    S.dve(lambda e: e.scalar_tensor_tensor(out=i1f[:], in0=i1f[:], scalar=128.0, in1=i2f[:], op0=ALU.mult, op1=ALU.add),
                          [i1f, i2f], [i1f])
                    S.dve(lambda e: e.tensor_scalar(out=i1f[:], in0=i1f[:], scalar1=float(i * NEXP), scalar2=None, op0=ALU.add), [i1f], [i1f])
                    S.dve(lambda e: e.tensor_copy(out=ei[:], in_=i1f[:].rearrange("p a b -> p (a b)")), [i1f], [ei])
                    yield

                def stage_b(t, j, par, nxt):
                    xt = xts[par]
                    h32 = h32s[par]
                    ei = eis[par]
                    wsm = wsms[par]
                    wflat = wsm[:].rearrange("p a b -> p (a b)")
                    nv = 0
                    for g in range(NG):
                        bufs = uvg[g % OPTS["NSETS"]]
                        sl0 = g * GS
                        for s_ in range(GS):
                            sl = sl0 + s_
                            b_ = bufs[s_]
                            if OPTS["nogather"]:
                                continue
                            S.add("pool", lambda e, b_=b_, sl=sl: e.indirect_dma_start(
                                out=b_[:], out_offset=None, in_=(uvb_d if OPTS["uvtab"] else puv_d),
                                in_offset=bass.IndirectOffsetOnAxis(ap=ei[:, sl:sl + 1], axis=0)),
                                [ei, RIN] + (R_uvb[i] if OPTS["uvtab"] else []), [b_], dma=b_)
                        for s_ in range(GS):
                            sl = sl0 + s_
                            b_ = bufs[s_]
                            if OPTS["nodots"]:
                                continue
                            S.dve(lambda e, b_=b_, sl=sl: e.scalar_tensor_tensor(out=tmp[:], in0=b_[:, 0:D], scalar=1.0, in1=h32[:], op0=ALU.mult,
                                                                           op1=ALU.mult, accum_out=av[:, sl:sl + 1]), [b_, h32], [tmp, av])
                        gsl = slice(sl0, sl0 + GS)
                        if OPTS["actgelu"]:
                            S.act(lambda e, gsl=gsl: e.activation(out=gb_[:, gsl], in_=av[:, gsl], func=AF.Gelu_apprx_tanh), [av], [gb_])
                        else:
                            S.dve(lambda e, gsl=gsl: e.tensor_tensor(out=ga[:, gsl], in0=av[:, gsl], in1=av[:, gsl], op=ALU.mult), [av], [ga])
                            S.dve(lambda e, gsl=gsl: e.tensor_scalar(out=ga[:, gsl], in0=ga[:, gsl], scalar1=0.044715, scalar2=1.0,
                                                                    op0=ALU.mult, op1=ALU.add), [ga], [ga])
                            S.dve(lambda e, gsl=gsl: e.tensor_tensor(out=ga[:, gsl], in0=ga[:, gsl], in1=av[:, gsl], op=ALU.mult), [ga, av], [ga])
                            S.act(lambda e, gsl=gsl: e.activation(out=gb_[:, gsl], in_=ga[:, gsl], func=AF.Tanh, scale=0.7978845608028654),
                                  [ga], [gb_])
                            S.dve(lambda e, gsl=gsl: e.tensor_scalar(out=gb_[:, gsl], in0=gb_[:, gsl], scalar1=1.0, scalar2=0.5,
                                                                    op0=ALU.add, op1=ALU.mult), [gb_], [gb_])
                            S.dve(lambda e, gsl=gsl: e.tensor_tensor(out=gb_[:, gsl], in0=gb_[:, gsl], in1=av[:, gsl], op=ALU.mult), [gb_, av], [gb_])
                        S.dve(lambda e, gsl=gsl: e.tensor_tensor(out=coef[:, gsl], in0=gb_[:, gsl], in1=wflat[:, gsl], op=ALU.mult),
                              [gb_, wsm], [coef])
                        for s_ in range(GS):
                            sl = sl0 + s_
                            b_ = bufs[s_]
                            tv_ = tv[nv % OPTS['NTV']]
                            nv += 1
                            if OPTS["novside"] and sl not in (0, 127):
                                continue
                            S.act(lambda e, b_=b_, sl=sl, tv_=tv_: e.activation(out=tv_[:], in_=b_[:, D:2 * D], func=AF.Copy,
                                                                         scale=coef[:, sl:sl + 1]), [b_, coef], [tv_])
                            for nh in range(2):
                                S.pe(lambda e, tv_=tv_, nh=nh, sl=sl: e.matmul(Y[:, nh * 512:(nh + 1) * 512], lhsT=ident[:],
                                                                         rhs=tv_[:, nh * 512:(nh + 1) * 512],
                                                                         start=(sl == 0), stop=(sl == 127)), [ident, tv_], [Y])
                        if nxt is not None and not OPTS["noroute"]:
                            next(nxt, None)
                    if nxt is not None:
                        for _ in nxt:
                            pass
                    epilogue(Y, [Y], xt, j, t, tmp, r, st, mv, rstd)

                tiles = []
                for (t0, ntl, latent, j) in seqs:
                    for tt in range(ntl):
                        tiles.append((t0 + tt, j))
                for _ in stage_a(tiles[0][0], tiles[0][1], 0):
                    pass
                for n_, (t, j) in enumerate(tiles):
                    nxt = None
                    if n_ + 1 < len(tiles):
                        nxt = stage_a(tiles[n_ + 1][0], tiles[n_ + 1][1], (n_ + 1) % 2)
                    stage_b(t, j, n_ % 2, nxt)
                S.barrier()
            S.es = root

        for i in range(DEPTH):
            first = (i == 0)
            modulation(i, 0)
            if i % 2 == 0:
                retention_layer(i, first)
            else:
                attention_layer(i, first)
            modulation(i, 1)
            if peer:
                peer_layer(i)
        S.finalize()
        stats = S.stats
    return nc, stats


def _consts(SQ):
    j = np.arange(128, dtype=np.float32)[:, None]
    i = np.arange(128, dtype=np.float32)[None, :]
    cst = np.zeros((128, 6, 128), np.float32)
    cst[:, 0] = np.maximum(i - j, 0.0)
    cst[:, 1] = (i >= j)
    cst[:, 2] = np.maximum(j - i, 0.0)
    cst[:, 3] = (j > i)
    cst[:, 4] = np.where(j >= i, 0.0, -30000.0)
    cst[:, 5] = np.where(j <= i, 0.0, -30000.0)
    p = np.arange(128, dtype=np.float32)
    pvec = np.zeros((128, 8), np.float32)
    pvec[:, 0] = p + 1
    pvec[:, 1] = 128 - p
    pvec[:, 2] = 127 - p
    pvec[:, 3] = p
    iota16 = np.tile(np.arange(16, dtype=np.float32)[None, :], (128, 1))
    pos = np.arange(SQ, dtype=np.float32)
    inv = (10000.0 ** (-np.arange(0, 256, 2, dtype=np.float32) / 256.0)).astype(np.float32)
    ang = (pos[:, None] * inv[None, :]).astype(np.float32)
    rrc, rrs = np.cos(ang).astype(np.float32), np.sin(ang).astype(np.float32)
    t = np.arange(SQ)
    inv2 = (10000.0 ** (-np.arange(0, 32, 2, dtype=np.float32) / 32.0)).astype(np.float32)
    ar = ((t // 64).astype(np.float32)[:, None] * inv2[None, :]).astype(np.float32)
    ac = ((t % 64).astype(np.float32)[:, None] * inv2[None, :]).astype(np.float32)
    rac = np.concatenate([np.cos(ar), np.cos(ac)], 1).astype(np.float32)
    ras = np.concatenate([np.sin(ar), np.sin(ac)], 1).astype(np.float32)
    return dict(cst=cst, pvec=pvec, iota16=iota16, rope_ret_cos=rrc, rope_ret_sin=rrs, rope_att_cos=rac, rope_att_sin=ras)


_PROG = {}
RUNKW = {}
LAST = {}


def run_step(inp, DEPTH, SQ, ncores, peer=True):
    key = (DEPTH, SQ, peer)
    if key not in _PROG:
        _PROG[key] = build_program(DEPTH=DEPTH, SQ=SQ, peer=peer)
    nc, stats = _PROG[key]
    NRET = (DEPTH + 1) // 2
    NATT = max(DEPTH // 2, 1)
    f = lambda a: np.ascontiguousarray(np.asarray(a, dtype=np.float32))
    cs = _consts(SQ)
    shared = dict(
        mod_w=f(inp["mod_w"]), mod_b=f(inp["mod_b"]), ln_g=f(inp["ln_g"]), ln_b=f(inp["ln_b"]),
        ret_w_in=f(inp["ret_w_in"]), ret_w_out=f(inp["ret_w_out"]), ret_decay=f(inp["ret_decay"]).reshape(NRET, 8),
        attn_w_in=f(inp["attn_w_in"])[:NATT], attn_w_out=f(inp["attn_w_out"])[:NATT], attn_sink=f(inp["attn_sink"])[:NATT],
        peer_wq=f(inp["peer_wq"]), peer_keys=f(inp["peer_keys"]).reshape(DEPTH, 16, 128, 128),
        peer_uv=np.ascontiguousarray(np.concatenate([f(inp["peer_u"]).reshape(DEPTH * 16384, 1024),
                                                     f(inp["peer_v"]).reshape(DEPTH * 16384, 1024)], axis=1)), **cs)
    xp, xs = f(inp["x_prompt"]), f(inp["x_sample"])
    in_maps = []
    for c in range(ncores):
        m = dict(shared)
        m["x"] = np.ascontiguousarray(np.concatenate([xp[2 * c], xp[2 * c + 1], xs[c]], 0))
        m["cond"] = np.ascontiguousarray(np.stack([f(inp["c_ctx"]), f(inp["c"])[c]], 0))
        m["sret_f"] = f(inp["state_ret_fwd"])[c]
        m["sret_b"] = f(inp["state_ret_bwd"])[c]
        m["ck"] = np.ascontiguousarray(f(inp["cache_k"])[c][:NATT].reshape(NATT, 512, 256))
        m["cv"] = np.ascontiguousarray(f(inp["cache_v"])[c][:NATT].reshape(NATT, 512, 256))
        in_maps.append(m)
    res = run_bass_kernel_spmd(nc, in_maps, core_ids=list(range(ncores)), **RUNKW)
    LAST['res'] = res
    rs = res.results
    B = 2 * ncores
    y = np.stack([r["y"] for r in rs], 0)
    yp = y[:, :512].reshape(B, 256, 1024)
    ys = y[:, 512:]
    nsf = np.concatenate([r["nsf"] for r in rs], 0)
    nsb = np.concatenate([r["nsb"] for r in rs], 0)
    nk = np.concatenate([r["nk"] for r in rs], 0).reshape(B, NATT, 256, 4, 64)
    nv = np.concatenate([r["nv"] for r in rs], 0).reshape(B, NATT, 256, 4, 64)
    return tuple(np.ascontiguousarray(a.astype(np.float32)) for a in (yp, ys, nsf, nsb, nk, nv))


def kernel(**inputs):
    return run_step(inputs, DEPTH=4, SQ=4096, ncores=8)
```

```python
import numpy as np
from contextlib import ExitStack
import concourse.bass as bass
import concourse.mybir as mybir
from concourse.bass_utils import run_bass_kernel_spmd

F32 = mybir.dt.float32
BF16 = mybir.dt.bfloat16
I32 = mybir.dt.int32
U32 = mybir.dt.uint32
ALU = mybir.AluOpType
AF = mybir.ActivationFunctionType
AX = mybir.AxisListType


class Res:
    __slots__ = ("name", "lw", "rd", "sem", "t")

    def __init__(self, name, t=None):
        self.name = name
        self.lw = None
        self.rd = {}
        self.sem = None
        self.t = t

    def __getitem__(self, k):
        return self.t[k]


class Op:
    __slots__ = ("eng", "fn", "reads", "writes", "dma", "deps", "hasdep", "ev", "waits", "idx")


class Sched:
    ENGS = ("pe", "dve", "act", "pool", "sp")

    def __init__(self, nc, es, max_dma_sems=94):
        self.nc = nc
        self.es = es
        self.es_root = es
        self.ops = []
        self.nres = 0
        self.dma_sems = []
        self.max_dma_sems = max_dma_sems
        self.rr = 0
        self.sem_load = []

    def sbuf(self, name, shape, dtype):
        self.nres += 1
        name = "sb%d_%s" % (self.nres, name)
        t = self.es.enter_context(self.nc.sbuf_tensor(name, list(shape), dtype))
        return Res(name, t)

    def psum(self, name, shape, dtype):
        self.nres += 1
        name = "ps%d_%s" % (self.nres, name)
        t = self.es.enter_context(self.nc.psum_tensor(name, list(shape), dtype))
        return Res(name, t)

    def dram(self, name):
        return Res(name, None)

    def _dsem(self, res):
        if res.sem is None:
            if len(self.dma_sems) < self.max_dma_sems:
                s = self.es_root.enter_context(self.nc.semaphore("dq%d" % len(self.dma_sems)))
                self.dma_sems.append(s)
                res.sem = len(self.dma_sems) - 1
                self.sem_load.append(0)
            else:
                res.sem = min(range(len(self.dma_sems)), key=lambda k: self.sem_load[k])
                self.sem_load[res.sem] += 64
        return res.sem

    def add(self, eng, fn, reads=(), writes=(), dma=None):
        op = Op()
        op.eng = eng
        op.fn = fn
        op.reads = tuple(reads)
        op.writes = tuple(writes)
        op.dma = None if dma is None else self._dsem(dma)
        if op.dma is not None:
            self.sem_load[op.dma] += 1
        op.idx = len(self.ops)
        self.ops.append(op)
        return op

    def barrier(self):
        op = Op()
        op.eng = None
        op.fn = None
        op.reads = ()
        op.writes = ()
        op.dma = None
        op.idx = len(self.ops)
        self.ops.append(op)

    def pe(self, fn, reads=(), writes=()):
        return self.add("pe", fn, reads, writes)

    def dve(self, fn, reads=(), writes=()):
        return self.add("dve", fn, reads, writes)

    def act(self, fn, reads=(), writes=()):
        return self.add("act", fn, reads, writes)

    def pool(self, fn, reads=(), writes=()):
        return self.add("pool", fn, reads, writes)

    def dma(self, out, in_, reads, writes, semres, eng="sp", **kw):
        return self.add(eng, lambda e: e.dma_start(out=out, in_=in_, **kw), reads, writes, dma=semres)

    def finalize(self):
        nc = self.nc
        ops = self.ops
        latest = {}
        bar = {}
        bar_pending = set()
        for op in ops:
            if op.eng is None:
                bar = dict(latest)
                bar_pending = set(self.ENGS)
                op.deps = []
                op.hasdep = False
                continue
            deps = {}
            if op.eng in bar_pending:
                bar_pending.discard(op.eng)
                for x in bar.values():
                    deps[x.idx] = x
            for r in op.reads:
                if r.lw is not None:
                    deps[r.lw.idx] = r.lw
            for w in op.writes:
                if w.lw is not None:
                    deps[w.lw.idx] = w.lw
                for x in w.rd.values():
                    deps[x.idx] = x
            deps.pop(op.idx, None)
            dl = []
            for d in deps.values():
                if op.eng == "pe" and d.eng == "pe" and op.dma is None and d.dma is None:
                    continue
                dl.append(d)
            op.deps = dl
            op.hasdep = False
            key = op.eng if op.dma is None else ("d", op.dma)
            latest[key] = op
            for r in op.reads:
                r.rd[key] = op
            for w in op.writes:
                w.lw = op
                w.rd = {}
        for op in ops:
            for d in op.deps:
                d.hasdep = True
        EPOCH = 30000
        engsem = {}

        def get_engsem(e, ep):
            if (e, ep) not in engsem:
                engsem[(e, ep)] = self.es_root.enter_context(nc.semaphore("eng_%s_%d" % (e, ep)))
            return engsem[(e, ep)]
        engcnt = {e: 0 for e in self.ENGS}
        dcnt = [0] * len(self.dma_sems)
        waited = {e: {} for e in self.ENGS}
        nwaits = 0
        ops = [o for o in ops if o.eng is not None]
        for op in ops:
            need = {}
            for d in op.deps:
                if d.dma is not None:
                    k = ("d", d.dma)
                    v = dcnt[d.dma]
                else:
                    k = ("e", d.eng, d.ev[0])
                    v = d.ev[1]
                if need.get(k, 0) < v:
                    need[k] = v
            w = []
            wd = waited[op.eng]
            for k, v in need.items():
                if wd.get(k, 0) >= v:
                    continue
                wd[k] = v
                w.append((k, v))
            op.waits = w
            nwaits += len(w)
            if op.dma is not None:
                dcnt[op.dma] += 16
                op.ev = dcnt[op.dma]
            elif op.hasdep:
                engcnt[op.eng] += 1
                ep, c = divmod(engcnt[op.eng] - 1, EPOCH)
                op.ev = (ep, c + 1)
                get_engsem(op.eng, ep)
            else:
                op.ev = None
        self.final_dcnt = dcnt
        self.stats = dict(nops=len(ops), nwaits=nwaits, engcnt=dict(engcnt),
                          per_eng={e: sum(1 for o in ops if o.eng == e) for e in self.ENGS})
        per = {e: [o for o in ops if o.eng == e] for e in self.ENGS}
        dma_sems = self.dma_sems

        def semof(k):
            return dma_sems[k[1]] if k[0] == "d" else engsem[(k[1], k[2])]

        def run(eng_obj, name):
            for op in per[name]:
                for k, v in op.waits:
                    eng_obj.wait_ge(semof(k), v)
                ins = op.fn(eng_obj)
                if op.dma is not None:
                    ins.then_inc(dma_sems[op.dma], 16)
                elif op.ev is not None:
                    ins.then_inc(engsem[(name, op.ev[0])], 1)
            if name == "sp":
                for i, s in enumerate(dma_sems):
                    if dcnt[i] > 0:
                        eng_obj.wait_ge(s, dcnt[i])

        with nc.Block() as block:
            @block.tensor
            def _(e):
                run(e, "pe")

            @block.vector
            def _(e):
                run(e, "dve")

            @block.scalar
            def _(e):
                run(e, "act")

            @block.gpsimd
            def _(e):
                run(e, "pool")

            @block.sync
            def _(e):
                run(e, "sp")

D = 1024
ALPHA = 8.0 ** 0.25
EPS = 1e-5
NEXP = 16384
OPTS = dict(GS=4, uvbf16=True, actgelu=True, nogather=0, nodots=0, novside=0, noroute=0, uvtab=1, NSETS=4, NTV=2)


def build_program(DEPTH=4, SQ=4096, peer=True):
    nc = bass.Bass("TRN2", target_bir_lowering=False)
    NRET = (DEPTH + 1) // 2
    NATT = DEPTH // 2
    NATTd = max(NATT, 1)
    NP = 4
    NS = SQ // 128
    NT = NP + NS
    seqs = [(0, 2, False, 0), (2, 2, False, 0), (4, NS, True, 1)]

    def din(name, shape, dt=F32):
        return nc.dram_tensor(name, list(shape), dt, kind="ExternalInput").ap()

    def dout(name, shape, dt=F32):
        return nc.dram_tensor(name, list(shape), dt, kind="ExternalOutput").ap()

    x_d = din("x", [NT * 128, D])
    cond_d = din("cond", [2, D])
    sretf_d = din("sret_f", [NRET, 4, 256, 512])
    sretb_d = din("sret_b", [NRET, 4, 256, 512])
    ck_d = din("ck", [NATTd, 512, 256])
    cv_d = din("cv", [NATTd, 512, 256])
    modw_d = din("mod_w", [DEPTH, D, 6 * D])
    modb_d = din("mod_b", [DEPTH, 6 * D])
    lng_d = din("ln_g", [DEPTH, 2, D])
    lnb_d = din("ln_b", [DEPTH, 2, D])
    rwin_d = din("ret_w_in", [NRET, D, 6144])
    rwout_d = din("ret_w_out", [NRET, 2048, D])
    rdec_d = din("ret_decay", [NRET, 8])
    awin_d = din("attn_w_in", [NATTd, D, 1536])
    awout_d = din("attn_w_out", [NATTd, D, D])
    asink_d = din("attn_sink", [NATTd, 16])
    pwq_d = din("peer_wq", [DEPTH, D, 2048])
    pkeys_d = din("peer_keys", [DEPTH, 16, 128, 128])
    puv_d = din("peer_uv", [DEPTH * NEXP, 2 * D])
    cst_d = din("cst", [128, 6, 128])
    pvec_d = din("pvec", [128, 8])
    iota_d = din("iota16", [128, 16])
    rrc_d = din("rope_ret_cos", [SQ, 128])
    rrs_d = din("rope_ret_sin", [SQ, 128])
    rac_d = din("rope_att_cos", [SQ, 32])
    ras_d = din("rope_att_sin", [SQ, 32])

    y_d = dout("y", [NT * 128, D])
    nsf_d = dout("nsf", [2, NRET, 4, 256, 512])
    nsb_d = dout("nsb", [2, NRET, 4, 256, 512])
    nk_d = dout("nk", [2, NATTd, 256, 256])
    nv_d = dout("nv", [2, NATTd, 256, 256])

    z_d = nc.dram_tensor("z_scr", [NT * 128, 6144], BF16).ap()
    pb_d = nc.dram_tensor("pb_scr", [NT * 128, 2048], F32).ap()
    qs_d = nc.dram_tensor("qs_scr", [NT * 128, 1024], BF16).ap()
    uvb_d = nc.dram_tensor("uvb_scr", [DEPTH * NEXP, 2 * D], BF16).ap() if OPTS["uvtab"] else None
    CVR = 512

    root = ExitStack()
    with root:
        S = Sched(nc, root)
        RIN = S.dram("inputs")
        R_y = [S.dram("y%d" % t) for t in range(NT)]
        R_z = [S.dram("z%d" % t) for t in range(NT)]
        R_pb = [S.dram("pb%d" % t) for t in range(NT)]
        R_qs = [S.dram("qs%d" % t) for t in range(NT)]
        R_o = S.dram("small_outs")
        R_uvb = [[S.dram("uvb%d_%d" % (i_, c_)) for c_ in range(NEXP // CVR)] for i_ in range(DEPTH)]
        cvt = S.dram("cvt")

        def convert_uv(i_):
            if not OPTS["uvtab"]:
                return
            for c_ in range(NEXP // CVR):
                r0 = i_ * NEXP + c_ * CVR
                S.dma(uvb_d[r0:r0 + CVR, :], puv_d[r0:r0 + CVR, :], [RIN], [R_uvb[i_][c_]], cvt, eng="pool")

        def rows(ap, t):
            return ap[t * 128:(t + 1) * 128, :]

        ident_f = S.sbuf("ident_f", [128, 128], F32)
        ident = S.sbuf("ident", [128, 128], BF16)
        ones1 = S.sbuf("ones1", [1, 128], F32)
        condT = S.sbuf("condT", [128, 2, 8], F32)
        modbc = S.sbuf("modbc", [128, 2, 3, D], F32)
        lnbc = S.sbuf("lnbc", [128, 2, D], F32)
        cst = S.sbuf("cst", [128, 6, 128], F32)
        pvec = S.sbuf("pvec", [128, 8], F32)
        iota16 = S.sbuf("iota16", [128, 16], F32)
        epsc = S.sbuf("epsc", [128, 1], F32)

        S.pool(lambda e: e.memset(ident_f[:], 0.0), [], [ident_f])
        S.pool(lambda e: e.affine_select(out=ident_f[:], in_=ident_f[:], pattern=[[-1, 128]],
                                         compare_op=ALU.not_equal, fill=1.0, base=0, channel_multiplier=1),
               [ident_f], [ident_f])
        S.dve(lambda e: e.tensor_copy(out=ident[:], in_=ident_f[:]), [ident_f], [ident])
        S.dve(lambda e: e.memset(ones1[:], 1.0), [], [ones1])
        S.dve(lambda e: e.memset(epsc[:], EPS), [], [epsc])
        S.dma(cst[:], cst_d, [RIN], [cst], cst)
        S.dma(pvec[:], pvec_d, [RIN], [pvec], pvec)
        S.dma(iota16[:], iota_d, [RIN], [iota16], iota16)
        S.dma(condT[:], cond_d.rearrange("j (kc p) -> p j kc", p=128), [RIN], [condT], condT,
              allow_slow_non_contiguous=True)
        S.act(lambda e: e.activation(out=condT[:], in_=condT[:], func=AF.Silu), [condT], [condT])

        def modulation(i, s):
            sc = ExitStack()
            S.es = sc
            with sc:
                wch = [S.sbuf("modw%d" % k, [128, 8, D], F32) for k in range(2)]
                condrep = S.sbuf("condrep", [128, 2, 8, 128], F32)
                S.dve(lambda e: e.tensor_copy(out=condrep[:].rearrange("p j k m -> p (j k) m"),
                                              in_=condT[:].rearrange("p j k -> p (j k)").unsqueeze(2).to_broadcast([128, 16, 128])),
                      [condT], [condrep])
                brow = S.sbuf("modbrow", [1, 3 * D], F32)
                mps = [S.psum("modps%d" % k, [128, 512], F32) for k in range(2)]
                S.dma(brow[:], modb_d[i:i + 1, s * 3 * D:(s + 1) * 3 * D], [RIN], [brow], brow)
                S.dma(lnbc[:, 0, :], lng_d[i, s:s + 1, :].to_broadcast([128, D]), [RIN], [lnbc], lnbc)
                S.dma(lnbc[:, 1, :], lnb_d[i, s:s + 1, :].to_broadcast([128, D]), [RIN], [lnbc], lnbc)
                n = 0
                for blk in range(3):
                    w = wch[blk % 2]
                    c0 = (s * 3 + blk) * D
                    S.dma(w[:], modw_d[i, :, c0:c0 + D].rearrange("(kc p) n -> p kc n", p=128), [RIN], [w], w)
                    for j in range(2):
                        for nh in range(2):
                            ps = mps[n % 2]
                            n += 1
                            for kc in range(8):
                                S.pe(lambda e, ps=ps, j=j, kc=kc, w=w, nh=nh: e.matmul(
                                    ps[:], lhsT=condrep[:, j, kc, :], rhs=w[:, kc, nh * 512:(nh + 1) * 512],
                                    start=(kc == 0), stop=False), [condrep, w], [ps])
                            S.pe(lambda e, ps=ps, blk=blk, nh=nh: e.matmul(
                                ps[:], lhsT=ones1[:], rhs=brow[:, blk * D + nh * 512: blk * D + (nh + 1) * 512],
                                start=False, stop=True), [ones1, brow], [ps])
                            add = 1.0 if blk == 1 else 0.0
                            S.act(lambda e, ps=ps, j=j, blk=blk, nh=nh, add=add: e.activation(
                                out=modbc[:, j, blk, nh * 512:(nh + 1) * 512], in_=ps[:], func=AF.Identity, bias=add, scale=1.0)
                                if add else e.copy(out=modbc[:, j, blk, nh * 512:(nh + 1) * 512], in_=ps[:]),
                                [ps], [modbc])
                S.barrier()
            S.es = root

        def load_w(dst, src2d, N, tag):
            v = src2d.rearrange("(kc p) n -> p kc n", p=128)
            for n0 in range(0, N, 2048):
                n1 = min(N, n0 + 2048)
                S.dma(dst[:, :, n0:n1], v[:, :, n0:n1], [RIN], [dst], dst, eng="pool")

        def prologue(xt, j, hb, hT, tp, h32=None):
            tgt = h32 if h32 is not None else hb
            S.dve(lambda e: e.tensor_tensor(out=tgt[:], in0=xt[:], in1=modbc[:, j, 1, :], op=ALU.mult), [xt, modbc], [tgt])
            if h32 is not None:
                S.pool(lambda e: e.tensor_tensor(out=h32[:], in0=h32[:], in1=modbc[:, j, 0, :], op=ALU.add), [h32, modbc], [h32])
                S.act(lambda e: e.copy(out=hb[:], in_=h32[:]), [h32], [hb])
            else:
                S.pool(lambda e: e.tensor_tensor(out=hb[:], in0=hb[:], in1=modbc[:, j, 0, :], op=ALU.add), [hb, modbc], [hb])
            for kc in range(8):
                S.pe(lambda e, kc=kc: e.transpose(out=tp[:, kc * 128:(kc + 1) * 128], in_=hb[:, kc * 128:(kc + 1) * 128],
                                                  identity=ident[:]), [hb, ident], [tp])
            S.act(lambda e: e.copy(out=hT[:].rearrange("p a b -> p (a b)"), in_=tp[:]), [tp], [hT])

        def epilogue(Y, yreads, xt, j, t, tmp, r, st, mv, rstd):
            S.dve(lambda e: e.tensor_tensor(out=tmp[:], in0=Y[:], in1=modbc[:, j, 2, :], op=ALU.mult), [modbc] + yreads, [tmp])
            S.dve(lambda e: e.scalar_tensor_tensor(out=r[:], in0=xt[:], scalar=ALPHA, in1=tmp[:], op0=ALU.mult, op1=ALU.add),
                  [xt, tmp], [r])
            for c in range(2):
                S.dve(lambda e, c=c: e.bn_stats(out=st[:, c, :], in_=r[:, c * 512:(c + 1) * 512]), [r], [st])
            S.dve(lambda e: e.bn_aggr(out=mv[:], in_=st[:].rearrange("p a b -> p (a b)")), [st], [mv])
            S.act(lambda e: e.activation(out=rstd[:], in_=mv[:, 1:2], func=AF.Sqrt, bias=epsc[:], scale=1.0), [mv, epsc], [rstd])
            S.dve(lambda e: e.reciprocal(out=rstd[:], in_=rstd[:]), [rstd], [rstd])
            S.dve(lambda e: e.tensor_scalar(out=r[:], in0=r[:], scalar1=mv[:, 0:1], scalar2=rstd[:, 0:1],
                                            op0=ALU.subtract, op1=ALU.mult), [r, mv, rstd], [r])
            S.pool(lambda e: e.tensor_tensor(out=r[:], in0=r[:], in1=lnbc[:, 0, :], op=ALU.mult), [r, lnbc], [r])
            S.pool(lambda e: e.tensor_tensor(out=tmp[:], in0=r[:], in1=lnbc[:, 1, :], op=ALU.add), [r, lnbc], [tmp])
            S.dma(rows(y_d, t), tmp[:], [tmp], [R_y[t]], tmp)

        def xsrc(first):
            return x_d if first else y_d

        def xres(first, t):
            return RIN if first else R_y[t]

        def retention_layer(i, first):
            jr = i // 2
            sc0 = ExitStack()
            S.es = sc0
            with sc0:
                lg = S.sbuf("lg", [128, 8], F32)
                qdec = S.sbuf("qdec", [128, 2, 4], F32)
                kdec = S.sbuf("kdec", [128, 2, 4], F32)
                cdec = S.sbuf("cdec", [128, 2, 4], F32)
                dmask = S.sbuf("dmask", [128, 4, 128], F32)
                dtmp = S.sbuf("dtmp", [128, 128], F32)
                c128 = S.sbuf("c128", [128, 1], F32)
                S.dve(lambda e: e.memset(c128[:], 128.0), [], [c128])
                S.dma(lg[:], rdec_d[jr:jr + 1, :].to_broadcast([128, 8]), [RIN], [lg], lg)
                S.act(lambda e: e.activation(out=lg[:], in_=lg[:], func=AF.Exp, scale=-1.0), [lg], [lg])
                S.act(lambda e: e.activation(out=lg[:], in_=lg[:], func=AF.Ln, bias=1.0, scale=1.0), [lg], [lg])
                S.dve(lambda e: e.tensor_scalar(out=lg[:], in0=lg[:], scalar1=-1.0, scalar2=None, op0=ALU.mult), [lg], [lg])
                for h in range(4):
                    for dr in range(2):
                        col = dr * 4 + h
                        pq = 0 if dr == 0 else 1
                        pk = 2 if dr == 0 else 3
                        S.act(lambda e, col=col, pq=pq, dr=dr, h=h: e.activation(
                            out=qdec[:, dr, h:h + 1], in_=lg[:, col:col + 1], func=AF.Exp, scale=pvec[:, pq:pq + 1]),
                            [lg, pvec], [qdec])
                        S.act(lambda e, col=col, pk=pk, dr=dr, h=h: e.activation(
                            out=kdec[:, dr, h:h + 1], in_=lg[:, col:col + 1], func=AF.Exp, scale=pvec[:, pk:pk + 1]),
                            [lg, pvec], [kdec])
                        S.act(lambda e, col=col, dr=dr, h=h: e.activation(
                            out=cdec[:, dr, h:h + 1], in_=lg[:, col:col + 1], func=AF.Exp, scale=c128[:, 0:1]),
                            [lg, c128], [cdec])
                    S.act(lambda e, h=h: e.activation(out=dmask[:, h, :], in_=cst[:, 0, :], func=AF.Exp, scale=lg[:, h:h + 1]),
                          [cst, lg], [dmask])
                    S.dve(lambda e, h=h: e.tensor_tensor(out=dmask[:, h, :], in0=dmask[:, h, :], in1=cst[:, 1, :], op=ALU.mult),
                          [dmask, cst], [dmask])
                    S.act(lambda e, h=h: e.activation(out=dtmp[:], in_=cst[:, 2, :], func=AF.Exp, scale=lg[:, 4 + h:5 + h]),
                          [cst, lg], [dtmp])
                    S.dve(lambda e: e.tensor_tensor(out=dtmp[:], in0=dtmp[:], in1=cst[:, 3, :], op=ALU.mult), [dtmp, cst], [dtmp])
                    S.dve(lambda e, h=h: e.tensor_tensor(out=dmask[:, h, :], in0=dmask[:, h, :], in1=dtmp[:], op=ALU.add),
                          [dmask, dtmp], [dmask])

                scz = ExitStack()
                S.es = scz
                with scz:
                    win = S.sbuf("rwin", [128, 8, 6144], BF16)
                    load_w(win, rwin_d[jr], 6144, "rwin")
                    convert_uv(i)
                    xts = [S.sbuf("zx%d" % k, [128, D], F32) for k in range(2)]
                    hb = S.sbuf("zhb", [128, D], BF16)
                    hT = S.sbuf("zhT", [128, 8, 128], BF16)
                    qk32 = S.sbuf("zqk32", [128, 2048], F32)
                    ra = S.sbuf("zra", [128, 1024], F32)
                    rb = S.sbuf("zrb", [128, 1024], F32)
                    zrow = [S.sbuf("zrow%d" % k, [128, 6144], BF16) for k in range(2)]
                    rc = [S.sbuf("zrc%d" % k, [128, 2, 128], F32) for k in range(2)]
                    tp = S.psum("ztp", [128, 1024], BF16)
                    zps = [S.psum("zps%d" % k, [128, 512], F32) for k in range(4)]
                    for (t0, ntl, latent, j) in seqs:
                        for tt in range(ntl):
                            t = t0 + tt
                            xt = xts[t % 2]
                            zr = zrow[t % 2]
                            S.dma(xt[:], rows(xsrc(first), t), [xres(first, t)], [xt], xt)
                            prologue(xt, j, hb, hT, tp)
                            if latent:
                                rct = rc[t % 2]
                                S.dma(rct[:, 0, :], rrc_d[tt * 128:(tt + 1) * 128, :], [RIN], [rct], rct)
                                S.dma(rct[:, 1, :], rrs_d[tt * 128:(tt + 1) * 128, :], [RIN], [rct], rct)
                            for nb in range(12):
                                ps = zps[nb % 4]
                                for kc in range(8):
                                    S.pe(lambda e, ps=ps, kc=kc, nb=nb: e.matmul(
                                        ps[:], lhsT=hT[:, kc, :], rhs=win[:, kc, nb * 512:(nb + 1) * 512],
                                        start=(kc == 0), stop=(kc == 7)), [hT, win], [ps])
                                sl = slice(nb * 512, (nb + 1) * 512)
                                if nb < 4:
                                    scl = 1.0 if nb < 2 else 0.0625
                                    if latent:
                                        S.act(lambda e, ps=ps, sl=sl, scl=scl: e.mul(out=qk32[:, sl], in_=ps[:], mul=scl), [ps], [qk32])
                                    else:
                                        S.act(lambda e, ps=ps, sl=sl, scl=scl, zr=zr: e.mul(out=zr[:, sl], in_=ps[:], mul=scl), [ps], [zr])
                                elif nb < 8:
                                    S.act(lambda e, ps=ps, sl=sl, zr=zr: e.copy(out=zr[:, sl], in_=ps[:]), [ps], [zr])
                                else:
                                    S.act(lambda e, ps=ps, sl=sl, zr=zr: e.activation(out=zr[:, sl], in_=ps[:], func=AF.Silu), [ps], [zr])
                            if latent:
                                v4 = qk32[:].rearrange("p (a b c) -> p a b c", a=8, b=2)
                                x1 = v4[:, :, 0, :]
                                x2 = v4[:, :, 1, :]
                                cosb = rct[:, 0, :].unsqueeze(1).to_broadcast([128, 8, 128])
                                sinb = rct[:, 1, :].unsqueeze(1).to_broadcast([128, 8, 128])
                                o4 = zr[:, 0:2048].rearrange("p (a b c) -> p a b c", a=8, b=2)
                                ra3 = ra[:].rearrange("p (a c) -> p a c", a=8)
                                rb3 = rb[:].rearrange("p (a c) -> p a c", a=8)
                                S.dve(lambda e, ra3=ra3, x1=x1, cosb=cosb: e.tensor_tensor(out=ra3, in0=x1, in1=cosb, op=ALU.mult), [qk32, rct], [ra])
                                S.pool(lambda e, rb3=rb3, x2=x2, sinb=sinb: e.tensor_tensor(out=rb3, in0=x2, in1=sinb, op=ALU.mult), [qk32, rct], [rb])
                                S.dve(lambda e, o4=o4, ra3=ra3, rb3=rb3: e.tensor_tensor(out=o4[:, :, 0, :], in0=ra3, in1=rb3, op=ALU.subtract), [ra, rb], [zr])
                                S.dve(lambda e, ra3=ra3, x1=x1, sinb=sinb: e.tensor_tensor(out=ra3, in0=x1, in1=sinb, op=ALU.mult), [qk32, rct, zr], [ra])
                                S.pool(lambda e, rb3=rb3, x2=x2, cosb=cosb: e.tensor_tensor(out=rb3, in0=x2, in1=cosb, op=ALU.mult), [qk32, rct, zr], [rb])
                                S.dve(lambda e, o4=o4, ra3=ra3, rb3=rb3: e.tensor_tensor(out=o4[:, :, 1, :], in0=ra3, in1=rb3, op=ALU.add), [ra, rb], [zr])
                            S.dma(rows(z_d, t), zr[:], [zr], [R_z[t]], zr)
                    S.barrier()
                S.es = sc0

                sca = ExitStack()
                S.es = sca
                with sca:
                    qkv = [S.sbuf("aqkv%d" % k, [128, 4096], BF16) for k in range(2)]
                    qdb = S.sbuf("aqdb", [128, 1024], BF16)
                    kdb = S.sbuf("akdb", [128, 1024], BF16)
                    qdbT = S.sbuf("aqdbT", [128, 8, 128], BF16)
                    Sb32 = S.sbuf("aSb32", [128, 4, 2, 512], F32)
                    Sbb = S.sbuf("aSbb", [128, 4, 2, 512], BF16)
                    pbt = [S.sbuf("apbt%d" % k, [128, 2048], F32) for k in range(2)]
                    tp = S.psum("atp", [128, 1024], BF16)
                    aps = [S.psum("aps%d" % k, [128, 512], F32) for k in range(6)]
                    for si, (t0, ntl, latent, j) in enumerate(seqs):
                        if latent:
                            S.dma(Sb32[:].rearrange("p h c v -> p (h c) v"),
                                  sretb_d[jr].rearrange("h (c p) v -> p (h c) v", p=128), [RIN], [Sb32], Sb32)
                        else:
                            S.dve(lambda e: e.memset(Sb32[:].rearrange("p h c v -> p (h c v)"), 0.0), [], [Sb32])
                        S.act(lambda e: e.copy(out=Sbb[:].rearrange("p h c v -> p (h c v)"),
                                               in_=Sb32[:].rearrange("p h c v -> p (h c v)")), [Sb32], [Sbb])
                        for tt in reversed(range(ntl)):
                            t = t0 + tt
                            qv = qkv[t % 2]
                            S.dma(qv[:], z_d[t * 128:(t + 1) * 128, 0:4096], [R_z[t]], [qv], qv)
                            for h in range(4):
                                S.dve(lambda e, h=h, qv=qv: e.tensor_scalar(out=qdb[:, h * 256:(h + 1) * 256], in0=qv[:, h * 256:(h + 1) * 256],
                                                                      scalar1=qdec[:, 1, h:h + 1], scalar2=None, op0=ALU.mult),
                                      [qv, qdec], [qdb])
                                S.pool(lambda e, h=h, qv=qv: e.tensor_scalar(out=kdb[:, h * 256:(h + 1) * 256],
                                                                       in0=qv[:, 1024 + h * 256:1024 + (h + 1) * 256],
                                                                       scalar1=kdec[:, 1, h:h + 1], scalar2=None, op0=ALU.mult),
                                       [qv, kdec], [kdb])
                            for c in range(8):
                                S.pe(lambda e, c=c: e.transpose(out=tp[:, c * 128:(c + 1) * 128], in_=qdb[:, c * 128:(c + 1) * 128],
                                                                identity=ident[:]), [qdb, ident], [tp])
                            S.act(lambda e: e.copy(out=qdbT[:].rearrange("p a b -> p (a b)"), in_=tp[:]), [tp], [qdbT])
                            pt = pbt[t % 2]
                            for h in range(4):
                                ps = aps[h % 2]
                                for dc in range(2):
                                    S.pe(lambda e, ps=ps, h=h, dc=dc: e.matmul(ps[:], lhsT=qdbT[:, h * 2 + dc, :], rhs=Sbb[:, h, dc, :],
                                                                          start=(dc == 0), stop=(dc == 1)), [qdbT, Sbb], [ps])
                                S.act(lambda e, ps=ps, h=h, pt=pt: e.copy(out=pt[:, h * 512:(h + 1) * 512], in_=ps[:]), [ps], [pt])
                            S.dma(rows(pb_d, t), pt[:], [pt], [R_pb[t]], pt)
                            n = 0
                            for h in range(4):
                                for dc in range(2):
                                    ps = aps[2 + n % 4]
                                    n += 1
                                    S.pe(lambda e, ps=ps, h=h, dc=dc, qv=qv: e.matmul(
                                        ps[:], lhsT=kdb[:, h * 256 + dc * 128: h * 256 + (dc + 1) * 128],
                                        rhs=qv[:, 2048 + h * 512: 2048 + (h + 1) * 512], start=True, stop=True), [kdb, qv], [ps])
                                    S.dve(lambda e, ps=ps, h=h, dc=dc: e.scalar_tensor_tensor(
                                        out=Sb32[:, h, dc, :], in0=Sb32[:, h, dc, :], scalar=cdec[:, 1, h:h + 1], in1=ps[:],
                                        op0=ALU.mult, op1=ALU.add), [Sb32, cdec, ps], [Sb32])
                            S.act(lambda e: e.copy(out=Sbb[:].rearrange("p h c v -> p (h c v)"),
                                                   in_=Sb32[:].rearrange("p h c v -> p (h c v)")), [Sb32], [Sbb])
                        if not latent:
                            S.dma(nsb_d[si, jr].rearrange("h (c p) v -> p (h c) v", p=128),
                                  Sb32[:].rearrange("p h c v -> p (h c) v"), [Sb32], [R_o], Sb32)
                    S.barrier()
                S.es = sc0

                scb = ExitStack()
                S.es = scb
                with scb:
                    wout = S.sbuf("rwout", [128, 16, D], BF16)
                    load_w(wout, rwout_d[jr], D, "rwout")
                    zt = [S.sbuf("bz%d" % k, [128, 6144], BF16) for k in range(2)]
                    pbt = S.sbuf("bpbt", [128, 2048], F32)
                    xts = [S.sbuf("bx%d" % k, [128, D], F32) for k in range(2)]
                    qdf = S.sbuf("bqdf", [128, 1024], BF16)
                    kdf = S.sbuf("bkdf", [128, 1024], BF16)
                    QT = S.sbuf("bQT", [128, 24, 128], BF16)
                    attm = S.sbuf("battm", [128, 512], BF16)
                    Sf32 = S.sbuf("bSf32", [128, 4, 2, 512], F32)
                    Sfb = S.sbuf("bSfb", [128, 4, 2, 512], BF16)
                    o32 = S.sbuf("bo32", [128, 2048], F32)
                    go = S.sbuf("bgo", [128, 2048], BF16)
                    goT = S.sbuf("bgoT", [128, 16, 128], BF16)
                    gst = S.sbuf("bgst", [128, 4, 6], F32)
                    gmv = S.sbuf("bgmv", [128, 4, 2], F32)
                    grs = S.sbuf("bgrs", [128, 4], F32)
                    tmp = S.sbuf("btmp", [128, D], F32)
                    r = S.sbuf("br", [128, D], F32)
                    st = S.sbuf("bst", [128, 2, 6], F32)
                    mv = S.sbuf("bmv", [128, 2], F32)
                    rstd = S.sbuf("brstd", [128, 1], F32)
                    tp = S.psum("btp", [128, 1024], BF16)
                    bps = [S.psum("bps%d" % k, [128, 512], F32) for k in range(5)]
                    Y = S.psum("bY", [128, 1024], F32)
                    for si, (t0, ntl, latent, j) in enumerate(seqs):
                        if latent:
                            S.dma(Sf32[:].rearrange("p h c v -> p (h c) v"),
                                  sretf_d[jr].rearrange("h (c p) v -> p (h c) v", p=128), [RIN], [Sf32], Sf32)
                        else:
                            S.dve(lambda e: e.memset(Sf32[:].rearrange("p h c v -> p (h c v)"), 0.0), [], [Sf32])
                        S.act(lambda e: e.copy(out=Sfb[:].rearrange("p h c v -> p (h c v)"),
                                               in_=Sf32[:].rearrange("p h c v -> p (h c v)")), [Sf32], [Sfb])
                        for tt in range(ntl):
                            t = t0 + tt
                            z = zt[t % 2]
                            xt = xts[t % 2]
                            S.dma(z[:], rows(z_d, t), [R_z[t]], [z], z)
                            S.dma(pbt[:], rows(pb_d, t), [R_pb[t]], [pbt], pbt)
                            S.dma(xt[:], rows(xsrc(first), t), [xres(first, t)], [xt], xt)
                            for h in range(4):
                                S.dve(lambda e, h=h, z=z: e.tensor_scalar(out=qdf[:, h * 256:(h + 1) * 256], in0=z[:, h * 256:(h + 1) * 256],
                                                                     scalar1=qdec[:, 0, h:h + 1], scalar2=None, op0=ALU.mult),
                                      [z, qdec], [qdf])
                                S.pool(lambda e, h=h, z=z: e.tensor_scalar(out=kdf[:, h * 256:(h + 1) * 256],
                                                                      in0=z[:, 1024 + h * 256:1024 + (h + 1) * 256],
                                                                      scalar1=kdec[:, 0, h:h + 1], scalar2=None, op0=ALU.mult),
                                       [z, kdec], [kdf])
                            for grp, (src, off, rr) in enumerate([(z, 0, [z]), (qdf, 0, [qdf]), (z, 1024, [z])]):
                                for c in range(8):
                                    S.pe(lambda e, c=c, src=src, off=off: e.transpose(
                                        out=tp[:, c * 128:(c + 1) * 128], in_=src[:, off + c * 128: off + (c + 1) * 128],
                                        identity=ident[:]), rr + [ident], [tp])
                                S.act(lambda e, grp=grp: e.copy(out=QT[:, grp * 8:(grp + 1) * 8, :].rearrange("p a b -> p (a b)"), in_=tp[:]),
                                      [tp], [QT])
                            pa = bps[4]
                            for h in range(4):
                                for dc in range(2):
                                    S.pe(lambda e, h=h, dc=dc: e.matmul(pa[:, h * 128:(h + 1) * 128], lhsT=QT[:, 16 + h * 2 + dc, :],
                                                                        rhs=QT[:, h * 2 + dc, :], start=(dc == 0), stop=(dc == 1)),
                                         [QT], [pa])
                            S.dve(lambda e: e.tensor_tensor(out=attm[:], in0=pa[:], in1=dmask[:].rearrange("p h i -> p (h i)"), op=ALU.mult),
                                  [pa, dmask], [attm])
                            for h in range(4):
                                ps = bps[h]
                                S.pe(lambda e, ps=ps, h=h, z=z: e.matmul(ps[:], lhsT=attm[:, h * 128:(h + 1) * 128],
                                                                    rhs=z[:, 2048 + h * 512:2048 + (h + 1) * 512], start=True, stop=False),
                                     [attm, z], [ps])
                                for dc in range(2):
                                    S.pe(lambda e, ps=ps, h=h, dc=dc: e.matmul(ps[:], lhsT=QT[:, 8 + h * 2 + dc, :], rhs=Sfb[:, h, dc, :],
                                                                          start=False, stop=(dc == 1)), [QT, Sfb], [ps])
                                S.dve(lambda e, ps=ps, h=h: e.tensor_tensor(out=o32[:, h * 512:(h + 1) * 512], in0=ps[:],
                                                                       in1=pbt[:, h * 512:(h + 1) * 512], op=ALU.add), [ps, pbt], [o32])
                                S.dve(lambda e, h=h: e.bn_stats(out=gst[:, h, :], in_=o32[:, h * 512:(h + 1) * 512]), [o32], [gst])
                                S.dve(lambda e, h=h: e.bn_aggr(out=gmv[:, h, :], in_=gst[:, h, :]), [gst], [gmv])
                            S.act(lambda e: e.activation(out=grs[:], in_=gmv[:, :, 1], func=AF.Sqrt, bias=epsc[:], scale=1.0), [gmv, epsc], [grs])
                            S.dve(lambda e: e.reciprocal(out=grs[:], in_=grs[:]), [grs], [grs])
                            for h in range(4):
                                S.dve(lambda e, h=h: e.tensor_scalar(out=o32[:, h * 512:(h + 1) * 512], in0=o32[:, h * 512:(h + 1) * 512],
                                                                     scalar1=gmv[:, h, 0:1], scalar2=grs[:, h:h + 1],
                                                                     op0=ALU.subtract, op1=ALU.mult), [o32, gmv, grs], [o32])
                            S.pool(lambda e, z=z: e.tensor_tensor(out=go[:], in0=o32[:], in1=z[:, 4096:6144], op=ALU.mult), [o32, z], [go])
                            for half in range(2):
                                for c in range(8):
                                    cc = half * 8 + c
                                    S.pe(lambda e, c=c, cc=cc: e.transpose(out=tp[:, c * 128:(c + 1) * 128], in_=go[:, cc * 128:(cc + 1) * 128],
                                                                           identity=ident[:]), [go, ident], [tp])
                                S.act(lambda e, half=half: e.copy(out=goT[:, half * 8:(half + 1) * 8, :].rearrange("p a b -> p (a b)"), in_=tp[:]),
                                      [tp], [goT])
                            for nh in range(2):
                                for kc in range(16):
                                    S.pe(lambda e, nh=nh, kc=kc: e.matmul(Y[:, nh * 512:(nh + 1) * 512], lhsT=goT[:, kc, :],
                                                                          rhs=wout[:, kc, nh * 512:(nh + 1) * 512],
                                                                          start=(kc == 0), stop=(kc == 15)), [goT, wout], [Y])
                            epilogue(Y, [Y], xt, j, t, tmp, r, st, mv, rstd)
                            n = 0
                            for h in range(4):
                                for dc in range(2):
                                    ps = bps[n % 4]
                                    n += 1
                                    S.pe(lambda e, ps=ps, h=h, dc=dc, z=z: e.matmul(
                                        ps[:], lhsT=kdf[:, h * 256 + dc * 128: h * 256 + (dc + 1) * 128],
                                        rhs=z[:, 2048 + h * 512: 2048 + (h + 1) * 512], start=True, stop=True), [kdf, z], [ps])
                                    S.dve(lambda e, ps=ps, h=h, dc=dc: e.scalar_tensor_tensor(
                                        out=Sf32[:, h, dc, :], in0=Sf32[:, h, dc, :], scalar=cdec[:, 0, h:h + 1], in1=ps[:],
                                        op0=ALU.mult, op1=ALU.add), [Sf32, cdec, ps], [Sf32])
                            S.act(lambda e: e.copy(out=Sfb[:].rearrange("p h c v -> p (h c v)"),
                                                   in_=Sf32[:].rearrange("p h c v -> p (h c v)")), [Sf32], [Sfb])
                        if not latent:
                            S.dma(nsf_d[si, jr].rearrange("h (c p) v -> p (h c) v", p=128),
                                  Sf32[:].rearrange("p h c v -> p (h c) v"), [Sf32], [R_o], Sf32)
                    S.barrier()
                S.es = sc0
            S.es = root

        def attention_layer(i, first):
            ja = i // 2
            sc0 = ExitStack()
            S.es = sc0
            with sc0:
                win = S.sbuf("awin", [128, 8, 1536], BF16)
                wout = S.sbuf("awout", [128, 8, D], BF16)
                load_w(win, awin_d[ja], 1536, "awin")
                load_w(wout, awout_d[ja], D, "awout")
                convert_uv(i)
                esink = S.sbuf("esink", [128, 16], F32)
                S.dma(esink[:], asink_d[ja:ja + 1, :].to_broadcast([128, 16]), [RIN], [esink], esink)
                S.act(lambda e: e.activation(out=esink[:], in_=esink[:], func=AF.Exp), [esink], [esink])
                mprev = S.sbuf("mprev", [128, 4, 128], BF16)
                mnext = S.sbuf("mnext", [128, 4, 128], BF16)
                S.dve(lambda e: e.tensor_copy(out=mprev[:], in_=cst[:, 4, :].unsqueeze(1).to_broadcast([128, 4, 128])), [cst], [mprev])
                S.dve(lambda e: e.tensor_copy(out=mnext[:], in_=cst[:, 5, :].unsqueeze(1).to_broadcast([128, 4, 128])), [cst], [mnext])
                NSm = max(NS, 2)
                KT = S.sbuf("aKT", [64, 4, NSm * 128], BF16)
                VL = S.sbuf("aVL", [128, NSm, 4, 65], BF16)
                CKT = S.sbuf("aCKT", [64, 4, 512], BF16)
                CV = S.sbuf("aCV", [128, 4, 4, 65], BF16)
                c32 = S.sbuf("ac32", [128, 4, 256], F32)
                cb = S.sbuf("acb", [128, 4, 256], BF16)
                xts = [S.sbuf("ax%d" % k, [128, D], F32) for k in range(2)]
                hb = S.sbuf("ahb", [128, D], BF16)
                hT = S.sbuf("ahT", [128, 8, 128], BF16)
                q32 = S.sbuf("aq32", [128, 1536], F32)
                ra = S.sbuf("ara", [128, 512], F32)
                rb = S.sbuf("arb", [128, 512], F32)
                qb = [S.sbuf("aqb%d" % k, [128, 1024], BF16) for k in range(2)]
                kb = S.sbuf("akb", [128, 256], BF16)
                rc = [S.sbuf("arc%d" % k, [128, 2, 32], F32) for k in range(2)]
                qT = S.sbuf("aqT", [64, 16, 128], BF16)
                PT = [S.sbuf("aPT%d" % k, [128, 512], BF16) for k in range(7)]
                rden = S.sbuf("arden", [128, 16], F32)
                on = S.sbuf("aon", [128, 1024], BF16)
                onT = S.sbuf("aonT", [128, 8, 128], BF16)
                tmp = S.sbuf("atmp", [128, D], F32)
                r = S.sbuf("ar", [128, D], F32)
                st = S.sbuf("ast", [128, 2, 6], F32)
                mv = S.sbuf("amv", [128, 2], F32)
                rstd = S.sbuf("arstd", [128, 1], F32)
                tp = S.psum("atp", [128, 1024], BF16)
                tq = S.psum("atq", [128, 2048], BF16)
                sps = [S.psum("asps%d" % k, [128, 512], F32) for k in range(2)]
                ops_ = [S.psum("aops%d" % k, [128, 4, 65], F32) for k in range(1)]
                Y = S.psum("aY", [128, 1024], F32)

                S.dve(lambda e: e.memset(VL[:].rearrange("p a b c -> p (a b c)"), 1.0), [], [VL])
                S.dve(lambda e: e.memset(CV[:].rearrange("p a b c -> p (a b c)"), 1.0), [], [CV])
                S.dma(c32[:], ck_d[ja].rearrange("(b p) f -> p b f", p=128), [RIN], [c32], c32)
                S.dve(lambda e: e.tensor_copy(out=cb[:], in_=c32[:]), [c32], [cb])
                for b in range(4):
                    for g in range(4):
                        S.pe(lambda e, b=b, g=g: e.transpose(out=tp[0:64, g * 128:(g + 1) * 128], in_=cb[:, b, g * 64:(g + 1) * 64],
                                                             identity=ident[:]), [cb, ident], [tp])
                    S.act(lambda e, b=b: e.copy(out=CKT[:, :, b * 128:(b + 1) * 128],
                                                in_=tp[0:64, 0:512].rearrange("p (g k) -> p g k", g=4)), [tp], [CKT])
                S.dma(c32[:], cv_d[ja].rearrange("(b p) f -> p b f", p=128), [RIN], [c32], c32)
                S.dve(lambda e: e.tensor_copy(out=CV[:, :, :, 0:64], in_=c32[:].rearrange("p b (g d) -> p b g d", g=4)), [c32], [CV])

                for si, (t0, ntl, latent, j) in enumerate(seqs):
                    for tt in range(ntl):
                        t = t0 + tt
                        xt = xts[t % 2]
                        S.dma(xt[:], rows(xsrc(first), t), [xres(first, t)], [xt], xt)
                        prologue(xt, j, hb, hT, tp)
                        if latent:
                            rct = rc[t % 2]
                            S.dma(rct[:, 0, :], rac_d[tt * 128:(tt + 1) * 128, :], [RIN], [rct], rct)
                            S.dma(rct[:, 1, :], ras_d[tt * 128:(tt + 1) * 128, :], [RIN], [rct], rct)
                        for nb in range(3):
                            ps = sps[nb % 2]
                            for kc in range(8):
                                S.pe(lambda e, ps=ps, kc=kc, nb=nb: e.matmul(ps[:], lhsT=hT[:, kc, :], rhs=win[:, kc, nb * 512:(nb + 1) * 512],
                                                                        start=(kc == 0), stop=(kc == 7)), [hT, win], [ps])
                            S.act(lambda e, ps=ps, nb=nb: e.copy(out=q32[:, nb * 512:(nb + 1) * 512], in_=ps[:]), [ps], [q32])
                        if not latent:
                            S.dma(nk_d[si, ja, tt * 128:(tt + 1) * 128, :], q32[:, 1024:1280], [q32], [R_o], q32)
                            S.dma(nv_d[si, ja, tt * 128:(tt + 1) * 128, :], q32[:, 1280:1536], [q32], [R_o], q32)
                        q_out = qb[t % 2]
                        if latent:
                            v5 = q32[:, 0:1280].rearrange("p (h a b c) -> p h a b c", h=20, a=2, b=2)
                            for a in range(2):
                                x1 = v5[:, :, a, 0, :]
                                x2 = v5[:, :, a, 1, :]
                                cosb = rct[:, 0, a * 16:(a + 1) * 16].unsqueeze(1).to_broadcast([128, 20, 16])
                                sinb = rct[:, 1, a * 16:(a + 1) * 16].unsqueeze(1).to_broadcast([128, 20, 16])
                                ra3 = ra[:, 0:320].rearrange("p (h c) -> p h c", h=20)
                                rb3 = rb[:, 0:320].rearrange("p (h c) -> p h c", h=20)
                                ra3b = ra[:, 320:640].rearrange("p (h c) -> p h c", h=20) if False else None
                                S.dve(lambda e, x1=x1, cosb=cosb, ra3=ra3: e.tensor_tensor(out=ra3, in0=x1, in1=cosb, op=ALU.mult), [q32, rct], [ra])
                                S.pool(lambda e, x2=x2, sinb=sinb, rb3=rb3: e.tensor_tensor(out=rb3, in0=x2, in1=sinb, op=ALU.mult), [q32, rct], [rb])
                                S.dve(lambda e, ra3=ra3, rb3=rb3: e.tensor_tensor(out=ra3, in0=ra3, in1=rb3, op=ALU.subtract), [ra, rb], [ra])
                                S.pool(lambda e, x1=x1, sinb=sinb, rb3=rb3: e.tensor_tensor(out=rb3, in0=x1, in1=sinb, op=ALU.mult), [q32, rct, ra], [rb])
                                S.dve(lambda e, x1=x1, ra3=ra3: e.tensor_copy(out=x1, in_=ra3), [ra, rb], [q32])
                                S.dve(lambda e, x2=x2, cosb=cosb, ra3=ra3: e.tensor_tensor(out=ra3, in0=x2, in1=cosb, op=ALU.mult), [q32, rct], [ra])
                                S.dve(lambda e, x2=x2, ra3=ra3, rb3=rb3: e.tensor_tensor(out=x2, in0=ra3, in1=rb3, op=ALU.add), [ra, rb], [q32])
                        S.act(lambda e, q_out=q_out: e.mul(out=q_out[:], in_=q32[:, 0:1024], mul=0.125), [q32], [q_out])
                        S.dma(rows(qs_d, t), q_out[:], [q_out], [R_qs[t]], q_out)
                        S.dve(lambda e: e.tensor_copy(out=kb[:], in_=q32[:, 1024:1280]), [q32], [kb])
                        S.dve(lambda e, tt=tt: e.tensor_copy(out=VL[:, tt, :, 0:64], in_=q32[:, 1280:1536].rearrange("p (g d) -> p g d", g=4)),
                              [q32], [VL])
                        for g in range(4):
                            S.pe(lambda e, g=g: e.transpose(out=tp[0:64, g * 128:(g + 1) * 128], in_=kb[:, g * 64:(g + 1) * 64],
                                                            identity=ident[:]), [kb, ident], [tp])
                        S.act(lambda e, tt=tt: e.copy(out=KT[:, :, tt * 128:(tt + 1) * 128],
                                                      in_=tp[0:64, 0:512].rearrange("p (g k) -> p g k", g=4)), [tp], [KT])
                    for tt in range(ntl):
                        t = t0 + tt
                        xt = xts[t % 2]
                        qin = qb[t % 2]
                        S.dma(xt[:], rows(xsrc(first), t), [xres(first, t)], [xt], xt)
                        S.dma(qin[:], rows(qs_d, t), [R_qs[t]], [qin], qin)
                        for h in range(16):
                            S.pe(lambda e, h=h, qin=qin: e.transpose(out=tq[0:64, h * 128:(h + 1) * 128], in_=qin[:, h * 64:(h + 1) * 64],
                                                                     identity=ident[:]), [qin, ident], [tq])
                        S.act(lambda e: e.copy(out=qT[:].rearrange("p a b -> p (a b)"), in_=tq[0:64, :]), [tq], [qT])
                        if latent:
                            blocks = []
                            if tt > 0:
                                blocks.append(("loc", tt - 1, mprev))
                            blocks.append(("loc", tt, None))
                            if tt < ntl - 1:
                                blocks.append(("loc", tt + 1, mnext))
                            for b in range(4):
                                blocks.append(("ctx", b, None))
                        else:
                            blocks = [("loc", b, None) for b in range(ntl)]
                        for g in range(4):
                            for bi, (kind, b, msk) in enumerate(blocks):
                                ps = sps[bi % 2]
                                kT_ap = (KT[:, g, b * 128:(b + 1) * 128] if kind == "loc" else CKT[:, g, b * 128:(b + 1) * 128])
                                kres = KT if kind == "loc" else CKT
                                S.pe(lambda e, ps=ps, kT_ap=kT_ap, g=g, msk=msk: e.matmul(
                                    ps[:], lhsT=kT_ap, rhs=qT[:, 4 * g:4 * g + 4, :], start=True, stop=(msk is None)), [kres, qT], [ps])
                                if msk is not None:
                                    S.pe(lambda e, ps=ps, msk=msk: e.matmul(ps[:], lhsT=ident[:], rhs=msk[:].rearrange("p a b -> p (a b)"),
                                                                           start=False, stop=True), [ident, msk], [ps])
                                S.act(lambda e, ps=ps, bi=bi: e.activation(out=PT[bi][:], in_=ps[:], func=AF.Exp), [ps], [PT[bi]])
                            og = ops_[0]
                            for hh in range(4):
                                for bi, (kind, b, msk) in enumerate(blocks):
                                    v_ap = (VL[:, b, g, :] if kind == "loc" else CV[:, b, g, :])
                                    vres = VL if kind == "loc" else CV
                                    S.pe(lambda e, og=og, hh=hh, bi=bi, v_ap=v_ap, nb=len(blocks): e.matmul(
                                        og[:, hh, :], lhsT=PT[bi][:, hh * 128:(hh + 1) * 128], rhs=v_ap,
                                        start=(bi == 0), stop=(bi == nb - 1)), [PT[bi], vres], [og])
                            S.dve(lambda e, og=og, g=g: e.tensor_tensor(out=rden[:, 4 * g:4 * g + 4], in0=og[:, :, 64],
                                                                   in1=esink[:, 4 * g:4 * g + 4], op=ALU.add), [og, esink], [rden])
                            S.dve(lambda e, g=g: e.reciprocal(out=rden[:, 4 * g:4 * g + 4], in_=rden[:, 4 * g:4 * g + 4]), [rden], [rden])
                            S.dve(lambda e, og=og, g=g: e.tensor_tensor(
                                out=on[:, g * 256:(g + 1) * 256].rearrange("p (h d) -> p h d", h=4), in0=og[:, :, 0:64],
                                in1=rden[:, 4 * g:4 * g + 4].unsqueeze(2).to_broadcast([128, 4, 64]), op=ALU.mult), [og, rden], [on])
                        for c in range(8):
                            S.pe(lambda e, c=c: e.transpose(out=tp[:, c * 128:(c + 1) * 128], in_=on[:, c * 128:(c + 1) * 128],
                                                            identity=ident[:]), [on, ident], [tp])
                        S.act(lambda e: e.copy(out=onT[:].rearrange("p a b -> p (a b)"), in_=tp[:]), [tp], [onT])
                        for nh in range(2):
                            for kc in range(8):
                                S.pe(lambda e, nh=nh, kc=kc: e.matmul(Y[:, nh * 512:(nh + 1) * 512], lhsT=onT[:, kc, :],
                                                                      rhs=wout[:, kc, nh * 512:(nh + 1) * 512],
                                                                      start=(kc == 0), stop=(kc == 7)), [onT, wout], [Y])
                        epilogue(Y, [Y], xt, j, t, tmp, r, st, mv, rstd)
                S.barrier()
            S.es = root

        def peer_layer(i):
            GS = OPTS["GS"]
            UVDT = BF16 if OPTS["uvbf16"] else F32
            NG = 128 // GS
            sc0 = ExitStack()
            S.es = sc0
            with sc0:
                wq = S.sbuf("pwq", [128, 8, 2048], BF16)
                load_w(wq, pwq_d[i], 2048, "pwq")
                keysT = S.sbuf("pkeysT", [128, 16, 128], BF16)
                tp = S.psum("ptp", [128, 1024], BF16)
                sck = ExitStack()
                S.es = sck
                with sck:
                    k32 = S.sbuf("pk32", [128, 16, 128], F32)
                    kbf = S.sbuf("pkbf", [128, 16, 128], BF16)
                    S.dma(k32[:], pkeys_d[i].rearrange("c n d -> n c d"), [RIN], [k32], k32)
                    S.dve(lambda e: e.tensor_copy(out=kbf[:], in_=k32[:]), [k32], [kbf])
                    for half in range(2):
                        for c in range(8):
                            cc = half * 8 + c
                            S.pe(lambda e, c=c, cc=cc: e.transpose(out=tp[:, c * 128:(c + 1) * 128], in_=kbf[:, cc, :], identity=ident[:]),
                                 [kbf, ident], [tp])
                        S.act(lambda e, half=half: e.copy(out=keysT[:, half * 8:(half + 1) * 8, :].rearrange("p a b -> p (a b)"), in_=tp[:]),
                              [tp], [keysT])
                    S.barrier()
                S.es = sc0

                xts = [S.sbuf("px%d" % k, [128, D], F32) for k in range(2)]
                h32s = [S.sbuf("ph32%d" % k, [128, D], F32) for k in range(2)]
                eis = [S.sbuf("pei%d" % k, [128, 128], I32) for k in range(2)]
                wsms = [S.sbuf("pwsm%d" % k, [128, 8, 16], F32) for k in range(2)]
                hb = S.sbuf("phb", [128, D], BF16)
                hT = S.sbuf("phT", [128, 8, 128], BF16)
                qb = S.sbuf("pqb", [128, 2048], BF16)
                qT = S.sbuf("pqT", [128, 16, 128], BF16)
                s32 = S.sbuf("ps32", [128, 16, 128], F32)
                wk = S.sbuf("pwk", [128, 512], F32)
                avR = [[S.dram("avR%d_%d" % (k, q)) for q in range(8)] for k in range(4)]
                junk = S.sbuf("pjunk", [128, D], BF16)
                gbR = [S.dram("gbR%d" % k) for k in range(4)]
                coefR = [S.dram("coefR%d" % k) for k in range(4)]
                svR = [S.dram("svR%d" % k) for k in range(16)]
                siuR = [S.dram("siuR%d" % k) for k in range(16)]
                wkR = [S.dram("wkR%d" % k) for k in range(4)]
                combR = [S.dram("combR%d" % k) for k in range(8)]
                csR = [S.dram("csR%d" % k) for k in range(8)]
                ciuR = [S.dram("ciuR%d" % k) for k in range(8)]
                sv = S.sbuf("psv", [128, 16, 16], F32)
                siu = S.sbuf("psiu", [128, 16, 16], U32)
                sif = S.sbuf("psif", [128, 16, 16], F32)
                comb = S.sbuf("pcomb", [128, 8, 256], F32)
                cs = S.sbuf("pcs", [128, 8, 16], F32)
                ciu = S.sbuf("pciu", [128, 8, 16], U32)
                cia = S.sbuf("pcia", [128, 8, 16], U32)
                cib = S.sbuf("pcib", [128, 8, 16], U32)
                caf = S.sbuf("pcaf", [128, 8, 16], F32)
                cbf = S.sbuf("pcbf", [128, 8, 16], F32)
                oh = [comb] * 2
                i1f = S.sbuf("pi1f", [128, 8, 16], F32)
                i2f = S.sbuf("pi2f", [128, 8, 16], F32)
                wsum = S.sbuf("pwsum", [128, 8], F32)
                av = S.sbuf("pav", [128, 128], F32)
                ga = S.sbuf("pga", [128, 128], F32)
                gb_ = S.sbuf("pgb", [128, 128], F32)
                coef = S.sbuf("pcoef", [128, 128], F32)
                uvg = [[S.sbuf("puv%d_%d" % (k, s_), [128, 2 * D], UVDT) for s_ in range(GS)] for k in range(OPTS["NSETS"])]
                tv = [S.sbuf("ptv%d" % k, [128, D], BF16) for k in range(OPTS["NTV"])]
                tmp = S.sbuf("ptmp", [128, D], F32)
                r = S.sbuf("pr", [128, D], F32)
                st = S.sbuf("pst", [128, 2, 6], F32)
                mv = S.sbuf("pmv", [128, 2], F32)
                rstd = S.sbuf("prstd", [128, 1], F32)
                tq = S.psum("ptq", [128, 2048], BF16)
                qps = [S.psum("pqps%d" % k, [128, 512], F32) for k in range(2)]
                Y = S.psum("pY", [128, 1024], F32)

                def stage_a(t, j, par):
                    xt = xts[par]
                    h32 = h32s[par]
                    ei = eis[par]
                    wsm = wsms[par]
                    S.dma(xt[:], rows(y_d, t), [R_y[t]], [xt], xt)
                    prologue(xt, j, hb, hT, tp, h32=h32)
                    yield
                    for nb in range(4):
                        ps = qps[nb % 2]
                        for kc in range(8):
                            S.pe(lambda e, ps=ps, kc=kc, nb=nb: e.matmul(ps[:], lhsT=hT[:, kc, :], rhs=wq[:, kc, nb * 512:(nb + 1) * 512],
                                                                    start=(kc == 0), stop=(kc == 7)), [hT, wq], [ps])
                        S.act(lambda e, ps=ps, nb=nb: e.copy(out=qb[:, nb * 512:(nb + 1) * 512], in_=ps[:]), [ps], [qb])
                    for c in range(16):
                        S.pe(lambda e, c=c: e.transpose(out=tq[:, c * 128:(c + 1) * 128], in_=qb[:, c * 128:(c + 1) * 128], identity=ident[:]),
                             [qb, ident], [tq])
                    S.act(lambda e: e.copy(out=qT[:].rearrange("p a b -> p (a b)"), in_=tq[:]), [tq], [qT])
                    for b4 in range(4):
                        ps = qps[b4 % 2]
                        for c4 in range(4):
                            c = b4 * 4 + c4
                            S.pe(lambda e, ps=ps, c=c, c4=c4: e.matmul(ps[:, c4 * 128:(c4 + 1) * 128], lhsT=qT[:, c, :], rhs=keysT[:, c, :],
                                                                  start=True, stop=True), [qT, keysT], [ps])
                        S.act(lambda e, ps=ps, b4=b4: e.copy(out=s32[:, b4 * 4:(b4 + 1) * 4, :].rearrange("p a b -> p (a b)"), in_=ps[:]),
                              [ps], [s32])
                    yield
                    for c0 in range(0, 16, 4):
                        grp = list(range(c0, c0 + 4))
                        for c in grp:
                            S.dve(lambda e, c=c: e.max(out=sv[:, c, 0:8], in_=s32[:, c, :]), [s32], [svR[c]])
                        for c in grp:
                            S.dve(lambda e, c=c: e.max_index(out=siu[:, c, 0:8], in_max=sv[:, c, 0:8], in_values=s32[:, c, :]),
                                  [s32, svR[c]], [siuR[c]])
                        yield
                        for k_, c in enumerate(grp):
                            S.dve(lambda e, c=c, k_=k_: e.match_replace(out=wk[:, k_ * 128:(k_ + 1) * 128], in_to_replace=sv[:, c, 0:8], in_values=s32[:, c, :],
                                                                      imm_value=-1e30), [s32, svR[c]], [wkR[k_]])
                        for k_, c in enumerate(grp):
                            S.dve(lambda e, c=c, k_=k_: e.max(out=sv[:, c, 8:16], in_=wk[:, k_ * 128:(k_ + 1) * 128]), [wkR[k_]], [svR[c]])
                        yield
                        for k_, c in enumerate(grp):
                            S.dve(lambda e, c=c, k_=k_: e.max_index(out=siu[:, c, 8:16], in_max=sv[:, c, 8:16], in_values=wk[:, k_ * 128:(k_ + 1) * 128]),
                                  [wkR[k_], svR[c]], [siuR[c]])
                        yield
                    S.dve(lambda e: e.tensor_copy(out=sif[:], in_=siu[:]), siuR, [sif])
                    sv4 = sv[:].rearrange("p (h two) m -> p h two m", two=2)
                    S.dve(lambda e: e.tensor_tensor(
                        out=comb[:].rearrange("p h (a b) -> p h a b", a=16),
                        in0=sv4[:, :, 0, :].unsqueeze(3).to_broadcast([128, 8, 16, 16]),
                        in1=sv4[:, :, 1, :].unsqueeze(2).to_broadcast([128, 8, 16, 16]), op=ALU.add), svR, combR)
                    yield
                    for p0 in range(0, 8, 2):
                        grp = list(range(p0, p0 + 2))
                        for p in grp:
                            S.dve(lambda e, p=p: e.max(out=cs[:, p, 0:8], in_=comb[:, p, :]), [combR[p]], [csR[p]])
                        for p in grp:
                            S.dve(lambda e, p=p: e.max_index(out=ciu[:, p, 0:8], in_max=cs[:, p, 0:8], in_values=comb[:, p, :]),
                                  [combR[p], csR[p]], [ciuR[p]])
                        yield
                        for k_, p in enumerate(grp):
                            S.dve(lambda e, p=p, k_=k_: e.match_replace(out=wk[:, k_ * 256:(k_ + 1) * 256], in_to_replace=cs[:, p, 0:8], in_values=comb[:, p, :],
                                                                      imm_value=-1e30), [combR[p], csR[p]], [wkR[2 * k_], wkR[2 * k_ + 1]])
                        for k_, p in enumerate(grp):
                            S.dve(lambda e, p=p, k_=k_: e.max(out=cs[:, p, 8:16], in_=wk[:, k_ * 256:(k_ + 1) * 256]), [wkR[2 * k_], wkR[2 * k_ + 1]], [csR[p]])
                        for k_, p in enumerate(grp):
                            S.dve(lambda e, p=p, k_=k_: e.max_index(out=ciu[:, p, 8:16], in_max=cs[:, p, 8:16], in_values=wk[:, k_ * 256:(k_ + 1) * 256]),
                                  [wkR[2 * k_], wkR[2 * k_ + 1], csR[p]], [ciuR[p]])
                        yield
                    S.dve(lambda e: e.tensor_tensor(out=wsm[:], in0=cs[:], in1=cs[:, :, 0:1].to_broadcast([128, 8, 16]), op=ALU.subtract),
                          csR, [wsm])
                    S.act(lambda e: e.activation(out=wsm[:], in_=wsm[:], func=AF.Exp), [wsm], [wsm])
                    S.dve(lambda e: e.reduce_sum(out=wsum[:], in_=wsm[:], axis=AX.X), [wsm], [wsum])
                    S.dve(lambda e: e.reciprocal(out=wsum[:], in_=wsum[:]), [wsum], [wsum])
                    S.dve(lambda e: e.tensor_tensor(out=wsm[:], in0=wsm[:], in1=wsum[:].unsqueeze(2).to_broadcast([128, 8, 16]), op=ALU.mult),
                          [wsm, wsum], [wsm])
                    yield
                    S.dve(lambda e: e.tensor_single_scalar(out=cia[:], in_=ciu[:], scalar=4, op=ALU.logical_shift_right), ciuR, [cia])
                    S.dve(lambda e: e.tensor_single_scalar(out=cib[:], in_=ciu[:], scalar=15, op=ALU.bitwise_and), ciuR, [cib])
                    yield
                    S.dve(lambda e: e.tensor_copy(out=caf[:], in_=cia[:]), [cia], [caf])
                    S.dve(lambda e: e.tensor_copy(out=cbf[:], in_=cib[:]), [cib], [cbf])
                    yield
                    sif4 = sif[:].rearrange("p (h two) m -> p h two m", two=2)
                    ohv = comb[:].rearrange("p h (a b) -> p h a b", a=16)
                    for (cf, which, dst, ohx) in ((caf, 0, i1f, oh[0]), (cbf, 1, i2f, oh[1])):
                        S.dve(lambda e, cf=cf, ohx=ohx: e.tensor_tensor(
                            out=ohv, in0=cf[:].unsqueeze(3).to_broadcast([128, 8, 16, 16]),
                            in1=iota16[:].unsqueeze(1).unsqueeze(1).to_broadcast([128, 8, 16, 16]), op=ALU.is_equal), [cf, iota16] + combR, [ohx] + combR)
                        S.dve(lambda e, which=which, ohx=ohx: e.tensor_tensor(
                            out=ohv, in0=ohv,
                            in1=sif4[:, :, which, :].unsqueeze(2).to_broadcast([128, 8, 16, 16]), op=ALU.mult), [ohx, sif] + combR, [ohx] + combR)
                        S.dve(lambda e, dst=dst, ohx=ohx: e.reduce_sum(out=dst[:].rearrange("p a b -> p (a b)"),
                                                              in_=ohv.rearrange("p a b c -> p (a b) c"), axis=AX.X), [ohx] + combR, [dst])
                        yield
                    S.dve(lambda e: e.scalar_tensor_tensor(out=i1f[:], in0=i1f[:], scalar=128.0, in1=i2f[:], op0=ALU.mult, op1=ALU.add),
                          [i1f, i2f], [i1f])
                    S.dve(lambda e: e.tensor_scalar(out=i1f[:], in0=i1f[:], scalar1=float(i * NEXP), scalar2=None, op0=ALU.add), [i1f], [i1f])
                    S.dve(lambda e: e.tensor_copy(out=ei[:], in_=i1f[:].rearrange("p a b -> p (a b)")), [i1f], [ei])
                    yield

                def stage_b(t, j, par, nxt):
                    xt = xts[par]
                    h32 = h32s[par]
                    ei = eis[par]
                    wsm = wsms[par]
                    wflat = wsm[:].rearrange("p a b -> p (a b)")
                    nvc = [0]
                    nyd = [0]
                    pending = []

                    def vside(g, bufs, sl0):
                        gsl = slice(sl0, sl0 + GS)
                        S.dve(lambda e, gsl=gsl: e.tensor_tensor(out=coef[:, gsl], in0=gb_[:, gsl], in1=wflat[:, gsl], op=ALU.mult),
                              [gbR[g % 4], wsm], [coefR[g % 4]])
                        for s_ in range(GS):
                            sl = sl0 + s_
                            b_ = bufs[s_]
                            tv_ = tv[nvc[0] % OPTS['NTV']]
                            nvc[0] += 1
                            S.act(lambda e, b_=b_, sl=sl, tv_=tv_: e.activation(out=tv_[:], in_=b_[:, D:2 * D], func=AF.Copy,
                                                                         scale=coef[:, sl:sl + 1]), [b_, coefR[g % 4]], [tv_])
                            for nh in range(2):
                                S.pe(lambda e, tv_=tv_, nh=nh, sl=sl: e.matmul(Y[:, nh * 512:(nh + 1) * 512], lhsT=ident[:],
                                                                         rhs=tv_[:, nh * 512:(nh + 1) * 512],
                                                                         start=(sl == 0), stop=(sl == 127)), [ident, tv_], [Y])

                    for g in range(NG):
                        bufs = uvg[g % OPTS["NSETS"]]
                        sl0 = g * GS
                        for s_ in range(GS):
                            sl = sl0 + s_
                            b_ = bufs[s_]
                            if OPTS["nogather"]:
                                continue
                            S.add("pool", lambda e, b_=b_, sl=sl: e.indirect_dma_start(
                                out=b_[:], out_offset=None, in_=(uvb_d if OPTS["uvtab"] else puv_d),
                                in_offset=bass.IndirectOffsetOnAxis(ap=ei[:, sl:sl + 1], axis=0)),
                                [ei, RIN] + (R_uvb[i] if OPTS["uvtab"] else []), [b_], dma=b_)
                        for s_ in range(GS):
                            sl = sl0 + s_
                            b_ = bufs[s_]
                            if OPTS["nodots"]:
                                continue
                            S.dve(lambda e, b_=b_, sl=sl: e.scalar_tensor_tensor(out=junk[:], in0=b_[:, 0:D], scalar=1.0, in1=h32[:], op0=ALU.mult,
                                                                           op1=ALU.mult, accum_out=av[:, sl:sl + 1]), [b_, h32], [avR[g % 4][s_]])
                        gsl = slice(sl0, sl0 + GS)
                        S.act(lambda e, gsl=gsl: e.activation(out=gb_[:, gsl], in_=av[:, gsl], func=AF.Gelu_apprx_tanh), avR[g % 4][:GS], [gbR[g % 4]])
                        pending.append((g, bufs, sl0))
                        if len(pending) > 1:
                            vside(*pending.pop(0))
                        if nxt is not None and not OPTS["noroute"]:
                            left = NYA[0] - nyd[0]
                            k_n = -(-left // (NG - g)) if left > 0 else 0
                            for _ in range(k_n):
                                nyd[0] += 1
                                next(nxt, None)
                    while pending:
                        vside(*pending.pop(0))
                    if nxt is not None:
                        for _ in nxt:
                            pass
                    epilogue(Y, [Y], xt, j, t, tmp, r, st, mv, rstd)

                tiles = []
                for (t0, ntl, latent, j) in seqs:
                    for tt in range(ntl):
                        tiles.append((t0 + tt, j))
                NYA = [0]
                for _ in stage_a(tiles[0][0], tiles[0][1], 0):
                    NYA[0] += 1
                for n_, (t, j) in enumerate(tiles):
                    nxt = None
                    if n_ + 1 < len(tiles):
                        nxt = stage_a(tiles[n_ + 1][0], tiles[n_ + 1][1], (n_ + 1) % 2)
                    stage_b(t, j, n_ % 2, nxt)
                S.barrier()
            S.es = root

        for i in range(DEPTH):
            first = (i == 0)
            modulation(i, 0)
            if i % 2 == 0:
                retention_layer(i, first)
            else:
                attention_layer(i, first)
            modulation(i, 1)
            if peer:
                peer_layer(i)
        S.finalize()
        stats = S.stats
    return nc, stats


def _consts(SQ):
    j = np.arange(128, dtype=np.float32)[:, None]
    i = np.arange(128, dtype=np.float32)[None, :]
    cst = np.zeros((128, 6, 128), np.float32)
    cst[:, 0] = np.maximum(i - j, 0.0)
    cst[:, 1] = (i >= j)
    cst[:, 2] = np.maximum(j - i, 0.0)
    cst[:, 3] = (j > i)
    cst[:, 4] = np.where(j >= i, 0.0, -30000.0)
    cst[:, 5] = np.where(j <= i, 0.0, -30000.0)
    p = np.arange(128, dtype=np.float32)
    pvec = np.zeros((128, 8), np.float32)
    pvec[:, 0] = p + 1
    pvec[:, 1] = 128 - p
    pvec[:, 2] = 127 - p
    pvec[:, 3] = p
    iota16 = np.tile(np.arange(16, dtype=np.float32)[None, :], (128, 1))
    pos = np.arange(SQ, dtype=np.float32)
    inv = (10000.0 ** (-np.arange(0, 256, 2, dtype=np.float32) / 256.0)).astype(np.float32)
    ang = (pos[:, None] * inv[None, :]).astype(np.float32)
    rrc, rrs = np.cos(ang).astype(np.float32), np.sin(ang).astype(np.float32)
    t = np.arange(SQ)
    inv2 = (10000.0 ** (-np.arange(0, 32, 2, dtype=np.float32) / 32.0)).astype(np.float32)
    ar = ((t // 64).astype(np.float32)[:, None] * inv2[None, :]).astype(np.float32)
    ac = ((t % 64).astype(np.float32)[:, None] * inv2[None, :]).astype(np.float32)
    rac = np.concatenate([np.cos(ar), np.cos(ac)], 1).astype(np.float32)
    ras = np.concatenate([np.sin(ar), np.sin(ac)], 1).astype(np.float32)
    return dict(cst=cst, pvec=pvec, iota16=iota16, rope_ret_cos=rrc, rope_ret_sin=rrs, rope_att_cos=rac, rope_att_sin=ras)


_PROG = {}
RUNKW = {}
LAST = {}


def run_step(inp, DEPTH, SQ, ncores, peer=True):
    key = (DEPTH, SQ, peer)
    if key not in _PROG:
        _PROG[key] = build_program(DEPTH=DEPTH, SQ=SQ, peer=peer)
    nc, stats = _PROG[key]
    NRET = (DEPTH + 1) // 2
    NATT = max(DEPTH // 2, 1)
    f = lambda a: np.ascontiguousarray(np.asarray(a, dtype=np.float32))
    cs = _consts(SQ)
    shared = dict(
        mod_w=f(inp["mod_w"]), mod_b=f(inp["mod_b"]), ln_g=f(inp["ln_g"]), ln_b=f(inp["ln_b"]),
        ret_w_in=f(inp["ret_w_in"]), ret_w_out=f(inp["ret_w_out"]), ret_decay=f(inp["ret_decay"]).reshape(NRET, 8),
        attn_w_in=f(inp["attn_w_in"])[:NATT], attn_w_out=f(inp["attn_w_out"])[:NATT], attn_sink=f(inp["attn_sink"])[:NATT],
        peer_wq=f(inp["peer_wq"]), peer_keys=f(inp["peer_keys"]).reshape(DEPTH, 16, 128, 128),
        peer_uv=np.ascontiguousarray(np.concatenate([f(inp["peer_u"]).reshape(DEPTH * 16384, 1024),
                                                     f(inp["peer_v"]).reshape(DEPTH * 16384, 1024)], axis=1)), **cs)
    xp, xs = f(inp["x_prompt"]), f(inp["x_sample"])
    in_maps = []
    for c in range(ncores):
        m = dict(shared)
        m["x"] = np.ascontiguousarray(np.concatenate([xp[2 * c], xp[2 * c + 1], xs[c]], 0))
        m["cond"] = np.ascontiguousarray(np.stack([f(inp["c_ctx"]), f(inp["c"])[c]], 0))
        m["sret_f"] = f(inp["state_ret_fwd"])[c]
        m["sret_b"] = f(inp["state_ret_bwd"])[c]
        m["ck"] = np.ascontiguousarray(f(inp["cache_k"])[c][:NATT].reshape(NATT, 512, 256))
        m["cv"] = np.ascontiguousarray(f(inp["cache_v"])[c][:NATT].reshape(NATT, 512, 256))
        in_maps.append(m)
    res = run_bass_kernel_spmd(nc, in_maps, core_ids=list(range(ncores)), **RUNKW)
    LAST['res'] = res
    rs = res.results
    B = 2 * ncores
    y = np.stack([r["y"] for r in rs], 0)
    yp = y[:, :512].reshape(B, 256, 1024)
    ys = y[:, 512:]
    nsf = np.concatenate([r["nsf"] for r in rs], 0)
    nsb = np.concatenate([r["nsb"] for r in rs], 0)
    nk = np.concatenate([r["nk"] for r in rs], 0).reshape(B, NATT, 256, 4, 64)
    nv = np.concatenate([r["nv"] for r in rs], 0).reshape(B, NATT, 256, 4, 64)
    return tuple(np.ascontiguousarray(a.astype(np.float32)) for a in (yp, ys, nsf, nsb, nk, nv))


def kernel(**inputs):
    return run_step(inputs, DEPTH=4, SQ=4096, ncores=8)
```

```python
import numpy as np
from contextlib import ExitStack
import concourse.bass as bass
import concourse.mybir as mybir
from concourse.bass_utils import run_bass_kernel_spmd

F32 = mybir.dt.float32
BF16 = mybir.dt.bfloat16
I32 = mybir.dt.int32
U32 = mybir.dt.uint32
ALU = mybir.AluOpType
AF = mybir.ActivationFunctionType
AX = mybir.AxisListType


class Res:
    __slots__ = ("name", "lw", "rd", "sem", "t")

    def __init__(self, name, t=None):
        self.name = name
        self.lw = None
        self.rd = {}
        self.sem = None
        self.t = t

    def __getitem__(self, k):
        return self.t[k]


class Op:
    __slots__ = ("eng", "fn", "reads", "writes", "dma", "deps", "hasdep", "ev", "waits", "idx")


class Sched:
    ENGS = ("pe", "dve", "act", "pool", "sp")

    def __init__(self, nc, es, max_dma_sems=94):
        self.nc = nc
        self.es = es
        self.es_root = es
        self.ops = []
        self.nres = 0
        self.dma_sems = []
        self.max_dma_sems = max_dma_sems
        self.rr = 0
        self.sem_load = []

    def sbuf(self, name, shape, dtype):
        self.nres += 1
        name = "sb%d_%s" % (self.nres, name)
        t = self.es.enter_context(self.nc.sbuf_tensor(name, list(shape), dtype))
        return Res(name, t)

    def psum(self, name, shape, dtype):
        self.nres += 1
        name = "ps%d_%s" % (self.nres, name)
        t = self.es.enter_context(self.nc.psum_tensor(name, list(shape), dtype))
        return Res(name, t)

    def dram(self, name):
        return Res(name, None)

    def _dsem(self, res):
        if res.sem is None:
            if len(self.dma_sems) < self.max_dma_sems:
                s = self.es_root.enter_context(self.nc.semaphore("dq%d" % len(self.dma_sems)))
                self.dma_sems.append(s)
                res.sem = len(self.dma_sems) - 1
                self.sem_load.append(0)
            else:
                res.sem = min(range(len(self.dma_sems)), key=lambda k: self.sem_load[k])
                self.sem_load[res.sem] += 64
        return res.sem

    def add(self, eng, fn, reads=(), writes=(), dma=None):
        op = Op()
        op.eng = eng
        op.fn = fn
        op.reads = tuple(reads)
        op.writes = tuple(writes)
        op.dma = None if dma is None else self._dsem(dma)
        if op.dma is not None:
            self.sem_load[op.dma] += 1
        op.idx = len(self.ops)
        self.ops.append(op)
        return op

    def barrier(self):
        op = Op()
        op.eng = None
        op.fn = None
        op.reads = ()
        op.writes = ()
        op.dma = None
        op.idx = len(self.ops)
        self.ops.append(op)

    def pe(self, fn, reads=(), writes=()):
        return self.add("pe", fn, reads, writes)

    def dve(self, fn, reads=(), writes=()):
        return self.add("dve", fn, reads, writes)

    def act(self, fn, reads=(), writes=()):
        return self.add("act", fn, reads, writes)

    def pool(self, fn, reads=(), writes=()):
        return self.add("pool", fn, reads, writes)

    def dma(self, out, in_, reads, writes, semres, eng="sp", **kw):
        return self.add(eng, lambda e: e.dma_start(out=out, in_=in_, **kw), reads, writes, dma=semres)

    def finalize(self):
        nc = self.nc
        ops = self.ops
        latest = {}
        bar = {}
        bar_pending = set()
        for op in ops:
            if op.eng is None:
                bar = dict(latest)
                bar_pending = set(self.ENGS)
                op.deps = []
                op.hasdep = False
                continue
            deps = {}
            if op.eng in bar_pending:
                bar_pending.discard(op.eng)
                for x in bar.values():
                    deps[x.idx] = x
            for r in op.reads:
                if r.lw is not None:
                    deps[r.lw.idx] = r.lw
            for w in op.writes:
                if w.lw is not None:
                    deps[w.lw.idx] = w.lw
                for x in w.rd.values():
                    deps[x.idx] = x
            deps.pop(op.idx, None)
            dl = []
            for d in deps.values():
                if op.eng == "pe" and d.eng == "pe" and op.dma is None and d.dma is None:
                    continue
                dl.append(d)
            op.deps = dl
            op.hasdep = False
            key = op.eng if op.dma is None else ("d", op.dma)
            latest[key] = op
            for r in op.reads:
                r.rd[key] = op
            for w in op.writes:
                w.lw = op
                w.rd = {}
        for op in ops:
            for d in op.deps:
                d.hasdep = True
        EPOCH = 30000
        engsem = {}

        def get_engsem(e, ep):
            if (e, ep) not in engsem:
                engsem[(e, ep)] = self.es_root.enter_context(nc.semaphore("eng_%s_%d" % (e, ep)))
            return engsem[(e, ep)]
        engcnt = {e: 0 for e in self.ENGS}
        dcnt = [0] * len(self.dma_sems)
        waited = {e: {} for e in self.ENGS}
        nwaits = 0
        ops = [o for o in ops if o.eng is not None]
        for op in ops:
            need = {}
            for d in op.deps:
                if d.dma is not None:
                    k = ("d", d.dma)
                    v = dcnt[d.dma]
                else:
                    k = ("e", d.eng, d.ev[0])
                    v = d.ev[1]
                if need.get(k, 0) < v:
                    need[k] = v
            w = []
            wd = waited[op.eng]
            for k, v in need.items():
                if wd.get(k, 0) >= v:
                    continue
                wd[k] = v
                w.append((k, v))
            op.waits = w
            nwaits += len(w)
            if op.dma is not None:
                dcnt[op.dma] += 16
                op.ev = dcnt[op.dma]
            elif op.hasdep:
                engcnt[op.eng] += 1
                ep, c = divmod(engcnt[op.eng] - 1, EPOCH)
                op.ev = (ep, c + 1)
                get_engsem(op.eng, ep)
            else:
                op.ev = None
        self.final_dcnt = dcnt
        self.stats = dict(nops=len(ops), nwaits=nwaits, engcnt=dict(engcnt),
                          per_eng={e: sum(1 for o in ops if o.eng == e) for e in self.ENGS})
        per = {e: [o for o in ops if o.eng == e] for e in self.ENGS}
        dma_sems = self.dma_sems

        def semof(k):
            return dma_sems[k[1]] if k[0] == "d" else engsem[(k[1], k[2])]

        def run(eng_obj, name):
            for op in per[name]:
                for k, v in op.waits:
                    eng_obj.wait_ge(semof(k), v)
                ins = op.fn(eng_obj)
                if op.dma is not None:
                    ins.then_inc(dma_sems[op.dma], 16)
                elif op.ev is not None:
                    ins.then_inc(engsem[(name, op.ev[0])], 1)
            if name == "sp":
                for i, s in enumerate(dma_sems):
                    if dcnt[i] > 0:
                        eng_obj.wait_ge(s, dcnt[i])

        with nc.Block() as block:
            @block.tensor
            def _(e):
                run(e, "pe")

            @block.vector
            def _(e):
                run(e, "dve")

            @block.scalar
            def _(e):
                run(e, "act")

            @block.gpsimd
            def _(e):
                run(e, "pool")

            @block.sync
            def _(e):
                run(e, "sp")

D = 1024
ALPHA = 8.0 ** 0.25
EPS = 1e-5
NEXP = 16384
OPTS = dict(GS=4, uvbf16=True, actgelu=True, nogather=0, nodots=0, novside=0, noroute=0, uvtab=1, NSETS=4, NTV=2)


def build_program(DEPTH=4, SQ=4096, peer=True):
    nc = bass.Bass("TRN2", target_bir_lowering=False)
    NRET = (DEPTH + 1) // 2
    NATT = DEPTH // 2
    NATTd = max(NATT, 1)
    NP = 4
    NS = SQ // 128
    NT = NP + NS
    seqs = [(0, 2, False, 0), (2, 2, False, 0), (4, NS, True, 1)]

    def din(name, shape, dt=F32):
        return nc.dram_tensor(name, list(shape), dt, kind="ExternalInput").ap()

    def dout(name, shape, dt=F32):
        return nc.dram_tensor(name, list(shape), dt, kind="ExternalOutput").ap()

    x_d = din("x", [NT * 128, D])
    cond_d = din("cond", [2, D])
    sretf_d = din("sret_f", [NRET, 4, 256, 512])
    sretb_d = din("sret_b", [NRET, 4, 256, 512])
    ck_d = din("ck", [NATTd, 512, 256])
    cv_d = din("cv", [NATTd, 512, 256])
    modw_d = din("mod_w", [DEPTH, D, 6 * D])
    modb_d = din("mod_b", [DEPTH, 6 * D])
    lng_d = din("ln_g", [DEPTH, 2, D])
    lnb_d = din("ln_b", [DEPTH, 2, D])
    rwin_d = din("ret_w_in", [NRET, D, 6144])
    rwout_d = din("ret_w_out", [NRET, 2048, D])
    rdec_d = din("ret_decay", [NRET, 8])
    awin_d = din("attn_w_in", [NATTd, D, 1536])
    awout_d = din("attn_w_out", [NATTd, D, D])
    asink_d = din("attn_sink", [NATTd, 16])
    pwq_d = din("peer_wq", [DEPTH, D, 2048])
    pkeys_d = din("peer_keys", [DEPTH, 16, 128, 128])
    puv_d = din("peer_uv", [DEPTH * NEXP, 2 * D])
    cst_d = din("cst", [128, 6, 128])
    pvec_d = din("pvec", [128, 8])
    iota_d = din("iota16", [128, 16])
    rrc_d = din("rope_ret_cos", [SQ, 128])
    rrs_d = din("rope_ret_sin", [SQ, 128])
    rac_d = din("rope_att_cos", [SQ, 32])
    ras_d = din("rope_att_sin", [SQ, 32])

    y_d = dout("y", [NT * 128, D])
    nsf_d = dout("nsf", [2, NRET, 4, 256, 512])
    nsb_d = dout("nsb", [2, NRET, 4, 256, 512])
    nk_d = dout("nk", [2, NATTd, 256, 256])
    nv_d = dout("nv", [2, NATTd, 256, 256])

    z_d = nc.dram_tensor("z_scr", [NT * 128, 6144], BF16).ap()
    pb_d = nc.dram_tensor("pb_scr", [NT * 128, 2048], F32).ap()
    qs_d = nc.dram_tensor("qs_scr", [NT * 128, 1024], BF16).ap()
    uvb_d = nc.dram_tensor("uvb_scr", [DEPTH * NEXP, 2 * D], BF16).ap() if OPTS["uvtab"] else None
    CVR = 512

    root = ExitStack()
    with root:
        S = Sched(nc, root)
        RIN = S.dram("inputs")
        R_y = [S.dram("y%d" % t) for t in range(NT)]
        R_z = [S.dram("z%d" % t) for t in range(NT)]
        R_pb = [S.dram("pb%d" % t) for t in range(NT)]
        R_qs = [S.dram("qs%d" % t) for t in range(NT)]
        R_o = S.dram("small_outs")
        R_uvb = [[S.dram("uvb%d_%d" % (i_, c_)) for c_ in range(NEXP // CVR)] for i_ in range(DEPTH)]
        cvt = S.dram("cvt")

        def convert_uv(i_):
            if not OPTS["uvtab"]:
                return
            for c_ in range(NEXP // CVR):
                r0 = i_ * NEXP + c_ * CVR
                S.dma(uvb_d[r0:r0 + CVR, :], puv_d[r0:r0 + CVR, :], [RIN], [R_uvb[i_][c_]], cvt, eng="pool")

        def rows(ap, t):
            return ap[t * 128:(t + 1) * 128, :]

        ident_f = S.sbuf("ident_f", [128, 128], F32)
        ident = S.sbuf("ident", [128, 128], BF16)
        ones1 = S.sbuf("ones1", [1, 128], F32)
        condT = S.sbuf("condT", [128, 2, 8], F32)
        modbc = S.sbuf("modbc", [128, 2, 3, D], F32)
        lnbc = S.sbuf("lnbc", [128, 2, D], F32)
        cst = S.sbuf("cst", [128, 6, 128], F32)
        pvec = S.sbuf("pvec", [128, 8], F32)
        iota16 = S.sbuf("iota16", [128, 16], F32)
        epsc = S.sbuf("epsc", [128, 1], F32)

        S.pool(lambda e: e.memset(ident_f[:], 0.0), [], [ident_f])
        S.pool(lambda e: e.affine_select(out=ident_f[:], in_=ident_f[:], pattern=[[-1, 128]],
                                         compare_op=ALU.not_equal, fill=1.0, base=0, channel_multiplier=1),
               [ident_f], [ident_f])
        S.dve(lambda e: e.tensor_copy(out=ident[:], in_=ident_f[:]), [ident_f], [ident])
        S.dve(lambda e: e.memset(ones1[:], 1.0), [], [ones1])
        S.dve(lambda e: e.memset(epsc[:], EPS), [], [epsc])
        S.dma(cst[:], cst_d, [RIN], [cst], cst)
        S.dma(pvec[:], pvec_d, [RIN], [pvec], pvec)
        S.dma(iota16[:], iota_d, [RIN], [iota16], iota16)
        S.dma(condT[:], cond_d.rearrange("j (kc p) -> p j kc", p=128), [RIN], [condT], condT,
              allow_slow_non_contiguous=True)
        S.act(lambda e: e.activation(out=condT[:], in_=condT[:], func=AF.Silu), [condT], [condT])

        def modulation(i, s):
            sc = ExitStack()
            S.es = sc
            with sc:
                wch = [S.sbuf("modw%d" % k, [128, 8, D], F32) for k in range(2)]
                condrep = S.sbuf("condrep", [128, 2, 8, 128], F32)
                S.dve(lambda e: e.tensor_copy(out=condrep[:].rearrange("p j k m -> p (j k) m"),
                                              in_=condT[:].rearrange("p j k -> p (j k)").unsqueeze(2).to_broadcast([128, 16, 128])),
                      [condT], [condrep])
                brow = S.sbuf("modbrow", [1, 3 * D], F32)
                mps = [S.psum("modps%d" % k, [128, 512], F32) for k in range(2)]
                S.dma(brow[:], modb_d[i:i + 1, s * 3 * D:(s + 1) * 3 * D], [RIN], [brow], brow)
                S.dma(lnbc[:, 0, :], lng_d[i, s:s + 1, :].to_broadcast([128, D]), [RIN], [lnbc], lnbc)
                S.dma(lnbc[:, 1, :], lnb_d[i, s:s + 1, :].to_broadcast([128, D]), [RIN], [lnbc], lnbc)
                n = 0
                for blk in range(3):
                    w = wch[blk % 2]
                    c0 = (s * 3 + blk) * D
                    S.dma(w[:], modw_d[i, :, c0:c0 + D].rearrange("(kc p) n -> p kc n", p=128), [RIN], [w], w)
                    for j in range(2):
                        for nh in range(2):
                            ps = mps[n % 2]
                            n += 1
                            for kc in range(8):
                                S.pe(lambda e, ps=ps, j=j, kc=kc, w=w, nh=nh: e.matmul(
                                    ps[:], lhsT=condrep[:, j, kc, :], rhs=w[:, kc, nh * 512:(nh + 1) * 512],
                                    start=(kc == 0), stop=False), [condrep, w], [ps])
                            S.pe(lambda e, ps=ps, blk=blk, nh=nh: e.matmul(
                                ps[:], lhsT=ones1[:], rhs=brow[:, blk * D + nh * 512: blk * D + (nh + 1) * 512],
                                start=False, stop=True), [ones1, brow], [ps])
                            add = 1.0 if blk == 1 else 0.0
                            S.act(lambda e, ps=ps, j=j, blk=blk, nh=nh, add=add: e.activation(
                                out=modbc[:, j, blk, nh * 512:(nh + 1) * 512], in_=ps[:], func=AF.Identity, bias=add, scale=1.0)
                                if add else e.copy(out=modbc[:, j, blk, nh * 512:(nh + 1) * 512], in_=ps[:]),
                                [ps], [modbc])
                S.barrier()
            S.es = root

        def load_w(dst, src2d, N, tag):
            v = src2d.rearrange("(kc p) n -> p kc n", p=128)
            for n0 in range(0, N, 2048):
                n1 = min(N, n0 + 2048)
                S.dma(dst[:, :, n0:n1], v[:, :, n0:n1], [RIN], [dst], dst, eng="pool")

        def prologue(xt, j, hb, hT, tp, h32=None, nopool=True):
            pl = S.dve if nopool else S.pool
            tgt = h32 if h32 is not None else hb
            S.dve(lambda e: e.tensor_tensor(out=tgt[:], in0=xt[:], in1=modbc[:, j, 1, :], op=ALU.mult), [xt, modbc], [tgt])
            if h32 is not None:
                pl(lambda e: e.tensor_tensor(out=h32[:], in0=h32[:], in1=modbc[:, j, 0, :], op=ALU.add), [h32, modbc], [h32])
                S.act(lambda e: e.copy(out=hb[:], in_=h32[:]), [h32], [hb])
            else:
                pl(lambda e: e.tensor_tensor(out=hb[:], in0=hb[:], in1=modbc[:, j, 0, :], op=ALU.add), [hb, modbc], [hb])
            for kc in range(8):
                S.pe(lambda e, kc=kc: e.transpose(out=tp[:, kc * 128:(kc + 1) * 128], in_=hb[:, kc * 128:(kc + 1) * 128],
                                                  identity=ident[:]), [hb, ident], [tp])
            S.act(lambda e: e.copy(out=hT[:].rearrange("p a b -> p (a b)"), in_=tp[:]), [tp], [hT])

        def epilogue(Y, yreads, xt, j, t, tmp, r, st, mv, rstd, nopool=True):
            pl = S.dve if nopool else S.pool
            S.dve(lambda e: e.tensor_tensor(out=tmp[:], in0=Y[:], in1=modbc[:, j, 2, :], op=ALU.mult), [modbc] + yreads, [tmp])
            S.dve(lambda e: e.scalar_tensor_tensor(out=r[:], in0=xt[:], scalar=ALPHA, in1=tmp[:], op0=ALU.mult, op1=ALU.add),
                  [xt, tmp], [r])
            for c in range(2):
                S.dve(lambda e, c=c: e.bn_stats(out=st[:, c, :], in_=r[:, c * 512:(c + 1) * 512]), [r], [st])
            S.dve(lambda e: e.bn_aggr(out=mv[:], in_=st[:].rearrange("p a b -> p (a b)")), [st], [mv])
            S.act(lambda e: e.activation(out=rstd[:], in_=mv[:, 1:2], func=AF.Sqrt, bias=epsc[:], scale=1.0), [mv, epsc], [rstd])
            S.dve(lambda e: e.reciprocal(out=rstd[:], in_=rstd[:]), [rstd], [rstd])
            S.dve(lambda e: e.tensor_scalar(out=r[:], in0=r[:], scalar1=mv[:, 0:1], scalar2=rstd[:, 0:1],
                                            op0=ALU.subtract, op1=ALU.mult), [r, mv, rstd], [r])
            pl(lambda e: e.tensor_tensor(out=r[:], in0=r[:], in1=lnbc[:, 0, :], op=ALU.mult), [r, lnbc], [r])
            pl(lambda e: e.tensor_tensor(out=tmp[:], in0=r[:], in1=lnbc[:, 1, :], op=ALU.add), [r, lnbc], [tmp])
            S.dma(rows(y_d, t), tmp[:], [tmp], [R_y[t]], tmp)

        def xsrc(first):
            return x_d if first else y_d

        def xres(first, t):
            return RIN if first else R_y[t]

        def retention_layer(i, first):
            jr = i // 2
            sc0 = ExitStack()
            S.es = sc0
            with sc0:
                lg = S.sbuf("lg", [128, 8], F32)
                qdec = S.sbuf("qdec", [128, 2, 4], F32)
                kdec = S.sbuf("kdec", [128, 2, 4], F32)
                cdec = S.sbuf("cdec", [128, 2, 4], F32)
                dmask = S.sbuf("dmask", [128, 4, 128], F32)
                dtmp = S.sbuf("dtmp", [128, 128], F32)
                c128 = S.sbuf("c128", [128, 1], F32)
                S.dve(lambda e: e.memset(c128[:], 128.0), [], [c128])
                S.dma(lg[:], rdec_d[jr:jr + 1, :].to_broadcast([128, 8]), [RIN], [lg], lg)
                S.act(lambda e: e.activation(out=lg[:], in_=lg[:], func=AF.Exp, scale=-1.0), [lg], [lg])
                S.act(lambda e: e.activation(out=lg[:], in_=lg[:], func=AF.Ln, bias=1.0, scale=1.0), [lg], [lg])
                S.dve(lambda e: e.tensor_scalar(out=lg[:], in0=lg[:], scalar1=-1.0, scalar2=None, op0=ALU.mult), [lg], [lg])
                for h in range(4):
                    for dr in range(2):
                        col = dr * 4 + h
                        pq = 0 if dr == 0 else 1
                        pk = 2 if dr == 0 else 3
                        S.act(lambda e, col=col, pq=pq, dr=dr, h=h: e.activation(
                            out=qdec[:, dr, h:h + 1], in_=lg[:, col:col + 1], func=AF.Exp, scale=pvec[:, pq:pq + 1]),
                            [lg, pvec], [qdec])
                        S.act(lambda e, col=col, pk=pk, dr=dr, h=h: e.activation(
                            out=kdec[:, dr, h:h + 1], in_=lg[:, col:col + 1], func=AF.Exp, scale=pvec[:, pk:pk + 1]),
                            [lg, pvec], [kdec])
                        S.act(lambda e, col=col, dr=dr, h=h: e.activation(
                            out=cdec[:, dr, h:h + 1], in_=lg[:, col:col + 1], func=AF.Exp, scale=c128[:, 0:1]),
                            [lg, c128], [cdec])
                    S.act(lambda e, h=h: e.activation(out=dmask[:, h, :], in_=cst[:, 0, :], func=AF.Exp, scale=lg[:, h:h + 1]),
                          [cst, lg], [dmask])
                    S.dve(lambda e, h=h: e.tensor_tensor(out=dmask[:, h, :], in0=dmask[:, h, :], in1=cst[:, 1, :], op=ALU.mult),
                          [dmask, cst], [dmask])
                    S.act(lambda e, h=h: e.activation(out=dtmp[:], in_=cst[:, 2, :], func=AF.Exp, scale=lg[:, 4 + h:5 + h]),
                          [cst, lg], [dtmp])
                    S.dve(lambda e: e.tensor_tensor(out=dtmp[:], in0=dtmp[:], in1=cst[:, 3, :], op=ALU.mult), [dtmp, cst], [dtmp])
                    S.dve(lambda e, h=h: e.tensor_tensor(out=dmask[:, h, :], in0=dmask[:, h, :], in1=dtmp[:], op=ALU.add),
                          [dmask, dtmp], [dmask])

                scz = ExitStack()
                S.es = scz
                with scz:
                    win = S.sbuf("rwin", [128, 8, 6144], BF16)
                    load_w(win, rwin_d[jr], 6144, "rwin")
                    convert_uv(i)
                    xts = [S.sbuf("zx%d" % k, [128, D], F32) for k in range(2)]
                    hb = S.sbuf("zhb", [128, D], BF16)
                    hT = S.sbuf("zhT", [128, 8, 128], BF16)
                    qk32 = S.sbuf("zqk32", [128, 2048], F32)
                    ra = S.sbuf("zra", [128, 1024], F32)
                    rb = S.sbuf("zrb", [128, 1024], F32)
                    zrow = [S.sbuf("zrow%d" % k, [128, 6144], BF16) for k in range(2)]
                    rc = [S.sbuf("zrc%d" % k, [128, 2, 128], F32) for k in range(2)]
                    tp = S.psum("ztp", [128, 1024], BF16)
                    zps = [S.psum("zps%d" % k, [128, 512], F32) for k in range(4)]
                    for (t0, ntl, latent, j) in seqs:
                        for tt in range(ntl):
                            t = t0 + tt
                            xt = xts[t % 2]
                            zr = zrow[t % 2]
                            S.dma(xt[:], rows(xsrc(first), t), [xres(first, t)], [xt], xt)
                            prologue(xt, j, hb, hT, tp)
                            if latent:
                                rct = rc[t % 2]
                                S.dma(rct[:, 0, :], rrc_d[tt * 128:(tt + 1) * 128, :], [RIN], [rct], rct)
                                S.dma(rct[:, 1, :], rrs_d[tt * 128:(tt + 1) * 128, :], [RIN], [rct], rct)
                            for nb in range(12):
                                ps = zps[nb % 4]
                                for kc in range(8):
                                    S.pe(lambda e, ps=ps, kc=kc, nb=nb: e.matmul(
                                        ps[:], lhsT=hT[:, kc, :], rhs=win[:, kc, nb * 512:(nb + 1) * 512],
                                        start=(kc == 0), stop=(kc == 7)), [hT, win], [ps])
                                sl = slice(nb * 512, (nb + 1) * 512)
                                if nb < 4:
                                    scl = 1.0 if nb < 2 else 0.0625
                                    if latent:
                                        S.act(lambda e, ps=ps, sl=sl, scl=scl: e.mul(out=qk32[:, sl], in_=ps[:], mul=scl), [ps], [qk32])
                                    else:
                                        S.act(lambda e, ps=ps, sl=sl, scl=scl, zr=zr: e.mul(out=zr[:, sl], in_=ps[:], mul=scl), [ps], [zr])
                                elif nb < 8:
                                    S.act(lambda e, ps=ps, sl=sl, zr=zr: e.copy(out=zr[:, sl], in_=ps[:]), [ps], [zr])
                                else:
                                    S.act(lambda e, ps=ps, sl=sl, zr=zr: e.activation(out=zr[:, sl], in_=ps[:], func=AF.Silu), [ps], [zr])
                            if latent:
                                v4 = qk32[:].rearrange("p (a b c) -> p a b c", a=8, b=2)
                                x1 = v4[:, :, 0, :]
                                x2 = v4[:, :, 1, :]
                                cosb = rct[:, 0, :].unsqueeze(1).to_broadcast([128, 8, 128])
                                sinb = rct[:, 1, :].unsqueeze(1).to_broadcast([128, 8, 128])
                                o4 = zr[:, 0:2048].rearrange("p (a b c) -> p a b c", a=8, b=2)
                                ra3 = ra[:].rearrange("p (a c) -> p a c", a=8)
                                rb3 = rb[:].rearrange("p (a c) -> p a c", a=8)
                                S.dve(lambda e, ra3=ra3, x1=x1, cosb=cosb: e.tensor_tensor(out=ra3, in0=x1, in1=cosb, op=ALU.mult), [qk32, rct], [ra])
                                S.pool(lambda e, rb3=rb3, x2=x2, sinb=sinb: e.tensor_tensor(out=rb3, in0=x2, in1=sinb, op=ALU.mult), [qk32, rct], [rb])
                                S.dve(lambda e, o4=o4, ra3=ra3, rb3=rb3: e.tensor_tensor(out=o4[:, :, 0, :], in0=ra3, in1=rb3, op=ALU.subtract), [ra, rb], [zr])
                                S.dve(lambda e, ra3=ra3, x1=x1, sinb=sinb: e.tensor_tensor(out=ra3, in0=x1, in1=sinb, op=ALU.mult), [qk32, rct, zr], [ra])
                                S.pool(lambda e, rb3=rb3, x2=x2, cosb=cosb: e.tensor_tensor(out=rb3, in0=x2, in1=cosb, op=ALU.mult), [qk32, rct, zr], [rb])
                                S.dve(lambda e, o4=o4, ra3=ra3, rb3=rb3: e.tensor_tensor(out=o4[:, :, 1, :], in0=ra3, in1=rb3, op=ALU.add), [ra, rb], [zr])
                            S.dma(rows(z_d, t), zr[:], [zr], [R_z[t]], zr)
                    S.barrier()
                S.es = sc0

                sca = ExitStack()
                S.es = sca
                with sca:
                    qkv = [S.sbuf("aqkv%d" % k, [128, 4096], BF16) for k in range(2)]
                    qdb = S.sbuf("aqdb", [128, 1024], BF16)
                    kdb = S.sbuf("akdb", [128, 1024], BF16)
                    qdbT = S.sbuf("aqdbT", [128, 8, 128], BF16)
                    Sb32 = S.sbuf("aSb32", [128, 4, 2, 512], F32)
                    Sbb = S.sbuf("aSbb", [128, 4, 2, 512], BF16)
                    pbt = [S.sbuf("apbt%d" % k, [128, 2048], F32) for k in range(2)]
                    tp = S.psum("atp", [128, 1024], BF16)
                    aps = [S.psum("aps%d" % k, [128, 512], F32) for k in range(6)]
                    for si, (t0, ntl, latent, j) in enumerate(seqs):
                        if latent:
                            S.dma(Sb32[:].rearrange("p h c v -> p (h c) v"),
                                  sretb_d[jr].rearrange("h (c p) v -> p (h c) v", p=128), [RIN], [Sb32], Sb32)
                        else:
                            S.dve(lambda e: e.memset(Sb32[:].rearrange("p h c v -> p (h c v)"), 0.0), [], [Sb32])
                        S.act(lambda e: e.copy(out=Sbb[:].rearrange("p h c v -> p (h c v)"),
                                               in_=Sb32[:].rearrange("p h c v -> p (h c v)")), [Sb32], [Sbb])
                        for tt in reversed(range(ntl)):
                            t = t0 + tt
                            qv = qkv[t % 2]
                            S.dma(qv[:], z_d[t * 128:(t + 1) * 128, 0:4096], [R_z[t]], [qv], qv)
                            for h in range(4):
                                S.dve(lambda e, h=h, qv=qv: e.tensor_scalar(out=qdb[:, h * 256:(h + 1) * 256], in0=qv[:, h * 256:(h + 1) * 256],
                                                                      scalar1=qdec[:, 1, h:h + 1], scalar2=None, op0=ALU.mult),
                                      [qv, qdec], [qdb])
                                S.pool(lambda e, h=h, qv=qv: e.tensor_scalar(out=kdb[:, h * 256:(h + 1) * 256],
                                                                       in0=qv[:, 1024 + h * 256:1024 + (h + 1) * 256],
                                                                       scalar1=kdec[:, 1, h:h + 1], scalar2=None, op0=ALU.mult),
                                       [qv, kdec], [kdb])
                            for c in range(8):
                                S.pe(lambda e, c=c: e.transpose(out=tp[:, c * 128:(c + 1) * 128], in_=qdb[:, c * 128:(c + 1) * 128],
                                                                identity=ident[:]), [qdb, ident], [tp])
                            S.act(lambda e: e.copy(out=qdbT[:].rearrange("p a b -> p (a b)"), in_=tp[:]), [tp], [qdbT])
                            pt = pbt[t % 2]
                            for h in range(4):
                                ps = aps[h % 2]
                                for dc in range(2):
                                    S.pe(lambda e, ps=ps, h=h, dc=dc: e.matmul(ps[:], lhsT=qdbT[:, h * 2 + dc, :], rhs=Sbb[:, h, dc, :],
                                                                          start=(dc == 0), stop=(dc == 1)), [qdbT, Sbb], [ps])
                                S.act(lambda e, ps=ps, h=h, pt=pt: e.copy(out=pt[:, h * 512:(h + 1) * 512], in_=ps[:]), [ps], [pt])
                            S.dma(rows(pb_d, t), pt[:], [pt], [R_pb[t]], pt)
                            n = 0
                            for h in range(4):
                                for dc in range(2):
                                    ps = aps[2 + n % 4]
                                    n += 1
                                    S.pe(lambda e, ps=ps, h=h, dc=dc, qv=qv: e.matmul(
                                        ps[:], lhsT=kdb[:, h * 256 + dc * 128: h * 256 + (dc + 1) * 128],
                                        rhs=qv[:, 2048 + h * 512: 2048 + (h + 1) * 512], start=True, stop=True), [kdb, qv], [ps])
                                    S.dve(lambda e, ps=ps, h=h, dc=dc: e.scalar_tensor_tensor(
                                        out=Sb32[:, h, dc, :], in0=Sb32[:, h, dc, :], scalar=cdec[:, 1, h:h + 1], in1=ps[:],
                                        op0=ALU.mult, op1=ALU.add), [Sb32, cdec, ps], [Sb32])
                            S.act(lambda e: e.copy(out=Sbb[:].rearrange("p h c v -> p (h c v)"),
                                                   in_=Sb32[:].rearrange("p h c v -> p (h c v)")), [Sb32], [Sbb])
                        if not latent:
                            S.dma(nsb_d[si, jr].rearrange("h (c p) v -> p (h c) v", p=128),
                                  Sb32[:].rearrange("p h c v -> p (h c) v"), [Sb32], [R_o], Sb32)
                    S.barrier()
                S.es = sc0

                scb = ExitStack()
                S.es = scb
                with scb:
                    wout = S.sbuf("rwout", [128, 16, D], BF16)
                    load_w(wout, rwout_d[jr], D, "rwout")
                    zt = [S.sbuf("bz%d" % k, [128, 6144], BF16) for k in range(2)]
                    pbt = S.sbuf("bpbt", [128, 2048], F32)
                    xts = [S.sbuf("bx%d" % k, [128, D], F32) for k in range(2)]
                    qdf = S.sbuf("bqdf", [128, 1024], BF16)
                    kdf = S.sbuf("bkdf", [128, 1024], BF16)
                    QT = S.sbuf("bQT", [128, 24, 128], BF16)
                    attm = S.sbuf("battm", [128, 512], BF16)
                    Sf32 = S.sbuf("bSf32", [128, 4, 2, 512], F32)
                    Sfb = S.sbuf("bSfb", [128, 4, 2, 512], BF16)
                    o32 = S.sbuf("bo32", [128, 2048], F32)
                    go = S.sbuf("bgo", [128, 2048], BF16)
                    goT = S.sbuf("bgoT", [128, 16, 128], BF16)
                    gst = S.sbuf("bgst", [128, 4, 6], F32)
                    gmv = S.sbuf("bgmv", [128, 4, 2], F32)
                    grs = S.sbuf("bgrs", [128, 4], F32)
                    tmp = S.sbuf("btmp", [128, D], F32)
                    r = S.sbuf("br", [128, D], F32)
                    st = S.sbuf("bst", [128, 2, 6], F32)
                    mv = S.sbuf("bmv", [128, 2], F32)
                    rstd = S.sbuf("brstd", [128, 1], F32)
                    tp = S.psum("btp", [128, 1024], BF16)
                    bps = [S.psum("bps%d" % k, [128, 512], F32) for k in range(5)]
                    Y = S.psum("bY", [128, 1024], F32)
                    for si, (t0, ntl, latent, j) in enumerate(seqs):
                        if latent:
                            S.dma(Sf32[:].rearrange("p h c v -> p (h c) v"),
                                  sretf_d[jr].rearrange("h (c p) v -> p (h c) v", p=128), [RIN], [Sf32], Sf32)
                        else:
                            S.dve(lambda e: e.memset(Sf32[:].rearrange("p h c v -> p (h c v)"), 0.0), [], [Sf32])
                        S.act(lambda e: e.copy(out=Sfb[:].rearrange("p h c v -> p (h c v)"),
                                               in_=Sf32[:].rearrange("p h c v -> p (h c v)")), [Sf32], [Sfb])
                        for tt in range(ntl):
                            t = t0 + tt
                            z = zt[t % 2]
                            xt = xts[t % 2]
                            S.dma(z[:], rows(z_d, t), [R_z[t]], [z], z)
                            S.dma(pbt[:], rows(pb_d, t), [R_pb[t]], [pbt], pbt)
                            S.dma(xt[:], rows(xsrc(first), t), [xres(first, t)], [xt], xt)
                            for h in range(4):
                                S.dve(lambda e, h=h, z=z: e.tensor_scalar(out=qdf[:, h * 256:(h + 1) * 256], in0=z[:, h * 256:(h + 1) * 256],
                                                                     scalar1=qdec[:, 0, h:h + 1], scalar2=None, op0=ALU.mult),
                                      [z, qdec], [qdf])
                                S.pool(lambda e, h=h, z=z: e.tensor_scalar(out=kdf[:, h * 256:(h + 1) * 256],
                                                                      in0=z[:, 1024 + h * 256:1024 + (h + 1) * 256],
                                                                      scalar1=kdec[:, 0, h:h + 1], scalar2=None, op0=ALU.mult),
                                       [z, kdec], [kdf])
                            for grp, (src, off, rr) in enumerate([(z, 0, [z]), (qdf, 0, [qdf]), (z, 1024, [z])]):
                                for c in range(8):
                                    S.pe(lambda e, c=c, src=src, off=off: e.transpose(
                                        out=tp[:, c * 128:(c + 1) * 128], in_=src[:, off + c * 128: off + (c + 1) * 128],
                                        identity=ident[:]), rr + [ident], [tp])
                                S.act(lambda e, grp=grp: e.copy(out=QT[:, grp * 8:(grp + 1) * 8, :].rearrange("p a b -> p (a b)"), in_=tp[:]),
                                      [tp], [QT])
                            pa = bps[4]
                            for h in range(4):
                                for dc in range(2):
                                    S.pe(lambda e, h=h, dc=dc: e.matmul(pa[:, h * 128:(h + 1) * 128], lhsT=QT[:, 16 + h * 2 + dc, :],
                                                                        rhs=QT[:, h * 2 + dc, :], start=(dc == 0), stop=(dc == 1)),
                                         [QT], [pa])
                            S.dve(lambda e: e.tensor_tensor(out=attm[:], in0=pa[:], in1=dmask[:].rearrange("p h i -> p (h i)"), op=ALU.mult),
                                  [pa, dmask], [attm])
                            for h in range(4):
                                ps = bps[h]
                                S.pe(lambda e, ps=ps, h=h, z=z: e.matmul(ps[:], lhsT=attm[:, h * 128:(h + 1) * 128],
                                                                    rhs=z[:, 2048 + h * 512:2048 + (h + 1) * 512], start=True, stop=False),
                                     [attm, z], [ps])
                                for dc in range(2):
                                    S.pe(lambda e, ps=ps, h=h, dc=dc: e.matmul(ps[:], lhsT=QT[:, 8 + h * 2 + dc, :], rhs=Sfb[:, h, dc, :],
                                                                          start=False, stop=(dc == 1)), [QT, Sfb], [ps])
                                S.dve(lambda e, ps=ps, h=h: e.tensor_tensor(out=o32[:, h * 512:(h + 1) * 512], in0=ps[:],
                                                                       in1=pbt[:, h * 512:(h + 1) * 512], op=ALU.add), [ps, pbt], [o32])
                                S.dve(lambda e, h=h: e.bn_stats(out=gst[:, h, :], in_=o32[:, h * 512:(h + 1) * 512]), [o32], [gst])
                                S.dve(lambda e, h=h: e.bn_aggr(out=gmv[:, h, :], in_=gst[:, h, :]), [gst], [gmv])
                            S.act(lambda e: e.activation(out=grs[:], in_=gmv[:, :, 1], func=AF.Sqrt, bias=epsc[:], scale=1.0), [gmv, epsc], [grs])
                            S.dve(lambda e: e.reciprocal(out=grs[:], in_=grs[:]), [grs], [grs])
                            for h in range(4):
                                S.dve(lambda e, h=h: e.tensor_scalar(out=o32[:, h * 512:(h + 1) * 512], in0=o32[:, h * 512:(h + 1) * 512],
                                                                     scalar1=gmv[:, h, 0:1], scalar2=grs[:, h:h + 1],
                                                                     op0=ALU.subtract, op1=ALU.mult), [o32, gmv, grs], [o32])
                            S.pool(lambda e, z=z: e.tensor_tensor(out=go[:], in0=o32[:], in1=z[:, 4096:6144], op=ALU.mult), [o32, z], [go])
                            for half in range(2):
                                for c in range(8):
                                    cc = half * 8 + c
                                    S.pe(lambda e, c=c, cc=cc: e.transpose(out=tp[:, c * 128:(c + 1) * 128], in_=go[:, cc * 128:(cc + 1) * 128],
                                                                           identity=ident[:]), [go, ident], [tp])
                                S.act(lambda e, half=half: e.copy(out=goT[:, half * 8:(half + 1) * 8, :].rearrange("p a b -> p (a b)"), in_=tp[:]),
                                      [tp], [goT])
                            for nh in range(2):
                                for kc in range(16):
                                    S.pe(lambda e, nh=nh, kc=kc: e.matmul(Y[:, nh * 512:(nh + 1) * 512], lhsT=goT[:, kc, :],
                                                                          rhs=wout[:, kc, nh * 512:(nh + 1) * 512],
                                                                          start=(kc == 0), stop=(kc == 15)), [goT, wout], [Y])
                            epilogue(Y, [Y], xt, j, t, tmp, r, st, mv, rstd)
                            n = 0
                            for h in range(4):
                                for dc in range(2):
                                    ps = bps[n % 4]
                                    n += 1
                                    S.pe(lambda e, ps=ps, h=h, dc=dc, z=z: e.matmul(
                                        ps[:], lhsT=kdf[:, h * 256 + dc * 128: h * 256 + (dc + 1) * 128],
                                        rhs=z[:, 2048 + h * 512: 2048 + (h + 1) * 512], start=True, stop=True), [kdf, z], [ps])
                                    S.dve(lambda e, ps=ps, h=h, dc=dc: e.scalar_tensor_tensor(
                                        out=Sf32[:, h, dc, :], in0=Sf32[:, h, dc, :], scalar=cdec[:, 0, h:h + 1], in1=ps[:],
                                        op0=ALU.mult, op1=ALU.add), [Sf32, cdec, ps], [Sf32])
                            S.act(lambda e: e.copy(out=Sfb[:].rearrange("p h c v -> p (h c v)"),
                                                   in_=Sf32[:].rearrange("p h c v -> p (h c v)")), [Sf32], [Sfb])
                        if not latent:
                            S.dma(nsf_d[si, jr].rearrange("h (c p) v -> p (h c) v", p=128),
                                  Sf32[:].rearrange("p h c v -> p (h c) v"), [Sf32], [R_o], Sf32)
                    S.barrier()
                S.es = sc0
            S.es = root

        def attention_layer(i, first):
            ja = i // 2
            sc0 = ExitStack()
            S.es = sc0
            with sc0:
                win = S.sbuf("awin", [128, 8, 1536], BF16)
                wout = S.sbuf("awout", [128, 8, D], BF16)
                load_w(win, awin_d[ja], 1536, "awin")
                load_w(wout, awout_d[ja], D, "awout")
                convert_uv(i)
                esink = S.sbuf("esink", [128, 16], F32)
                S.dma(esink[:], asink_d[ja:ja + 1, :].to_broadcast([128, 16]), [RIN], [esink], esink)
                S.act(lambda e: e.activation(out=esink[:], in_=esink[:], func=AF.Exp), [esink], [esink])
                mprev = S.sbuf("mprev", [128, 4, 128], BF16)
                mnext = S.sbuf("mnext", [128, 4, 128], BF16)
                S.dve(lambda e: e.tensor_copy(out=mprev[:], in_=cst[:, 4, :].unsqueeze(1).to_broadcast([128, 4, 128])), [cst], [mprev])
                S.dve(lambda e: e.tensor_copy(out=mnext[:], in_=cst[:, 5, :].unsqueeze(1).to_broadcast([128, 4, 128])), [cst], [mnext])
                NSm = max(NS, 2)
                KT = S.sbuf("aKT", [64, 4, NSm * 128], BF16)
                VL = S.sbuf("aVL", [128, NSm, 4, 65], BF16)
                CKT = S.sbuf("aCKT", [64, 4, 512], BF16)
                CV = S.sbuf("aCV", [128, 4, 4, 65], BF16)
                c32 = S.sbuf("ac32", [128, 4, 256], F32)
                cb = S.sbuf("acb", [128, 4, 256], BF16)
                xts = [S.sbuf("ax%d" % k, [128, D], F32) for k in range(2)]
                hb = S.sbuf("ahb", [128, D], BF16)
                hT = S.sbuf("ahT", [128, 8, 128], BF16)
                q32 = S.sbuf("aq32", [128, 1536], F32)
                ra = S.sbuf("ara", [128, 512], F32)
                rb = S.sbuf("arb", [128, 512], F32)
                qb = [S.sbuf("aqb%d" % k, [128, 1024], BF16) for k in range(2)]
                kb = S.sbuf("akb", [128, 256], BF16)
                rc = [S.sbuf("arc%d" % k, [128, 2, 32], F32) for k in range(2)]
                qT = S.sbuf("aqT", [64, 16, 128], BF16)
                PT = [S.sbuf("aPT%d" % k, [128, 512], BF16) for k in range(7)]
                rden = S.sbuf("arden", [128, 16], F32)
                on = S.sbuf("aon", [128, 1024], BF16)
                onT = S.sbuf("aonT", [128, 8, 128], BF16)
                tmp = S.sbuf("atmp", [128, D], F32)
                r = S.sbuf("ar", [128, D], F32)
                st = S.sbuf("ast", [128, 2, 6], F32)
                mv = S.sbuf("amv", [128, 2], F32)
                rstd = S.sbuf("arstd", [128, 1], F32)
                tp = S.psum("atp", [128, 1024], BF16)
                tq = S.psum("atq", [128, 2048], BF16)
                sps = [S.psum("asps%d" % k, [128, 512], F32) for k in range(2)]
                ops_ = [S.psum("aops%d" % k, [128, 4, 65], F32) for k in range(1)]
                Y = S.psum("aY", [128, 1024], F32)

                S.dve(lambda e: e.memset(VL[:].rearrange("p a b c -> p (a b c)"), 1.0), [], [VL])
                S.dve(lambda e: e.memset(CV[:].rearrange("p a b c -> p (a b c)"), 1.0), [], [CV])
                S.dma(c32[:], ck_d[ja].rearrange("(b p) f -> p b f", p=128), [RIN], [c32], c32)
                S.dve(lambda e: e.tensor_copy(out=cb[:], in_=c32[:]), [c32], [cb])
                for b in range(4):
                    for g in range(4):
                        S.pe(lambda e, b=b, g=g: e.transpose(out=tp[0:64, g * 128:(g + 1) * 128], in_=cb[:, b, g * 64:(g + 1) * 64],
                                                             identity=ident[:]), [cb, ident], [tp])
                    S.act(lambda e, b=b: e.copy(out=CKT[:, :, b * 128:(b + 1) * 128],
                                                in_=tp[0:64, 0:512].rearrange("p (g k) -> p g k", g=4)), [tp], [CKT])
                S.dma(c32[:], cv_d[ja].rearrange("(b p) f -> p b f", p=128), [RIN], [c32], c32)
                S.dve(lambda e: e.tensor_copy(out=CV[:, :, :, 0:64], in_=c32[:].rearrange("p b (g d) -> p b g d", g=4)), [c32], [CV])

                for si, (t0, ntl, latent, j) in enumerate(seqs):
                    for tt in range(ntl):
                        t = t0 + tt
                        xt = xts[t % 2]
                        S.dma(xt[:], rows(xsrc(first), t), [xres(first, t)], [xt], xt)
                        prologue(xt, j, hb, hT, tp)
                        if latent:
                            rct = rc[t % 2]
                            S.dma(rct[:, 0, :], rac_d[tt * 128:(tt + 1) * 128, :], [RIN], [rct], rct)
                            S.dma(rct[:, 1, :], ras_d[tt * 128:(tt + 1) * 128, :], [RIN], [rct], rct)
                        for nb in range(3):
                            ps = sps[nb % 2]
                            for kc in range(8):
                                S.pe(lambda e, ps=ps, kc=kc, nb=nb: e.matmul(ps[:], lhsT=hT[:, kc, :], rhs=win[:, kc, nb * 512:(nb + 1) * 512],
                                                                        start=(kc == 0), stop=(kc == 7)), [hT, win], [ps])
                            S.act(lambda e, ps=ps, nb=nb: e.copy(out=q32[:, nb * 512:(nb + 1) * 512], in_=ps[:]), [ps], [q32])
                        if not latent:
                            S.dma(nk_d[si, ja, tt * 128:(tt + 1) * 128, :], q32[:, 1024:1280], [q32], [R_o], q32)
                            S.dma(nv_d[si, ja, tt * 128:(tt + 1) * 128, :], q32[:, 1280:1536], [q32], [R_o], q32)
                        q_out = qb[t % 2]
                        if latent:
                            v5 = q32[:, 0:1280].rearrange("p (h a b c) -> p h a b c", h=20, a=2, b=2)
                            for a in range(2):
                                x1 = v5[:, :, a, 0, :]
                                x2 = v5[:, :, a, 1, :]
                                cosb = rct[:, 0, a * 16:(a + 1) * 16].unsqueeze(1).to_broadcast([128, 20, 16])
                                sinb = rct[:, 1, a * 16:(a + 1) * 16].unsqueeze(1).to_broadcast([128, 20, 16])
                                ra3 = ra[:, 0:320].rearrange("p (h c) -> p h c", h=20)
                                rb3 = rb[:, 0:320].rearrange("p (h c) -> p h c", h=20)
                                ra3b = ra[:, 320:640].rearrange("p (h c) -> p h c", h=20) if False else None
                                S.dve(lambda e, x1=x1, cosb=cosb, ra3=ra3: e.tensor_tensor(out=ra3, in0=x1, in1=cosb, op=ALU.mult), [q32, rct], [ra])
                                S.pool(lambda e, x2=x2, sinb=sinb, rb3=rb3: e.tensor_tensor(out=rb3, in0=x2, in1=sinb, op=ALU.mult), [q32, rct], [rb])
                                S.dve(lambda e, ra3=ra3, rb3=rb3: e.tensor_tensor(out=ra3, in0=ra3, in1=rb3, op=ALU.subtract), [ra, rb], [ra])
                                S.pool(lambda e, x1=x1, sinb=sinb, rb3=rb3: e.tensor_tensor(out=rb3, in0=x1, in1=sinb, op=ALU.mult), [q32, rct, ra], [rb])
                                S.dve(lambda e, x1=x1, ra3=ra3: e.tensor_copy(out=x1, in_=ra3), [ra, rb], [q32])
                                S.dve(lambda e, x2=x2, cosb=cosb, ra3=ra3: e.tensor_tensor(out=ra3, in0=x2, in1=cosb, op=ALU.mult), [q32, rct], [ra])
                                S.dve(lambda e, x2=x2, ra3=ra3, rb3=rb3: e.tensor_tensor(out=x2, in0=ra3, in1=rb3, op=ALU.add), [ra, rb], [q32])
                        S.act(lambda e, q_out=q_out: e.mul(out=q_out[:], in_=q32[:, 0:1024], mul=0.125), [q32], [q_out])
                        S.dma(rows(qs_d, t), q_out[:], [q_out], [R_qs[t]], q_out)
                        S.dve(lambda e: e.tensor_copy(out=kb[:], in_=q32[:, 1024:1280]), [q32], [kb])
                        S.dve(lambda e, tt=tt: e.tensor_copy(out=VL[:, tt, :, 0:64], in_=q32[:, 1280:1536].rearrange("p (g d) -> p g d", g=4)),
                              [q32], [VL])
                        for g in range(4):
                            S.pe(lambda e, g=g: e.transpose(out=tp[0:64, g * 128:(g + 1) * 128], in_=kb[:, g * 64:(g + 1) * 64],
                                                            identity=ident[:]), [kb, ident], [tp])
                        S.act(lambda e, tt=tt: e.copy(out=KT[:, :, tt * 128:(tt + 1) * 128],
                                                      in_=tp[0:64, 0:512].rearrange("p (g k) -> p g k", g=4)), [tp], [KT])
                    for tt in range(ntl):
                        t = t0 + tt
                        xt = xts[t % 2]
                        qin = qb[t % 2]
                        S.dma(xt[:], rows(xsrc(first), t), [xres(first, t)], [xt], xt)
                        S.dma(qin[:], rows(qs_d, t), [R_qs[t]], [qin], qin)
                        for h in range(16):
                            S.pe(lambda e, h=h, qin=qin: e.transpose(out=tq[0:64, h * 128:(h + 1) * 128], in_=qin[:, h * 64:(h + 1) * 64],
                                                                     identity=ident[:]), [qin, ident], [tq])
                        S.act(lambda e: e.copy(out=qT[:].rearrange("p a b -> p (a b)"), in_=tq[0:64, :]), [tq], [qT])
                        if latent:
                            blocks = []
                            if tt > 0:
                                blocks.append(("loc", tt - 1, mprev))
                            blocks.append(("loc", tt, None))
                            if tt < ntl - 1:
                                blocks.append(("loc", tt + 1, mnext))
                            for b in range(4):
                                blocks.append(("ctx", b, None))
                        else:
                            blocks = [("loc", b, None) for b in range(ntl)]
                        for g in range(4):
                            for bi, (kind, b, msk) in enumerate(blocks):
                                ps = sps[bi % 2]
                                kT_ap = (KT[:, g, b * 128:(b + 1) * 128] if kind == "loc" else CKT[:, g, b * 128:(b + 1) * 128])
                                kres = KT if kind == "loc" else CKT
                                S.pe(lambda e, ps=ps, kT_ap=kT_ap, g=g, msk=msk: e.matmul(
                                    ps[:], lhsT=kT_ap, rhs=qT[:, 4 * g:4 * g + 4, :], start=True, stop=(msk is None)), [kres, qT], [ps])
                                if msk is not None:
                                    S.pe(lambda e, ps=ps, msk=msk: e.matmul(ps[:], lhsT=ident[:], rhs=msk[:].rearrange("p a b -> p (a b)"),
                                                                           start=False, stop=True), [ident, msk], [ps])
                                S.act(lambda e, ps=ps, bi=bi: e.activation(out=PT[bi][:], in_=ps[:], func=AF.Exp), [ps], [PT[bi]])
                            og = ops_[0]
                            for hh in range(4):
                                for bi, (kind, b, msk) in enumerate(blocks):
                                    v_ap = (VL[:, b, g, :] if kind == "loc" else CV[:, b, g, :])
                                    vres = VL if kind == "loc" else CV
                                    S.pe(lambda e, og=og, hh=hh, bi=bi, v_ap=v_ap, nb=len(blocks): e.matmul(
                                        og[:, hh, :], lhsT=PT[bi][:, hh * 128:(hh + 1) * 128], rhs=v_ap,
                                        start=(bi == 0), stop=(bi == nb - 1)), [PT[bi], vres], [og])
                            S.dve(lambda e, og=og, g=g: e.tensor_tensor(out=rden[:, 4 * g:4 * g + 4], in0=og[:, :, 64],
                                                                   in1=esink[:, 4 * g:4 * g + 4], op=ALU.add), [og, esink], [rden])
                            S.dve(lambda e, g=g: e.reciprocal(out=rden[:, 4 * g:4 * g + 4], in_=rden[:, 4 * g:4 * g + 4]), [rden], [rden])
                            S.dve(lambda e, og=og, g=g: e.tensor_tensor(
                                out=on[:, g * 256:(g + 1) * 256].rearrange("p (h d) -> p h d", h=4), in0=og[:, :, 0:64],
                                in1=rden[:, 4 * g:4 * g + 4].unsqueeze(2).to_broadcast([128, 4, 64]), op=ALU.mult), [og, rden], [on])
                        for c in range(8):
                            S.pe(lambda e, c=c: e.transpose(out=tp[:, c * 128:(c + 1) * 128], in_=on[:, c * 128:(c + 1) * 128],
                                                            identity=ident[:]), [on, ident], [tp])
                        S.act(lambda e: e.copy(out=onT[:].rearrange("p a b -> p (a b)"), in_=tp[:]), [tp], [onT])
                        for nh in range(2):
                            for kc in range(8):
                                S.pe(lambda e, nh=nh, kc=kc: e.matmul(Y[:, nh * 512:(nh + 1) * 512], lhsT=onT[:, kc, :],
                                                                      rhs=wout[:, kc, nh * 512:(nh + 1) * 512],
                                                                      start=(kc == 0), stop=(kc == 7)), [onT, wout], [Y])
                        epilogue(Y, [Y], xt, j, t, tmp, r, st, mv, rstd)
                S.barrier()
            S.es = root

        def peer_layer(i):
            GS = OPTS["GS"]
            UVDT = BF16 if OPTS["uvbf16"] else F32
            NG = 128 // GS
            sc0 = ExitStack()
            S.es = sc0
            with sc0:
                wq = S.sbuf("pwq", [128, 8, 2048], BF16)
                load_w(wq, pwq_d[i], 2048, "pwq")
                keysT = S.sbuf("pkeysT", [128, 16, 128], BF16)
                tp = S.psum("ptp", [128, 1024], BF16)
                sck = ExitStack()
                S.es = sck
                with sck:
                    k32 = S.sbuf("pk32", [128, 16, 128], F32)
                    kbf = S.sbuf("pkbf", [128, 16, 128], BF16)
                    S.dma(k32[:], pkeys_d[i].rearrange("c n d -> n c d"), [RIN], [k32], k32)
                    S.dve(lambda e: e.tensor_copy(out=kbf[:], in_=k32[:]), [k32], [kbf])
                    for half in range(2):
                        for c in range(8):
                            cc = half * 8 + c
                            S.pe(lambda e, c=c, cc=cc: e.transpose(out=tp[:, c * 128:(c + 1) * 128], in_=kbf[:, cc, :], identity=ident[:]),
                                 [kbf, ident], [tp])
                        S.act(lambda e, half=half: e.copy(out=keysT[:, half * 8:(half + 1) * 8, :].rearrange("p a b -> p (a b)"), in_=tp[:]),
                              [tp], [keysT])
                    S.barrier()
                S.es = sc0

                xts = [S.sbuf("px%d" % k, [128, D], F32) for k in range(2)]
                h32s = [S.sbuf("ph32%d" % k, [128, D], F32) for k in range(2)]
                eis = [S.sbuf("pei%d" % k, [128, 128], I32) for k in range(2)]
                wsms = [S.sbuf("pwsm%d" % k, [128, 8, 16], F32) for k in range(2)]
                hb = S.sbuf("phb", [128, D], BF16)
                hT = S.sbuf("phT", [128, 8, 128], BF16)
                qb = S.sbuf("pqb", [128, 2048], BF16)
                qT = S.sbuf("pqT", [128, 16, 128], BF16)
                s32 = S.sbuf("ps32", [128, 16, 128], F32)
                wk = S.sbuf("pwk", [128, 512], F32)
                avR = [[S.dram("avR%d_%d" % (k, q)) for q in range(8)] for k in range(4)]
                junk = S.sbuf("pjunk", [128, D], BF16)
                gbR = [S.dram("gbR%d" % k) for k in range(4)]
                coefR = [S.dram("coefR%d" % k) for k in range(4)]
                svR = [S.dram("svR%d" % k) for k in range(16)]
                siuR = [S.dram("siuR%d" % k) for k in range(16)]
                wkR = [S.dram("wkR%d" % k) for k in range(4)]
                combR = [S.dram("combR%d" % k) for k in range(8)]
                csR = [S.dram("csR%d" % k) for k in range(8)]
                ciuR = [S.dram("ciuR%d" % k) for k in range(8)]
                sv = S.sbuf("psv", [128, 16, 16], F32)
                siu = S.sbuf("psiu", [128, 16, 16], U32)
                sif = S.sbuf("psif", [128, 16, 16], F32)
                comb = S.sbuf("pcomb", [128, 8, 256], F32)
                cs = S.sbuf("pcs", [128, 8, 16], F32)
                ciu = S.sbuf("pciu", [128, 8, 16], U32)
                cia = S.sbuf("pcia", [128, 8, 16], U32)
                cib = S.sbuf("pcib", [128, 8, 16], U32)
                caf = S.sbuf("pcaf", [128, 8, 16], F32)
                cbf = S.sbuf("pcbf", [128, 8, 16], F32)
                oh = [comb] * 2
                i1f = S.sbuf("pi1f", [128, 8, 16], F32)
                i2f = S.sbuf("pi2f", [128, 8, 16], F32)
                wsum = S.sbuf("pwsum", [128, 8], F32)
                av = S.sbuf("pav", [128, 128], F32)
                ga = S.sbuf("pga", [128, 128], F32)
                gb_ = S.sbuf("pgb", [128, 128], F32)
                coef = S.sbuf("pcoef", [128, 128], F32)
                uvg = [[S.sbuf("puv%d_%d" % (k, s_), [128, 2 * D], UVDT) for s_ in range(GS)] for k in range(OPTS["NSETS"])]
                tv = [S.sbuf("ptv%d" % k, [128, D], BF16) for k in range(OPTS["NTV"])]
                tmp = S.sbuf("ptmp", [128, D], F32)
                r = S.sbuf("pr", [128, D], F32)
                st = S.sbuf("pst", [128, 2, 6], F32)
                mv = S.sbuf("pmv", [128, 2], F32)
                rstd = S.sbuf("prstd", [128, 1], F32)
                tq = S.psum("ptq", [128, 1024], BF16)
                qps = [S.psum("pqps%d" % k, [128, 512], F32) for k in range(2)]
                Ys = [S.psum("pY%d" % k, [128, 1024], F32) for k in range(2)]

                def stage_a(t, j, par):
                    xt = xts[par]
                    h32 = h32s[par]
                    ei = eis[par]
                    wsm = wsms[par]
                    S.dma(xt[:], rows(y_d, t), [R_y[t]], [xt], xt)
                    prologue(xt, j, hb, hT, tp, h32=h32, nopool=True)
                    yield
                    for nb in range(4):
                        ps = qps[nb % 2]
                        for kc in range(8):
                            S.pe(lambda e, ps=ps, kc=kc, nb=nb: e.matmul(ps[:], lhsT=hT[:, kc, :], rhs=wq[:, kc, nb * 512:(nb + 1) * 512],
                                                                    start=(kc == 0), stop=(kc == 7)), [hT, wq], [ps])
                        S.act(lambda e, ps=ps, nb=nb: e.copy(out=qb[:, nb * 512:(nb + 1) * 512], in_=ps[:]), [ps], [qb])
                    for half in range(2):
                        for c in range(8):
                            cc = half * 8 + c
                            S.pe(lambda e, c=c, cc=cc: e.transpose(out=tq[:, c * 128:(c + 1) * 128], in_=qb[:, cc * 128:(cc + 1) * 128],
                                                                   identity=ident[:]), [qb, ident], [tq])
                        S.act(lambda e, half=half: e.copy(out=qT[:, half * 8:(half + 1) * 8, :].rearrange("p a b -> p (a b)"), in_=tq[:]),
                              [tq], [qT])
                    for b4 in range(4):
                        ps = qps[b4 % 2]
                        for c4 in range(4):
                            c = b4 * 4 + c4
                            S.pe(lambda e, ps=ps, c=c, c4=c4: e.matmul(ps[:, c4 * 128:(c4 + 1) * 128], lhsT=qT[:, c, :], rhs=keysT[:, c, :],
                                                                  start=True, stop=True), [qT, keysT], [ps])
                        S.act(lambda e, ps=ps, b4=b4: e.copy(out=s32[:, b4 * 4:(b4 + 1) * 4, :].rearrange("p a b -> p (a b)"), in_=ps[:]),
                              [ps], [s32])
                    yield
                    for c0 in range(0, 16, 4):
                        grp = list(range(c0, c0 + 4))
                        for c in grp:
                            S.dve(lambda e, c=c: e.max(out=sv[:, c, 0:8], in_=s32[:, c, :]), [s32], [svR[c]])
                        for c in grp:
                            S.dve(lambda e, c=c: e.max_index(out=siu[:, c, 0:8], in_max=sv[:, c, 0:8], in_values=s32[:, c, :]),
                                  [s32, svR[c]], [siuR[c]])
                        yield
                        for k_, c in enumerate(grp):
                            S.dve(lambda e, c=c, k_=k_: e.match_replace(out=wk[:, k_ * 128:(k_ + 1) * 128], in_to_replace=sv[:, c, 0:8], in_values=s32[:, c, :],
                                                                      imm_value=-1e30), [s32, svR[c]], [wkR[k_]])
                        for k_, c in enumerate(grp):
                            S.dve(lambda e, c=c, k_=k_: e.max(out=sv[:, c, 8:16], in_=wk[:, k_ * 128:(k_ + 1) * 128]), [wkR[k_]], [svR[c]])
                        yield
                        for k_, c in enumerate(grp):
                            S.dve(lambda e, c=c, k_=k_: e.max_index(out=siu[:, c, 8:16], in_max=sv[:, c, 8:16], in_values=wk[:, k_ * 128:(k_ + 1) * 128]),
                                  [wkR[k_], svR[c]], [siuR[c]])
                        yield
                    S.dve(lambda e: e.tensor_copy(out=sif[:], in_=siu[:]), siuR, [sif])
                    sv4 = sv[:].rearrange("p (h two) m -> p h two m", two=2)
                    S.dve(lambda e: e.tensor_tensor(
                        out=comb[:].rearrange("p h (a b) -> p h a b", a=16),
                        in0=sv4[:, :, 0, :].unsqueeze(3).to_broadcast([128, 8, 16, 16]),
                        in1=sv4[:, :, 1, :].unsqueeze(2).to_broadcast([128, 8, 16, 16]), op=ALU.add), svR, combR)
                    yield
                    for p0 in range(0, 8, 2):
                        grp = list(range(p0, p0 + 2))
                        for p in grp:
                            S.dve(lambda e, p=p: e.max(out=cs[:, p, 0:8], in_=comb[:, p, :]), [combR[p]], [csR[p]])
                        for p in grp:
                            S.dve(lambda e, p=p: e.max_index(out=ciu[:, p, 0:8], in_max=cs[:, p, 0:8], in_values=comb[:, p, :]),
                                  [combR[p], csR[p]], [ciuR[p]])
                        yield
                        for k_, p in enumerate(grp):
                            S.dve(lambda e, p=p, k_=k_: e.match_replace(out=wk[:, k_ * 256:(k_ + 1) * 256], in_to_replace=cs[:, p, 0:8], in_values=comb[:, p, :],
                                                                      imm_value=-1e30), [combR[p], csR[p]], [wkR[2 * k_], wkR[2 * k_ + 1]])
                        for k_, p in enumerate(grp):
                            S.dve(lambda e, p=p, k_=k_: e.max(out=cs[:, p, 8:16], in_=wk[:, k_ * 256:(k_ + 1) * 256]), [wkR[2 * k_], wkR[2 * k_ + 1]], [csR[p]])
                        for k_, p in enumerate(grp):
                            S.dve(lambda e, p=p, k_=k_: e.max_index(out=ciu[:, p, 8:16], in_max=cs[:, p, 8:16], in_values=wk[:, k_ * 256:(k_ + 1) * 256]),
                                  [wkR[2 * k_], wkR[2 * k_ + 1], csR[p]], [ciuR[p]])
                        yield
                    S.dve(lambda e: e.tensor_tensor(out=wsm[:], in0=cs[:], in1=cs[:, :, 0:1].to_broadcast([128, 8, 16]), op=ALU.subtract),
                          csR, [wsm])
                    S.act(lambda e: e.activation(out=wsm[:], in_=wsm[:], func=AF.Exp), [wsm], [wsm])
                    S.dve(lambda e: e.reduce_sum(out=wsum[:], in_=wsm[:], axis=AX.X), [wsm], [wsum])
                    S.dve(lambda e: e.reciprocal(out=wsum[:], in_=wsum[:]), [wsum], [wsum])
                    S.dve(lambda e: e.tensor_tensor(out=wsm[:], in0=wsm[:], in1=wsum[:].unsqueeze(2).to_broadcast([128, 8, 16]), op=ALU.mult),
                          [wsm, wsum], [wsm])
                    yield
                    S.dve(lambda e: e.tensor_single_scalar(out=cia[:], in_=ciu[:], scalar=4, op=ALU.logical_shift_right), ciuR, [cia])
                    S.dve(lambda e: e.tensor_single_scalar(out=cib[:], in_=ciu[:], scalar=15, op=ALU.bitwise_and), ciuR, [cib])
                    yield
                    S.dve(lambda e: e.tensor_copy(out=caf[:], in_=cia[:]), [cia], [caf])
                    S.dve(lambda e: e.tensor_copy(out=cbf[:], in_=cib[:]), [cib], [cbf])
                    yield
                    sif4 = sif[:].rearrange("p (h two) m -> p h two m", two=2)
                    ohv = comb[:].rearrange("p h (a b) -> p h a b", a=16)
                    for (cf, which, dst, ohx) in ((caf, 0, i1f, oh[0]), (cbf, 1, i2f, oh[1])):
                        S.dve(lambda e, cf=cf, ohx=ohx: e.tensor_tensor(
                            out=ohv, in0=cf[:].unsqueeze(3).to_broadcast([128, 8, 16, 16]),
                            in1=iota16[:].unsqueeze(1).unsqueeze(1).to_broadcast([128, 8, 16, 16]), op=ALU.is_equal), [cf, iota16] + combR, [ohx] + combR)
                        S.dve(lambda e, which=which, ohx=ohx: e.tensor_tensor(
                            out=ohv, in0=ohv,
                            in1=sif4[:, :, which, :].unsqueeze(2).to_broadcast([128, 8, 16, 16]), op=ALU.mult), [ohx, sif] + combR, [ohx] + combR)
                        S.dve(lambda e, dst=dst, ohx=ohx: e.reduce_sum(out=dst[:].rearrange("p a b -> p (a b)"),
                                                              in_=ohv.rearrange("p a b c -> p (a b) c"), axis=AX.X), [ohx] + combR, [dst])
                        yield
                    S.dve(lambda e: e.scalar_tensor_tensor(out=i1f[:], in0=i1f[:], scalar=128.0, in1=i2f[:], op0=ALU.mult, op1=ALU.add),
                          [i1f, i2f], [i1f])
                    S.dve(lambda e: e.tensor_scalar(out=i1f[:], in0=i1f[:], scalar1=float(i * NEXP), scalar2=None, op0=ALU.add), [i1f], [i1f])
                    S.dve(lambda e: e.tensor_copy(out=ei[:], in_=i1f[:].rearrange("p a b -> p (a b)")), [i1f], [ei])
                    yield

                def stage_b(t, j, par, nxt, prev_ep):
                    xt = xts[par]
                    h32 = h32s[par]
                    ei = eis[par]
                    wsm = wsms[par]
                    wflat = wsm[:].rearrange("p a b -> p (a b)")
                    Y = Ys[par]
                    nvc = [0]
                    nyd = [0]
                    pending = []

                    def vside(g, bufs, sl0):
                        gsl = slice(sl0, sl0 + GS)
                        S.dve(lambda e, gsl=gsl: e.tensor_tensor(out=coef[:, gsl], in0=gb_[:, gsl], in1=wflat[:, gsl], op=ALU.mult),
                              [gbR[g % 4], wsm], [coefR[g % 4]])
                        for s_ in range(GS):
                            sl = sl0 + s_
                            b_ = bufs[s_]
                            tv_ = tv[nvc[0] % OPTS['NTV']]
                            nvc[0] += 1
                            S.act(lambda e, b_=b_, sl=sl, tv_=tv_: e.activation(out=tv_[:], in_=b_[:, D:2 * D], func=AF.Copy,
                                                                         scale=coef[:, sl:sl + 1]), [b_, coefR[g % 4]], [tv_])
                            for nh in range(2):
                                S.pe(lambda e, tv_=tv_, nh=nh, sl=sl: e.matmul(Y[:, nh * 512:(nh + 1) * 512], lhsT=ident[:],
                                                                         rhs=tv_[:, nh * 512:(nh + 1) * 512],
                                                                         start=(sl == 0), stop=(sl == 127)), [ident, tv_], [Y])

                    for g in range(NG):
                        bufs = uvg[g % OPTS["NSETS"]]
                        sl0 = g * GS
                        for s_ in range(GS):
                            sl = sl0 + s_
                            b_ = bufs[s_]
                            if OPTS["nogather"]:
                                continue
                            S.add("pool", lambda e, b_=b_, sl=sl: e.indirect_dma_start(
                                out=b_[:], out_offset=None, in_=(uvb_d if OPTS["uvtab"] else puv_d),
                                in_offset=bass.IndirectOffsetOnAxis(ap=ei[:, sl:sl + 1], axis=0)),
                                [ei, RIN] + (R_uvb[i] if OPTS["uvtab"] else []), [b_], dma=b_)
                        for s_ in range(GS):
                            sl = sl0 + s_
                            b_ = bufs[s_]
                            if OPTS["nodots"]:
                                continue
                            S.dve(lambda e, b_=b_, sl=sl: e.scalar_tensor_tensor(out=junk[:], in0=b_[:, 0:D], scalar=1.0, in1=h32[:], op0=ALU.mult,
                                                                           op1=ALU.mult, accum_out=av[:, sl:sl + 1]), [b_, h32], [avR[g % 4][s_]])
                        gsl = slice(sl0, sl0 + GS)
                        S.act(lambda e, gsl=gsl: e.activation(out=gb_[:, gsl], in_=av[:, gsl], func=AF.Gelu_apprx_tanh), avR[g % 4][:GS], [gbR[g % 4]])
                        pending.append((g, bufs, sl0))
                        if len(pending) > 1:
                            vside(*pending.pop(0))
                        if g == 1 and prev_ep is not None:
                            prev_ep()
                            prev_ep = None
                        if nxt is not None and not OPTS["noroute"] and g >= 2:
                            left = NYA[0] - nyd[0]
                            k_n = -(-left // (NG - g)) if left > 0 else 0
                            for _ in range(k_n):
                                nyd[0] += 1
                                next(nxt, None)
                    while pending:
                        vside(*pending.pop(0))
                    if prev_ep is not None:
                        prev_ep()
                    if nxt is not None:
                        for _ in nxt:
                            pass
                    return lambda: epilogue(Y, [Y], xt, j, t, tmp, r, st, mv, rstd, nopool=True)

                tiles = []
                for (t0, ntl, latent, j) in seqs:
                    for tt in range(ntl):
                        tiles.append((t0 + tt, j))
                NYA = [0]
                for _ in stage_a(tiles[0][0], tiles[0][1], 0):
                    NYA[0] += 1
                ep = None
                for n_, (t, j) in enumerate(tiles):
                    nxt = None
                    if n_ + 1 < len(tiles):
                        nxt = stage_a(tiles[n_ + 1][0], tiles[n_ + 1][1], (n_ + 1) % 2)
                    ep = stage_b(t, j, n_ % 2, nxt, ep)
                ep()
                S.barrier()
            S.es = root

        for i in range(DEPTH):
            first = (i == 0)
            modulation(i, 0)
            if i % 2 == 0:
                retention_layer(i, first)
            else:
                attention_layer(i, first)
            modulation(i, 1)
            if peer:
                peer_layer(i)
        S.finalize()
        stats = S.stats
    return nc, stats


def _consts(SQ):
    j = np.arange(128, dtype=np.float32)[:, None]
    i = np.arange(128, dtype=np.float32)[None, :]
    cst = np.zeros((128, 6, 128), np.float32)
    cst[:, 0] = np.maximum(i - j, 0.0)
    cst[:, 1] = (i >= j)
    cst[:, 2] = np.maximum(j - i, 0.0)
    cst[:, 3] = (j > i)
    cst[:, 4] = np.where(j >= i, 0.0, -30000.0)
    cst[:, 5] = np.where(j <= i, 0.0, -30000.0)
    p = np.arange(128, dtype=np.float32)
    pvec = np.zeros((128, 8), np.float32)
    pvec[:, 0] = p + 1
    pvec[:, 1] = 128 - p
    pvec[:, 2] = 127 - p
    pvec[:, 3] = p
    iota16 = np.tile(np.arange(16, dtype=np.float32)[None, :], (128, 1))
    pos = np.arange(SQ, dtype=np.float32)
    inv = (10000.0 ** (-np.arange(0, 256, 2, dtype=np.float32) / 256.0)).astype(np.float32)
    ang = (pos[:, None] * inv[None, :]).astype(np.float32)
    rrc, rrs = np.cos(ang).astype(np.float32), np.sin(ang).astype(np.float32)
    t = np.arange(SQ)
    inv2 = (10000.0 ** (-np.arange(0, 32, 2, dtype=np.float32) / 32.0)).astype(np.float32)
    ar = ((t // 64).astype(np.float32)[:, None] * inv2[None, :]).astype(np.float32)
    ac = ((t % 64).astype(np.float32)[:, None] * inv2[None, :]).astype(np.float32)
    rac = np.concatenate([np.cos(ar), np.cos(ac)], 1).astype(np.float32)
    ras = np.concatenate([np.sin(ar), np.sin(ac)], 1).astype(np.float32)
    return dict(cst=cst, pvec=pvec, iota16=iota16, rope_ret_cos=rrc, rope_ret_sin=rrs, rope_att_cos=rac, rope_att_sin=ras)


_PROG = {}
RUNKW = {}
LAST = {}


def run_step(inp, DEPTH, SQ, ncores, peer=True):
    key = (DEPTH, SQ, peer)
    if key not in _PROG:
        _PROG[key] = build_program(DEPTH=DEPTH, SQ=SQ, peer=peer)
    nc, stats = _PROG[key]
    NRET = (DEPTH + 1) // 2
    NATT = max(DEPTH // 2, 1)
    f = lambda a: np.ascontiguousarray(np.asarray(a, dtype=np.float32))
    cs = _consts(SQ)
    shared = dict(
        mod_w=f(inp["mod_w"]), mod_b=f(inp["mod_b"]), ln_g=f(inp["ln_g"]), ln_b=f(inp["ln_b"]),
        ret_w_in=f(inp["ret_w_in"]), ret_w_out=f(inp["ret_w_out"]), ret_decay=f(inp["ret_decay"]).reshape(NRET, 8),
        attn_w_in=f(inp["attn_w_in"])[:NATT], attn_w_out=f(inp["attn_w_out"])[:NATT], attn_sink=f(inp["attn_sink"])[:NATT],
        peer_wq=f(inp["peer_wq"]), peer_keys=f(inp["peer_keys"]).reshape(DEPTH, 16, 128, 128),
        peer_uv=np.ascontiguousarray(np.concatenate([f(inp["peer_u"]).reshape(DEPTH * 16384, 1024),
                                                     f(inp["peer_v"]).reshape(DEPTH * 16384, 1024)], axis=1)), **cs)
    xp, xs = f(inp["x_prompt"]), f(inp["x_sample"])
    in_maps = []
    for c in range(ncores):
        m = dict(shared)
        m["x"] = np.ascontiguousarray(np.concatenate([xp[2 * c], xp[2 * c + 1], xs[c]], 0))
        m["cond"] = np.ascontiguousarray(np.stack([f(inp["c_ctx"]), f(inp["c"])[c]], 0))
        m["sret_f"] = f(inp["state_ret_fwd"])[c]
        m["sret_b"] = f(inp["state_ret_bwd"])[c]
        m["ck"] = np.ascontiguousarray(f(inp["cache_k"])[c][:NATT].reshape(NATT, 512, 256))
        m["cv"] = np.ascontiguousarray(f(inp["cache_v"])[c][:NATT].reshape(NATT, 512, 256))
        in_maps.append(m)
    res = run_bass_kernel_spmd(nc, in_maps, core_ids=list(range(ncores)), **RUNKW)
    LAST['res'] = res
    rs = res.results
    B = 2 * ncores
    y = np.stack([r["y"] for r in rs], 0)
    yp = y[:, :512].reshape(B, 256, 1024)
    ys = y[:, 512:]
    nsf = np.concatenate([r["nsf"] for r in rs], 0)
    nsb = np.concatenate([r["nsb"] for r in rs], 0)
    nk = np.concatenate([r["nk"] for r in rs], 0).reshape(B, NATT, 256, 4, 64)
    nv = np.concatenate([r["nv"] for r in rs], 0).reshape(B, NATT, 256, 4, 64)
    return tuple(np.ascontiguousarray(a.astype(np.float32)) for a in (yp, ys, nsf, nsb, nk, nv))


def kernel(**inputs):
    return run_step(inputs, DEPTH=4, SQ=4096, ncores=8)
```

```python
import numpy as np
from contextlib import ExitStack
import concourse.bass as bass
import concourse.mybir as mybir
from concourse.bass_utils import run_bass_kernel_spmd

F32 = mybir.dt.float32
BF16 = mybir.dt.bfloat16
I32 = mybir.dt.int32
U32 = mybir.dt.uint32
ALU = mybir.AluOpType
AF = mybir.ActivationFunctionType
AX = mybir.AxisListType


class Res:
    __slots__ = ("name", "lw", "rd", "sem", "t")

    def __init__(self, name, t=None):
        self.name = name
        self.lw = None
        self.rd = {}
        self.sem = None
        self.t = t

    def __getitem__(self, k):
        return self.t[k]


class Op:
    __slots__ = ("eng", "fn", "reads", "writes", "dma", "deps", "hasdep", "ev", "waits", "idx")


class Sched:
    ENGS = ("pe", "dve", "act", "pool", "sp")

    def __init__(self, nc, es, max_dma_sems=94):
        self.nc = nc
        self.es = es
        self.es_root = es
        self.ops = []
        self.nres = 0
        self.dma_sems = []
        self.max_dma_sems = max_dma_sems
        self.rr = 0
        self.sem_load = []

    def sbuf(self, name, shape, dtype):
        self.nres += 1
        name = "sb%d_%s" % (self.nres, name)
        t = self.es.enter_context(self.nc.sbuf_tensor(name, list(shape), dtype))
        return Res(name, t)

    def psum(self, name, shape, dtype):
        self.nres += 1
        name = "ps%d_%s" % (self.nres, name)
        t = self.es.enter_context(self.nc.psum_tensor(name, list(shape), dtype))
        return Res(name, t)

    def dram(self, name):
        return Res(name, None)

    def _dsem(self, res):
        if res.sem is None:
            if len(self.dma_sems) < self.max_dma_sems:
                s = self.es_root.enter_context(self.nc.semaphore("dq%d" % len(self.dma_sems)))
                self.dma_sems.append(s)
                res.sem = len(self.dma_sems) - 1
                self.sem_load.append(0)
            else:
                res.sem = min(range(len(self.dma_sems)), key=lambda k: self.sem_load[k])
                self.sem_load[res.sem] += 64
        return res.sem

    def add(self, eng, fn, reads=(), writes=(), dma=None):
        op = Op()
        op.eng = eng
        op.fn = fn
        op.reads = tuple(reads)
        op.writes = tuple(writes)
        op.dma = None if dma is None else self._dsem(dma)
        if op.dma is not None:
            self.sem_load[op.dma] += 1
        op.idx = len(self.ops)
        self.ops.append(op)
        return op

    def barrier(self):
        op = Op()
        op.eng = None
        op.fn = None
        op.reads = ()
        op.writes = ()
        op.dma = None
        op.idx = len(self.ops)
        self.ops.append(op)

    def pe(self, fn, reads=(), writes=()):
        return self.add("pe", fn, reads, writes)

    def dve(self, fn, reads=(), writes=()):
        return self.add("dve", fn, reads, writes)

    def act(self, fn, reads=(), writes=()):
        return self.add("act", fn, reads, writes)

    def pool(self, fn, reads=(), writes=()):
        return self.add("pool", fn, reads, writes)

    def dma(self, out, in_, reads, writes, semres, eng="sp", **kw):
        return self.add(eng, lambda e: e.dma_start(out=out, in_=in_, **kw), reads, writes, dma=semres)

    def finalize(self):
        nc = self.nc
        ops = self.ops
        latest = {}
        bar = {}
        bar_pending = set()
        for op in ops:
            if op.eng is None:
                bar = dict(latest)
                bar_pending = set(self.ENGS)
                op.deps = []
                op.hasdep = False
                continue
            deps = {}
            if op.eng in bar_pending:
                bar_pending.discard(op.eng)
                for x in bar.values():
                    deps[x.idx] = x
            for r in op.reads:
                if r.lw is not None:
                    deps[r.lw.idx] = r.lw
            for w in op.writes:
                if w.lw is not None:
                    deps[w.lw.idx] = w.lw
                for x in w.rd.values():
                    deps[x.idx] = x
            deps.pop(op.idx, None)
            dl = []
            for d in deps.values():
                if op.eng == "pe" and d.eng == "pe" and op.dma is None and d.dma is None:
                    continue
                dl.append(d)
            op.deps = dl
            op.hasdep = False
            key = op.eng if op.dma is None else ("d", op.dma)
            latest[key] = op
            for r in op.reads:
                r.rd[key] = op
            for w in op.writes:
                w.lw = op
                w.rd = {}
        for op in ops:
            for d in op.deps:
                d.hasdep = True
        EPOCH = 30000
        engsem = {}

        def get_engsem(e, ep):
            if (e, ep) not in engsem:
                engsem[(e, ep)] = self.es_root.enter_context(nc.semaphore("eng_%s_%d" % (e, ep)))
            return engsem[(e, ep)]
        engcnt = {e: 0 for e in self.ENGS}
        dcnt = [0] * len(self.dma_sems)
        waited = {e: {} for e in self.ENGS}
        nwaits = 0
        ops = [o for o in ops if o.eng is not None]
        for op in ops:
            need = {}
            for d in op.deps:
                if d.dma is not None:
                    k = ("d", d.dma)
                    v = dcnt[d.dma]
                else:
                    k = ("e", d.eng, d.ev[0])
                    v = d.ev[1]
                if need.get(k, 0) < v:
                    need[k] = v
            w = []
            wd = waited[op.eng]
            for k, v in need.items():
                if wd.get(k, 0) >= v:
                    continue
                wd[k] = v
                w.append((k, v))
            op.waits = w
            nwaits += len(w)
            if op.dma is not None:
                dcnt[op.dma] += 16
                op.ev = dcnt[op.dma]
            elif op.hasdep:
                engcnt[op.eng] += 1
                ep, c = divmod(engcnt[op.eng] - 1, EPOCH)
                op.ev = (ep, c + 1)
                get_engsem(op.eng, ep)
            else:
                op.ev = None
        self.final_dcnt = dcnt
        self.stats = dict(nops=len(ops), nwaits=nwaits, engcnt=dict(engcnt),
                          per_eng={e: sum(1 for o in ops if o.eng == e) for e in self.ENGS})
        per = {e: [o for o in ops if o.eng == e] for e in self.ENGS}
        dma_sems = self.dma_sems

        def semof(k):
            return dma_sems[k[1]] if k[0] == "d" else engsem[(k[1], k[2])]

        def run(eng_obj, name):
            for op in per[name]:
                for k, v in op.waits:
                    eng_obj.wait_ge(semof(k), v)
                ins = op.fn(eng_obj)
                if op.dma is not None:
                    ins.then_inc(dma_sems[op.dma], 16)
                elif op.ev is not None:
                    ins.then_inc(engsem[(name, op.ev[0])], 1)
            if name == "sp":
                for i, s in enumerate(dma_sems):
                    if dcnt[i] > 0:
                        eng_obj.wait_ge(s, dcnt[i])

        with nc.Block() as block:
            @block.tensor
            def _(e):
                run(e, "pe")

            @block.vector
            def _(e):
                run(e, "dve")

            @block.scalar
            def _(e):
                run(e, "act")

            @block.gpsimd
            def _(e):
                run(e, "pool")

            @block.sync
            def _(e):
                run(e, "sp")

D = 1024
ALPHA = 8.0 ** 0.25
EPS = 1e-5
NEXP = 16384
OPTS = dict(GS=4, uvbf16=True, actgelu=True, nogather=0, nodots=0, novside=0, noroute=0, uvtab=1, NSETS=4, NTV=2)


def build_program(DEPTH=4, SQ=4096, peer=True):
    nc = bass.Bass("TRN2", target_bir_lowering=False)
    NRET = (DEPTH + 1) // 2
    NATT = DEPTH // 2
    NATTd = max(NATT, 1)
    NP = 4
    NS = SQ // 128
    NT = NP + NS
    seqs = [(0, 2, False, 0), (2, 2, False, 0), (4, NS, True, 1)]

    def din(name, shape, dt=F32):
        return nc.dram_tensor(name, list(shape), dt, kind="ExternalInput").ap()

    def dout(name, shape, dt=F32):
        return nc.dram_tensor(name, list(shape), dt, kind="ExternalOutput").ap()

    x_d = din("x", [NT * 128, D])
    cond_d = din("cond", [2, D])
    sretf_d = din("sret_f", [NRET, 4, 256, 512])
    sretb_d = din("sret_b", [NRET, 4, 256, 512])
    ck_d = din("ck", [NATTd, 512, 256])
    cv_d = din("cv", [NATTd, 512, 256])
    modw_d = din("mod_w", [DEPTH, D, 6 * D])
    modb_d = din("mod_b", [DEPTH, 6 * D])
    lng_d = din("ln_g", [DEPTH, 2, D])
    lnb_d = din("ln_b", [DEPTH, 2, D])
    rwin_d = din("ret_w_in", [NRET, D, 6144])
    rwout_d = din("ret_w_out", [NRET, 2048, D])
    rdec_d = din("ret_decay", [NRET, 8])
    awin_d = din("attn_w_in", [NATTd, D, 1536])
    awout_d = din("attn_w_out", [NATTd, D, D])
    asink_d = din("attn_sink", [NATTd, 16])
    pwq_d = din("peer_wq", [DEPTH, D, 2048])
    pkeys_d = din("peer_keys", [DEPTH, 16, 128, 128])
    puv_d = din("peer_uv", [DEPTH * NEXP, 2 * D])
    cst_d = din("cst", [128, 6, 128])
    pvec_d = din("pvec", [128, 8])
    iota_d = din("iota16", [128, 16])
    rrc_d = din("rope_ret_cos", [SQ, 128])
    rrs_d = din("rope_ret_sin", [SQ, 128])
    rac_d = din("rope_att_cos", [SQ, 32])
    ras_d = din("rope_att_sin", [SQ, 32])

    y_d = dout("y", [NT * 128, D])
    nsf_d = dout("nsf", [2, NRET, 4, 256, 512])
    nsb_d = dout("nsb", [2, NRET, 4, 256, 512])
    nk_d = dout("nk", [2, NATTd, 256, 256])
    nv_d = dout("nv", [2, NATTd, 256, 256])

    z_d = nc.dram_tensor("z_scr", [NT * 128, 6144], BF16).ap()
    pb_d = nc.dram_tensor("pb_scr", [NT * 128, 2048], F32).ap()
    qs_d = nc.dram_tensor("qs_scr", [NT * 128, 1024], BF16).ap()
    uvb_d = nc.dram_tensor("uvb_scr", [DEPTH * NEXP, 2 * D], BF16).ap() if OPTS["uvtab"] else None
    CVR = 512

    root = ExitStack()
    with root:
        S = Sched(nc, root)
        RIN = S.dram("inputs")
        R_y = [S.dram("y%d" % t) for t in range(NT)]
        R_z = [S.dram("z%d" % t) for t in range(NT)]
        R_pb = [S.dram("pb%d" % t) for t in range(NT)]
        R_qs = [S.dram("qs%d" % t) for t in range(NT)]
        R_o = S.dram("small_outs")
        R_uvb = [[S.dram("uvb%d_%d" % (i_, c_)) for c_ in range(NEXP // CVR)] for i_ in range(DEPTH)]
        cvt = S.dram("cvt")

        def convert_uv(i_):
            if not OPTS["uvtab"]:
                return
            for c_ in range(NEXP // CVR):
                r0 = i_ * NEXP + c_ * CVR
                S.dma(uvb_d[r0:r0 + CVR, :], puv_d[r0:r0 + CVR, :], [RIN], [R_uvb[i_][c_]], cvt, eng="pool")

        def rows(ap, t):
            return ap[t * 128:(t + 1) * 128, :]

        ident_f = S.sbuf("ident_f", [128, 128], F32)
        ident = S.sbuf("ident", [128, 128], BF16)
        ones1 = S.sbuf("ones1", [1, 128], F32)
        condT = S.sbuf("condT", [128, 2, 8], F32)
        modbc = S.sbuf("modbc", [128, 2, 3, D], F32)
        lnbc = S.sbuf("lnbc", [128, 2, D], F32)
        cst = S.sbuf("cst", [128, 6, 128], F32)
        pvec = S.sbuf("pvec", [128, 8], F32)
        iota16 = S.sbuf("iota16", [128, 16], F32)
        epsc = S.sbuf("epsc", [128, 1], F32)

        S.pool(lambda e: e.memset(ident_f[:], 0.0), [], [ident_f])
        S.pool(lambda e: e.affine_select(out=ident_f[:], in_=ident_f[:], pattern=[[-1, 128]],
                                         compare_op=ALU.not_equal, fill=1.0, base=0, channel_multiplier=1),
               [ident_f], [ident_f])
        S.dve(lambda e: e.tensor_copy(out=ident[:], in_=ident_f[:]), [ident_f], [ident])
        S.dve(lambda e: e.memset(ones1[:], 1.0), [], [ones1])
        S.dve(lambda e: e.memset(epsc[:], EPS), [], [epsc])
        S.dma(cst[:], cst_d, [RIN], [cst], cst)
        S.dma(pvec[:], pvec_d, [RIN], [pvec], pvec)
        S.dma(iota16[:], iota_d, [RIN], [iota16], iota16)
        S.dma(condT[:], cond_d.rearrange("j (kc p) -> p j kc", p=128), [RIN], [condT], condT,
              allow_slow_non_contiguous=True)
        S.act(lambda e: e.activation(out=condT[:], in_=condT[:], func=AF.Silu), [condT], [condT])

        def modulation(i, s):
            sc = ExitStack()
            S.es = sc
            with sc:
                wch = [S.sbuf("modw%d" % k, [128, 8, D], F32) for k in range(2)]
                condrep = S.sbuf("condrep", [128, 2, 8, 128], F32)
                S.dve(lambda e: e.tensor_copy(out=condrep[:].rearrange("p j k m -> p (j k) m"),
                                              in_=condT[:].rearrange("p j k -> p (j k)").unsqueeze(2).to_broadcast([128, 16, 128])),
                      [condT], [condrep])
                brow = S.sbuf("modbrow", [1, 3 * D], F32)
                mps = [S.psum("modps%d" % k, [128, 512], F32) for k in range(2)]
                S.dma(brow[:], modb_d[i:i + 1, s * 3 * D:(s + 1) * 3 * D], [RIN], [brow], brow)
                S.dma(lnbc[:, 0, :], lng_d[i, s:s + 1, :].to_broadcast([128, D]), [RIN], [lnbc], lnbc)
                S.dma(lnbc[:, 1, :], lnb_d[i, s:s + 1, :].to_broadcast([128, D]), [RIN], [lnbc], lnbc)
                n = 0
                for blk in range(3):
                    w = wch[blk % 2]
                    c0 = (s * 3 + blk) * D
                    S.dma(w[:], modw_d[i, :, c0:c0 + D].rearrange("(kc p) n -> p kc n", p=128), [RIN], [w], w)
                    for j in range(2):
                        for nh in range(2):
                            ps = mps[n % 2]
                            n += 1
                            for kc in range(8):
                                S.pe(lambda e, ps=ps, j=j, kc=kc, w=w, nh=nh: e.matmul(
                                    ps[:], lhsT=condrep[:, j, kc, :], rhs=w[:, kc, nh * 512:(nh + 1) * 512],
                                    start=(kc == 0), stop=False), [condrep, w], [ps])
                            S.pe(lambda e, ps=ps, blk=blk, nh=nh: e.matmul(
                                ps[:], lhsT=ones1[:], rhs=brow[:, blk * D + nh * 512: blk * D + (nh + 1) * 512],
                                start=False, stop=True), [ones1, brow], [ps])
                            add = 1.0 if blk == 1 else 0.0
                            S.act(lambda e, ps=ps, j=j, blk=blk, nh=nh, add=add: e.activation(
                                out=modbc[:, j, blk, nh * 512:(nh + 1) * 512], in_=ps[:], func=AF.Identity, bias=add, scale=1.0)
                                if add else e.copy(out=modbc[:, j, blk, nh * 512:(nh + 1) * 512], in_=ps[:]),
                                [ps], [modbc])
                S.barrier()
            S.es = root

        def load_w(dst, src2d, N, tag):
            v = src2d.rearrange("(kc p) n -> p kc n", p=128)
            for n0 in range(0, N, 2048):
                n1 = min(N, n0 + 2048)
                S.dma(dst[:, :, n0:n1], v[:, :, n0:n1], [RIN], [dst], dst, eng="pool")

        def prologue(xt, j, hb, hT, tp, h32=None, nopool=True):
            pl = S.dve if nopool else S.pool
            tgt = h32 if h32 is not None else hb
            S.dve(lambda e: e.tensor_tensor(out=tgt[:], in0=xt[:], in1=modbc[:, j, 1, :], op=ALU.mult), [xt, modbc], [tgt])
            if h32 is not None:
                pl(lambda e: e.tensor_tensor(out=h32[:], in0=h32[:], in1=modbc[:, j, 0, :], op=ALU.add), [h32, modbc], [h32])
                S.act(lambda e: e.copy(out=hb[:], in_=h32[:]), [h32], [hb])
            else:
                pl(lambda e: e.tensor_tensor(out=hb[:], in0=hb[:], in1=modbc[:, j, 0, :], op=ALU.add), [hb, modbc], [hb])
            for kc in range(8):
                S.pe(lambda e, kc=kc: e.transpose(out=tp[:, kc * 128:(kc + 1) * 128], in_=hb[:, kc * 128:(kc + 1) * 128],
                                                  identity=ident[:]), [hb, ident], [tp])
            S.act(lambda e: e.copy(out=hT[:].rearrange("p a b -> p (a b)"), in_=tp[:]), [tp], [hT])

        def epilogue(Y, yreads, xt, j, t, tmp, r, st, mv, rstd, nopool=True, store_eng="sp"):
            pl = S.dve if nopool else S.pool
            S.dve(lambda e: e.tensor_tensor(out=tmp[:], in0=Y[:], in1=modbc[:, j, 2, :], op=ALU.mult), [modbc] + yreads, [tmp])
            S.dve(lambda e: e.scalar_tensor_tensor(out=r[:], in0=xt[:], scalar=ALPHA, in1=tmp[:], op0=ALU.mult, op1=ALU.add),
                  [xt, tmp], [r])
            for c in range(2):
                S.dve(lambda e, c=c: e.bn_stats(out=st[:, c, :], in_=r[:, c * 512:(c + 1) * 512]), [r], [st])
            S.dve(lambda e: e.bn_aggr(out=mv[:], in_=st[:].rearrange("p a b -> p (a b)")), [st], [mv])
            S.act(lambda e: e.activation(out=rstd[:], in_=mv[:, 1:2], func=AF.Sqrt, bias=epsc[:], scale=1.0), [mv, epsc], [rstd])
            S.dve(lambda e: e.reciprocal(out=rstd[:], in_=rstd[:]), [rstd], [rstd])
            S.dve(lambda e: e.tensor_scalar(out=r[:], in0=r[:], scalar1=mv[:, 0:1], scalar2=rstd[:, 0:1],
                                            op0=ALU.subtract, op1=ALU.mult), [r, mv, rstd], [r])
            pl(lambda e: e.tensor_tensor(out=r[:], in0=r[:], in1=lnbc[:, 0, :], op=ALU.mult), [r, lnbc], [r])
            pl(lambda e: e.tensor_tensor(out=tmp[:], in0=r[:], in1=lnbc[:, 1, :], op=ALU.add), [r, lnbc], [tmp])
            S.dma(rows(y_d, t), tmp[:], [tmp], [R_y[t]], tmp, eng=store_eng)

        def xsrc(first):
            return x_d if first else y_d

        def xres(first, t):
            return RIN if first else R_y[t]

        def retention_layer(i, first):
            jr = i // 2
            sc0 = ExitStack()
            S.es = sc0
            with sc0:
                lg = S.sbuf("lg", [128, 8], F32)
                qdec = S.sbuf("qdec", [128, 2, 4], F32)
                kdec = S.sbuf("kdec", [128, 2, 4], F32)
                cdec = S.sbuf("cdec", [128, 2, 4], F32)
                dmask = S.sbuf("dmask", [128, 4, 128], F32)
                dtmp = S.sbuf("dtmp", [128, 128], F32)
                c128 = S.sbuf("c128", [128, 1], F32)
                S.dve(lambda e: e.memset(c128[:], 128.0), [], [c128])
                S.dma(lg[:], rdec_d[jr:jr + 1, :].to_broadcast([128, 8]), [RIN], [lg], lg)
                S.act(lambda e: e.activation(out=lg[:], in_=lg[:], func=AF.Exp, scale=-1.0), [lg], [lg])
                S.act(lambda e: e.activation(out=lg[:], in_=lg[:], func=AF.Ln, bias=1.0, scale=1.0), [lg], [lg])
                S.dve(lambda e: e.tensor_scalar(out=lg[:], in0=lg[:], scalar1=-1.0, scalar2=None, op0=ALU.mult), [lg], [lg])
                for h in range(4):
                    for dr in range(2):
                        col = dr * 4 + h
                        pq = 0 if dr == 0 else 1
                        pk = 2 if dr == 0 else 3
                        S.act(lambda e, col=col, pq=pq, dr=dr, h=h: e.activation(
                            out=qdec[:, dr, h:h + 1], in_=lg[:, col:col + 1], func=AF.Exp, scale=pvec[:, pq:pq + 1]),
                            [lg, pvec], [qdec])
                        S.act(lambda e, col=col, pk=pk, dr=dr, h=h: e.activation(
                            out=kdec[:, dr, h:h + 1], in_=lg[:, col:col + 1], func=AF.Exp, scale=pvec[:, pk:pk + 1]),
                            [lg, pvec], [kdec])
                        S.act(lambda e, col=col, dr=dr, h=h: e.activation(
                            out=cdec[:, dr, h:h + 1], in_=lg[:, col:col + 1], func=AF.Exp, scale=c128[:, 0:1]),
                            [lg, c128], [cdec])
                    S.act(lambda e, h=h: e.activation(out=dmask[:, h, :], in_=cst[:, 0, :], func=AF.Exp, scale=lg[:, h:h + 1]),
                          [cst, lg], [dmask])
                    S.dve(lambda e, h=h: e.tensor_tensor(out=dmask[:, h, :], in0=dmask[:, h, :], in1=cst[:, 1, :], op=ALU.mult),
                          [dmask, cst], [dmask])
                    S.act(lambda e, h=h: e.activation(out=dtmp[:], in_=cst[:, 2, :], func=AF.Exp, scale=lg[:, 4 + h:5 + h]),
                          [cst, lg], [dtmp])
                    S.dve(lambda e: e.tensor_tensor(out=dtmp[:], in0=dtmp[:], in1=cst[:, 3, :], op=ALU.mult), [dtmp, cst], [dtmp])
                    S.dve(lambda e, h=h: e.tensor_tensor(out=dmask[:, h, :], in0=dmask[:, h, :], in1=dtmp[:], op=ALU.add),
                          [dmask, dtmp], [dmask])

                scz = ExitStack()
                S.es = scz
                with scz:
                    win = S.sbuf("rwin", [128, 8, 6144], BF16)
                    load_w(win, rwin_d[jr], 6144, "rwin")
                    convert_uv(i)
                    xts = [S.sbuf("zx%d" % k, [128, D], F32) for k in range(2)]
                    hbs = [S.sbuf("zhb%d" % k, [128, D], BF16) for k in range(2)]
                    hTs = [S.sbuf("zhT%d" % k, [128, 8, 128], BF16) for k in range(2)]
                    qk32s = [S.sbuf("zqk32%d" % k, [128, 2048], F32) for k in range(2)]
                    ra = S.sbuf("zra", [128, 1024], F32)
                    rb = S.sbuf("zrb", [128, 1024], F32)
                    zrow = [S.sbuf("zrow%d" % k, [128, 6144], BF16) for k in range(2)]
                    rc = [S.sbuf("zrc%d" % k, [128, 2, 128], F32) for k in range(2)]
                    tps = [S.psum("ztp%d" % k, [128, 1024], BF16) for k in range(2)]
                    zps = [S.psum("zps%d" % k, [128, 512], F32) for k in range(4)]
                    ztiles = [(t0 + tt, tt, latent, j) for (t0, ntl, latent, j) in seqs for tt in range(ntl)]

                    def zpro(t, tt, latent, j):
                        xt = xts[t % 2]
                        S.dma(xt[:], rows(xsrc(first), t), [xres(first, t)], [xt], xt)
                        prologue(xt, j, hbs[t % 2], hTs[t % 2], tps[t % 2])
                        if latent:
                            rct = rc[t % 2]
                            S.dma(rct[:, 0, :], rrc_d[tt * 128:(tt + 1) * 128, :], [RIN], [rct], rct)
                            S.dma(rct[:, 1, :], rrs_d[tt * 128:(tt + 1) * 128, :], [RIN], [rct], rct)

                    zpro(*ztiles[0])
                    for zi, (t, tt, latent, j) in enumerate(ztiles):
                        if True:
                            zr = zrow[t % 2]
                            hT = hTs[t % 2]
                            qk32 = qk32s[t % 2]
                            rct = rc[t % 2]
                            for nb in range(12):
                                ps = zps[nb % 4]
                                for kc in range(8):
                                    S.pe(lambda e, ps=ps, kc=kc, nb=nb, hT=hT: e.matmul(
                                        ps[:], lhsT=hT[:, kc, :], rhs=win[:, kc, nb * 512:(nb + 1) * 512],
                                        start=(kc == 0), stop=(kc == 7)), [hT, win], [ps])
                                sl = slice(nb * 512, (nb + 1) * 512)
                                if nb < 4:
                                    scl = 1.0 if nb < 2 else 0.0625
                                    if latent:
                                        S.act(lambda e, ps=ps, sl=sl, scl=scl, qk32=qk32: e.mul(out=qk32[:, sl], in_=ps[:], mul=scl), [ps], [qk32])
                                    else:
                                        S.act(lambda e, ps=ps, sl=sl, scl=scl, zr=zr: e.mul(out=zr[:, sl], in_=ps[:], mul=scl), [ps], [zr])
                                elif nb < 8:
                                    S.act(lambda e, ps=ps, sl=sl, zr=zr: e.copy(out=zr[:, sl], in_=ps[:]), [ps], [zr])
                                else:
                                    S.act(lambda e, ps=ps, sl=sl, zr=zr: e.activation(out=zr[:, sl], in_=ps[:], func=AF.Silu), [ps], [zr])
                            if zi + 1 < len(ztiles):
                                zpro(*ztiles[zi + 1])
                            if latent:
                                v4 = qk32[:].rearrange("p (a b c) -> p a b c", a=8, b=2)
                                x1 = v4[:, :, 0, :]
                                x2 = v4[:, :, 1, :]
                                cosb = rct[:, 0, :].unsqueeze(1).to_broadcast([128, 8, 128])
                                sinb = rct[:, 1, :].unsqueeze(1).to_broadcast([128, 8, 128])
                                o4 = zr[:, 0:2048].rearrange("p (a b c) -> p a b c", a=8, b=2)
                                ra3 = ra[:].rearrange("p (a c) -> p a c", a=8)
                                rb3 = rb[:].rearrange("p (a c) -> p a c", a=8)
                                S.dve(lambda e, ra3=ra3, x1=x1, cosb=cosb: e.tensor_tensor(out=ra3, in0=x1, in1=cosb, op=ALU.mult), [qk32, rct], [ra])
                                S.dve(lambda e, rb3=rb3, x2=x2, sinb=sinb: e.tensor_tensor(out=rb3, in0=x2, in1=sinb, op=ALU.mult), [qk32, rct], [rb])
                                S.dve(lambda e, o4=o4, ra3=ra3, rb3=rb3: e.tensor_tensor(out=o4[:, :, 0, :], in0=ra3, in1=rb3, op=ALU.subtract), [ra, rb], [zr])
                                S.dve(lambda e, ra3=ra3, x1=x1, sinb=sinb: e.tensor_tensor(out=ra3, in0=x1, in1=sinb, op=ALU.mult), [qk32, rct, zr], [ra])
                                S.dve(lambda e, rb3=rb3, x2=x2, cosb=cosb: e.tensor_tensor(out=rb3, in0=x2, in1=cosb, op=ALU.mult), [qk32, rct, zr], [rb])
                                S.dve(lambda e, o4=o4, ra3=ra3, rb3=rb3: e.tensor_tensor(out=o4[:, :, 1, :], in0=ra3, in1=rb3, op=ALU.add), [ra, rb], [zr])
                            S.dma(rows(z_d, t), zr[:], [zr], [R_z[t]], zr)
                    S.barrier()
                S.es = sc0

                sca = ExitStack()
                S.es = sca
                with sca:
                    qkv = [S.sbuf("aqkv%d" % k, [128, 4096], BF16) for k in range(2)]
                    qdb = S.sbuf("aqdb", [128, 1024], BF16)
                    kdb = S.sbuf("akdb", [128, 1024], BF16)
                    qdbT = S.sbuf("aqdbT", [128, 8, 128], BF16)
                    Sb32 = S.sbuf("aSb32", [128, 4, 2, 512], F32)
                    Sbb = S.sbuf("aSbb", [128, 4, 2, 512], BF16)
                    pbt = [S.sbuf("apbt%d" % k, [128, 2048], F32) for k in range(2)]
                    tp = S.psum("atp", [128, 1024], BF16)
                    aps = [S.psum("aps%d" % k, [128, 512], F32) for k in range(6)]
                    for si, (t0, ntl, latent, j) in enumerate(seqs):
                        if latent:
                            S.dma(Sb32[:].rearrange("p h c v -> p (h c) v"),
                                  sretb_d[jr].rearrange("h (c p) v -> p (h c) v", p=128), [RIN], [Sb32], Sb32)
                        else:
                            S.dve(lambda e: e.memset(Sb32[:].rearrange("p h c v -> p (h c v)"), 0.0), [], [Sb32])
                        S.act(lambda e: e.copy(out=Sbb[:].rearrange("p h c v -> p (h c v)"),
                                               in_=Sb32[:].rearrange("p h c v -> p (h c v)")), [Sb32], [Sbb])
                        for tt in reversed(range(ntl)):
                            t = t0 + tt
                            qv = qkv[t % 2]
                            S.dma(qv[:], z_d[t * 128:(t + 1) * 128, 0:4096], [R_z[t]], [qv], qv)
                            for h in range(4):
                                S.dve(lambda e, h=h, qv=qv: e.tensor_scalar(out=qdb[:, h * 256:(h + 1) * 256], in0=qv[:, h * 256:(h + 1) * 256],
                                                                      scalar1=qdec[:, 1, h:h + 1], scalar2=None, op0=ALU.mult),
                                      [qv, qdec], [qdb])
                                S.dve(lambda e, h=h, qv=qv: e.tensor_scalar(out=kdb[:, h * 256:(h + 1) * 256],
                                                                       in0=qv[:, 1024 + h * 256:1024 + (h + 1) * 256],
                                                                       scalar1=kdec[:, 1, h:h + 1], scalar2=None, op0=ALU.mult),
                                       [qv, kdec], [kdb])
                            for c in range(8):
                                S.pe(lambda e, c=c: e.transpose(out=tp[:, c * 128:(c + 1) * 128], in_=qdb[:, c * 128:(c + 1) * 128],
                                                                identity=ident[:]), [qdb, ident], [tp])
                            S.act(lambda e: e.copy(out=qdbT[:].rearrange("p a b -> p (a b)"), in_=tp[:]), [tp], [qdbT])
                            pt = pbt[t % 2]
                            for h in range(4):
                                ps = aps[h % 2]
                                for dc in range(2):
                                    S.pe(lambda e, ps=ps, h=h, dc=dc: e.matmul(ps[:], lhsT=qdbT[:, h * 2 + dc, :], rhs=Sbb[:, h, dc, :],
                                                                          start=(dc == 0), stop=(dc == 1)), [qdbT, Sbb], [ps])
                                S.act(lambda e, ps=ps, h=h, pt=pt: e.copy(out=pt[:, h * 512:(h + 1) * 512], in_=ps[:]), [ps], [pt])
                            S.dma(rows(pb_d, t), pt[:], [pt], [R_pb[t]], pt)
                            n = 0
                            for h in range(4):
                                for dc in range(2):
                                    ps = aps[2 + n % 4]
                                    n += 1
                                    S.pe(lambda e, ps=ps, h=h, dc=dc, qv=qv: e.matmul(
                                        ps[:], lhsT=kdb[:, h * 256 + dc * 128: h * 256 + (dc + 1) * 128],
                                        rhs=qv[:, 2048 + h * 512: 2048 + (h + 1) * 512], start=True, stop=True), [kdb, qv], [ps])
                                    S.dve(lambda e, ps=ps, h=h, dc=dc: e.scalar_tensor_tensor(
                                        out=Sb32[:, h, dc, :], in0=Sb32[:, h, dc, :], scalar=cdec[:, 1, h:h + 1], in1=ps[:],
                                        op0=ALU.mult, op1=ALU.add), [Sb32, cdec, ps], [Sb32])
                            S.act(lambda e: e.copy(out=Sbb[:].rearrange("p h c v -> p (h c v)"),
                                                   in_=Sb32[:].rearrange("p h c v -> p (h c v)")), [Sb32], [Sbb])
                        if not latent:
                            S.dma(nsb_d[si, jr].rearrange("h (c p) v -> p (h c) v", p=128),
                                  Sb32[:].rearrange("p h c v -> p (h c) v"), [Sb32], [R_o], Sb32)
                    S.barrier()
                S.es = sc0

                scb = ExitStack()
                S.es = scb
                with scb:
                    wout = S.sbuf("rwout", [128, 16, D], BF16)
                    load_w(wout, rwout_d[jr], D, "rwout")
                    zt = [S.sbuf("bz%d" % k, [128, 6144], BF16) for k in range(2)]
                    pbt = S.sbuf("bpbt", [128, 2048], F32)
                    xts = [S.sbuf("bx%d" % k, [128, D], F32) for k in range(2)]
                    qdf = S.sbuf("bqdf", [128, 1024], BF16)
                    kdf = S.sbuf("bkdf", [128, 1024], BF16)
                    QT = S.sbuf("bQT", [128, 24, 128], BF16)
                    attm = S.sbuf("battm", [128, 512], BF16)
                    Sf32 = S.sbuf("bSf32", [128, 4, 2, 512], F32)
                    Sfb = S.sbuf("bSfb", [128, 4, 2, 512], BF16)
                    o32 = S.sbuf("bo32", [128, 2048], F32)
                    go = S.sbuf("bgo", [128, 2048], BF16)
                    goT = S.sbuf("bgoT", [128, 16, 128], BF16)
                    gst = S.sbuf("bgst", [128, 4, 6], F32)
                    gmv = S.sbuf("bgmv", [128, 4, 2], F32)
                    grs = S.sbuf("bgrs", [128, 4], F32)
                    tmp = S.sbuf("btmp", [128, D], F32)
                    r = S.sbuf("br", [128, D], F32)
                    st = S.sbuf("bst", [128, 2, 6], F32)
                    mv = S.sbuf("bmv", [128, 2], F32)
                    rstd = S.sbuf("brstd", [128, 1], F32)
                    tp = S.psum("btp", [128, 1024], BF16)
                    bps = [S.psum("bps%d" % k, [128, 512], F32) for k in range(5)]
                    Y = S.psum("bY", [128, 1024], F32)
                    for si, (t0, ntl, latent, j) in enumerate(seqs):
                        if latent:
                            S.dma(Sf32[:].rearrange("p h c v -> p (h c) v"),
                                  sretf_d[jr].rearrange("h (c p) v -> p (h c) v", p=128), [RIN], [Sf32], Sf32)
                        else:
                            S.dve(lambda e: e.memset(Sf32[:].rearrange("p h c v -> p (h c v)"), 0.0), [], [Sf32])
                        S.act(lambda e: e.copy(out=Sfb[:].rearrange("p h c v -> p (h c v)"),
                                               in_=Sf32[:].rearrange("p h c v -> p (h c v)")), [Sf32], [Sfb])
                        for tt in range(ntl):
                            t = t0 + tt
                            z = zt[t % 2]
                            xt = xts[t % 2]
                            S.dma(z[:], rows(z_d, t), [R_z[t]], [z], z)
                            S.dma(pbt[:], rows(pb_d, t), [R_pb[t]], [pbt], pbt)
                            S.dma(xt[:], rows(xsrc(first), t), [xres(first, t)], [xt], xt)
                            for h in range(4):
                                S.dve(lambda e, h=h, z=z: e.tensor_scalar(out=qdf[:, h * 256:(h + 1) * 256], in0=z[:, h * 256:(h + 1) * 256],
                                                                     scalar1=qdec[:, 0, h:h + 1], scalar2=None, op0=ALU.mult),
                                      [z, qdec], [qdf])
                                S.dve(lambda e, h=h, z=z: e.tensor_scalar(out=kdf[:, h * 256:(h + 1) * 256],
                                                                      in0=z[:, 1024 + h * 256:1024 + (h + 1) * 256],
                                                                      scalar1=kdec[:, 0, h:h + 1], scalar2=None, op0=ALU.mult),
                                       [z, kdec], [kdf])
                            for grp, (src, off, rr) in enumerate([(z, 0, [z]), (qdf, 0, [qdf]), (z, 1024, [z])]):
                                for c in range(8):
                                    S.pe(lambda e, c=c, src=src, off=off: e.transpose(
                                        out=tp[:, c * 128:(c + 1) * 128], in_=src[:, off + c * 128: off + (c + 1) * 128],
                                        identity=ident[:]), rr + [ident], [tp])
                                S.act(lambda e, grp=grp: e.copy(out=QT[:, grp * 8:(grp + 1) * 8, :].rearrange("p a b -> p (a b)"), in_=tp[:]),
                                      [tp], [QT])
                            pa = bps[4]
                            for h in range(4):
                                for dc in range(2):
                                    S.pe(lambda e, h=h, dc=dc: e.matmul(pa[:, h * 128:(h + 1) * 128], lhsT=QT[:, 16 + h * 2 + dc, :],
                                                                        rhs=QT[:, h * 2 + dc, :], start=(dc == 0), stop=(dc == 1)),
                                         [QT], [pa])
                            S.dve(lambda e: e.tensor_tensor(out=attm[:], in0=pa[:], in1=dmask[:].rearrange("p h i -> p (h i)"), op=ALU.mult),
                                  [pa, dmask], [attm])
                            for h in range(4):
                                ps = bps[h]
                                S.pe(lambda e, ps=ps, h=h, z=z: e.matmul(ps[:], lhsT=attm[:, h * 128:(h + 1) * 128],
                                                                    rhs=z[:, 2048 + h * 512:2048 + (h + 1) * 512], start=True, stop=False),
                                     [attm, z], [ps])
                                for dc in range(2):
                                    S.pe(lambda e, ps=ps, h=h, dc=dc: e.matmul(ps[:], lhsT=QT[:, 8 + h * 2 + dc, :], rhs=Sfb[:, h, dc, :],
                                                                          start=False, stop=(dc == 1)), [QT, Sfb], [ps])
                                S.dve(lambda e, ps=ps, h=h: e.tensor_tensor(out=o32[:, h * 512:(h + 1) * 512], in0=ps[:],
                                                                       in1=pbt[:, h * 512:(h + 1) * 512], op=ALU.add), [ps, pbt], [o32])
                                S.dve(lambda e, h=h: e.bn_stats(out=gst[:, h, :], in_=o32[:, h * 512:(h + 1) * 512]), [o32], [gst])
                                S.dve(lambda e, h=h: e.bn_aggr(out=gmv[:, h, :], in_=gst[:, h, :]), [gst], [gmv])
                            S.act(lambda e: e.activation(out=grs[:], in_=gmv[:, :, 1], func=AF.Sqrt, bias=epsc[:], scale=1.0), [gmv, epsc], [grs])
                            S.dve(lambda e: e.reciprocal(out=grs[:], in_=grs[:]), [grs], [grs])
                            for h in range(4):
                                S.dve(lambda e, h=h: e.tensor_scalar(out=o32[:, h * 512:(h + 1) * 512], in0=o32[:, h * 512:(h + 1) * 512],
                                                                     scalar1=gmv[:, h, 0:1], scalar2=grs[:, h:h + 1],
                                                                     op0=ALU.subtract, op1=ALU.mult), [o32, gmv, grs], [o32])
                            S.dve(lambda e, z=z: e.tensor_tensor(out=go[:], in0=o32[:], in1=z[:, 4096:6144], op=ALU.mult), [o32, z], [go])
                            for half in range(2):
                                for c in range(8):
                                    cc = half * 8 + c
                                    S.pe(lambda e, c=c, cc=cc: e.transpose(out=tp[:, c * 128:(c + 1) * 128], in_=go[:, cc * 128:(cc + 1) * 128],
                                                                           identity=ident[:]), [go, ident], [tp])
                                S.act(lambda e, half=half: e.copy(out=goT[:, half * 8:(half + 1) * 8, :].rearrange("p a b -> p (a b)"), in_=tp[:]),
                                      [tp], [goT])
                            for nh in range(2):
                                for kc in range(16):
                                    S.pe(lambda e, nh=nh, kc=kc: e.matmul(Y[:, nh * 512:(nh + 1) * 512], lhsT=goT[:, kc, :],
                                                                          rhs=wout[:, kc, nh * 512:(nh + 1) * 512],
                                                                          start=(kc == 0), stop=(kc == 15)), [goT, wout], [Y])
                            epilogue(Y, [Y], xt, j, t, tmp, r, st, mv, rstd, store_eng="pool")
                            n = 0
                            for h in range(4):
                                for dc in range(2):
                                    ps = bps[n % 4]
                                    n += 1
                                    S.pe(lambda e, ps=ps, h=h, dc=dc, z=z: e.matmul(
                                        ps[:], lhsT=kdf[:, h * 256 + dc * 128: h * 256 + (dc + 1) * 128],
                                        rhs=z[:, 2048 + h * 512: 2048 + (h + 1) * 512], start=True, stop=True), [kdf, z], [ps])
                                    S.dve(lambda e, ps=ps, h=h, dc=dc: e.scalar_tensor_tensor(
                                        out=Sf32[:, h, dc, :], in0=Sf32[:, h, dc, :], scalar=cdec[:, 0, h:h + 1], in1=ps[:],
                                        op0=ALU.mult, op1=ALU.add), [Sf32, cdec, ps], [Sf32])
                            S.act(lambda e: e.copy(out=Sfb[:].rearrange("p h c v -> p (h c v)"),
                                                   in_=Sf32[:].rearrange("p h c v -> p (h c v)")), [Sf32], [Sfb])
                        if not latent:
                            S.dma(nsf_d[si, jr].rearrange("h (c p) v -> p (h c) v", p=128),
                                  Sf32[:].rearrange("p h c v -> p (h c) v"), [Sf32], [R_o], Sf32)
                    S.barrier()
                S.es = sc0
            S.es = root

        def attention_layer(i, first):
            ja = i // 2
            sc0 = ExitStack()
            S.es = sc0
            with sc0:
                win = S.sbuf("awin", [128, 8, 1536], BF16)
                wout = S.sbuf("awout", [128, 8, D], BF16)
                load_w(win, awin_d[ja], 1536, "awin")
                load_w(wout, awout_d[ja], D, "awout")
                convert_uv(i)
                esink = S.sbuf("esink", [128, 16], F32)
                S.dma(esink[:], asink_d[ja:ja + 1, :].to_broadcast([128, 16]), [RIN], [esink], esink)
                S.act(lambda e: e.activation(out=esink[:], in_=esink[:], func=AF.Exp), [esink], [esink])
                mprev = S.sbuf("mprev", [128, 4, 128], BF16)
                mnext = S.sbuf("mnext", [128, 4, 128], BF16)
                S.dve(lambda e: e.tensor_copy(out=mprev[:], in_=cst[:, 4, :].unsqueeze(1).to_broadcast([128, 4, 128])), [cst], [mprev])
                S.dve(lambda e: e.tensor_copy(out=mnext[:], in_=cst[:, 5, :].unsqueeze(1).to_broadcast([128, 4, 128])), [cst], [mnext])
                NSm = max(NS, 2)
                KT = S.sbuf("aKT", [64, 4, NSm * 128], BF16)
                VL = S.sbuf("aVL", [128, NSm, 4, 65], BF16)
                CKT = S.sbuf("aCKT", [64, 4, 512], BF16)
                CV = S.sbuf("aCV", [128, 4, 4, 65], BF16)
                c32 = S.sbuf("ac32", [128, 4, 256], F32)
                cb = S.sbuf("acb", [128, 4, 256], BF16)
                xts = [S.sbuf("ax%d" % k, [128, D], F32) for k in range(2)]
                hbs = [S.sbuf("ahb%d" % k, [128, D], BF16) for k in range(2)]
                hTs = [S.sbuf("ahT%d" % k, [128, 8, 128], BF16) for k in range(2)]
                q32s = [S.sbuf("aq32%d" % k, [128, 1536], F32) for k in range(2)]
                ra = S.sbuf("ara", [128, 512], F32)
                rb = S.sbuf("arb", [128, 512], F32)
                qb = [S.sbuf("aqb%d" % k, [128, 1024], BF16) for k in range(2)]
                kb = S.sbuf("akb", [128, 256], BF16)
                rc = [S.sbuf("arc%d" % k, [128, 2, 32], F32) for k in range(2)]
                qT = S.sbuf("aqT", [64, 16, 128], BF16)
                PT = [S.sbuf("aPT%d" % k, [128, 512], BF16) for k in range(7)]
                rden = S.sbuf("arden", [128, 16], F32)
                on = S.sbuf("aon", [128, 1024], BF16)
                onT = S.sbuf("aonT", [128, 8, 128], BF16)
                tmp = S.sbuf("atmp", [128, D], F32)
                r = S.sbuf("ar", [128, D], F32)
                st = S.sbuf("ast", [128, 2, 6], F32)
                mv = S.sbuf("amv", [128, 2], F32)
                rstd = S.sbuf("arstd", [128, 1], F32)
                tp = S.psum("atp", [128, 1024], BF16)
                tq = S.psum("atq", [128, 2048], BF16)
                sps = [S.psum("asps%d" % k, [128, 512], F32) for k in range(2)]
                ops_ = [S.psum("aops%d" % k, [128, 4, 65], F32) for k in range(1)]
                Y = S.psum("aY", [128, 1024], F32)

                S.dve(lambda e: e.memset(VL[:].rearrange("p a b c -> p (a b c)"), 1.0), [], [VL])
                S.dve(lambda e: e.memset(CV[:].rearrange("p a b c -> p (a b c)"), 1.0), [], [CV])
                S.dma(c32[:], ck_d[ja].rearrange("(b p) f -> p b f", p=128), [RIN], [c32], c32)
                S.dve(lambda e: e.tensor_copy(out=cb[:], in_=c32[:]), [c32], [cb])
                for b in range(4):
                    for g in range(4):
                        S.pe(lambda e, b=b, g=g: e.transpose(out=tp[0:64, g * 128:(g + 1) * 128], in_=cb[:, b, g * 64:(g + 1) * 64],
                                                             identity=ident[:]), [cb, ident], [tp])
                    S.act(lambda e, b=b: e.copy(out=CKT[:, :, b * 128:(b + 1) * 128],
                                                in_=tp[0:64, 0:512].rearrange("p (g k) -> p g k", g=4)), [tp], [CKT])
                S.dma(c32[:], cv_d[ja].rearrange("(b p) f -> p b f", p=128), [RIN], [c32], c32)
                S.dve(lambda e: e.tensor_copy(out=CV[:, :, :, 0:64], in_=c32[:].rearrange("p b (g d) -> p b g d", g=4)), [c32], [CV])

                for si, (t0, ntl, latent, j) in enumerate(seqs):
                    def apro(t_, tt_, j=j, latent=latent):
                        xt_ = xts[t_ % 2]
                        S.dma(xt_[:], rows(xsrc(first), t_), [xres(first, t_)], [xt_], xt_)
                        prologue(xt_, j, hbs[t_ % 2], hTs[t_ % 2], tp)
                        if latent:
                            rct_ = rc[t_ % 2]
                            S.dma(rct_[:, 0, :], rac_d[tt_ * 128:(tt_ + 1) * 128, :], [RIN], [rct_], rct_)
                            S.dma(rct_[:, 1, :], ras_d[tt_ * 128:(tt_ + 1) * 128, :], [RIN], [rct_], rct_)

                    apro(t0, 0)
                    for tt in range(ntl):
                        t = t0 + tt
                        hT = hTs[t % 2]
                        q32 = q32s[t % 2]
                        rct = rc[t % 2]
                        for nb in range(3):
                            ps = sps[nb % 2]
                            for kc in range(8):
                                S.pe(lambda e, ps=ps, kc=kc, nb=nb, hT=hT: e.matmul(ps[:], lhsT=hT[:, kc, :], rhs=win[:, kc, nb * 512:(nb + 1) * 512],
                                                                        start=(kc == 0), stop=(kc == 7)), [hT, win], [ps])
                            S.act(lambda e, ps=ps, nb=nb, q32=q32: e.copy(out=q32[:, nb * 512:(nb + 1) * 512], in_=ps[:]), [ps], [q32])
                        if tt + 1 < ntl:
                            apro(t + 1, tt + 1)
                        if not latent:
                            S.dma(nk_d[si, ja, tt * 128:(tt + 1) * 128, :], q32[:, 1024:1280], [q32], [R_o], q32)
                            S.dma(nv_d[si, ja, tt * 128:(tt + 1) * 128, :], q32[:, 1280:1536], [q32], [R_o], q32)
                        q_out = qb[t % 2]
                        if latent:
                            v5 = q32[:, 0:1280].rearrange("p (h a b c) -> p h a b c", h=20, a=2, b=2)
                            for a in range(2):
                                x1 = v5[:, :, a, 0, :]
                                x2 = v5[:, :, a, 1, :]
                                cosb = rct[:, 0, a * 16:(a + 1) * 16].unsqueeze(1).to_broadcast([128, 20, 16])
                                sinb = rct[:, 1, a * 16:(a + 1) * 16].unsqueeze(1).to_broadcast([128, 20, 16])
                                ra3 = ra[:, 0:320].rearrange("p (h c) -> p h c", h=20)
                                rb3 = rb[:, 0:320].rearrange("p (h c) -> p h c", h=20)
                                ra3b = ra[:, 320:640].rearrange("p (h c) -> p h c", h=20) if False else None
                                S.dve(lambda e, x1=x1, cosb=cosb, ra3=ra3: e.tensor_tensor(out=ra3, in0=x1, in1=cosb, op=ALU.mult), [q32, rct], [ra])
                                S.dve(lambda e, x2=x2, sinb=sinb, rb3=rb3: e.tensor_tensor(out=rb3, in0=x2, in1=sinb, op=ALU.mult), [q32, rct], [rb])
                                S.dve(lambda e, ra3=ra3, rb3=rb3: e.tensor_tensor(out=ra3, in0=ra3, in1=rb3, op=ALU.subtract), [ra, rb], [ra])
                                S.dve(lambda e, x1=x1, sinb=sinb, rb3=rb3: e.tensor_tensor(out=rb3, in0=x1, in1=sinb, op=ALU.mult), [q32, rct, ra], [rb])
                                S.dve(lambda e, x1=x1, ra3=ra3: e.tensor_copy(out=x1, in_=ra3), [ra, rb], [q32])
                                S.dve(lambda e, x2=x2, cosb=cosb, ra3=ra3: e.tensor_tensor(out=ra3, in0=x2, in1=cosb, op=ALU.mult), [q32, rct], [ra])
                                S.dve(lambda e, x2=x2, ra3=ra3, rb3=rb3: e.tensor_tensor(out=x2, in0=ra3, in1=rb3, op=ALU.add), [ra, rb], [q32])
                        S.act(lambda e, q_out=q_out, q32=q32: e.mul(out=q_out[:], in_=q32[:, 0:1024], mul=0.125), [q32], [q_out])
                        S.dma(rows(qs_d, t), q_out[:], [q_out], [R_qs[t]], q_out)
                        S.dve(lambda e, q32=q32: e.tensor_copy(out=kb[:], in_=q32[:, 1024:1280]), [q32], [kb])
                        S.dve(lambda e, tt=tt, q32=q32: e.tensor_copy(out=VL[:, tt, :, 0:64], in_=q32[:, 1280:1536].rearrange("p (g d) -> p g d", g=4)),
                              [q32], [VL])
                        for g in range(4):
                            S.pe(lambda e, g=g: e.transpose(out=tp[0:64, g * 128:(g + 1) * 128], in_=kb[:, g * 64:(g + 1) * 64],
                                                            identity=ident[:]), [kb, ident], [tp])
                        S.act(lambda e, tt=tt: e.copy(out=KT[:, :, tt * 128:(tt + 1) * 128],
                                                      in_=tp[0:64, 0:512].rearrange("p (g k) -> p g k", g=4)), [tp], [KT])
                    for tt in range(ntl):
                        t = t0 + tt
                        xt = xts[t % 2]
                        qin = qb[t % 2]
                        S.dma(xt[:], rows(xsrc(first), t), [xres(first, t)], [xt], xt)
                        S.dma(qin[:], rows(qs_d, t), [R_qs[t]], [qin], qin)
                        for h in range(16):
                            S.pe(lambda e, h=h, qin=qin: e.transpose(out=tq[0:64, h * 128:(h + 1) * 128], in_=qin[:, h * 64:(h + 1) * 64],
                                                                     identity=ident[:]), [qin, ident], [tq])
                        S.act(lambda e: e.copy(out=qT[:].rearrange("p a b -> p (a b)"), in_=tq[0:64, :]), [tq], [qT])
                        if latent:
                            blocks = []
                            if tt > 0:
                                blocks.append(("loc", tt - 1, mprev))
                            blocks.append(("loc", tt, None))
                            if tt < ntl - 1:
                                blocks.append(("loc", tt + 1, mnext))
                            for b in range(4):
                                blocks.append(("ctx", b, None))
                        else:
                            blocks = [("loc", b, None) for b in range(ntl)]
                        for g in range(4):
                            for bi, (kind, b, msk) in enumerate(blocks):
                                ps = sps[bi % 2]
                                kT_ap = (KT[:, g, b * 128:(b + 1) * 128] if kind == "loc" else CKT[:, g, b * 128:(b + 1) * 128])
                                kres = KT if kind == "loc" else CKT
                                S.pe(lambda e, ps=ps, kT_ap=kT_ap, g=g, msk=msk: e.matmul(
                                    ps[:], lhsT=kT_ap, rhs=qT[:, 4 * g:4 * g + 4, :], start=True, stop=(msk is None)), [kres, qT], [ps])
                                if msk is not None:
                                    S.pe(lambda e, ps=ps, msk=msk: e.matmul(ps[:], lhsT=ident[:], rhs=msk[:].rearrange("p a b -> p (a b)"),
                                                                           start=False, stop=True), [ident, msk], [ps])
                                S.act(lambda e, ps=ps, bi=bi: e.activation(out=PT[bi][:], in_=ps[:], func=AF.Exp), [ps], [PT[bi]])
                            og = ops_[0]
                            for hh in range(4):
                                for bi, (kind, b, msk) in enumerate(blocks):
                                    v_ap = (VL[:, b, g, :] if kind == "loc" else CV[:, b, g, :])
                                    vres = VL if kind == "loc" else CV
                                    S.pe(lambda e, og=og, hh=hh, bi=bi, v_ap=v_ap, nb=len(blocks): e.matmul(
                                        og[:, hh, :], lhsT=PT[bi][:, hh * 128:(hh + 1) * 128], rhs=v_ap,
                                        start=(bi == 0), stop=(bi == nb - 1)), [PT[bi], vres], [og])
                            S.dve(lambda e, og=og, g=g: e.tensor_tensor(out=rden[:, 4 * g:4 * g + 4], in0=og[:, :, 64],
                                                                   in1=esink[:, 4 * g:4 * g + 4], op=ALU.add), [og, esink], [rden])
                            S.dve(lambda e, g=g: e.reciprocal(out=rden[:, 4 * g:4 * g + 4], in_=rden[:, 4 * g:4 * g + 4]), [rden], [rden])
                            S.dve(lambda e, og=og, g=g: e.tensor_tensor(
                                out=on[:, g * 256:(g + 1) * 256].rearrange("p (h d) -> p h d", h=4), in0=og[:, :, 0:64],
                                in1=rden[:, 4 * g:4 * g + 4].unsqueeze(2).to_broadcast([128, 4, 64]), op=ALU.mult), [og, rden], [on])
                        for c in range(8):
                            S.pe(lambda e, c=c: e.transpose(out=tp[:, c * 128:(c + 1) * 128], in_=on[:, c * 128:(c + 1) * 128],
                                                            identity=ident[:]), [on, ident], [tp])
                        S.act(lambda e: e.copy(out=onT[:].rearrange("p a b -> p (a b)"), in_=tp[:]), [tp], [onT])
                        for nh in range(2):
                            for kc in range(8):
                                S.pe(lambda e, nh=nh, kc=kc: e.matmul(Y[:, nh * 512:(nh + 1) * 512], lhsT=onT[:, kc, :],
                                                                      rhs=wout[:, kc, nh * 512:(nh + 1) * 512],
                                                                      start=(kc == 0), stop=(kc == 7)), [onT, wout], [Y])
                        epilogue(Y, [Y], xt, j, t, tmp, r, st, mv, rstd)
                S.barrier()
            S.es = root

        def peer_layer(i):
            GS = OPTS["GS"]
            UVDT = BF16 if OPTS["uvbf16"] else F32
            NG = 128 // GS
            sc0 = ExitStack()
            S.es = sc0
            with sc0:
                wq = S.sbuf("pwq", [128, 8, 2048], BF16)
                load_w(wq, pwq_d[i], 2048, "pwq")
                keysT = S.sbuf("pkeysT", [128, 16, 128], BF16)
                tp = S.psum("ptp", [128, 1024], BF16)
                sck = ExitStack()
                S.es = sck
                with sck:
                    k32 = S.sbuf("pk32", [128, 16, 128], F32)
                    kbf = S.sbuf("pkbf", [128, 16, 128], BF16)
                    S.dma(k32[:], pkeys_d[i].rearrange("c n d -> n c d"), [RIN], [k32], k32)
                    S.dve(lambda e: e.tensor_copy(out=kbf[:], in_=k32[:]), [k32], [kbf])
                    for half in range(2):
                        for c in range(8):
                            cc = half * 8 + c
                            S.pe(lambda e, c=c, cc=cc: e.transpose(out=tp[:, c * 128:(c + 1) * 128], in_=kbf[:, cc, :], identity=ident[:]),
                                 [kbf, ident], [tp])
                        S.act(lambda e, half=half: e.copy(out=keysT[:, half * 8:(half + 1) * 8, :].rearrange("p a b -> p (a b)"), in_=tp[:]),
                              [tp], [keysT])
                    S.barrier()
                S.es = sc0

                xts = [S.sbuf("px%d" % k, [128, D], F32) for k in range(2)]
                h32s = [S.sbuf("ph32%d" % k, [128, D], F32) for k in range(2)]
                eis = [S.sbuf("pei%d" % k, [128, 128], I32) for k in range(2)]
                wsms = [S.sbuf("pwsm%d" % k, [128, 8, 16], F32) for k in range(2)]
                hb = S.sbuf("phb", [128, D], BF16)
                hT = S.sbuf("phT", [128, 8, 128], BF16)
                qb = S.sbuf("pqb", [128, 2048], BF16)
                qT = S.sbuf("pqT", [128, 16, 128], BF16)
                s32 = S.sbuf("ps32", [128, 16, 128], F32)
                wk = S.sbuf("pwk", [128, 512], F32)
                avR = [[S.dram("avR%d_%d" % (k, q)) for q in range(8)] for k in range(4)]
                junk = S.sbuf("pjunk", [128, D], BF16)
                gbR = [S.dram("gbR%d" % k) for k in range(4)]
                coefR = [S.dram("coefR%d" % k) for k in range(4)]
                svR = [S.dram("svR%d" % k) for k in range(16)]
                siuR = [S.dram("siuR%d" % k) for k in range(16)]
                wkR = [S.dram("wkR%d" % k) for k in range(4)]
                combR = [S.dram("combR%d" % k) for k in range(8)]
                csR = [S.dram("csR%d" % k) for k in range(8)]
                ciuR = [S.dram("ciuR%d" % k) for k in range(8)]
                sv = S.sbuf("psv", [128, 16, 16], F32)
                siu = S.sbuf("psiu", [128, 16, 16], U32)
                sif = S.sbuf("psif", [128, 16, 16], F32)
                comb = S.sbuf("pcomb", [128, 8, 256], F32)
                cs = S.sbuf("pcs", [128, 8, 16], F32)
                ciu = S.sbuf("pciu", [128, 8, 16], U32)
                cia = S.sbuf("pcia", [128, 8, 16], U32)
                cib = S.sbuf("pcib", [128, 8, 16], U32)
                caf = S.sbuf("pcaf", [128, 8, 16], F32)
                cbf = S.sbuf("pcbf", [128, 8, 16], F32)
                oh = [comb] * 2
                i1f = S.sbuf("pi1f", [128, 8, 16], F32)
                i2f = S.sbuf("pi2f", [128, 8, 16], F32)
                wsum = S.sbuf("pwsum", [128, 8], F32)
                av = S.sbuf("pav", [128, 128], F32)
                ga = S.sbuf("pga", [128, 128], F32)
                gb_ = S.sbuf("pgb", [128, 128], F32)
                coef = S.sbuf("pcoef", [128, 128], F32)
                uvg = [[S.sbuf("puv%d_%d" % (k, s_), [128, 2 * D], UVDT) for s_ in range(GS)] for k in range(OPTS["NSETS"])]
                tv = [S.sbuf("ptv%d" % k, [128, D], BF16) for k in range(OPTS["NTV"])]
                tmp = S.sbuf("ptmp", [128, D], F32)
                r = S.sbuf("pr", [128, D], F32)
                st = S.sbuf("pst", [128, 2, 6], F32)
                mv = S.sbuf("pmv", [128, 2], F32)
                rstd = S.sbuf("prstd", [128, 1], F32)
                tq = S.psum("ptq", [128, 1024], BF16)
                qps = [S.psum("pqps%d" % k, [128, 512], F32) for k in range(2)]
                Ys = [S.psum("pY%d" % k, [128, 1024], F32) for k in range(2)]

                def stage_a(t, j, par):
                    xt = xts[par]
                    h32 = h32s[par]
                    ei = eis[par]
                    wsm = wsms[par]
                    S.dma(xt[:], rows(y_d, t), [R_y[t]], [xt], xt)
                    prologue(xt, j, hb, hT, tp, h32=h32, nopool=True)
                    yield
                    for nb in range(4):
                        ps = qps[nb % 2]
                        for kc in range(8):
                            S.pe(lambda e, ps=ps, kc=kc, nb=nb: e.matmul(ps[:], lhsT=hT[:, kc, :], rhs=wq[:, kc, nb * 512:(nb + 1) * 512],
                                                                    start=(kc == 0), stop=(kc == 7)), [hT, wq], [ps])
                        S.act(lambda e, ps=ps, nb=nb: e.copy(out=qb[:, nb * 512:(nb + 1) * 512], in_=ps[:]), [ps], [qb])
                    for half in range(2):
                        for c in range(8):
                            cc = half * 8 + c
                            S.pe(lambda e, c=c, cc=cc: e.transpose(out=tq[:, c * 128:(c + 1) * 128], in_=qb[:, cc * 128:(cc + 1) * 128],
                                                                   identity=ident[:]), [qb, ident], [tq])
                        S.act(lambda e, half=half: e.copy(out=qT[:, half * 8:(half + 1) * 8, :].rearrange("p a b -> p (a b)"), in_=tq[:]),
                              [tq], [qT])
                    for b4 in range(4):
                        ps = qps[b4 % 2]
                        for c4 in range(4):
                            c = b4 * 4 + c4
                            S.pe(lambda e, ps=ps, c=c, c4=c4: e.matmul(ps[:, c4 * 128:(c4 + 1) * 128], lhsT=qT[:, c, :], rhs=keysT[:, c, :],
                                                                  start=True, stop=True), [qT, keysT], [ps])
                        S.act(lambda e, ps=ps, b4=b4: e.copy(out=s32[:, b4 * 4:(b4 + 1) * 4, :].rearrange("p a b -> p (a b)"), in_=ps[:]),
                              [ps], [s32])
                    yield
                    for c0 in range(0, 16, 4):
                        grp = list(range(c0, c0 + 4))
                        for c in grp:
                            S.dve(lambda e, c=c: e.max(out=sv[:, c, 0:8], in_=s32[:, c, :]), [s32], [svR[c]])
                        for c in grp:
                            S.dve(lambda e, c=c: e.max_index(out=siu[:, c, 0:8], in_max=sv[:, c, 0:8], in_values=s32[:, c, :]),
                                  [s32, svR[c]], [siuR[c]])
                        yield
                        for k_, c in enumerate(grp):
                            S.dve(lambda e, c=c, k_=k_: e.match_replace(out=wk[:, k_ * 128:(k_ + 1) * 128], in_to_replace=sv[:, c, 0:8], in_values=s32[:, c, :],
                                                                      imm_value=-1e30), [s32, svR[c]], [wkR[k_]])
                        for k_, c in enumerate(grp):
                            S.dve(lambda e, c=c, k_=k_: e.max(out=sv[:, c, 8:16], in_=wk[:, k_ * 128:(k_ + 1) * 128]), [wkR[k_]], [svR[c]])
                        yield
                        for k_, c in enumerate(grp):
                            S.dve(lambda e, c=c, k_=k_: e.max_index(out=siu[:, c, 8:16], in_max=sv[:, c, 8:16], in_values=wk[:, k_ * 128:(k_ + 1) * 128]),
                                  [wkR[k_], svR[c]], [siuR[c]])
                        yield
                    S.dve(lambda e: e.tensor_copy(out=sif[:], in_=siu[:]), siuR, [sif])
                    sv4 = sv[:].rearrange("p (h two) m -> p h two m", two=2)
                    S.dve(lambda e: e.tensor_tensor(
                        out=comb[:].rearrange("p h (a b) -> p h a b", a=16),
                        in0=sv4[:, :, 0, :].unsqueeze(3).to_broadcast([128, 8, 16, 16]),
                        in1=sv4[:, :, 1, :].unsqueeze(2).to_broadcast([128, 8, 16, 16]), op=ALU.add), svR, combR)
                    yield
                    for p0 in range(0, 8, 2):
                        grp = list(range(p0, p0 + 2))
                        for p in grp:
                            S.dve(lambda e, p=p: e.max(out=cs[:, p, 0:8], in_=comb[:, p, :]), [combR[p]], [csR[p]])
                        for p in grp:
                            S.dve(lambda e, p=p: e.max_index(out=ciu[:, p, 0:8], in_max=cs[:, p, 0:8], in_values=comb[:, p, :]),
                                  [combR[p], csR[p]], [ciuR[p]])
                        yield
                        for k_, p in enumerate(grp):
                            S.dve(lambda e, p=p, k_=k_: e.match_replace(out=wk[:, k_ * 256:(k_ + 1) * 256], in_to_replace=cs[:, p, 0:8], in_values=comb[:, p, :],
                                                                      imm_value=-1e30), [combR[p], csR[p]], [wkR[2 * k_], wkR[2 * k_ + 1]])
                        for k_, p in enumerate(grp):
                            S.dve(lambda e, p=p, k_=k_: e.max(out=cs[:, p, 8:16], in_=wk[:, k_ * 256:(k_ + 1) * 256]), [wkR[2 * k_], wkR[2 * k_ + 1]], [csR[p]])
                        for k_, p in enumerate(grp):
                            S.dve(lambda e, p=p, k_=k_: e.max_index(out=ciu[:, p, 8:16], in_max=cs[:, p, 8:16], in_values=wk[:, k_ * 256:(k_ + 1) * 256]),
                                  [wkR[2 * k_], wkR[2 * k_ + 1], csR[p]], [ciuR[p]])
                        yield
                    S.dve(lambda e: e.tensor_tensor(out=wsm[:], in0=cs[:], in1=cs[:, :, 0:1].to_broadcast([128, 8, 16]), op=ALU.subtract),
                          csR, [wsm])
                    S.act(lambda e: e.activation(out=wsm[:], in_=wsm[:], func=AF.Exp), [wsm], [wsm])
                    S.dve(lambda e: e.reduce_sum(out=wsum[:], in_=wsm[:], axis=AX.X), [wsm], [wsum])
                    S.dve(lambda e: e.reciprocal(out=wsum[:], in_=wsum[:]), [wsum], [wsum])
                    S.dve(lambda e: e.tensor_tensor(out=wsm[:], in0=wsm[:], in1=wsum[:].unsqueeze(2).to_broadcast([128, 8, 16]), op=ALU.mult),
                          [wsm, wsum], [wsm])
                    yield
                    S.dve(lambda e: e.tensor_single_scalar(out=cia[:], in_=ciu[:], scalar=4, op=ALU.logical_shift_right), ciuR, [cia])
                    S.dve(lambda e: e.tensor_single_scalar(out=cib[:], in_=ciu[:], scalar=15, op=ALU.bitwise_and), ciuR, [cib])
                    yield
                    S.dve(lambda e: e.tensor_copy(out=caf[:], in_=cia[:]), [cia], [caf])
                    S.dve(lambda e: e.tensor_copy(out=cbf[:], in_=cib[:]), [cib], [cbf])
                    yield
                    sif4 = sif[:].rearrange("p (h two) m -> p h two m", two=2)
                    ohv = comb[:].rearrange("p h (a b) -> p h a b", a=16)
                    for (cf, which, dst, ohx) in ((caf, 0, i1f, oh[0]), (cbf, 1, i2f, oh[1])):
                        S.dve(lambda e, cf=cf, ohx=ohx: e.tensor_tensor(
                            out=ohv, in0=cf[:].unsqueeze(3).to_broadcast([128, 8, 16, 16]),
                            in1=iota16[:].unsqueeze(1).unsqueeze(1).to_broadcast([128, 8, 16, 16]), op=ALU.is_equal), [cf, iota16] + combR, [ohx] + combR)
                        S.dve(lambda e, which=which, ohx=ohx: e.tensor_tensor(
                            out=ohv, in0=ohv,
                            in1=sif4[:, :, which, :].unsqueeze(2).to_broadcast([128, 8, 16, 16]), op=ALU.mult), [ohx, sif] + combR, [ohx] + combR)
                        S.dve(lambda e, dst=dst, ohx=ohx: e.reduce_sum(out=dst[:].rearrange("p a b -> p (a b)"),
                                                              in_=ohv.rearrange("p a b c -> p (a b) c"), axis=AX.X), [ohx] + combR, [dst])
                        yield
                    S.dve(lambda e: e.scalar_tensor_tensor(out=i1f[:], in0=i1f[:], scalar=128.0, in1=i2f[:], op0=ALU.mult, op1=ALU.add),
                          [i1f, i2f], [i1f])
                    S.dve(lambda e: e.tensor_scalar(out=i1f[:], in0=i1f[:], scalar1=float(i * NEXP), scalar2=None, op0=ALU.add), [i1f], [i1f])
                    S.dve(lambda e: e.tensor_copy(out=ei[:], in_=i1f[:].rearrange("p a b -> p (a b)")), [i1f], [ei])
                    yield

                def stage_b(t, j, par, nxt, prev_ep):
                    xt = xts[par]
                    h32 = h32s[par]
                    ei = eis[par]
                    wsm = wsms[par]
                    wflat = wsm[:].rearrange("p a b -> p (a b)")
                    Y = Ys[par]
                    nvc = [0]
                    nyd = [0]
                    pending = []

                    def vside(g, bufs, sl0):
                        gsl = slice(sl0, sl0 + GS)
                        S.dve(lambda e, gsl=gsl: e.tensor_tensor(out=coef[:, gsl], in0=gb_[:, gsl], in1=wflat[:, gsl], op=ALU.mult),
                              [gbR[g % 4], wsm], [coefR[g % 4]])
                        for s_ in range(GS):
                            sl = sl0 + s_
                            b_ = bufs[s_]
                            tv_ = tv[nvc[0] % OPTS['NTV']]
                            nvc[0] += 1
                            S.act(lambda e, b_=b_, sl=sl, tv_=tv_: e.activation(out=tv_[:], in_=b_[:, D:2 * D], func=AF.Copy,
                                                                         scale=coef[:, sl:sl + 1]), [b_, coefR[g % 4]], [tv_])
                            for nh in range(2):
                                S.pe(lambda e, tv_=tv_, nh=nh, sl=sl: e.matmul(Y[:, nh * 512:(nh + 1) * 512], lhsT=ident[:],
                                                                         rhs=tv_[:, nh * 512:(nh + 1) * 512],
                                                                         start=(sl == 0), stop=(sl == 127)), [ident, tv_], [Y])

                    for g in range(NG):
                        bufs = uvg[g % OPTS["NSETS"]]
                        sl0 = g * GS
                        for s_ in range(GS):
                            sl = sl0 + s_
                            b_ = bufs[s_]
                            if OPTS["nogather"]:
                                continue
                            S.add("pool", lambda e, b_=b_, sl=sl: e.indirect_dma_start(
                                out=b_[:], out_offset=None, in_=(uvb_d if OPTS["uvtab"] else puv_d),
                                in_offset=bass.IndirectOffsetOnAxis(ap=ei[:, sl:sl + 1], axis=0)),
                                [ei, RIN] + (R_uvb[i] if OPTS["uvtab"] else []), [b_], dma=b_)
                        for s_ in range(GS):
                            sl = sl0 + s_
                            b_ = bufs[s_]
                            if OPTS["nodots"]:
                                continue
                            S.dve(lambda e, b_=b_, sl=sl: e.scalar_tensor_tensor(out=junk[:], in0=b_[:, 0:D], scalar=1.0, in1=h32[:], op0=ALU.mult,
                                                                           op1=ALU.mult, accum_out=av[:, sl:sl + 1]), [b_, h32], [avR[g % 4][s_]])
                        gsl = slice(sl0, sl0 + GS)
                        S.act(lambda e, gsl=gsl: e.activation(out=gb_[:, gsl], in_=av[:, gsl], func=AF.Gelu_apprx_tanh), avR[g % 4][:GS], [gbR[g % 4]])
                        pending.append((g, bufs, sl0))
                        if len(pending) > 1:
                            vside(*pending.pop(0))
                        if g == 1 and prev_ep is not None:
                            prev_ep()
                            prev_ep = None
                        if nxt is not None and not OPTS["noroute"] and g >= 2:
                            left = NYA[0] - nyd[0]
                            k_n = -(-left // (NG - g)) if left > 0 else 0
                            for _ in range(k_n):
                                nyd[0] += 1
                                next(nxt, None)
                    while pending:
                        vside(*pending.pop(0))
                    if prev_ep is not None:
                        prev_ep()
                    if nxt is not None:
                        for _ in nxt:
                            pass
                    return lambda: epilogue(Y, [Y], xt, j, t, tmp, r, st, mv, rstd, nopool=True)

                tiles = []
                for (t0, ntl, latent, j) in seqs:
                    for tt in range(ntl):
                        tiles.append((t0 + tt, j))
                NYA = [0]
                for _ in stage_a(tiles[0][0], tiles[0][1], 0):
                    NYA[0] += 1
                ep = None
                for n_, (t, j) in enumerate(tiles):
                    nxt = None
                    if n_ + 1 < len(tiles):
                        nxt = stage_a(tiles[n_ + 1][0], tiles[n_ + 1][1], (n_ + 1) % 2)
                    ep = stage_b(t, j, n_ % 2, nxt, ep)
                ep()
                S.barrier()
            S.es = root

        for i in range(DEPTH):
            first = (i == 0)
            modulation(i, 0)
            if i % 2 == 0:
                retention_layer(i, first)
            else:
                attention_layer(i, first)
            modulation(i, 1)
            if peer:
                peer_layer(i)
        S.finalize()
        stats = S.stats
    return nc, stats


def _consts(SQ):
    j = np.arange(128, dtype=np.float32)[:, None]
    i = np.arange(128, dtype=np.float32)[None, :]
    cst = np.zeros((128, 6, 128), np.float32)
    cst[:, 0] = np.maximum(i - j, 0.0)
    cst[:, 1] = (i >= j)
    cst[:, 2] = np.maximum(j - i, 0.0)
    cst[:, 3] = (j > i)
    cst[:, 4] = np.where(j >= i, 0.0, -30000.0)
    cst[:, 5] = np.where(j <= i, 0.0, -30000.0)
    p = np.arange(128, dtype=np.float32)
    pvec = np.zeros((128, 8), np.float32)
    pvec[:, 0] = p + 1
    pvec[:, 1] = 128 - p
    pvec[:, 2] = 127 - p
    pvec[:, 3] = p
    iota16 = np.tile(np.arange(16, dtype=np.float32)[None, :], (128, 1))
    pos = np.arange(SQ, dtype=np.float32)
    inv = (10000.0 ** (-np.arange(0, 256, 2, dtype=np.float32) / 256.0)).astype(np.float32)
    ang = (pos[:, None] * inv[None, :]).astype(np.float32)
    rrc, rrs = np.cos(ang).astype(np.float32), np.sin(ang).astype(np.float32)
    t = np.arange(SQ)
    inv2 = (10000.0 ** (-np.arange(0, 32, 2, dtype=np.float32) / 32.0)).astype(np.float32)
    ar = ((t // 64).astype(np.float32)[:, None] * inv2[None, :]).astype(np.float32)
    ac = ((t % 64).astype(np.float32)[:, None] * inv2[None, :]).astype(np.float32)
    rac = np.concatenate([np.cos(ar), np.cos(ac)], 1).astype(np.float32)
    ras = np.concatenate([np.sin(ar), np.sin(ac)], 1).astype(np.float32)
    return dict(cst=cst, pvec=pvec, iota16=iota16, rope_ret_cos=rrc, rope_ret_sin=rrs, rope_att_cos=rac, rope_att_sin=ras)


_PROG = {}
RUNKW = {}
LAST = {}


def run_step(inp, DEPTH, SQ, ncores, peer=True):
    key = (DEPTH, SQ, peer)
    if key not in _PROG:
        _PROG[key] = build_program(DEPTH=DEPTH, SQ=SQ, peer=peer)
    nc, stats = _PROG[key]
    NRET = (DEPTH + 1) // 2
    NATT = max(DEPTH // 2, 1)
    f = lambda a: np.ascontiguousarray(np.asarray(a, dtype=np.float32))
    cs = _consts(SQ)
    shared = dict(
        mod_w=f(inp["mod_w"]), mod_b=f(inp["mod_b"]), ln_g=f(inp["ln_g"]), ln_b=f(inp["ln_b"]),
        ret_w_in=f(inp["ret_w_in"]), ret_w_out=f(inp["ret_w_out"]), ret_decay=f(inp["ret_decay"]).reshape(NRET, 8),
        attn_w_in=f(inp["attn_w_in"])[:NATT], attn_w_out=f(inp["attn_w_out"])[:NATT], attn_sink=f(inp["attn_sink"])[:NATT],
        peer_wq=f(inp["peer_wq"]), peer_keys=f(inp["peer_keys"]).reshape(DEPTH, 16, 128, 128),
        peer_uv=np.ascontiguousarray(np.concatenate([f(inp["peer_u"]).reshape(DEPTH * 16384, 1024),
                                                     f(inp["peer_v"]).reshape(DEPTH * 16384, 1024)], axis=1)), **cs)
    xp, xs = f(inp["x_prompt"]), f(inp["x_sample"])
    in_maps = []
    for c in range(ncores):
        m = dict(shared)
        m["x"] = np.ascontiguousarray(np.concatenate([xp[2 * c], xp[2 * c + 1], xs[c]], 0))
        m["cond"] = np.ascontiguousarray(np.stack([f(inp["c_ctx"]), f(inp["c"])[c]], 0))
        m["sret_f"] = f(inp["state_ret_fwd"])[c]
        m["sret_b"] = f(inp["state_ret_bwd"])[c]
        m["ck"] = np.ascontiguousarray(f(inp["cache_k"])[c][:NATT].reshape(NATT, 512, 256))
        m["cv"] = np.ascontiguousarray(f(inp["cache_v"])[c][:NATT].reshape(NATT, 512, 256))
        in_maps.append(m)
    res = run_bass_kernel_spmd(nc, in_maps, core_ids=list(range(ncores)), **RUNKW)
    LAST['res'] = res
    rs = res.results
    B = 2 * ncores
    y = np.stack([r["y"] for r in rs], 0)
    yp = y[:, :512].reshape(B, 256, 1024)
    ys = y[:, 512:]
    nsf = np.concatenate([r["nsf"] for r in rs], 0)
    nsb = np.concatenate([r["nsb"] for r in rs], 0)
    nk = np.concatenate([r["nk"] for r in rs], 0).reshape(B, NATT, 256, 4, 64)
    nv = np.concatenate([r["nv"] for r in rs], 0).reshape(B, NATT, 256, 4, 64)
    return tuple(np.ascontiguousarray(a.astype(np.float32)) for a in (yp, ys, nsf, nsb, nk, nv))


def kernel(**inputs):
    return run_step(inputs, DEPTH=4, SQ=4096, ncores=8)
```
